# Optimizing a Trainium2 kernel written in Bass

```python
import math
import jax, jax.numpy as jnp
from jax import lax
import numpy as np

D_MODEL = 4096
BATCH = 4
SEQ = 4096
DEPTH = 1

HEAD_DIM = 128
D_MIX = D_MODEL
N_HEADS = D_MIX // HEAD_DIM
GDN_HEADS = N_HEADS // 2
NSA_HEADS = N_HEADS - GDN_HEADS
NSA_KV_HEADS = max(1, NSA_HEADS // 4)
GDN_DK = GDN_HEADS * HEAD_DIM
NSA_DQ = NSA_HEADS * HEAD_DIM
NSA_DKV = NSA_KV_HEADS * HEAD_DIM
GDN_CONV = 4
GDN_CHUNK = 64
CMP_BLOCK = 32
CMP_STRIDE = 16
SEL_BLOCK = 64
SEL_TOPN = 16
SEL_LOCAL = 2
SEL_QBLOCK = 32
WINDOW = 512
WIN_QBLOCK = 128
WIN_PREV_BLOCKS = -(-WINDOW // WIN_QBLOCK)
D_FF = 4 * D_MODEL
EPS = 1e-6
IN_SIZES = (GDN_DK, GDN_DK, GDN_DK, GDN_DK, GDN_HEADS, GDN_HEADS,
            NSA_DQ, NSA_DKV, NSA_DKV, NSA_DKV, NSA_DKV, NSA_DKV, NSA_DKV, 3 * NSA_HEADS)
D_IN = 4 * GDN_DK + 2 * GDN_HEADS + NSA_DQ + 6 * NSA_DKV + 3 * NSA_HEADS

kernel_name = "hybrid_gdn_nsa_block"


def rms_norm(x, gain):
    xf = x.astype(jnp.float32)
    y = xf * lax.rsqrt(jnp.mean(xf * xf, axis=-1, keepdims=True) + EPS)
    return (y * gain.astype(jnp.float32)).astype(x.dtype)


def l2_normalize(x):
    xf = x.astype(jnp.float32)
    return xf * lax.rsqrt(jnp.sum(xf * xf, axis=-1, keepdims=True) + EPS)


def masked_softmax(s, mask):
    s = jnp.where(mask, s.astype(jnp.float32), -jnp.inf)
    m = jnp.max(s, axis=-1, keepdims=True)
    m = jnp.where(jnp.isfinite(m), m, 0.0)
    e = jnp.exp(s - m)
    d = jnp.sum(e, axis=-1, keepdims=True)
    return e / jnp.where(d > 0, d, 1.0)


def alibi_slopes(n):
    return 2.0 ** (-8.0 * jnp.arange(1, n + 1, dtype=jnp.float32) / n)


def split_columns(proj):
    out, start = [], 0
    for size in IN_SIZES:
        out.append(proj[..., start:start + size])
        start += size
    return out


def causal_depthwise_conv(x, w):
    k = w.shape[0]
    return lax.conv_general_dilated(
        x, w[:, None, :].astype(x.dtype), window_strides=(1,), padding=[(k - 1, 0)],
        dimension_numbers=("NWC", "WIO", "NWC"), feature_group_count=x.shape[-1])


def gated_delta_rule_chunked(q, k, v, g, beta):
    f32 = jnp.float32
    B, H, T, dk = q.shape
    dv = v.shape[-1]
    C = GDN_CHUNK
    N = T // C
    q = q.astype(f32).reshape(B, H, N, C, dk)
    k = k.astype(f32).reshape(B, H, N, C, dk)
    v = v.astype(f32).reshape(B, H, N, C, dv)
    g = g.astype(f32).reshape(B, H, N, C)
    beta = beta.astype(f32).reshape(B, H, N, C)
    gam = jnp.cumsum(g, axis=-1)
    causal = jnp.tril(jnp.ones((C, C), bool))
    strict = jnp.tril(jnp.ones((C, C), bool), -1)
    diff = gam[..., :, None] - gam[..., None, :]
    decay = jnp.where(causal, jnp.exp(jnp.where(causal, diff, 0.0)), 0.0)
    kk = jnp.einsum('bhncd,bhnsd->bhncs', k, k)
    lower = jnp.where(strict, beta[..., :, None] * kk * decay, 0.0)
    a = lower + jnp.eye(C, dtype=f32)
    rhs = jnp.concatenate([v * beta[..., None], k * (beta * jnp.exp(gam))[..., None]], axis=-1)
    sol = lax.linalg.triangular_solve(a, rhs, left_side=True, lower=True, unit_diagonal=True)
    u, w = sol[..., :dv], sol[..., dv:]
    qk = jnp.einsum('bhncd,bhnsd->bhncs', q, k) * decay
    q_dec = q * jnp.exp(gam)[..., None]
    g_last = gam[..., -1]
    k_dec = k * jnp.exp(g_last[..., None] - gam)[..., None]

    def step(S, inp):
        u_n, w_n, qk_n, qd_n, kd_n, gl_n = inp
        v_new = u_n - jnp.einsum('bhcd,bhde->bhce', w_n, S)
        o = jnp.einsum('bhcd,bhde->bhce', qd_n, S) + jnp.einsum('bhcs,bhse->bhce', qk_n, v_new)
        S = S * jnp.exp(gl_n)[..., None, None] + jnp.einsum('bhcd,bhce->bhde', kd_n, v_new)
        return S, o

    xs = tuple(jnp.moveaxis(t, 2, 0) for t in (u, w, qk, q_dec, k_dec, g_last))
    _, o = lax.scan(step, jnp.zeros((B, H, dk, dv), f32), xs)
    return jnp.moveaxis(o, 0, 2).reshape(B, H, T, dv)


def gdn_mixer(q, k, v, z, b_raw, a_raw, conv_w, a_log, dt_bias, norm_w):
    f32 = jnp.float32
    B, T, _ = q.shape
    qkv = jax.nn.silu(causal_depthwise_conv(jnp.concatenate([q, k, v], axis=-1), conv_w))
    q, k, v = jnp.split(qkv, 3, axis=-1)
    heads = lambda t: t.reshape(B, T, GDN_HEADS, HEAD_DIM).transpose(0, 2, 1, 3)
    q = l2_normalize(heads(q)) * (HEAD_DIM ** -0.5)
    k = l2_normalize(heads(k))
    v = heads(v)
    beta = jax.nn.sigmoid(b_raw.astype(f32)).transpose(0, 2, 1)
    g = (-jnp.exp(a_log.astype(f32)) *
         jax.nn.softplus(a_raw.astype(f32) + dt_bias.astype(f32))).transpose(0, 2, 1)
    o = gated_delta_rule_chunked(q, k, v, g, beta).transpose(0, 2, 1, 3)
    o = rms_norm(o, norm_w) * jax.nn.silu(z.reshape(B, T, GDN_HEADS, HEAD_DIM).astype(f32))
    return o.reshape(B, T, GDN_DK).astype(z.dtype)


def compress_blocks(x, pos, w1, w2):
    B, G, T, D = x.shape
    r = CMP_BLOCK // CMP_STRIDE
    n = T // CMP_STRIDE
    pieces = x.reshape(B, G, n, CMP_STRIDE, D)
    blocks = jnp.concatenate([pieces[:, :, i:n - r + 1 + i] for i in range(r)], axis=3)
    blocks = (blocks + pos.astype(x.dtype)).reshape(B, G, n - r + 1, CMP_BLOCK * D)
    return jax.nn.silu(blocks @ w1) @ w2


def cmp_to_sel_overlap(n_cmp, n_sel):
    start = jnp.arange(n_cmp)[:, None] * CMP_STRIDE
    lo = jnp.arange(n_sel)[None, :] * SEL_BLOCK
    return ((start <= lo + SEL_BLOCK - 1) & (start + CMP_BLOCK - 1 >= lo)).astype(jnp.float32)


def selected_attention(qh, ks, vs, sel_idx, slopes):
    B, G, R, T, D = qh.shape
    n = sel_idx.shape[-1]
    nq = T // SEL_QBLOCK
    kb = ks.reshape(B, G, T // SEL_BLOCK, SEL_BLOCK, D)
    vb = vs.reshape(B, G, T // SEL_BLOCK, SEL_BLOCK, D)
    gather = jax.vmap(jax.vmap(lambda blocks, ix: blocks[ix]))
    q_chunks = jnp.moveaxis(qh.reshape(B, G, R, nq, SEL_QBLOCK, D), 3, 0)
    i_chunks = jnp.moveaxis(sel_idx.reshape(B, G, nq, SEL_QBLOCK, n), 2, 0)
    offs = jnp.arange(SEL_BLOCK)
    n_keys = n * SEL_BLOCK

    def one_chunk(args):
        qc, ic, c0 = args
        kg = gather(kb, ic).reshape(B, G, SEL_QBLOCK, n_keys, D)
        vg = gather(vb, ic).reshape(B, G, SEL_QBLOCK, n_keys, D)
        t = c0 * SEL_QBLOCK + jnp.arange(SEL_QBLOCK)
        kpos = (ic[..., None] * SEL_BLOCK + offs).reshape(B, G, SEL_QBLOCK, n_keys)
        dist = (t[:, None] - kpos).astype(jnp.float32)
        s = (jnp.einsum('bgrqd,bgqkd->bgrqk', qc, kg).astype(jnp.float32)
             - slopes[:, :, None, None] * dist[:, :, None])
        p = masked_softmax(s, (dist >= 0)[:, :, None])
        return jnp.einsum('bgrqk,bgqkd->bgrqd', p.astype(vg.dtype), vg)

    o = lax.map(one_chunk, (q_chunks, i_chunks, jnp.arange(nq)))
    return jnp.moveaxis(o, 0, 3).reshape(B, G, R, T, D)


def window_attention(qh, kw, vw, slopes):
    B, G, R, T, D = qh.shape
    qb_len = WIN_QBLOCK
    nb = T // qb_len
    npv = WIN_PREV_BLOCKS
    pad = npv * qb_len
    kp = jnp.pad(kw, ((0, 0), (0, 0), (pad, 0), (0, 0))).reshape(B, G, nb + npv, qb_len, D)
    vp = jnp.pad(vw, ((0, 0), (0, 0), (pad, 0), (0, 0))).reshape(B, G, nb + npv, qb_len, D)
    kband = jnp.concatenate([kp[:, :, i:i + nb] for i in range(npv + 1)], axis=3)
    vband = jnp.concatenate([vp[:, :, i:i + nb] for i in range(npv + 1)], axis=3)
    qb = qh.reshape(B, G, R, nb, qb_len, D)
    t = jnp.arange(T).reshape(nb, qb_len)
    kpos = jnp.arange(nb)[:, None] * qb_len - pad + jnp.arange((npv + 1) * qb_len)[None, :]
    dist = t[:, :, None] - kpos[:, None, :]
    mask = (dist >= 0) & (dist < WINDOW) & (kpos[:, None, :] >= 0)
    s = (jnp.einsum('bgrnqd,bgnkd->bgrnqk', qb, kband).astype(jnp.float32)
         - slopes[:, :, None, None, None] * dist.astype(jnp.float32))
    p = masked_softmax(s, mask)
    o = jnp.einsum('bgrnqk,bgnkd->bgrnqd', p.astype(vband.dtype), vband)
    return o.reshape(B, G, R, T, D)


def nsa_mixer(q, k_c, v_c, k_s, v_s, k_w, v_w, gate_raw,
              pos_k, w1_k, w2_k, pos_v, w1_v, w2_v):
    B, T, _ = q.shape
    G = NSA_KV_HEADS
    R = NSA_HEADS // G
    D = HEAD_DIM
    slopes = alibi_slopes(NSA_HEADS).reshape(G, R)
    qh = q.reshape(B, T, G, R, D).transpose(0, 2, 3, 1, 4) * (D ** -0.5)
    kv = lambda t: t.reshape(B, T, G, D).transpose(0, 2, 1, 3)
    t_pos = jnp.arange(T)
    kc = compress_blocks(kv(k_c), pos_k, w1_k, w2_k)
    vc = compress_blocks(kv(v_c), pos_v, w1_v, w2_v)
    n_cmp = kc.shape[2]
    cmp_end = jnp.arange(n_cmp) * CMP_STRIDE + CMP_BLOCK - 1
    dist_c = t_pos[:, None] - cmp_end[None, :]
    s_c = (jnp.einsum('bgrtd,bgnd->bgrtn', qh, kc).astype(jnp.float32)
           - slopes[:, :, None, None] * dist_c.astype(jnp.float32))
    p_c = masked_softmax(s_c, dist_c >= 0)
    o_c = jnp.einsum('bgrtn,bgnd->bgrtd', p_c.astype(vc.dtype), vc)
    n_sel = T // SEL_BLOCK
    imp = jnp.einsum('bgrtn,ns->bgts', p_c, cmp_to_sel_overlap(n_cmp, n_sel))
    blk = jnp.arange(n_sel)[None, :]
    cur = (t_pos // SEL_BLOCK)[:, None]
    forced = (blk == 0) | ((cur - blk) < SEL_LOCAL)
    imp = jnp.where(forced, jnp.inf, imp)
    imp = jnp.where(blk <= cur, imp, -jnp.inf)
    _, sel_idx = lax.top_k(imp, min(SEL_TOPN, n_sel))
    o_s = selected_attention(qh, kv(k_s), kv(v_s), sel_idx, slopes)
    o_w = window_attention(qh, kv(k_w), kv(v_w), slopes)
    gates = jax.nn.sigmoid(gate_raw.astype(jnp.float32)).reshape(B, T, G, R, 3).transpose(0, 2, 3, 1, 4)
    o = gates[..., 0:1] * o_c + gates[..., 1:2] * o_s + gates[..., 2:3] * o_w
    return o.transpose(0, 3, 1, 2, 4).reshape(B, T, NSA_DQ).astype(q.dtype)


def setup_inputs(seed: int = 0) -> dict:
    key = jax.random.key(seed)
    ks = jax.random.split(key, 24)
    f32 = jnp.float32
    nrm = lambda k, shape, scale: jax.random.normal(k, shape, f32) * scale
    L = DEPTH
    x = nrm(ks[0], (BATCH, SEQ, D_MODEL), 1.0)
    c = nrm(ks[1], (BATCH, D_MODEL), 1.0)
    ada_w = nrm(ks[2], (L, D_MODEL, 6 * D_MODEL), D_MODEL ** -0.5)
    ada_b = nrm(ks[3], (L, 6 * D_MODEL), 0.02)
    norm1_w = 1.0 + nrm(ks[4], (L, D_MODEL), 0.02)
    w_in = nrm(ks[5], (L, D_MODEL, D_IN), D_MODEL ** -0.5)
    gdn_conv_w = nrm(ks[6], (L, GDN_CONV, 3 * GDN_DK), GDN_CONV ** -0.5)
    gdn_a_log = jnp.log(jax.random.uniform(ks[7], (L, GDN_HEADS), f32, 1.0, 16.0))
    dt = jnp.exp(jax.random.uniform(ks[8], (L, GDN_HEADS), f32, math.log(1e-3), math.log(1e-1)))
    gdn_dt_bias = dt + jnp.log(-jnp.expm1(-dt))
    gdn_norm_w = 1.0 + nrm(ks[9], (L, HEAD_DIM), 0.02)
    cmp_pos_k = nrm(ks[10], (L, CMP_BLOCK, HEAD_DIM), 0.02)
    cmp_w1_k = nrm(ks[11], (L, CMP_BLOCK * HEAD_DIM, HEAD_DIM), (CMP_BLOCK * HEAD_DIM) ** -0.5)
    cmp_w2_k = nrm(ks[12], (L, HEAD_DIM, HEAD_DIM), HEAD_DIM ** -0.5)
    cmp_pos_v = nrm(ks[13], (L, CMP_BLOCK, HEAD_DIM), 0.02)
    cmp_w1_v = nrm(ks[14], (L, CMP_BLOCK * HEAD_DIM, HEAD_DIM), (CMP_BLOCK * HEAD_DIM) ** -0.5)
    cmp_w2_v = nrm(ks[15], (L, HEAD_DIM, HEAD_DIM), HEAD_DIM ** -0.5)
    w_out = nrm(ks[16], (L, D_MIX, D_MODEL), D_MIX ** -0.5)
    norm2_w = 1.0 + nrm(ks[17], (L, D_MODEL), 0.02)
    w_up = nrm(ks[18], (L, D_MODEL, D_FF), D_MODEL ** -0.5)
    w_down = nrm(ks[19], (L, D_FF, D_MODEL), D_FF ** -0.5)
    final_norm_w = 1.0 + nrm(ks[20], (D_MODEL,), 0.02)
    return {"x": x, "c": c, "ada_w": ada_w, "ada_b": ada_b, "norm1_w": norm1_w, "w_in": w_in,
            "gdn_conv_w": gdn_conv_w, "gdn_a_log": gdn_a_log, "gdn_dt_bias": gdn_dt_bias,
            "gdn_norm_w": gdn_norm_w, "cmp_pos_k": cmp_pos_k, "cmp_w1_k": cmp_w1_k,
            "cmp_w2_k": cmp_w2_k, "cmp_pos_v": cmp_pos_v, "cmp_w1_v": cmp_w1_v,
            "cmp_w2_v": cmp_w2_v, "w_out": w_out, "norm2_w": norm2_w, "w_up": w_up,
            "w_down": w_down, "final_norm_w": final_norm_w}


def reference(x, c, ada_w, ada_b, norm1_w, w_in, gdn_conv_w, gdn_a_log, gdn_dt_bias, gdn_norm_w,
              cmp_pos_k, cmp_w1_k, cmp_w2_k, cmp_pos_v, cmp_w1_v, cmp_w2_v, w_out, norm2_w,
              w_up, w_down, final_norm_w):
    for i in range(DEPTH):
        mod = jax.nn.silu(c) @ ada_w[i] + ada_b[i]
        sh1, sc1, g1, sh2, sc2, g2 = jnp.split(mod[:, None, :], 6, axis=-1)
        h = rms_norm(x, norm1_w[i]) * (1.0 + sc1) + sh1
        (gq, gk, gv, gz, gb, ga, nq, nkc, nvc, nks, nvs, nkw, nvw, ngate) = split_columns(h @ w_in[i])
        o_a = gdn_mixer(gq, gk, gv, gz, gb, ga, gdn_conv_w[i], gdn_a_log[i], gdn_dt_bias[i], gdn_norm_w[i])
        o_b = nsa_mixer(nq, nkc, nvc, nks, nvs, nkw, nvw, ngate,
                        cmp_pos_k[i], cmp_w1_k[i], cmp_w2_k[i], cmp_pos_v[i], cmp_w1_v[i], cmp_w2_v[i])
        mix = jnp.concatenate([o_a, o_b], axis=-1)
        x = x + g1 * (mix @ w_out[i])
        h = rms_norm(x, norm2_w[i]) * (1.0 + sc2) + sh2
        x = x + g2 * (jnp.square(jax.nn.relu(h @ w_up[i])) @ w_down[i])
    return rms_norm(x, final_norm_w)
```

```python
import numpy as np
import ml_dtypes
from contextlib import ExitStack
import concourse.bass as bass
import concourse.mybir as mybir
from concourse.bass_utils import run_bass_kernel_spmd

F32 = mybir.dt.float32
BF16 = mybir.dt.bfloat16
I32 = mybir.dt.int32
ALU = mybir.AluOpType
AF = mybir.ActivationFunctionType
AX = mybir.AxisListType

ENGS = ("pe", "act", "dve", "pool", "sp")

T = 4096
D = 4096
DFF = 16384
NEG = -30000.0


class Prog:
    def __init__(self, nc):
        self.nc = nc
        self.ops = {e: [] for e in ENGS}
        self.nops = {e: 0 for e in ENGS}
        self.dcount = {}
        self.res = {}
        self.waited = {}
        self.awaited = {e: set() for e in ENGS}
        self.bankacc = {}
        self.dring = {}
        self.stack = ExitStack()

    def sb(self, name, shape, dt, stack=None):
        self.uid = getattr(self, "uid", 0) + 1
        name = "%s_u%d" % (name, self.uid)
        return (stack or self.stack).enter_context(self.nc.sbuf_tensor(name, list(shape), dt))

    def ps(self, name, shape, dt=F32, stack=None):
        return (stack or self.stack).enter_context(self.nc.psum_tensor(name, list(shape), dt))

    def _deps(self, eng, reads, writes):
        deps = {}
        for r in reads:
            ent = self.res.get(r)
            if ent is not None and ent[0] is not None:
                k, i = ent[0]
                if deps.get(k, -1) < i:
                    deps[k] = i
        for w in writes:
            ent = self.res.get(w)
            if ent is not None:
                if ent[0] is not None:
                    k, i = ent[0]
                    if deps.get(k, -1) < i:
                        deps[k] = i
                for k, i in ent[1].items():
                    if deps.get(k, -1) < i:
                        deps[k] = i
        waits = []
        for k, i in deps.items():
            if k == eng and eng == "pe":
                continue
            if self.waited.get((eng, k), -1) >= i:
                continue
            self.waited[(eng, k)] = i
            waits.append((k, i))
            if k in self.awaited:
                self.awaited[k].add(i)
        return waits

    def _update(self, tok, reads, writes):
        k, i = tok
        for w in writes:
            self.res[w] = [tok, {}]
        for r in reads:
            ent = self.res.get(r)
            if ent is None:
                ent = self.res[r] = [None, {}]
            if ent[1].get(k, -1) < i:
                ent[1][k] = i

    @staticmethod
    def _bank(key):
        if key.startswith("psb"):
            return int(key[3])
        if key.startswith("ps"):
            return int(key[2])
        return None

    def _bank_deps(self, eng, reads, writes, waits):
        banks = set()
        for k_ in list(reads) + list(writes):
            b = self._bank(k_)
            if b is not None:
                banks.add(b)
        for b in banks:
            acc = self.bankacc.setdefault(b, {})
            for k, i in acc.items():
                if k == eng:
                    continue
                if self.waited.get((eng, k), -1) >= i:
                    continue
                self.waited[(eng, k)] = i
                waits.append((k, i))
                self.awaited[k].add(i)
        return banks

    def op(self, eng, fn, reads=(), writes=()):
        waits = self._deps(eng, reads, writes)
        banks = self._bank_deps(eng, reads, writes, waits)
        for b in banks:
            self.bankacc[b][eng] = self.nops[eng] + 1
        self.nops[eng] += 1
        tok = (eng, self.nops[eng])
        self.ops[eng].append([waits, fn, "c", self.nops[eng]])
        self._update(tok, reads, writes)
        return tok

    NRING = {"sp": 40, "pool": 16, "act": 8}

    def dma(self, q, out, in_, sem, reads=(), writes=(), **kw):
        waits = self._deps(q, reads, writes)
        n = self.dring.get(q, 0)
        self.dring[q] = n + 1
        sem = "%s_r%d" % (q, n % self.NRING[q])
        prev = self.dcount.get(sem, 0)
        if prev and self.waited.get((q, sem), -1) < prev:
            self.waited[(q, sem)] = prev
            waits.append((sem, prev))
        self.dcount[sem] = prev + 16
        tok = (sem, self.dcount[sem])
        self.ops[q].append([waits, lambda e: e.dma_start(out=out, in_=in_, **kw), "d", sem])
        self._update(tok, reads, writes)
        return tok

    def cc(self, fn, sem, reads=(), writes=()):
        waits = self._deps("pool", reads, writes)
        self.dcount[sem] = self.dcount.get(sem, 0) + 1
        tok = (sem, self.dcount[sem])
        self.ops["pool"].append([waits, fn, "k", sem])
        self._update(tok, reads, writes)
        return tok

    def barrier(self):
        for e in ENGS:
            waits = []
            for k in ENGS:
                if k == e or self.nops[k] == 0:
                    continue
                i = self.nops[k]
                if self.waited.get((e, k), -1) >= i:
                    continue
                self.waited[(e, k)] = i
                self.awaited[k].add(i)
                waits.append((k, i))
            for k, c in self.dcount.items():
                if self.waited.get((e, k), -1) >= c:
                    continue
                self.waited[(e, k)] = c
                waits.append((k, c))
            if waits:
                self.ops[e].append([waits, None, "w", None])
        self.res = {}

    def finish(self):
        waits = [(k, c) for k, c in self.dcount.items()]
        self.ops["sp"].append([waits, None, "w", None])

    def emit(self):
        nc = self.nc
        sems = {}
        for e in ENGS:
            if self.nops[e]:
                sems[e] = self.stack.enter_context(nc.semaphore("s_" + e))
        for k in self.dcount:
            sems[k] = self.stack.enter_context(nc.semaphore("d_" + k))
        vmap = {}
        for e in ENGS:
            aw = sorted(self.awaited[e])
            vmap[e] = {idx: n + 1 for n, idx in enumerate(aw)}

        def val(k, i):
            return vmap[k][i] if k in vmap else i

        def run(e, engobj):
            for waits, fn, kind, extra in self.ops[e]:
                for k, i in waits:
                    engobj.wait_ge(sems[k], val(k, i))
                if kind == "w":
                    continue
                ins = fn(engobj)
                if kind == "c":
                    if extra in vmap[e]:
                        ins.then_inc(sems[e], 1)
                elif kind == "d":
                    ins.then_inc(sems[extra], 16)
                else:
                    ins.then_inc(sems[extra], 1)

        with nc.Block() as block:
            @block.tensor
            def _(t):
                run("pe", t)

            @block.scalar
            def _(t):
                run("act", t)

            @block.vector
            def _(t):
                run("dve", t)

            @block.gpsimd
            def _(t):
                run("pool", t)

            @block.sync
            def _(t):
                run("sp", t)


class Rot:
    def __init__(self, items):
        self.items = items
        self.i = 0

    def next(self):
        it = self.items[self.i % len(self.items)]
        self.i += 1
        return it


def const_tables():
    c = {}
    p = np.arange(128)
    q = np.arange(512)
    win = np.zeros((128, 8, 512), np.float32)
    for j in range(8):
        dist = q[None, :] - (128 * (j - 4) + p[:, None])
        win[:, j, :] = np.where((dist >= 0) & (dist < 512), 0.0, NEG)
    c["winT"] = win.astype(ml_dtypes.bfloat16)
    n = (np.arange(2)[None, :, None] * 128 + p[:, None, None])
    t = np.arange(T)[None, None, :]
    c["cmT"] = np.where((t >= 16 * n + 31) & (n < 255), 0.0, NEG).astype(ml_dtypes.bfloat16)
    s = np.arange(64)[:, None, None]
    kc = np.arange(32)[None, :, None]
    m = np.arange(128)[None, None, :]
    c["selE"] = (s == 2 * kc + m // 64).astype(np.float32).astype(ml_dtypes.bfloat16)
    n = (np.arange(2)[None, :, None] * 128 + p[:, None, None])
    sb = np.arange(64)[None, None, :]
    ov = ((16 * n <= 64 * sb + 63) & (16 * n + 31 >= 64 * sb) & (n < 255)).astype(np.float32)
    c["ovl"] = ov.astype(ml_dtypes.bfloat16)
    tt = (np.arange(32)[None, :, None] * 128 + p[:, None, None])
    cur = tt // 64
    blk = np.arange(64)[None, None, :]
    frc = np.zeros((128, 32, 64), np.float32)
    forced = (blk == 0) | ((cur - blk) < 2)
    frc = np.where(forced, 1e30 * (1.0 + 0.25 * (blk % 4)), frc)
    frc = np.where(blk <= cur, frc, -1e30)
    c["frc"] = frc.astype(np.float32)
    c["triu"] = (p[:, None] <= p[None, :]).astype(np.float32)
    c["dmask"] = np.where(p[None, :] >= p[:, None], 0.0, NEG).astype(np.float32)
    oh = (np.arange(32)[:, None, None] == np.arange(32)[None, :, None]).astype(np.float32)
    c["oneh"] = np.broadcast_to(oh, (32, 32, 128)).copy().astype(np.float32)
    c["trow"] = np.arange(512, dtype=np.float32)[None, :]
    a_ = p[:, None]; b_ = p[None, :]
    mu = np.zeros((128, 7, 128), np.float32)
    for l in range(7):
        sz_ = 2 ** l
        mu[:, l, :] = ((a_ // (2 * sz_) == b_ // (2 * sz_)) & ((a_ % (2 * sz_)) < sz_) & ((b_ % (2 * sz_)) >= sz_)).astype(np.float32)
    c["mskU"] = mu
    c["mskL"] = np.ascontiguousarray(mu.transpose(2, 1, 0))
    return c


def alibi_tables(hh):
    slopes = 2.0 ** (-8.0 * np.arange(1, 17, dtype=np.float64) / 16.0)
    p = np.arange(128, dtype=np.float64)
    hs = slopes[8 * hh:8 * hh + 8]
    rel = np.arange(32, dtype=np.float64)
    kb = hs[None, :, None] * (128.0 * (rel[None, None, :] - 28.0) + p[:, None, None])
    ncx = np.arange(2, dtype=np.float64)
    G = np.arange(8, dtype=np.float64)
    cb = hs[None, :, None, None] * (16.0 * (ncx[None, None, :, None] * 128 + p[:, None, None, None]) + 31.0
                                    - 512.0 * G[None, None, None, :])
    nsl = -hs[None, :]
    return kb.astype(np.float32), cb.astype(np.float32), nsl.astype(np.float32)


def w_in_cols(hh):
    GDN_DK, NSA_DQ, DKV = 2048, 2048, 512
    sizes = (GDN_DK, GDN_DK, GDN_DK, GDN_DK, 16, 16, NSA_DQ, DKV, DKV, DKV, DKV, DKV, DKV, 48)
    off = np.concatenate([[0], np.cumsum(sizes)])
    gq, gk, gv, gz, gb, ga, nq, nkc, nvc, nks, nvs, nkw, nvw, ngate = [int(o) for o in off[:-1]]
    h8 = np.arange(1024) + 1024 * hh
    g2 = np.arange(256) + 256 * hh
    cols = []
    for base in (gq, gk, gv):
        cols.append(base + h8)
    cols.append(nq + h8)
    for base in (nkc, nvc, nks, nkw):
        cols.append(base + g2)
    cols.append(gz + h8)
    cols.append(nvs + g2)
    cols.append(nvw + g2)
    cols.append(gb + 8 * hh + np.arange(8))
    cols.append(ga + 8 * hh + np.arange(8))
    cols.append(ngate + 24 * hh + np.arange(24))
    return np.concatenate(cols)


N_FM = 40
W_IN_COLS = 40 * 128 + 3 * 512 + 64


def build_program(dbg=None, nphase=99, start=0, only_heads=8, gdn_chunks=32, gdbg=None, lite=False):
    nc = bass.Bass("TRN2", target_bir_lowering=False)
    P = Prog(nc)
    ck = "ExternalOutput" if dbg else "Internal"

    need = {"x_b": (1,), "ada_w": (0,), "w_in_c": (0, 1), "w_out": (4,), "w_up": (4,), "w_down": (4,), "x_h": (4,),
            "w1_k": (3,), "w1_v": (3,)}

    def din(name, shape, dt=F32):
        if lite and name in need and not any(start <= ph_ <= nphase - 0 for ph_ in need[name]):
            shape = [1, 1]
        return nc.dram_tensor(name, list(shape), dt, kind="ExternalInput").ap()

    def dscr(name, shape, dt, dbgout=False, ph=99, last=99):
        if ph < start <= last:
            return nc.dram_tensor(name, list(shape), dt, kind="ExternalInput").ap()
        return nc.dram_tensor(name, list(shape), dt, kind=("ExternalOutput" if (dbg and dbgout) else "Internal")).ap()

    x_b = din("x_b", [T, D]); x_h = din("x_h", [2048, D]); cT = din("cT", [128, 32])
    ada_w = din("ada_w", [D, 6 * D]); ada_b = din("ada_b", [1, 6 * D])
    n1w = din("n1w", [1, D]); n2w = din("n2w", [1, D]); fnw = din("fnw", [1, D])
    w_in_c = din("w_in_c", [D, W_IN_COLS])
    convw = din("convw", [128, 24 * 4]); alog = din("alog", [1, 8]); dtb = din("dtb", [1, 8]); gnw = din("gnw", [1, 128])
    pos_k = din("pos_k", [32, 128]); w1_k = din("w1_k", [4096, 128]); w2_k = din("w2_k", [128, 128])
    pos_v = din("pos_v", [32, 128]); w1_v = din("w1_v", [4096, 128]); w2_v = din("w2_v", [128, 128])
    w_out = din("w_out", [D, D]); w_up = din("w_up", [D, DFF]); w_down = din("w_down", [DFF, D])
    selv = din("selv", [128, 2])
    t_winT = din("t_winT", [128, 8, 512], BF16); t_cmT = din("t_cmT", [128, 2, T], BF16)
    t_selE = din("t_selE", [64, 32, 128], BF16); t_ovl = din("t_ovl", [128, 2, 64], BF16)
    t_frc = din("t_frc", [128, 32, 64]); t_triu = din("t_triu", [128, 128]); t_dmask = din("t_dmask", [128, 128])
    t_oneh = din("t_oneh", [32, 32, 128]); t_trow = din("t_trow", [1, 512])
    t_mskU = din("t_mskU", [128, 7, 128]); t_mskL = din("t_mskL", [128, 7, 128])
    t_kb = din("t_kb", [128, 8, 32]); t_cb = din("t_cb", [128, 8, 2, 8]); t_nsl = din("t_nsl", [1, 8])

    out_h = nc.dram_tensor("out_h", [2048, D], F32, kind="ExternalOutput").ap()

    modv = dscr("modv", [8, D], F32, dbgout=True, ph=0)
    w_fm = dscr("w_fm", [N_FM, 128, 32, 128], BF16)
    w_tm = dscr("w_tm", [4, 128, 32, 512], BF16)
    gqT = dscr("gqT", [8, 128, T], BF16, True, ph=1, last=2); gkT = dscr("gkT", [8, 128, T], BF16, True, ph=1, last=2)
    gk_tm = dscr("gk_tm", [8, T, 128], BF16, True, ph=1, last=2); gv_tm = dscr("gv_tm", [8, T, 128], BF16, True, ph=1, last=2)
    sz_tm = dscr("sz_tm", [T, 1024], BF16, True, ph=1, last=2)
    nqT = dscr("nqT", [8, 128, T], BF16, True, ph=1, last=3)
    kvT = dscr("kvT", [8, 128, T], BF16, True, ph=1, last=3)
    vsw_tm = dscr("vsw_tm", [T, 512], BF16, True, ph=1, last=3)
    sm_tm = dscr("sm_tm", [T, 64], F32, True, ph=1, last=3)
    mixT = dscr("mixT", [16, 128, T], BF16, True)
    mixG = dscr("mixG", [16, 2 * 128, T], BF16, ph=3)
    w_out_b = dscr("w_out_b", [8, 128, 32, 512], BF16)
    w_up_b = dscr("w_up_b", [128, 128, 32, 128], BF16)
    w_dn_b = dscr("w_dn_b", [8, 16, 128, 8, 512], BF16)
    x1s = dscr("x1s", [2048, D], F32, True)

    ident = P.sb("ident", [128, 128], BF16)
    identf = P.sb("identf", [128, 128], F32)
    onesb = P.sb("onesb", [128, 128], BF16)
    onesf = P.sb("onesf", [128, 128], F32)
    P.op("pool", lambda e: e.memset(ident[:], 1.0), writes=["ident"])
    P.op("pool", lambda e: e.affine_select(ident[:], ident[:], pattern=[[-1, 128]], compare_op=ALU.is_equal,
                                           fill=0.0, base=0, channel_multiplier=1), reads=["ident"], writes=["ident"])
    P.op("pool", lambda e: e.memset(identf[:], 1.0), writes=["identf"])
    P.op("pool", lambda e: e.affine_select(identf[:], identf[:], pattern=[[-1, 128]], compare_op=ALU.is_equal,
                                           fill=0.0, base=0, channel_multiplier=1), reads=["identf"], writes=["identf"])
    P.op("pool", lambda e: e.memset(onesb[:], 1.0), writes=["onesb"])
    P.op("pool", lambda e: e.memset(onesf[:], 1.0), writes=["onesf"])

    psb = [P.ps("psb%d" % i, [128, 512], F32) for i in range(8)]

    for f in range(N_FM if start <= 1 else 0):
        P.dma("pool", w_fm[f], w_in_c[:, f * 128:(f + 1) * 128].rearrange("(kc p) f -> p kc f", p=128), "cvt",
              writes=["w_fm%d" % f])
    for g in range(3 if start <= 1 else 0):
        o = N_FM * 128 + g * 512
        P.dma("pool", w_tm[g], w_in_c[:, o:o + 512].rearrange("(kc p) f -> p kc f", p=128), "cvt", writes=["w_tm%d" % g])
    o = N_FM * 128 + 3 * 512
    if start <= 1:
      P.dma("pool", w_tm[3][:, :, 0:64], w_in_c[:, o:o + 64].rearrange("(kc p) f -> p kc f", p=128), "cvt", writes=["w_tm3"])
    if nphase >= 4:
        for n in range(8):
            P.dma("pool", w_out_b[n], w_out[:, n * 512:(n + 1) * 512].rearrange("(kc p) f -> p kc f", p=128), "cvt",
                  writes=["w_out_b%d" % n])
        for fg in range(16):
            src = w_up[:, fg * 1024:(fg + 1) * 1024].rearrange("(kc p) (c f) -> p c kc f", p=128, f=128)
            for cc_ in range(8):
                P.dma("pool", w_up_b[fg * 8 + cc_], src[:, cc_], "cvt", writes=["w_up_b%d" % (fg * 8 + cc_)])
        for n in range(8):
            for fg in range(16):
                src = w_down[fg * 1024:(fg + 1) * 1024, n * 512:(n + 1) * 512].rearrange("(j p) f -> p j f", p=128)
                P.dma("pool", w_dn_b[n, fg], src, "cvt", writes=["w_dn_b%d_%d" % (n, fg)])

    with ExitStack() as st:
      def _ph0():
        sT = P.sb("p0_sT", [128, 32], F32, st)
        awt = [P.sb("p0_aw%d" % i, [128, 32, 512], F32, st) for i in range(2)]
        row = [P.sb("p0_row%d" % i, [1, 512], F32, st) for i in range(2)]
        bro = [P.sb("p0_bro%d" % i, [1, 512], F32, st) for i in range(2)]
        nro = [P.sb("p0_nro%d" % i, [1, 512], F32, st) for i in range(2)]
        P.dma("sp", sT[:], cT[:, :], "p0c", writes=["sT"])
        P.op("act", lambda e: e.activation(sT[:], sT[:], AF.Silu), reads=["sT"], writes=["sT"])
        P.dma("sp", modv[6:7, :], fnw[:, :], "p0s", writes=["modv"])
        NB = 48
        for nb in range(NB):
            a = awt[nb % 2]
            kind = nb // 8
            cs = (nb % 8) * 512
            P.dma("sp", a[:], ada_w[:, nb * 512:(nb + 1) * 512].rearrange("(kc p) f -> p kc f", p=128), "p0w%d" % (nb % 2),
                  writes=["aw%d" % (nb % 2)])
            P.dma("sp", bro[nb % 2][:], ada_b[:, nb * 512:(nb + 1) * 512], "p0b%d" % (nb % 2), writes=["bro%d" % (nb % 2)])
            if kind in (1, 4):
                P.dma("sp", nro[nb % 2][:], (n1w if kind == 1 else n2w)[:, cs:cs + 512], "p0n%d" % (nb % 2), writes=["nro%d" % (nb % 2)])
            pb = psb[nb % 2]
            for kc in range(32):
                P.op("pe", lambda e, a=a, pb=pb, kc=kc: e.matmul(pb[0:1, :], sT[:, kc:kc + 1], a[:, kc, :], start=(kc == 0), stop=(kc == 31)),
                     reads=["sT", "aw%d" % (nb % 2)], writes=["ps%d" % (nb % 2)])
            r = row[nb % 2]
            br_ = bro[nb % 2]; nr_ = nro[nb % 2]
            P.op("dve", lambda e, r=r, pb=pb, br_=br_: e.tensor_tensor(r[:], pb[0:1, :], br_[:], op=ALU.add),
                 reads=["ps%d" % (nb % 2), "bro%d" % (nb % 2)], writes=["row%d" % (nb % 2)])
            if kind in (1, 4):
                P.op("dve", lambda e, r=r, nr_=nr_: e.scalar_tensor_tensor(out=r[:], in0=r[:], scalar=1.0, in1=nr_[:], op0=ALU.add, op1=ALU.mult),
                     reads=["row%d" % (nb % 2), "nro%d" % (nb % 2)], writes=["row%d" % (nb % 2)])
            dst = {0: 1, 1: 0, 2: 2, 3: 4, 4: 3, 5: 5}[kind]
            P.dma("sp", modv[dst:dst + 1, cs:cs + 512], r[:], "p0s", reads=["row%d" % (nb % 2)], writes=["modv"])
      if start <= 0:
        _ph0()
    P.barrier()
    if nphase < 1:
        P.finish(); P.emit(); return nc

    with ExitStack() as st:
      def _ph1():
        w1b = P.sb("p1_w1b", [128, D], F32, st)
        sh1b = P.sb("p1_sh1b", [128, D], F32, st)
        hT = P.sb("p1_hT", [128, 32, 1024], BF16, st)
        xt = [P.sb("p1_x%d" % i, [128, D], F32, st) for i in range(2)]
        hb = P.sb("p1_hb", [128, D], BF16, st)
        wfb = [P.sb("p1_wf%d" % i, [128, 32, 128], BF16, st) for i in range(3)]
        wtb = [P.sb("p1_wt%d" % i, [128, 32, 256], BF16, st) for i in range(1)]
        cw = P.sb("p1_cw", [128, 96], F32, st)
        halo = P.sb("p1_halo", [128, 24, 3], F32, st)
        raw = [P.sb("p1_raw%d" % i, [128, 515], F32, st) for i in range(2)]
        acc = [P.sb("p1_acc%d" % i, [128, 512], F32, st) for i in range(2)]
        sq = [P.sb("p1_sq%d" % i, [128, 512], BF16, st) for i in range(2)]
        rin = [P.sb("p1_rin%d" % i, [128, 512], F32, st) for i in range(2)]
        ob = [P.sb("p1_ob%d" % i, [128, 512], BF16, st) for i in range(3)]
        tmb = [P.sb("p1_tm%d" % i, [128, 512], BF16, st) for i in range(2)]
        smf = [P.sb("p1_smf%d" % i, [128, 64], F32, st) for i in range(2)]
        stat = P.sb("p1_stat", [128, 8], F32, st)
        dtbb = P.sb("p1_dtbb", [128, 8], F32, st)
        nab = P.sb("p1_nab", [128, 8], F32, st)
        P.dma("sp", w1b[:], modv[0:1, :].partition_broadcast(128), "p1c", reads=["modv"], writes=["w1b"])
        P.dma("sp", sh1b[:], modv[1:2, :].partition_broadcast(128), "p1c", reads=["modv"], writes=["sh1b"])
        P.dma("sp", cw[:], convw[:, :], "p1c", writes=["cw"])
        P.dma("sp", dtbb[:], dtb[0:1, :].partition_broadcast(128), "p1c", writes=["dtbb"])
        P.dma("sp", nab[:], alog[0:1, :].partition_broadcast(128), "p1c", writes=["nab"])
        P.op("act", lambda e: e.activation(nab[:], nab[:], AF.Exp), reads=["nab"], writes=["nab"])
        P.op("dve", lambda e: e.tensor_scalar(nab[:], nab[:], -1.0, None, op0=ALU.mult), reads=["nab"], writes=["nab"])
        P.op("pool", lambda e: e.memset(halo[:], 0.0), writes=["halo"])
        psT = [psb[0], psb[1]]
        psM = Rot([(psb[2], "psb2"), (psb[3], "psb3"), (psb[4], "psb4"), (psb[5], "psb5")])
        psN = Rot([(psb[6], "psb6"), (psb[7], "psb7")])
        rawR = Rot(list(zip(raw, ["raw0", "raw1"]))); accR = Rot(list(zip(acc, ["acc0", "acc1"])))
        sqR = Rot(list(zip(sq, ["sq0", "sq1"]))); rinR = Rot(list(zip(rin, ["rin0", "rin1"])))
        obR = Rot(list(zip(ob, ["ob0", "ob1", "ob2"]))); tmR = Rot(list(zip(tmb, ["tm0", "tm1"])))
        smR = Rot(list(zip(smf, ["smf0", "smf1"])))
        evq = Rot(["act", "dve"])

        def load_x(sbk, i):
            j = (sbk * 8 + i)
            P.dma("sp", xt[j % 2][:], x_b[j * 128:(j + 1) * 128, :], "p1x%d" % (j % 2), writes=["xt%d" % (j % 2)])

        load_x(0, 0)
        for sbk in range(4):
            t0s = sbk * 1024
            for i in range(8):
                j = sbk * 8 + i
                if j + 1 < 32:
                    load_x((j + 1) // 8, (j + 1) % 8)
                x = xt[j % 2]; xk = "xt%d" % (j % 2)
                P.op("act", lambda e, x=x: e.activation(hb[:], x[:], AF.Square, accum_out=stat[:, 0:1]), reads=[xk], writes=["hb", "st0"])
                P.op("act", lambda e: e.activation(stat[:, 1:2], stat[:, 0:1], AF.Sqrt, bias=1e-6, scale=1.0 / D), reads=["st0"], writes=["st1"])
                P.op("dve", lambda e: e.reciprocal(stat[:, 2:3], stat[:, 1:2]), reads=["st1"], writes=["st2"])
                P.op("dve", lambda e, x=x: e.scalar_tensor_tensor(out=x[:], in0=x[:], scalar=stat[:, 2:3], in1=w1b[:], op0=ALU.mult, op1=ALU.mult),
                     reads=[xk, "st2", "w1b"], writes=[xk])
                P.op("pool", lambda e, x=x: e.tensor_tensor(hb[:], x[:], sh1b[:], op=ALU.add), reads=[xk, "sh1b"], writes=["hb"])
                for g in range(4):
                    pt = psT[g % 2]; pk = "psb%d" % (g % 2)
                    ptb = pt[:].bitcast(BF16)
                    for u in range(8):
                        kc = g * 8 + u
                        P.op("pe", lambda e, ptb=ptb, u=u, kc=kc: e.transpose(ptb[:, u * 128:(u + 1) * 128], hb[:, kc * 128:(kc + 1) * 128], ident[:]),
                             reads=["hb", "ident"], writes=[pk])
                    q_ = evq.next()
                    dstap = hT[:, g * 8:(g + 1) * 8, i * 128:(i + 1) * 128]
                    srcap = ptb[:, 0:1024].rearrange("p (u t) -> p u t", u=8)
                    if q_ == "act":
                        P.op("act", lambda e, d=dstap, s=srcap: e.copy(d, s), reads=[pk], writes=["hT%d_%d" % (i, g)])
                    else:
                        P.op("dve", lambda e, d=dstap, s=srcap: e.tensor_copy(d, s), reads=[pk], writes=["hT%d_%d" % (i, g)])
            hT_keys = ["hT%d_%d" % (i, g) for i in range(8) for g in range(4)]
            hTh = [[("hT%d_%d" % (i, g)) for i in range(th * 4, th * 4 + 4) for g in range(4)] for th in range(2)]

            def load_wf(f):
                P.dma("sp", wfb[f % 3][:], w_fm[f], "p1wf%d" % (f % 3), reads=["w_fm%d" % f], writes=["wf%d" % (f % 3)])

            load_wf(0); load_wf(1)
            for f in range(N_FM):
                if f + 2 < N_FM:
                    load_wf(f + 2)
                w = wfb[f % 3]; wk = "wf%d" % (f % 3)
                for th in range(2):
                    pm, pmk = psM.next()
                    for kc in range(32):
                        P.op("pe", lambda e, pm=pm, w=w, kc=kc, th=th: e.matmul(pm[:, :], w[:, kc, :], hT[:, kc, th * 512:(th + 1) * 512],
                                                                              start=(kc == 0), stop=(kc == 31)),
                             reads=[wk] + (hTh[th] if kc in (0, 31) else []), writes=[pmk])
                    tok0 = t0s + th * 512
                    if f < 24:
                        kind = f // 8
                        hd = f % 8
                        rw, rk = rawR.next(); ac, ak = accR.next()
                        P.op("pool", lambda e, rw=rw, f=f: e.tensor_copy(rw[:, 0:3], halo[:, f, :]), reads=["halo%d" % f], writes=[rk + "h"])
                        P.op("act", lambda e, rw=rw, pm=pm: e.copy(rw[:, 3:515], pm[:, :]), reads=[pmk], writes=[rk])
                        P.op("pool", lambda e, rw=rw, f=f: e.tensor_copy(halo[:, f, :], rw[:, 512:515]), reads=[rk], writes=["halo%d" % f])
                        P.op("dve", lambda e, rw=rw, ac=ac, f=f: e.tensor_scalar(ac[:], rw[:, 3:515], cw[:, f * 4 + 3:f * 4 + 4], None, op0=ALU.mult),
                             reads=[rk, "cw"], writes=[ak])
                        for jj in (2, 1, 0):
                            P.op("dve", lambda e, rw=rw, ac=ac, f=f, jj=jj: e.scalar_tensor_tensor(out=ac[:], in0=rw[:, jj:jj + 512], scalar=cw[:, f * 4 + jj:f * 4 + jj + 1],
                                                                                                in1=ac[:], op0=ALU.mult, op1=ALU.add),
                                 reads=[rk, rk + "h", ak, "cw"], writes=[ak])
                        o_, ok = obR.next()
                        if kind == 2:
                            P.op("act", lambda e, ac=ac, o_=o_: e.activation(o_[:], ac[:], AF.Silu), reads=[ak], writes=[ok])
                        else:
                            P.op("act", lambda e, ac=ac: e.activation(ac[:], ac[:], AF.Silu), reads=[ak], writes=[ak])
                            s_, sk = sqR.next(); ri, rik = rinR.next()
                            P.op("pool", lambda e, ac=ac, s_=s_: e.tensor_tensor(s_[:], ac[:], ac[:], op=ALU.mult), reads=[ak], writes=[sk])
                            pn, pnk = psN.next()
                            P.op("pe", lambda e, pn=pn, s_=s_: e.matmul(pn[:, :], onesb[:], s_[:], start=True, stop=True), reads=[sk, "onesb"], writes=[pnk])
                            P.op("act", lambda e, pn=pn, ri=ri: e.activation(ri[:], pn[:, :], AF.Sqrt, bias=1e-6, scale=1.0), reads=[pnk], writes=[rik])
                            P.op("dve", lambda e, ri=ri: e.reciprocal(ri[:], ri[:]), reads=[rik], writes=[rik])
                            scl = (128.0 ** -0.5) if kind == 0 else 1.0
                            P.op("dve", lambda e, ac=ac, ri=ri, o_=o_, scl=scl: e.scalar_tensor_tensor(out=o_[:], in0=ac[:], scalar=scl, in1=ri[:], op0=ALU.mult, op1=ALU.mult),
                                 reads=[ak, rik], writes=[ok])
                        if kind == 0:
                            P.dma("sp", gqT[hd, :, tok0:tok0 + 512], o_[:], "p1o", reads=[ok], writes=["gqT"])
                        elif kind == 1:
                            P.dma("sp", gkT[hd, :, tok0:tok0 + 512], o_[:], "p1o", reads=[ok], writes=["gkT"])
                        if kind >= 1:
                            pt = psT[(f + th) % 2]; pk = "psb%d" % ((f + th) % 2)
                            ptb = pt[:].bitcast(BF16)
                            for u in range(4):
                                P.op("pe", lambda e, ptb=ptb, u=u, o_=o_: e.transpose(ptb[:, u * 128:(u + 1) * 128], o_[:, u * 128:(u + 1) * 128], ident[:]),
                                     reads=[ok, "ident"], writes=[pk])
                            tm_, tk = tmR.next()
                            P.op("act", lambda e, tm_=tm_, ptb=ptb: e.copy(tm_[:], ptb[:, 0:512]), reads=[pk], writes=[tk])
                            dstt = (gk_tm if kind == 1 else gv_tm)[hd, tok0:tok0 + 512, :].rearrange("(u p) d -> p u d", p=128)
                            P.dma("sp", dstt, tm_[:].rearrange("p (u d) -> p u d", u=4), "p1o", reads=[tk], writes=["gtm"])
                    else:
                        o_, ok = obR.next()
                        if f < 32:
                            P.op("act", lambda e, o_=o_, pm=pm: e.activation(o_[:], pm[:, :], AF.Copy, scale=128.0 ** -0.5), reads=[pmk], writes=[ok])
                            P.dma("sp", nqT[f - 24, :, tok0:tok0 + 512], o_[:], "p1o", reads=[ok], writes=["nqT"])
                        else:
                            P.op("act", lambda e, o_=o_, pm=pm: e.copy(o_[:], pm[:, :]), reads=[pmk], writes=[ok])
                            P.dma("sp", kvT[f - 32, :, tok0:tok0 + 512], o_[:], "p1o", reads=[ok], writes=["kvT"])
            for g2 in range(7):
                wt = wtb[0]
                g = g2 // 2 if g2 < 6 else 3
                hf = g2 % 2
                ncol = 256 if g < 3 else 64
                c0 = hf * 256 if g < 3 else 0
                P.dma("sp", wt[:, :, 0:ncol], w_tm[g][:, :, c0:c0 + ncol], "p1wt", reads=["w_tm%d" % g], writes=["wt"])
                for i in range(8):
                    pm, pmk = psM.next()
                    for kc in range(32):
                        P.op("pe", lambda e, pm=pm, wt=wt, kc=kc, i=i, ncol=ncol: e.matmul(pm[:, 0:ncol], hT[:, kc, i * 128:(i + 1) * 128], wt[:, kc, 0:ncol],
                                                                                        start=(kc == 0), stop=(kc == 31)),
                             reads=["wt"] + (["hT%d_%d" % (i, gg) for gg in range(4)] if kc in (0, 31) else []), writes=[pmk])
                    tok0 = t0s + i * 128
                    if g < 2:
                        o_, ok = obR.next()
                        P.op("act", lambda e, o_=o_, pm=pm: e.activation(o_[:, 0:256], pm[:, 0:256], AF.Silu), reads=[pmk], writes=[ok])
                        P.dma("sp", sz_tm[tok0:tok0 + 128, g * 512 + c0:g * 512 + c0 + 256], o_[:, 0:256], "p1o", reads=[ok], writes=["sz_tm"])
                    elif g == 2:
                        o_, ok = obR.next()
                        P.op("act", lambda e, o_=o_, pm=pm: e.copy(o_[:, 0:256], pm[:, 0:256]), reads=[pmk], writes=[ok])
                        P.dma("sp", vsw_tm[tok0:tok0 + 128, c0:c0 + 256], o_[:, 0:256], "p1o", reads=[ok], writes=["vsw_tm"])
                    else:
                        s_, sk = smR.next()
                        P.op("act", lambda e, s_=s_, pm=pm: e.activation(s_[:, 0:8], pm[:, 0:8], AF.Sigmoid), reads=[pmk], writes=[sk + "a"])
                        P.op("act", lambda e, s_=s_, pm=pm: e.activation(s_[:, 16:40], pm[:, 16:40], AF.Sigmoid), reads=[pmk], writes=[sk + "c"])
                        P.op("dve", lambda e, s_=s_, pm=pm: e.tensor_tensor(s_[:, 8:16], pm[:, 8:16], dtbb[:], op=ALU.add), reads=[pmk, "dtbb"], writes=[sk + "b"])
                        P.op("act", lambda e, s_=s_: e.activation(s_[:, 8:16], s_[:, 8:16], AF.Exp), reads=[sk + "b"], writes=[sk + "b"])
                        P.op("act", lambda e, s_=s_: e.activation(s_[:, 8:16], s_[:, 8:16], AF.Ln, bias=1.0, scale=1.0), reads=[sk + "b"], writes=[sk + "b"])
                        P.op("dve", lambda e, s_=s_: e.tensor_tensor(s_[:, 8:16], s_[:, 8:16], nab[:], op=ALU.mult), reads=[sk + "b", "nab"], writes=[sk + "b"])
                        P.op("pool", lambda e, s_=s_: e.memset(s_[:, 40:64], 0.0), writes=[sk + "d"])
                        P.dma("sp", sm_tm[tok0:tok0 + 128, :], s_[:], "p1o", reads=[sk + "a", sk + "b", sk + "c", sk + "d"], writes=["sm_tm"])
      if start <= 1:
        _ph1()
    P.barrier()
    if nphase < 2:
        P.finish(); P.emit(); return nc

    with ExitStack() as st:
      def _ph2():
        sm_all = P.sb("p2_sm", [128, 32, 64], F32, st)
        triu = P.sb("p2_triu", [128, 128], F32, st)
        dmask = P.sb("p2_dmask", [128, 128], F32, st)
        oneh = P.sb("p2_oneh", [32, 32, 128], F32, st)
        gnwb = P.sb("p2_gnwb", [128, 128], F32, st)
        mskU = P.sb("p2_mskU", [128, 7, 128], F32, st)
        mskL = P.sb("p2_mskL", [128, 7, 128], F32, st)
        P.dma("sp", mskU[:], t_mskU[:, :, :], "p2c", writes=["mskU"])
        P.dma("sp", mskL[:], t_mskL[:, :, :], "p2c", writes=["mskL"])
        P.dma("sp", sm_all[:], sm_tm.rearrange("(c p) f -> p c f", p=128), "p2c", reads=["sm_tm"], writes=["sm_all"])
        P.dma("sp", triu[:], t_triu[:, :], "p2c", writes=["triu"])
        P.dma("sp", dmask[:], t_dmask[:, :], "p2c", writes=["dmask"])
        P.dma("sp", oneh[:], t_oneh[:, :, :], "p2c", writes=["oneh"])
        P.dma("sp", gnwb[:], gnw[0:1, :].partition_broadcast(128), "p2c", writes=["gnwb"])
        HB = []
        for par in range(2):
            hbuf = {}
            for nm in ("qT", "kT"):
                hbuf[nm] = P.sb("p2_%s%d" % (nm, par), [128, T], BF16, st)
            for nm in ("ktm", "vtm", "sz", "XT", "QKD", "QdT", "Kd"):
                hbuf[nm] = P.sb("p2_%s%d" % (nm, par), [128, 32, 128], BF16, st)
            hbuf["mix"] = P.sb("p2_mix%d" % par, [128, T], BF16, st) if par == 0 else HB[0]["mix"]
            hbuf["sc"] = P.sb("p2_sc%d" % par, [128, 6, 32], F32, st)
            hbuf["gamT"] = P.sb("p2_gamT%d" % par, [32, 128], F32, st)
            HB.append(hbuf)
        S = P.sb("p2_S", [128, 128], F32, st)
        Sb = P.sb("p2_Sb", [128, 128], BF16, st)
        tA = [dict(dd=P.sb("p2_dd%d" % i, [128, 128], F32, st), DT=P.sb("p2_DT%d" % i, [128, 128], F32, st),
                   Eg=P.sb("p2_Eg%d" % i, [128, 128], F32, st), Mf=P.sb("p2_Mf%d" % i, [128, 128], F32, st),
                   Lf=P.sb("p2_Lf%d" % i, [128, 128], F32, st), Cu=P.sb("p2_Cu%d" % i, [128, 128], F32, st), Cl=P.sb("p2_Cl%d" % i, [128, 128], F32, st),
                   T1=P.sb("p2_T1%d" % i, [128, 128], F32, st), T1p=P.sb("p2_T1p%d" % i, [128, 128], F32, st),
                   U=[P.sb("p2_U%d_%d" % (i, j), [128, 128], F32, st) for j in range(2)],
                   L=[P.sb("p2_L%d_%d" % (i, j), [128, 128], F32, st) for j in range(2)]) for i in range(2)]
        tB = [dict(R=P.sb("p2_R%d" % i, [128, 128], BF16, st), vn=P.sb("p2_vn%d" % i, [128, 128], BF16, st),
                   j1=P.sb("p2_j1%d" % i, [128, 128], F32, st), j2=P.sb("p2_j2%d" % i, [128, 128], F32, st),
                   om=P.sb("p2_om%d" % i, [128, 128], BF16, st), st=P.sb("p2_st%d" % i, [128, 4], F32, st)) for i in range(2)]

        def head_load(hd):
            par = hd % 2; hb_ = HB[par]; k = "h%d" % par
            P.dma("sp", hb_["qT"][:], gqT[hd], "p2l%d" % par, reads=["gqT"], writes=[k + "qT"])
            P.dma("sp", hb_["kT"][:], gkT[hd], "p2l%d" % par, reads=["gkT"], writes=[k + "kT"])
            P.dma("sp", hb_["ktm"][:], gk_tm[hd].rearrange("(c p) d -> p c d", p=128), "p2l%d" % par, reads=["gtm"], writes=[k + "ktm"])
            P.dma("sp", hb_["vtm"][:], gv_tm[hd].rearrange("(c p) d -> p c d", p=128), "p2l%d" % par, reads=["gtm"], writes=[k + "vtm"])
            P.dma("sp", hb_["sz"][:], sz_tm[:, hd * 128:(hd + 1) * 128].rearrange("(c p) d -> p c d", p=128), "p2l%d" % par,
                  reads=["sz_tm"], writes=[k + "sz"])

        def head_pre(hd):
            par = hd % 2; hb_ = HB[par]; k = "h%d" % par
            sc = hb_["sc"]
            g_h = sc[:, 5, :]
            P.op("dve", lambda e: e.tensor_copy(sc[:, 5, :], sm_all[:, :, 8 + hd]), reads=["sm_all"], writes=[k + "gh"])
            P.op("pe", lambda e: e.matmul(psb[0][:, 0:32], triu[:], g_h, start=True, stop=True), reads=[k + "gh", "triu"], writes=["ps0a"])
            P.op("pe", lambda e: e.matmul(psb[0][:, 32:64], onesf[:], g_h, start=True, stop=True), reads=[k + "gh", "onesf"], writes=["ps0b"])
            if gdbg == "pre0":
                return
            P.op("act", lambda e: e.copy(sc[:, 0, :], psb[0][:, 0:32]), reads=["ps0a"], writes=[k + "gam"])
            P.op("dve", lambda e: e.tensor_scalar(sc[:, 1, :], psb[0][:, 0:32], -1.0, None, op0=ALU.mult), reads=["ps0a"], writes=[k + "ngam"])
            P.op("act", lambda e: e.activation(sc[:, 2, :], psb[0][:, 0:32], AF.Exp), reads=["ps0a"], writes=[k + "negeg"])
            P.op("dve", lambda e: e.tensor_scalar(sc[:, 2, :], sc[:, 2, :], -1.0, None, op0=ALU.mult), reads=[k + "negeg"], writes=[k + "negeg"])
            P.op("dve", lambda e: e.tensor_tensor(sc[:, 3, :], psb[0][:, 32:64], sc[:, 0, :], op=ALU.subtract), reads=["ps0b", k + "gam"], writes=[k + "kdec"])
            P.op("act", lambda e: e.activation(sc[:, 3, :], sc[:, 3, :], AF.Exp), reads=[k + "kdec"], writes=[k + "kdec"])
            P.op("act", lambda e: e.activation(sc[:, 4, :], psb[0][:, 32:64], AF.Exp), reads=["ps0b"], writes=[k + "egl"])
            if gdbg == "pre1":
                return
            P.op("pe", lambda e: e.matmul(psb[0][0:32, 128:256], sc[:, 0, :], identf[:], start=True, stop=True), reads=[k + "gam", "identf"], writes=["ps0c"])
            P.op("act", lambda e: e.copy(hb_["gamT"][:], psb[0][0:32, 128:256]), reads=["ps0c"], writes=[k + "gamT"])

        def GA(hd, c):
            par = hd % 2; hb_ = HB[par]; k = "h%d" % par
            pc = c % 2; ta = tA[pc]; a_ = "A%d" % pc
            sc = hb_["sc"]
            cs = slice(c * 128, (c + 1) * 128)
            qTc = hb_["qT"][:, cs]; kTc = hb_["kT"][:, cs]
            bA = 1 + 2 * pc; bB = 2 + 2 * pc
            pG = psb[bA][:, 0:128]; pGk = "ps%dG" % bA
            pKK = psb[bA][:, 128:256]; pQK = psb[bA][:, 256:384]; pKk = "ps%dK" % bA
            pP = psb[bB][:, 0:128]; pQ = psb[bB][:, 128:256]
            pPk = "ps%dP" % bB; pQk = "ps%dQ" % bB
            pY = psb[bB][:, 256:384]; pYk = "ps%dY" % bB
            pLt = psb[bA][:, 384:512]; pLk = "ps%dL" % bA
            P.op("pe", lambda e: e.matmul(pG, oneh[:, c, :], hb_["gamT"][:], start=True, stop=True), reads=["oneh", k + "gamT"], writes=[pGk])
            P.op("dve", lambda e: e.scalar_tensor_tensor(out=ta["dd"][:], in0=pG, scalar=sc[:, 1, c:c + 1], in1=dmask[:], op0=ALU.add, op1=ALU.add),
                 reads=[pGk, k + "ngam", "dmask"], writes=[a_ + "dd"])
            P.op("act", lambda e: e.activation(ta["DT"][:], ta["dd"][:], AF.Exp), reads=[a_ + "dd"], writes=[a_ + "DT"])
            P.op("act", lambda e: e.activation(ta["Eg"][:], pG, AF.Exp), reads=[pGk], writes=[a_ + "Eg"])
            P.op("pool", lambda e: e.tensor_tensor(hb_["QdT"][:, c, :], qTc, ta["Eg"][:], op=ALU.mult), reads=[k + "qT", a_ + "Eg"], writes=[k + "QdT%d" % c])
            P.op("pool", lambda e: e.tensor_scalar(hb_["Kd"][:, c, :], hb_["ktm"][:, c, :], sc[:, 3, c:c + 1], None, op0=ALU.mult),
                 reads=[k + "ktm", k + "kdec"], writes=[k + "Kd%d" % c])
            P.op("pe", lambda e: e.matmul(pKK, kTc, kTc, start=True, stop=True), reads=[k + "kT"], writes=[pKk + "a"])
            P.op("pe", lambda e: e.matmul(pQK, kTc, qTc, start=True, stop=True), reads=[k + "kT", k + "qT"], writes=[pKk + "b"])
            P.op("dve", lambda e: e.scalar_tensor_tensor(out=ta["Mf"][:], in0=pKK, scalar=sm_all[:, c, hd:hd + 1], in1=ta["DT"][:], op0=ALU.mult, op1=ALU.mult),
                 reads=[pKk + "a", "sm_all", a_ + "DT"], writes=[a_ + "Mf"])
            P.op("dve", lambda e: e.tensor_tensor(hb_["QKD"][:, c, :], pQK, ta["DT"][:], op=ALU.mult), reads=[pKk + "b", a_ + "DT"], writes=[k + "QKD%d" % c])
            Mf, Lf, Cu, Cl, T1, T1p, U, L = ta["Mf"], ta["Lf"], ta["Cu"], ta["Cl"], ta["T1"], ta["T1p"], ta["U"], ta["L"]
            pT1 = psb[bB][:, 0:128]; pT1p = psb[bB][:, 128:256]; pT2 = psb[bB][:, 256:384]; pT2p = psb[bB][:, 384:512]
            kb_ = "ps%d" % bB
            P.op("pe", lambda e: e.matmul(pLt, Mf[:], identf[:], start=True, stop=True), reads=[a_ + "Mf", "identf"], writes=[pLk])
            P.op("act", lambda e: e.copy(Lf[:], pLt), reads=[pLk], writes=[a_ + "Lf"])
            P.op("pool", lambda e: e.tensor_tensor(Cu[:], Mf[:], mskU[:, 0, :], op=ALU.mult), reads=[a_ + "Mf", "mskU"], writes=[a_ + "Cu"])
            P.op("pool", lambda e: e.tensor_tensor(Cl[:], Lf[:], mskL[:, 0, :], op=ALU.mult), reads=[a_ + "Lf", "mskL"], writes=[a_ + "Cl"])
            P.op("dve", lambda e: e.tensor_tensor(U[0][:], identf[:], Cu[:], op=ALU.subtract), reads=["identf", a_ + "Cu"], writes=[a_ + "U0"])
            P.op("dve", lambda e: e.tensor_tensor(L[0][:], identf[:], Cl[:], op=ALU.subtract), reads=["identf", a_ + "Cl"], writes=[a_ + "L0"])
            cur = 0
            for lvl in range(1, 7):
                nx = 1 - cur
                lastl = (lvl == 6)
                P.op("pool", lambda e, lvl=lvl: e.tensor_tensor(Cl[:], Lf[:], mskL[:, lvl, :], op=ALU.mult), reads=[a_ + "Lf", "mskL"], writes=[a_ + "Cl"])
                P.op("pe", lambda e, cur=cur: e.matmul(pT1, Cl[:], U[cur][:], start=True, stop=True), reads=[a_ + "Cl", a_ + "U%d" % cur], writes=[kb_ + "a"])
                P.op("act", lambda e: e.copy(T1[:], pT1), reads=[kb_ + "a"], writes=[a_ + "T1"])
                if not lastl:
                    P.op("pool", lambda e, lvl=lvl: e.tensor_tensor(Cu[:], Mf[:], mskU[:, lvl, :], op=ALU.mult), reads=[a_ + "Mf", "mskU"], writes=[a_ + "Cu"])
                    P.op("pe", lambda e, cur=cur: e.matmul(pT1p, Cu[:], L[cur][:], start=True, stop=True), reads=[a_ + "Cu", a_ + "L%d" % cur], writes=[kb_ + "b"])
                    P.op("dve", lambda e: e.tensor_copy(T1p[:], pT1p), reads=[kb_ + "b"], writes=[a_ + "T1p"])
                P.op("pe", lambda e, cur=cur: e.matmul(pT2, L[cur][:], T1[:], start=True, stop=True), reads=[a_ + "L%d" % cur, a_ + "T1"], writes=[kb_ + "c"])
                if not lastl:
                    P.op("pe", lambda e, cur=cur: e.matmul(pT2p, U[cur][:], T1p[:], start=True, stop=True), reads=[a_ + "U%d" % cur, a_ + "T1p"], writes=[kb_ + "d"])
                    P.op("dve", lambda e, cur=cur, nx=nx: e.tensor_tensor(U[nx][:], U[cur][:], pT2, op=ALU.subtract), reads=[kb_ + "c", a_ + "U%d" % cur], writes=[a_ + "U%d" % nx])
                    P.op("dve", lambda e, cur=cur, nx=nx: e.tensor_tensor(L[nx][:], L[cur][:], pT2p, op=ALU.subtract), reads=[kb_ + "d", a_ + "L%d" % cur], writes=[a_ + "L%d" % nx])
                else:
                    P.op("dve", lambda e, cur=cur: e.tensor_tensor(hb_["XT"][:, c, :], U[cur][:], pT2, op=ALU.subtract), reads=[kb_ + "c", a_ + "U%d" % cur], writes=[k + "XT%d" % c])
                cur = nx

        def GB(hd, c):
            par = hd % 2; hb_ = HB[par]; k = "h%d" % par
            pc = c % 2; tb = tB[pc]; b_ = "B%d" % pc
            sc = hb_["sc"]
            cs = slice(c * 128, (c + 1) * 128)
            kTc = hb_["kT"][:, cs]
            pKS = psb[5][:, 0:128]; pVN = psb[5][:, 128:256]
            pO = psb[6][:, 0:128] if pc == 0 else psb[0][:, 256:384]
            pOk = "ps6_O" if pc == 0 else "ps0_O"
            pD = psb[7][:, 0:128]
            pOt = psb[7][:].bitcast(BF16)[:, 512:640]
            if c == 0:
                P.op("pool", lambda e: e.memset(S[:], 0.0), writes=["S"])
                P.op("pool", lambda e: e.memset(Sb[:], 0.0), writes=["Sb"])
            P.op("pe", lambda e: e.matmul(pKS, kTc, Sb[:], start=True, stop=True), reads=[k + "kT", "Sb"], writes=["ps5a"])
            P.op("dve", lambda e: e.scalar_tensor_tensor(out=tb["R"][:], in0=pKS, scalar=sc[:, 2, c:c + 1], in1=hb_["vtm"][:, c, :], op0=ALU.mult, op1=ALU.add),
                 reads=["ps5a", k + "negeg", k + "vtm"], writes=[b_ + "R"])
            P.op("pe", lambda e: e.matmul(pVN, hb_["XT"][:, c, :], tb["R"][:], start=True, stop=True), reads=[k + "XT%d" % c, b_ + "R"], writes=["ps5b"])
            P.op("act", lambda e: e.activation(tb["vn"][:], pVN, AF.Copy, scale=sm_all[:, c, hd:hd + 1]), reads=["ps5b", "sm_all"], writes=[b_ + "vn"])
            P.op("pe", lambda e: e.matmul(pO, hb_["QdT"][:, c, :], Sb[:], start=True, stop=False), reads=[k + "QdT%d" % c, "Sb"], writes=[pOk])
            P.op("pe", lambda e: e.matmul(pO, hb_["QKD"][:, c, :], tb["vn"][:], start=False, stop=True), reads=[k + "QKD%d" % c, b_ + "vn"], writes=[pOk])
            P.op("pe", lambda e: e.matmul(pD, hb_["Kd"][:, c, :], tb["vn"][:], start=True, stop=True), reads=[k + "Kd%d" % c, b_ + "vn"], writes=["ps7d"])
            P.op("dve", lambda e: e.scalar_tensor_tensor(out=S[:], in0=S[:], scalar=sc[:, 4, c:c + 1], in1=pD, op0=ALU.mult, op1=ALU.add),
                 reads=["S", k + "egl", "ps7d"], writes=["S"])
            P.op("act", lambda e: e.copy(Sb[:], S[:]), reads=["S"], writes=["Sb"])
            P.op("act", lambda e: e.activation(tb["j1"][:], pO, AF.Square, accum_out=tb["st"][:, 0:1]), reads=[pOk], writes=[b_ + "j1", b_ + "s0"])
            P.op("act", lambda e: e.activation(tb["st"][:, 1:2], tb["st"][:, 0:1], AF.Sqrt, bias=1e-6, scale=1.0 / 128.0), reads=[b_ + "s0"], writes=[b_ + "s1"])
            P.op("dve", lambda e: e.reciprocal(tb["st"][:, 2:3], tb["st"][:, 1:2]), reads=[b_ + "s1"], writes=[b_ + "s2"])
            P.op("dve", lambda e: e.scalar_tensor_tensor(out=tb["j2"][:], in0=pO, scalar=tb["st"][:, 2:3], in1=gnwb[:], op0=ALU.mult, op1=ALU.mult),
                 reads=[pOk, b_ + "s2", "gnwb"], writes=[b_ + "j2"])
            P.op("pool", lambda e: e.tensor_tensor(tb["om"][:], tb["j2"][:], hb_["sz"][:, c, :], op=ALU.mult), reads=[b_ + "j2", k + "sz"], writes=[b_ + "om"])
            P.op("pe", lambda e: e.transpose(pOt, tb["om"][:], ident[:]), reads=[b_ + "om", "ident"], writes=["ps7t"])
            P.op("act", lambda e: e.copy(hb_["mix"][:, cs], pOt), reads=["ps7t"], writes=["mix%d" % c])
            if c == 31:
                P.dma("sp", mixT[hd], hb_["mix"][:], "p2o", reads=["mix%d" % cc_ for cc_ in range(32)], writes=["mixT"])

        head_load(0)
        if gdbg == "load":
            pass
        elif gdbg in ("pre", "pre0", "pre1"):
            head_pre(0)
        elif gdbg == "ga1":
            head_pre(0); GA(0, 0)
        elif gdbg == "ga":
            head_pre(0)
            for c in range(32):
                GA(0, c)
        elif gdbg == "gb1":
            head_pre(0)
            for c in range(32):
                GA(0, c)
            GB(0, 0)
        else:
          NH = only_heads
          def dump(name, ap_, shape, dt, reads):
              t_ = nc.dram_tensor(name, list(shape), dt, kind="ExternalOutput").ap()
              P.dma("sp", t_, ap_, "dbgd", reads=reads)
          for hd in range(NH + 1):
            if dbg and gdbg == "dump" and hd == NH:
                hb_ = HB[(NH - 1) % 2]; k = "h%d" % ((NH - 1) % 2)
                allk = [k + "%s%d" % (nm, c_) for nm in ("XT", "QKD", "QdT", "Kd") for c_ in range(32)]
                for nm in ("XT", "QKD", "QdT", "Kd", "ktm", "vtm", "sz"):
                    dump("d_" + nm, hb_[nm][:], [128, 32, 128], BF16, allk + [k + "ktm", k + "vtm", k + "sz"])
                dump("d_sc", hb_["sc"][:], [128, 6, 32], F32, [k + x for x in ("gam", "ngam", "negeg", "kdec", "egl", "gh")])
                dump("d_gamT", hb_["gamT"][:], [32, 128], F32, [k + "gamT"])
                dump("d_kT", hb_["kT"][:], [128, T], BF16, [k + "kT"])
            if hd < NH:
                head_pre(hd)
            for c in range(32):
                if hd < NH:
                    GA(hd, c)
                if hd > 0:
                    GB(hd - 1, c)
            if hd + 1 < NH:
                head_load(hd + 1)
      if start <= 2:
        _ph2()
    P.barrier()
    if nphase < 3:
        P.finish(); P.emit(); return nc

    with ExitStack() as st:
      def _ph3():
        sm_all = P.sb("p3_sm", [128, 32, 64], F32, st)
        winT = P.sb("p3_winT", [128, 8, 512], BF16, st)
        cmT = P.sb("p3_cmT", [128, 2, T], BF16, st)
        selE = P.sb("p3_selE", [64, 32, 128], BF16, st)
        frc = P.sb("p3_frc", [128, 32, 64], F32, st)
        kbt = P.sb("p3_kb", [128, 8, 32], F32, st)
        cbt = P.sb("p3_cb", [128, 8, 2, 8], F32, st)
        nslt = P.sb("p3_nsl", [1, 8], F32, st)
        trow = P.sb("p3_trow", [1, 512], F32, st)
        zer = P.sb("p3_zer", [128, 512], BF16, st)
        w1b = [P.sb("p3_w1%d" % i, [128, 32, 128], BF16, st) for i in range(2)]
        w2b = [P.sb("p3_w2%d" % i, [128, 128], BF16, st) for i in range(2)]
        posf = [P.sb("p3_pos%d" % i, [32, 128], F32, st) for i in range(2)]
        posT = [P.sb("p3_posT%d" % i, [128, 32], BF16, st) for i in range(2)]
        cvec = [P.sb("p3_cvec%d" % i, [128, 1], F32, st) for i in range(2)]
        kT4 = [P.sb("p3_kT%d" % i, [128, T], BF16, st) for i in range(4)]
        kcD = P.sb("p3_kcD", [128, 16, 256], BF16, st)
        hcm = P.sb("p3_hcm", [128, 256], BF16, st)
        kcmpT = P.sb("p3_kcmpT", [128, 256], BF16, st)
        VC = P.sb("p3_VC", [128, 2, 193], BF16, st)
        VS = P.sb("p3_VS", [128, 32, 129], BF16, st)
        VW = P.sb("p3_VW", [128, 32, 129], BF16, st)
        negK = P.sb("p3_negK", [1, 4], F32, st)
        kmx = P.sb("p3_kmx", [1, 32], F32, st)
        sqt = [P.sb("p3_sq%d" % i, [128, 512], BF16, st) for i in range(2)]
        qsb = [P.sb("p3_q%d" % i, [128, 4, 512], BF16, st) for i in range(2)]
        qn = P.sb("p3_qn", [1, 512], F32, st)
        srow = P.sb("p3_srow", [1, 512], F32, st)
        rrow = P.sb("p3_rrow", [1, 3, 512], BF16, st)
        PT = [P.sb("p3_PT%d" % i, [128, 512], BF16, st) for i in range(3)]
        oacc = P.sb("p3_oacc", [128, 4, 4, 128], F32, st)
        imp = P.sb("p3_imp", [128, 4, 64], F32, st)
        impp = P.sb("p3_impp", [128, 64], F32, st)
        impq = P.sb("p3_impq", [128, 64], F32, st)
        mx8 = P.sb("p3_mx8", [128, 16], F32, st)
        nsel = P.sb("p3_nsel", [128, 64], BF16, st)
        negselT = P.sb("p3_nselT", [64, 512], BF16, st)
        stt = P.sb("p3_stt", [128, 16], F32, st)
        ofin = P.sb("p3_ofin", [128, 128], BF16, st)
        mstage = [P.sb("p3_ms%d" % i, [128, 512], BF16, st) for i in range(2)]
        P.dma("sp", sm_all[:], sm_tm.rearrange("(c p) f -> p c f", p=128), "x", reads=["sm_tm"], writes=["sm_all"])
        for nm_, t_, src_ in (("winT", winT, t_winT), ("cmT", cmT, t_cmT), ("selE", selE, t_selE), ("frc", frc, t_frc),
                              ("kbt", kbt, t_kb), ("cbt", cbt, t_cb), ("nslt", nslt, t_nsl), ("trow", trow, t_trow)):
            P.dma("sp", t_[:], src_, "x", writes=[nm_])
        P.op("pool", lambda e: e.memset(zer[:], 0.0), writes=["zer"])
        for i, (w1_, w2_, pos_) in enumerate(((w1_k, w2_k, pos_k), (w1_v, w2_v, pos_v))):
            P.dma("pool", w1b[i][:], w1_.rearrange("(j d) o -> d j o", d=128), "x", writes=["w1b%d" % i])
            P.dma("pool", w2b[i][:], w2_[:, :], "x", writes=["w2b%d" % i])
            P.dma("sp", posf[i][:], pos_[:, :], "x", writes=["posf%d" % i])
            P.op("pe", lambda e, i=i: e.matmul(psb[6][:, 0:32], posf[i][:], identf[0:32, 0:32], start=True, stop=True), reads=["posf%d" % i, "identf"], writes=["ps6"])
            P.op("act", lambda e, i=i: e.copy(posT[i][:], psb[6][:, 0:32]), reads=["ps6"], writes=["posT%d" % i])
            for j in range(32):
                P.op("pe", lambda e, i=i, j=j: e.matmul(psb[6][:, 64:65], w1b[i][:, j, :], posT[i][:, j:j + 1], start=(j == 0), stop=(j == 31)),
                     reads=["w1b%d" % i, "posT%d" % i], writes=["ps6"])
            P.op("act", lambda e, i=i: e.copy(cvec[i][:], psb[6][:, 64:65]), reads=["ps6"], writes=["cvec%d" % i])

        psS = Rot([(psb[0], "ps0"), (psb[1], "ps1")])
        PTR = Rot([(PT[0], "PT0"), (PT[1], "PT1"), (PT[2], "PT2")])
        sqR = Rot([(sqt[0], "sq0"), (sqt[1], "sq1")])

        def zero_bank(b):
            P.op("pe", lambda e, b=b: e.matmul(psb[b][:, :], zer[:, 0:128], zer[:, :], start=True, stop=True, skip_group_check=True),
                 reads=["zer"], writes=["ps%d" % b])

        def do_group(gl):
            gk = "g"
            for i in range(4):
                P.dma("sp", kT4[i][:], kvT[2 * i + gl], "x", reads=["kvT"], writes=["kT4_%d" % i])
            P.dma("sp", VS[:, :, 0:128], vsw_tm[:, gl * 128:(gl + 1) * 128].rearrange("(c p) d -> p c d", p=128), "x", reads=["vsw_tm"], writes=["VSd"])
            P.dma("sp", VW[:, :, 0:128], vsw_tm[:, 256 + gl * 128:256 + (gl + 1) * 128].rearrange("(c p) d -> p c d", p=128), "x", reads=["vsw_tm"], writes=["VWd"])
            P.op("pool", lambda e: e.memset(VS[:, :, 128:129], 1.0), writes=["VS1"])
            P.op("pool", lambda e: e.memset(VW[:, :, 128:129], 1.0), writes=["VW1"])
            for i in range(2):
                src = kT4[i]
                P.op("dve", lambda e, src=src: e.tensor_copy(kcD[:], src[:].rearrange("p (n r) -> p r n", r=16)), reads=["kT4_%d" % i], writes=["kcD"])
                for j in range(32):
                    P.op("pe", lambda e, i=i, j=j: e.matmul(psb[6][:, 0:255], w1b[i][:, j, :], kcD[:, j % 16, j // 16:j // 16 + 255], start=(j == 0), stop=(j == 31)),
                         reads=["w1b%d" % i, "kcD"], writes=["ps6"])
                P.op("pool", lambda e: e.memset(hcm[:, 255:256], 0.0), writes=["hcm1"])
                P.op("act", lambda e, i=i: e.activation(hcm[:, 0:255], psb[6][:, 0:255], AF.Silu, bias=cvec[i][:, 0:1], scale=1.0), reads=["ps6", "cvec%d" % i], writes=["hcm"])
                if i == 0:
                    P.op("pe", lambda e: e.matmul(psb[7][:, 0:256], w2b[0][:], hcm[:], start=True, stop=True), reads=["w2b0", "hcm", "hcm1"], writes=["ps7"])
                    P.op("act", lambda e: e.copy(kcmpT[:], psb[7][:, 0:256]), reads=["ps7"], writes=["kcmpT"])
                else:
                    for nc_ in range(2):
                        P.op("pe", lambda e, nc_=nc_: e.matmul(psb[7][:, nc_ * 128:(nc_ + 1) * 128], hcm[:, nc_ * 128:(nc_ + 1) * 128], w2b[1][:], start=True, stop=True),
                             reads=["w2b1", "hcm", "hcm1"], writes=["ps7"])
                    P.op("act", lambda e: e.copy(VC[:, :, 0:128], psb[7][:, 0:256].rearrange("p (c d) -> p c d", c=2)), reads=["ps7"], writes=["VCd"])
                    P.op("pool", lambda e: e.memset(VC[:, :, 128:129], 1.0), writes=["VC1"])
                    P.dma("sp", VC[:, :, 129:193], t_ovl[:, :, :], "x", writes=["VCo"])
            for br, (src, ncols) in enumerate(((kcmpT, 256), (kT4[2], T), (kT4[3], T))):
                nch = max(1, ncols // 512)
                for cc_ in range(nch):
                    w_ = min(512, ncols)
                    s_, sk = sqR.next()
                    P.op("pool", lambda e, s_=s_, src=src, cc_=cc_, w_=w_: e.tensor_tensor(s_[:, 0:w_], src[:, cc_ * 512:cc_ * 512 + w_], src[:, cc_ * 512:cc_ * 512 + w_], op=ALU.mult),
                         reads=["kcmpT", "kT4_2", "kT4_3"], writes=[sk])
                    P.op("pe", lambda e, s_=s_, w_=w_: e.matmul(psb[6][0:1, 0:w_], onesb[:, 0:1], s_[:, 0:w_], start=True, stop=True), reads=[sk, "onesb"], writes=["ps6"])
                    P.op("dve", lambda e, br=br, cc_=cc_, w_=w_: e.reduce_max(kmx[:, br * 8 + cc_:br * 8 + cc_ + 1], psb[6][0:1, 0:w_], axis=AX.X), reads=["ps6"], writes=["kmx"])
                P.op("dve", lambda e, br=br, nch=nch: e.reduce_max(negK[:, br:br + 1], kmx[:, br * 8:br * 8 + nch], axis=AX.X), reads=["kmx"], writes=["negK%d" % br])
                P.op("act", lambda e, br=br: e.activation(negK[:, br:br + 1], negK[:, br:br + 1], AF.Sqrt), reads=["negK%d" % br], writes=["negK%d" % br])
                P.op("dve", lambda e, br=br: e.tensor_scalar(negK[:, br:br + 1], negK[:, br:br + 1], -1.0, None, op0=ALU.mult), reads=["negK%d" % br], writes=["negK%d" % br])

            def do_G(G):
                q_ = qsb[G % 2]; qk_ = "q%d" % (G % 2)
                for r in range(4):
                    P.dma("sp", q_[:, r, :], nqT[gl * 4 + r][:, G * 512:(G + 1) * 512], "x", reads=["nqT"], writes=[qk_ + "_%d" % r])
                P.op("pool", lambda e: e.memset(imp[:], 0.0), writes=["imp"])

                def make_rrow(r):
                    hr = gl * 4 + r
                    s_, sk = sqR.next()
                    P.op("pool", lambda e, s_=s_: e.tensor_tensor(s_[:], q_[:, r, :], q_[:, r, :], op=ALU.mult), reads=[qk_ + "_%d" % r], writes=[sk])
                    P.op("pe", lambda e, s_=s_: e.matmul(psb[6][0:1, :], onesb[:, 0:1], s_[:], start=True, stop=True), reads=[sk, "onesb"], writes=["ps6"])
                    P.op("act", lambda e: e.activation(qn[:], psb[6][0:1, :], AF.Sqrt), reads=["ps6"], writes=["qn"])
                    P.op("dve", lambda e: e.tensor_scalar(srow[:], trow[:], nslt[0:1, hr:hr + 1], None, op0=ALU.mult), reads=["trow", "nslt"], writes=["srow"])
                    for br in range(3):
                        P.op("dve", lambda e, br=br: e.scalar_tensor_tensor(out=rrow[:, br, :], in0=qn[:], scalar=negK[0:1, br:br + 1], in1=srow[:], op0=ALU.mult, op1=ALU.add),
                             reads=["qn", "negK%d" % br, "srow"], writes=["rrow%d" % br])

                def scores(kTsrc, kkey, kc, r, br, extra):
                    ps_, psk = psS.next()
                    P.op("pe", lambda e: e.matmul(ps_[:, :], kTsrc[:, kc * 128:(kc + 1) * 128], q_[:, r, :], start=True, stop=False),
                         reads=[kkey, qk_ + "_%d" % r], writes=[psk])
                    P.op("pe", lambda e: e.matmul(ps_[:, :], onesb[0:1, :], rrow[0:1, br, :], start=False, stop=(len(extra) == 0)),
                         reads=["onesb", "rrow%d" % br], writes=[psk])
                    for ei, (l_, r_, rk_) in enumerate(extra):
                        P.op("pe", lambda e, l_=l_, r_=r_, ei=ei: e.matmul(ps_[:, :], l_, r_, start=False, stop=(ei == len(extra) - 1)), reads=rk_, writes=[psk])
                    return ps_, psk

                def pass1(r):
                    hr = gl * 4 + r
                    make_rrow(r)
                    zero_bank(2); zero_bank(3)
                    ncs = (0, 1) if G >= 4 else (0,)
                    for nc_ in ncs:
                        ps_, psk = scores(kcmpT, "kcmpT", nc_, r, 0, [(ident[:], cmT[:, nc_, G * 512:(G + 1) * 512], ["ident", "cmT"])])
                        pt_, ptk = PTR.next()
                        P.op("act", lambda e, ps_=ps_, pt_=pt_, nc_=nc_: e.activation(pt_[:], ps_[:, :], AF.Exp, bias=cbt[:, hr, nc_, G:G + 1], scale=1.0), reads=[psk, "cbt"], writes=[ptk])
                        for m in range(4):
                            b_ = 2 + m // 2; o_ = (m % 2) * 193
                            P.op("pe", lambda e, pt_=pt_, m=m, b_=b_, o_=o_, nc_=nc_: e.matmul(psb[b_][:, o_:o_ + 193], pt_[:, m * 128:(m + 1) * 128], VC[:, nc_, :], start=False, stop=(nc_ == ncs[-1]), skip_group_check=True),
                                 reads=[ptk, "VCd", "VC1", "VCo"], writes=["ps%d" % b_])
                    for m in range(4):
                        b_ = 2 + m // 2; o_ = (m % 2) * 193; qt = G * 4 + m
                        po = psb[b_]
                        P.op("dve", lambda e, po=po, o_=o_: e.tensor_scalar(stt[:, 0:1], po[:, o_ + 128:o_ + 129], 1e-30, None, op0=ALU.max), reads=["ps%d" % b_], writes=["stt0"])
                        P.op("dve", lambda e: e.reciprocal(stt[:, 1:2], stt[:, 0:1]), reads=["stt0"], writes=["stt1"])
                        P.op("dve", lambda e, po=po, o_=o_, m=m: e.scalar_tensor_tensor(out=imp[:, m, :], in0=po[:, o_ + 129:o_ + 193], scalar=stt[:, 1:2], in1=imp[:, m, :], op0=ALU.mult, op1=ALU.add),
                             reads=["ps%d" % b_, "stt1", "imp"], writes=["imp"])
                        P.op("dve", lambda e, qt=qt, hr=hr: e.tensor_tensor(stt[:, 2:3], stt[:, 1:2], sm_all[:, qt, 16 + hr * 3:16 + hr * 3 + 1], op=ALU.mult), reads=["stt1", "sm_all"], writes=["stt2"])
                        P.op("act", lambda e, po=po, o_=o_, r=r, m=m: e.activation(oacc[:, r, m, :], po[:, o_:o_ + 128], AF.Copy, scale=stt[:, 2:3]), reads=["ps%d" % b_, "stt2"], writes=["oacc%d_%d" % (r, m)])
                def select(m):
                    qt = G * 4 + m
                    P.op("dve", lambda e, m=m, qt=qt: e.tensor_tensor(impp[:], imp[:, m, :], frc[:, qt, :], op=ALU.add), reads=["imp", "frc"], writes=["impp"])
                    P.op("dve", lambda e: e.max(out=mx8[:, 0:8], in_=impp[:]), reads=["impp"], writes=["mx8a"])
                    P.op("dve", lambda e: e.match_replace(out=impq[:], in_to_replace=mx8[:, 0:8], in_values=impp[:], imm_value=-3e38), reads=["impp", "mx8a"], writes=["impq"])
                    P.op("dve", lambda e: e.max(out=mx8[:, 8:16], in_=impq[:]), reads=["impq"], writes=["mx8b"])
                    P.op("dve", lambda e: e.tensor_scalar(impq[:], impp[:], mx8[:, 15:16], None, op0=ALU.is_ge), reads=["impp", "mx8b"], writes=["impq"])
                    P.op("dve", lambda e: e.tensor_scalar(nsel[:], impq[:], 1.0, -NEG, op0=ALU.subtract, op1=ALU.mult), reads=["impq"], writes=["nsel"])
                    P.op("pe", lambda e: e.transpose(psb[7][:].bitcast(BF16)[0:64, 0:128], nsel[:], ident[:]), reads=["nsel", "ident"], writes=["ps7"])
                    P.op("act", lambda e, m=m: e.copy(negselT[:, m * 128:(m + 1) * 128], psb[7][:].bitcast(BF16)[0:64, 0:128]), reads=["ps7"], writes=["nselT%d" % m])
                def pass2(r):
                    hr = gl * 4 + r
                    make_rrow(r)
                    for b_ in (2, 3, 4, 5):
                        zero_bank(b_)
                    last = 4 * G + 3
                    for kc in range(0, last + 1):
                        extra = [(selE[:, kc, :], negselT[:, :], ["selE"] + ["nselT%d" % m for m in range(4)])]
                        if kc >= 4 * G:
                            extra.append((ident[:], winT[:, 4 + kc - 4 * G, :], ["ident", "winT"]))
                        ps_, psk = scores(kT4[2], "kT4_2", kc, r, 1, extra)
                        pt_, ptk = PTR.next()
                        rel_ = kc - 4 * G + 28
                        P.op("act", lambda e, ps_=ps_, pt_=pt_, rel_=rel_: e.activation(pt_[:], ps_[:, :], AF.Exp, bias=kbt[:, hr, rel_:rel_ + 1], scale=1.0), reads=[psk, "kbt"], writes=[ptk])
                        for m in range(4):
                            b_ = 2 + m // 2; o_ = (m % 2) * 129
                            P.op("pe", lambda e, pt_=pt_, m=m, b_=b_, o_=o_, kc=kc: e.matmul(psb[b_][:, o_:o_ + 129], pt_[:, m * 128:(m + 1) * 128], VS[:, kc, :], start=False, stop=(kc == last), skip_group_check=True),
                                 reads=[ptk, "VSd", "VS1"], writes=["ps%d" % b_])
                    for kc in range(max(0, 4 * G - 4), last + 1):
                        extra = [(ident[:], winT[:, 4 + kc - 4 * G, :], ["ident", "winT"])]
                        ps_, psk = scores(kT4[3], "kT4_3", kc, r, 2, extra)
                        pt_, ptk = PTR.next()
                        rel_ = kc - 4 * G + 28
                        P.op("act", lambda e, ps_=ps_, pt_=pt_, rel_=rel_: e.activation(pt_[:], ps_[:, :], AF.Exp, bias=kbt[:, hr, rel_:rel_ + 1], scale=1.0), reads=[psk, "kbt"], writes=[ptk])
                        for m in range(4):
                            b_ = 4 + m // 2; o_ = (m % 2) * 129
                            P.op("pe", lambda e, pt_=pt_, m=m, b_=b_, o_=o_, kc=kc: e.matmul(psb[b_][:, o_:o_ + 129], pt_[:, m * 128:(m + 1) * 128], VW[:, kc, :], start=False, stop=(kc == last), skip_group_check=True),
                                 reads=[ptk, "VWd", "VW1"], writes=["ps%d" % b_])
                    ms_ = mstage[(G * 4 + r) % 2]; msk = "ms%d" % ((G * 4 + r) % 2)
                    for m in range(4):
                        qt = G * 4 + m
                        oa = oacc[:, r, m, :]; oak = "oacc%d_%d" % (r, m)
                        for bi, (bb, gcol) in enumerate(((2 + m // 2, 1), (4 + m // 2, 2))):
                            po = psb[bb]; o_ = (m % 2) * 129
                            P.op("dve", lambda e, po=po, o_=o_, bi=bi: e.reciprocal(stt[:, 4 + bi:5 + bi], po[:, o_ + 128:o_ + 129]), reads=["ps%d" % bb], writes=["stt%d" % (4 + bi)])
                            P.op("dve", lambda e, bi=bi, qt=qt, gcol=gcol: e.tensor_tensor(stt[:, 6 + bi:7 + bi], stt[:, 4 + bi:5 + bi], sm_all[:, qt, 16 + hr * 3 + gcol:16 + hr * 3 + gcol + 1], op=ALU.mult),
                                 reads=["stt%d" % (4 + bi), "sm_all"], writes=["stt%d" % (6 + bi)])
                            P.op("dve", lambda e, po=po, o_=o_, bi=bi, oa=oa: e.scalar_tensor_tensor(out=oa, in0=po[:, o_:o_ + 128], scalar=stt[:, 6 + bi:7 + bi], in1=oa, op0=ALU.mult, op1=ALU.add),
                                 reads=["ps%d" % bb, "stt%d" % (6 + bi), oak], writes=[oak])
                        P.op("act", lambda e, oa=oa: e.copy(ofin[:], oa), reads=[oak], writes=["ofin"])
                        P.op("pe", lambda e: e.transpose(psb[7][:].bitcast(BF16)[:, 256:384], ofin[:], ident[:]), reads=["ofin", "ident"], writes=["ps7"])
                        P.op("act", lambda e, ms_=ms_, m=m: e.copy(ms_[:, m * 128:(m + 1) * 128], psb[7][:].bitcast(BF16)[:, 256:384]), reads=["ps7"], writes=[msk + "_%d" % m])
                    P.dma("sp", mixT[8 + hr][:, G * 512:(G + 1) * 512], ms_[:], "x", reads=[msk + "_%d" % m for m in range(4)], writes=["mixT"])
                for r in range(4):
                    pass1(r)
                for m in range(4):
                    select(m)
                for r in range(4):
                    pass2(r)

            for G in range(8):
                do_G(G)

        for gl in range(2):
            do_group(gl)
      if start <= 3:
        _ph3()
    P.barrier()
    if nphase < 4:
        P.finish(); P.emit(); return nc

    if start <= 3:
        rg = [[0, 1], [2, 3], [4, 5], [6, 7]]
        for j in range(16):
            P.cc(lambda e, j=j: e.collective_compute("AllGather", ALU.bypass, replica_groups=rg, ins=[mixT[j]], outs=[mixG[j]]),
                 "ccx", reads=["mixT"], writes=["mixG"])
        P.barrier()

    def _ph4():
        def gsrc(kc):
            if kc < 16:
                r_, j_ = kc // 8, kc % 8
            else:
                r_, j_ = (kc - 16) // 8, 8 + (kc - 16) % 8
            return mixG[j_][r_ * 128:(r_ + 1) * 128, :]

        selt = P.sb("p4_sel", [128, 2], F32)
        rst = P.sb("p4_rst", [128, 4, 4], F32)
        P.dma("sp", selt[:], selv[:, :], "x", writes=["selt"])
        for tb in range(4):
            tok0 = tb * 512
            with ExitStack() as st:
                mixsel = P.sb("p4_mixsel", [128, 32, 512], BF16, st)
                mA = [P.sb("p4_mA%d" % i, [128, 4, 512], BF16, st) for i in range(2)]
                mB = [P.sb("p4_mB%d" % i, [128, 4, 512], BF16, st) for i in range(2)]
                wo = [P.sb("p4_wo%d" % i, [128, 32, 512], BF16, st) for i in range(2)]
                g1b = P.sb("p4_g1b", [128, D], F32, st)
                xp = [P.sb("p4_xp%d" % i, [128, 512], F32, st) for i in range(3)]
                yp = [P.sb("p4_yp%d" % i, [128, 512], F32, st) for i in range(3)]
                jk = P.sb("p4_jk", [128, 512], BF16, st)
                ss1 = P.sb("p4_ss1", [128, 4, 8], F32, st)
                P.dma("sp", g1b[:], modv[2:3, :].partition_broadcast(128), "x", reads=["modv"], writes=["g1b"])
                for q4 in range(8):
                    a_ = mA[q4 % 2]; b_ = mB[q4 % 2]
                    for u in range(4):
                        kc = q4 * 4 + u
                        P.dma("sp", a_[:, u, :], gsrc(kc)[:, tok0:tok0 + 512], "x", reads=["mixG"], writes=["mA%d" % (q4 % 2)])
                        P.dma("sp", b_[:, u, :], gsrc(kc)[:, 2048 + tok0:2048 + tok0 + 512], "x", reads=["mixG"], writes=["mB%d" % (q4 % 2)])
                    dst = mixsel[:, q4 * 4:(q4 + 1) * 4, :]
                    P.op("dve", lambda e, a_=a_, dst=dst: e.tensor_scalar(dst, a_[:], selt[:, 0:1], None, op0=ALU.mult), reads=["mA%d" % (q4 % 2), "selt"], writes=["mixsel%d" % q4])
                    P.op("dve", lambda e, b_=b_, dst=dst: e.scalar_tensor_tensor(out=dst, in0=b_[:], scalar=selt[:, 1:2], in1=dst, op0=ALU.mult, op1=ALU.add),
                         reads=["mB%d" % (q4 % 2), "selt", "mixsel%d" % q4], writes=["mixsel%d" % q4])
                msk_all = ["mixsel%d" % q4 for q4 in range(8)]
                P.dma("sp", wo[0][:], w_out_b[0], "x", reads=["w_out_b0"], writes=["wo0"])
                it = 0
                for n in range(8):
                    if n + 1 < 8:
                        P.dma("sp", wo[(n + 1) % 2][:], w_out_b[n + 1], "x", reads=["w_out_b%d" % (n + 1)], writes=["wo%d" % ((n + 1) % 2)])
                    w_ = wo[n % 2]; wk = "wo%d" % (n % 2)
                    for m in range(4):
                        b = 2 + (it % 4); it += 1
                        x_ = xp[it % 3]; xk = "xp%d" % (it % 3); y_ = yp[it % 3]; yk = "yp%d" % (it % 3)
                        P.dma("sp", x_[:], x_h[tok0 + m * 128:tok0 + (m + 1) * 128, n * 512:(n + 1) * 512], "x", writes=[xk])
                        for kc in range(32):
                            P.op("pe", lambda e, b=b, w_=w_, kc=kc, m=m: e.matmul(psb[b][:, :], mixsel[:, kc, m * 128:(m + 1) * 128], w_[:, kc, :], start=(kc == 0), stop=(kc == 31)),
                                 reads=[wk] + (msk_all if kc in (0, 31) else []), writes=["ps%d" % b])
                        P.op("dve", lambda e, b=b, y_=y_, n=n: e.tensor_tensor(y_[:], psb[b][:, :], g1b[:, n * 512:(n + 1) * 512], op=ALU.mult), reads=["ps%d" % b, "g1b"], writes=[yk])
                        P.op("pool", lambda e, y_=y_, x_=x_: e.tensor_tensor(y_[:], y_[:], x_[:], op=ALU.add), reads=[yk, xk], writes=[yk])
                        P.op("act", lambda e, y_=y_, m=m, n=n: e.activation(jk[:], y_[:], AF.Square, accum_out=ss1[:, m, n:n + 1]), reads=[yk], writes=["jk", "ss1_%d_%d" % (m, n)])
                        P.dma("sp", x1s[tok0 + m * 128:tok0 + (m + 1) * 128, n * 512:(n + 1) * 512], y_[:], "x", reads=[yk], writes=["x1s"])
                for m in range(4):
                    P.op("dve", lambda e, m=m: e.reduce_sum(rst[:, m, 0:1], ss1[:, m, :], axis=AX.X), reads=["ss1_%d_%d" % (m, n) for n in range(8)], writes=["rst%d" % m])
                    P.op("act", lambda e, m=m: e.activation(rst[:, m, 1:2], rst[:, m, 0:1], AF.Sqrt, bias=1e-6, scale=1.0 / D), reads=["rst%d" % m], writes=["rst%d" % m])
                    P.op("dve", lambda e, m=m: e.reciprocal(rst[:, m, 2:3], rst[:, m, 1:2]), reads=["rst%d" % m], writes=["rst%d" % m])
            P.barrier()
            with ExitStack() as st, ExitStack() as sth:
                hidT = P.sb("p4_hidT", [128, 128, 512], BF16, st)
                h2T = P.sb("p4_h2T", [128, 32, 512], BF16, sth)
                with ExitStack() as st2:
                    w2b = P.sb("p4_w2b", [128, 2048], F32, st2)
                    sh2b = P.sb("p4_sh2b", [128, 2048], F32, st2)
                    xr = P.sb("p4_xr", [128, D], F32, st2)
                    hb2 = P.sb("p4_hb2", [128, D], BF16, st2)
                    for m in range(4):
                        P.dma("sp", xr[:], x1s[tok0 + m * 128:tok0 + (m + 1) * 128, :], "x", reads=["x1s"], writes=["xr"])
                        for hf in range(2):
                            cs_ = slice(hf * 2048, (hf + 1) * 2048)
                            P.dma("sp", w2b[:], modv[3:4, cs_].partition_broadcast(128), "x", reads=["modv"], writes=["w2b"])
                            P.dma("sp", sh2b[:], modv[4:5, cs_].partition_broadcast(128), "x", reads=["modv"], writes=["sh2b"])
                            P.op("dve", lambda e, m=m, cs_=cs_: e.scalar_tensor_tensor(out=xr[:, cs_], in0=xr[:, cs_], scalar=rst[:, m, 2:3], in1=w2b[:], op0=ALU.mult, op1=ALU.mult),
                                 reads=["xr", "rst%d" % m, "w2b"], writes=["xr"])
                            P.op("pool", lambda e, cs_=cs_: e.tensor_tensor(hb2[:, cs_], xr[:, cs_], sh2b[:], op=ALU.add), reads=["xr", "sh2b"], writes=["hb2"])
                        for g in range(4):
                            pk = "ps%d" % (g % 2)
                            ptb = psb[g % 2][:].bitcast(BF16)
                            for u in range(8):
                                kc = g * 8 + u
                                P.op("pe", lambda e, ptb=ptb, u=u, kc=kc: e.transpose(ptb[:, u * 128:(u + 1) * 128], hb2[:, kc * 128:(kc + 1) * 128], ident[:]),
                                     reads=["hb2", "ident"], writes=[pk])
                            dstap = h2T[:, g * 8:(g + 1) * 8, m * 128:(m + 1) * 128]
                            srcap = ptb[:, 0:1024].rearrange("p (u t) -> p u t", u=8)
                            P.op("act", lambda e, d=dstap, s_=srcap: e.copy(d, s_), reads=[pk], writes=["h2T%d_%d" % (m, g)])
                P.barrier()
                st3 = ExitStack()
                wu = [P.sb("p4_wu%d" % i, [128, 32, 128], BF16, st3) for i in range(3)]
                rl = [P.sb("p4_rl%d" % i, [128, 512], F32, st3) for i in range(2)]
                P.dma("sp", wu[0][:], w_up_b[0], "x", reads=["w_up_b0"], writes=["wu0"])
                P.dma("sp", wu[1][:], w_up_b[1], "x", reads=["w_up_b1"], writes=["wu1"])
                for fc in range(128):
                    if fc + 2 < 128:
                        P.dma("sp", wu[(fc + 2) % 3][:], w_up_b[fc + 2], "x", reads=["w_up_b%d" % (fc + 2)], writes=["wu%d" % ((fc + 2) % 3)])
                    w_ = wu[fc % 3]; wk = "wu%d" % (fc % 3)
                    b = 2 + fc % 4
                    for kc in range(32):
                        P.op("pe", lambda e, b=b, w_=w_, kc=kc: e.matmul(psb[b][:, :], w_[:, kc, :], h2T[:, kc, :], start=(kc == 0), stop=(kc == 31)),
                             reads=[wk, "h2T"], writes=["ps%d" % b])
                    r_ = rl[fc % 2]; rk = "rl%d" % (fc % 2)
                    P.op("act", lambda e, b=b, r_=r_: e.activation(r_[:], psb[b][:, :], AF.Relu), reads=["ps%d" % b], writes=[rk])
                    P.op("dve", lambda e, r_=r_, fc=fc: e.tensor_tensor(hidT[:, fc, :], r_[:], r_[:], op=ALU.mult), reads=[rk], writes=["hidT"])
                P.barrier()
                st3.close()
                sth.close()
                wd = [P.sb("p4_wd%d" % i, [128, 8, 512], BF16, st) for i in range(3)]
                g2b = P.sb("p4_g2b", [128, D], F32, st)
                xp2 = [P.sb("p4_xq%d" % i, [128, 512], F32, st) for i in range(3)]
                yp2 = [P.sb("p4_yq%d" % i, [128, 512], F32, st) for i in range(3)]
                jk2 = P.sb("p4_jk2", [128, 512], BF16, st)
                ss2 = P.sb("p4_ss2", [128, 4, 8], F32, st)
                P.dma("sp", g2b[:], modv[5:6, :].partition_broadcast(128), "x", reads=["modv"], writes=["g2b"])
                seq = [(n, fg) for n in range(8) for fg in range(16)]
                for i_ in range(2):
                    n_, fg_ = seq[i_]
                    P.dma("sp", wd[i_ % 3][:], w_dn_b[n_, fg_], "x", reads=["w_dn_b%d_%d" % (n_, fg_)], writes=["wd%d" % (i_ % 3)])
                it = 0
                for si, (n, fg) in enumerate(seq):
                    if si + 2 < len(seq):
                        n_, fg_ = seq[si + 2]
                        P.dma("sp", wd[(si + 2) % 3][:], w_dn_b[n_, fg_], "x", reads=["w_dn_b%d_%d" % (n_, fg_)], writes=["wd%d" % ((si + 2) % 3)])
                    w_ = wd[si % 3]; wk = "wd%d" % (si % 3)
                    for m in range(4):
                        b = 2 + m
                        for j in range(8):
                            P.op("pe", lambda e, b=b, w_=w_, j=j, m=m, fg=fg: e.matmul(psb[b][:, :], hidT[:, fg * 8 + j, m * 128:(m + 1) * 128], w_[:, j, :],
                                                                                  start=(fg == 0 and j == 0), stop=(fg == 15 and j == 7)),
                                 reads=[wk, "hidT"], writes=["ps%d" % b])
                    if fg == 15:
                        for m in range(4):
                            b = 2 + m; it += 1
                            x_ = xp2[it % 3]; xk = "xq%d" % (it % 3); y_ = yp2[it % 3]; yk = "yq%d" % (it % 3)
                            rows = slice(tok0 + m * 128, tok0 + (m + 1) * 128); cols = slice(n * 512, (n + 1) * 512)
                            P.dma("sp", x_[:], x1s[rows, cols], "x", reads=["x1s"], writes=[xk])
                            P.op("dve", lambda e, b=b, y_=y_, n=n: e.tensor_tensor(y_[:], psb[b][:, :], g2b[:, n * 512:(n + 1) * 512], op=ALU.mult), reads=["ps%d" % b, "g2b"], writes=[yk])
                            P.op("pool", lambda e, y_=y_, x_=x_: e.tensor_tensor(y_[:], y_[:], x_[:], op=ALU.add), reads=[yk, xk], writes=[yk])
                            P.op("act", lambda e, y_=y_, m=m, n=n: e.activation(jk2[:], y_[:], AF.Square, accum_out=ss2[:, m, n:n + 1]), reads=[yk], writes=["jk2", "ss2_%d_%d" % (m, n)])
                            P.dma("sp", x1s[rows, cols], y_[:], "x", reads=[yk, "x1s"], writes=["x1s"])
                for m in range(4):
                    P.op("dve", lambda e, m=m: e.reduce_sum(rst[:, m, 0:1], ss2[:, m, :], axis=AX.X), reads=["ss2_%d_%d" % (m, n) for n in range(8)], writes=["rst%d" % m])
                    P.op("act", lambda e, m=m: e.activation(rst[:, m, 1:2], rst[:, m, 0:1], AF.Sqrt, bias=1e-6, scale=1.0 / D), reads=["rst%d" % m], writes=["rst%d" % m])
                    P.op("dve", lambda e, m=m: e.reciprocal(rst[:, m, 2:3], rst[:, m, 1:2]), reads=["rst%d" % m], writes=["rst%d" % m])
            P.barrier()
            with ExitStack() as st:
                fnb = P.sb("p4_fnb", [128, D], F32, st)
                xo = [P.sb("p4_xo%d" % i, [128, D], F32, st) for i in range(2)]
                P.dma("sp", fnb[:], modv[6:7, :].partition_broadcast(128), "x", reads=["modv"], writes=["fnb"])
                for m in range(4):
                    x_ = xo[m % 2]; xk = "xo%d" % (m % 2)
                    rows = slice(tok0 + m * 128, tok0 + (m + 1) * 128)
                    P.dma("sp", x_[:], x1s[rows, :], "x", reads=["x1s"], writes=[xk])
                    P.op("dve", lambda e, x_=x_, m=m: e.scalar_tensor_tensor(out=x_[:], in0=x_[:], scalar=rst[:, m, 2:3], in1=fnb[:], op0=ALU.mult, op1=ALU.mult),
                         reads=[xk, "rst%d" % m, "fnb"], writes=[xk])
                    P.dma("sp", out_h[rows, :], x_[:], "x", reads=[xk], writes=["out_h"])
            P.barrier()

    _ph4()
    P.finish()
    P.emit()
    return nc


def core_inputs(inp, b, hh, consts):
    f32 = np.float32
    d = {}
    d["x_b"] = np.ascontiguousarray(inp["x"][b])
    d["x_h"] = np.ascontiguousarray(inp["x"][b, hh * 2048:(hh + 1) * 2048])
    d["cT"] = np.ascontiguousarray(inp["c"][b].reshape(32, 128).T)
    d["ada_w"] = inp["ada_w"][0]
    d["ada_b"] = inp["ada_b"][0][None, :]
    d["n1w"] = inp["norm1_w"][0][None, :]
    d["n2w"] = inp["norm2_w"][0][None, :]
    d["fnw"] = inp["final_norm_w"][None, :]
    cols = w_in_cols(hh)
    wc = np.zeros((D, W_IN_COLS), f32)
    wc[:, :cols.size] = inp["w_in"][0][:, cols]
    d["w_in_c"] = wc
    cwv = inp["gdn_conv_w"][0]
    cw = np.zeros((128, 24, 4), f32)
    for f in range(24):
        kind, hd = f // 8, f % 8
        ch = kind * 2048 + (8 * hh + hd) * 128 + np.arange(128)
        cw[:, f, :] = cwv[:, ch].T
    d["convw"] = cw.reshape(128, 96)
    d["alog"] = inp["gdn_a_log"][0][None, 8 * hh:8 * hh + 8].astype(f32)
    d["dtb"] = inp["gdn_dt_bias"][0][None, 8 * hh:8 * hh + 8].astype(f32)
    d["gnw"] = inp["gdn_norm_w"][0][None, :]
    for s in ("k", "v"):
        d["pos_" + s] = inp["cmp_pos_" + s][0]
        d["w1_" + s] = inp["cmp_w1_" + s][0]
        d["w2_" + s] = inp["cmp_w2_" + s][0]
    d["w_out"] = inp["w_out"][0]
    d["w_up"] = inp["w_up"][0]
    d["w_down"] = inp["w_down"][0]
    sv = np.zeros((128, 2), f32)
    sv[:, hh] = 1.0
    d["selv"] = sv
    for k, v in consts.items():
        d["t_" + k] = v
    kb, cb, nsl = alibi_tables(hh)
    d["t_kb"] = kb
    d["t_cb"] = cb
    d["t_nsl"] = nsl
    return {k: np.ascontiguousarray(v) for k, v in d.items()}


def kernel(**inputs):
    inp = {k: np.asarray(v) for k, v in inputs.items()}
    consts = const_tables()
    nc = build_program()
    in_maps = [core_inputs(inp, cid // 2, cid % 2, consts) for cid in range(8)]
    res = run_bass_kernel_spmd(nc, in_maps, core_ids=list(range(8)))
    out = np.zeros((4, T, D), np.float32)
    for cid in range(8):
        b, hh = cid // 2, cid % 2
        out[b, hh * 2048:(hh + 1) * 2048] = res.results[cid]["out_h"]
    return out
```

```python
import numpy as np
import ml_dtypes
from contextlib import ExitStack
import concourse.bass as bass
import concourse.mybir as mybir
from concourse.bass_utils import run_bass_kernel_spmd

F32 = mybir.dt.float32
BF16 = mybir.dt.bfloat16
I32 = mybir.dt.int32
ALU = mybir.AluOpType
AF = mybir.ActivationFunctionType
AX = mybir.AxisListType

ENGS = ("pe", "act", "dve", "pool", "sp")

T = 4096
D = 4096
DFF = 16384
NEG = -30000.0


class Prog:
    def __init__(self, nc):
        self.nc = nc
        self.ops = {e: [] for e in ENGS}
        self.nops = {e: 0 for e in ENGS}
        self.dcount = {}
        self.res = {}
        self.waited = {}
        self.awaited = {e: set() for e in ENGS}
        self.bankacc = {}
        self.dring = {}
        self.stack = ExitStack()

    def sb(self, name, shape, dt, stack=None):
        self.uid = getattr(self, "uid", 0) + 1
        name = "%s_u%d" % (name, self.uid)
        return (stack or self.stack).enter_context(self.nc.sbuf_tensor(name, list(shape), dt))

    def ps(self, name, shape, dt=F32, stack=None):
        return (stack or self.stack).enter_context(self.nc.psum_tensor(name, list(shape), dt))

    def _deps(self, eng, reads, writes):
        deps = {}
        for r in reads:
            ent = self.res.get(r)
            if ent is not None and ent[0] is not None:
                k, i = ent[0]
                if deps.get(k, -1) < i:
                    deps[k] = i
        for w in writes:
            ent = self.res.get(w)
            if ent is not None:
                if ent[0] is not None:
                    k, i = ent[0]
                    if deps.get(k, -1) < i:
                        deps[k] = i
                for k, i in ent[1].items():
                    if deps.get(k, -1) < i:
                        deps[k] = i
        waits = []
        for k, i in deps.items():
            if k == eng and eng == "pe":
                continue
            if self.waited.get((eng, k), -1) >= i:
                continue
            self.waited[(eng, k)] = i
            waits.append((k, i))
            if k in self.awaited:
                self.awaited[k].add(i)
        return waits

    def _update(self, tok, reads, writes):
        k, i = tok
        for w in writes:
            self.res[w] = [tok, {}]
        for r in reads:
            ent = self.res.get(r)
            if ent is None:
                ent = self.res[r] = [None, {}]
            if ent[1].get(k, -1) < i:
                ent[1][k] = i

    @staticmethod
    def _bank(key):
        if key.startswith("psb"):
            return int(key[3])
        if key.startswith("ps"):
            return int(key[2])
        return None

    def _bank_deps(self, eng, reads, writes, waits):
        banks = set()
        for k_ in list(reads) + list(writes):
            b = self._bank(k_)
            if b is not None:
                banks.add(b)
        for b in banks:
            acc = self.bankacc.setdefault(b, {})
            for k, i in acc.items():
                if k == eng:
                    continue
                if self.waited.get((eng, k), -1) >= i:
                    continue
                self.waited[(eng, k)] = i
                waits.append((k, i))
                self.awaited[k].add(i)
        return banks

    def op(self, eng, fn, reads=(), writes=()):
        waits = self._deps(eng, reads, writes)
        banks = self._bank_deps(eng, reads, writes, waits)
        for b in banks:
            self.bankacc[b][eng] = self.nops[eng] + 1
        self.nops[eng] += 1
        tok = (eng, self.nops[eng])
        self.ops[eng].append([waits, fn, "c", self.nops[eng]])
        self._update(tok, reads, writes)
        return tok

    NRING = {"sp": 40, "pool": 16, "act": 8}

    def dma(self, q, out, in_, sem, reads=(), writes=(), **kw):
        waits = self._deps(q, reads, writes)
        n = self.dring.get(q, 0)
        self.dring[q] = n + 1
        sem = "%s_r%d" % (q, n % self.NRING[q])
        prev = self.dcount.get(sem, 0)
        if prev and self.waited.get((q, sem), -1) < prev:
            self.waited[(q, sem)] = prev
            waits.append((sem, prev))
        self.dcount[sem] = prev + 16
        tok = (sem, self.dcount[sem])
        self.ops[q].append([waits, lambda e: e.dma_start(out=out, in_=in_, **kw), "d", sem])
        self._update(tok, reads, writes)
        return tok

    def cc(self, fn, sem, reads=(), writes=()):
        waits = self._deps("pool", reads, writes)
        self.dcount[sem] = self.dcount.get(sem, 0) + 1
        tok = (sem, self.dcount[sem])
        self.ops["pool"].append([waits, fn, "k", sem])
        self._update(tok, reads, writes)
        return tok

    def barrier(self):
        for e in ENGS:
            waits = []
            for k in ENGS:
                if k == e or self.nops[k] == 0:
                    continue
                i = self.nops[k]
                if self.waited.get((e, k), -1) >= i:
                    continue
                self.waited[(e, k)] = i
                self.awaited[k].add(i)
                waits.append((k, i))
            for k, c in self.dcount.items():
                if self.waited.get((e, k), -1) >= c:
                    continue
                self.waited[(e, k)] = c
                waits.append((k, c))
            if waits:
                self.ops[e].append([waits, None, "w", None])
        self.res = {}

    def finish(self):
        waits = [(k, c) for k, c in self.dcount.items()]
        self.ops["sp"].append([waits, None, "w", None])

    def emit(self):
        nc = self.nc
        sems = {}
        for e in ENGS:
            if self.nops[e]:
                sems[e] = self.stack.enter_context(nc.semaphore("s_" + e))
        for k in self.dcount:
            sems[k] = self.stack.enter_context(nc.semaphore("d_" + k))
        vmap = {}
        for e in ENGS:
            aw = sorted(self.awaited[e])
            vmap[e] = {idx: n + 1 for n, idx in enumerate(aw)}

        def val(k, i):
            return vmap[k][i] if k in vmap else i

        def run(e, engobj):
            for waits, fn, kind, extra in self.ops[e]:
                for k, i in waits:
                    engobj.wait_ge(sems[k], val(k, i))
                if kind == "w":
                    continue
                ins = fn(engobj)
                if kind == "c":
                    if extra in vmap[e]:
                        ins.then_inc(sems[e], 1)
                elif kind == "d":
                    ins.then_inc(sems[extra], 16)
                else:
                    ins.then_inc(sems[extra], 1)

        with nc.Block() as block:
            @block.tensor
            def _(t):
                run("pe", t)

            @block.scalar
            def _(t):
                run("act", t)

            @block.vector
            def _(t):
                run("dve", t)

            @block.gpsimd
            def _(t):
                run("pool", t)

            @block.sync
            def _(t):
                run("sp", t)


class Rot:
    def __init__(self, items):
        self.items = items
        self.i = 0

    def next(self):
        it = self.items[self.i % len(self.items)]
        self.i += 1
        return it


def const_tables():
    c = {}
    p = np.arange(128)
    q = np.arange(512)
    win = np.zeros((128, 8, 512), np.float32)
    for j in range(8):
        dist = q[None, :] - (128 * (j - 4) + p[:, None])
        win[:, j, :] = np.where((dist >= 0) & (dist < 512), 0.0, NEG)
    c["winT"] = win.astype(ml_dtypes.bfloat16)
    n = (np.arange(2)[None, :, None] * 128 + p[:, None, None])
    t = np.arange(T)[None, None, :]
    c["cmT"] = np.where((t >= 16 * n + 31) & (n < 255), 0.0, NEG).astype(ml_dtypes.bfloat16)
    s = np.arange(64)[:, None, None]
    kc = np.arange(32)[None, :, None]
    m = np.arange(128)[None, None, :]
    c["selE"] = (s == 2 * kc + m // 64).astype(np.float32).astype(ml_dtypes.bfloat16)
    n = (np.arange(2)[None, :, None] * 128 + p[:, None, None])
    sb = np.arange(64)[None, None, :]
    ov = ((16 * n <= 64 * sb + 63) & (16 * n + 31 >= 64 * sb) & (n < 255)).astype(np.float32)
    c["ovl"] = ov.astype(ml_dtypes.bfloat16)
    tt = (np.arange(32)[None, :, None] * 128 + p[:, None, None])
    cur = tt // 64
    blk = np.arange(64)[None, None, :]
    frc = np.zeros((128, 32, 64), np.float32)
    forced = (blk == 0) | ((cur - blk) < 2)
    frc = np.where(forced, 1e30 * (1.0 + 0.25 * (blk % 4)), frc)
    frc = np.where(blk <= cur, frc, -1e30)
    c["frc"] = frc.astype(np.float32)
    c["triu"] = (p[:, None] <= p[None, :]).astype(np.float32)
    c["dmask"] = np.where(p[None, :] >= p[:, None], 0.0, NEG).astype(np.float32)
    oh = (np.arange(32)[:, None, None] == np.arange(32)[None, :, None]).astype(np.float32)
    c["oneh"] = np.broadcast_to(oh, (32, 32, 128)).copy().astype(np.float32)
    c["trow"] = np.arange(512, dtype=np.float32)[None, :]
    a_ = p[:, None]; b_ = p[None, :]
    mu = np.zeros((128, 7, 128), np.float32)
    for l in range(7):
        sz_ = 2 ** l
        mu[:, l, :] = ((a_ // (2 * sz_) == b_ // (2 * sz_)) & ((a_ % (2 * sz_)) < sz_) & ((b_ % (2 * sz_)) >= sz_)).astype(np.float32)
    c["mskU"] = mu
    c["mskL"] = np.ascontiguousarray(mu.transpose(2, 1, 0))
    return c


def alibi_tables(hh):
    slopes = 2.0 ** (-8.0 * np.arange(1, 17, dtype=np.float64) / 16.0)
    p = np.arange(128, dtype=np.float64)
    hs = slopes[8 * hh:8 * hh + 8]
    rel = np.arange(32, dtype=np.float64)
    kb = hs[None, :, None] * (128.0 * (rel[None, None, :] - 28.0) + p[:, None, None])
    ncx = np.arange(2, dtype=np.float64)
    G = np.arange(8, dtype=np.float64)
    cb = hs[None, :, None, None] * (16.0 * (ncx[None, None, :, None] * 128 + p[:, None, None, None]) + 31.0
                                    - 512.0 * G[None, None, None, :])
    nsl = -hs[None, :]
    return kb.astype(np.float32), cb.astype(np.float32), nsl.astype(np.float32)


def w_in_cols(hh):
    GDN_DK, NSA_DQ, DKV = 2048, 2048, 512
    sizes = (GDN_DK, GDN_DK, GDN_DK, GDN_DK, 16, 16, NSA_DQ, DKV, DKV, DKV, DKV, DKV, DKV, 48)
    off = np.concatenate([[0], np.cumsum(sizes)])
    gq, gk, gv, gz, gb, ga, nq, nkc, nvc, nks, nvs, nkw, nvw, ngate = [int(o) for o in off[:-1]]
    h8 = np.arange(1024) + 1024 * hh
    g2 = np.arange(256) + 256 * hh
    cols = []
    for base in (gq, gk, gv):
        cols.append(base + h8)
    cols.append(nq + h8)
    for base in (nkc, nvc, nks, nkw):
        cols.append(base + g2)
    cols.append(gz + h8)
    cols.append(nvs + g2)
    cols.append(nvw + g2)
    cols.append(gb + 8 * hh + np.arange(8))
    cols.append(ga + 8 * hh + np.arange(8))
    cols.append(ngate + 24 * hh + np.arange(24))
    return np.concatenate(cols)


N_FM = 40
W_IN_COLS = 40 * 128 + 3 * 512 + 64


def build_program(dbg=None, nphase=99, start=0, only_heads=8, gdn_chunks=32, gdbg=None, lite=False):
    nc = bass.Bass("TRN2", target_bir_lowering=False)
    P = Prog(nc)
    ck = "ExternalOutput" if dbg else "Internal"

    need = {"x_b": (1,), "ada_w": (0,), "w_in_c": (0, 1), "w_out": (4,), "w_up": (4,), "w_down": (4,), "x_h": (4,),
            "w1_k": (3,), "w1_v": (3,)}

    def din(name, shape, dt=F32):
        if lite and name in need and not any(start <= ph_ <= nphase - 0 for ph_ in need[name]):
            shape = [1, 1]
        return nc.dram_tensor(name, list(shape), dt, kind="ExternalInput").ap()

    def dscr(name, shape, dt, dbgout=False, ph=99, last=99):
        if ph < start <= last:
            return nc.dram_tensor(name, list(shape), dt, kind="ExternalInput").ap()
        return nc.dram_tensor(name, list(shape), dt, kind=("ExternalOutput" if (dbg and dbgout) else "Internal")).ap()

    x_b = din("x_b", [T, D]); x_h = din("x_h", [2048, D]); cT = din("cT", [128, 32])
    ada_w = din("ada_w", [D, 6 * D]); ada_b = din("ada_b", [1, 6 * D])
    n1w = din("n1w", [1, D]); n2w = din("n2w", [1, D]); fnw = din("fnw", [1, D])
    w_in_c = din("w_in_c", [D, W_IN_COLS])
    convw = din("convw", [128, 24 * 4]); alog = din("alog", [1, 8]); dtb = din("dtb", [1, 8]); gnw = din("gnw", [1, 128])
    pos_k = din("pos_k", [32, 128]); w1_k = din("w1_k", [4096, 128]); w2_k = din("w2_k", [128, 128])
    pos_v = din("pos_v", [32, 128]); w1_v = din("w1_v", [4096, 128]); w2_v = din("w2_v", [128, 128])
    w_out = din("w_out", [D, D]); w_up = din("w_up", [D, DFF]); w_down = din("w_down", [DFF, D])
    selv = din("selv", [128, 2])
    t_winT = din("t_winT", [128, 8, 512], BF16); t_cmT = din("t_cmT", [128, 2, T], BF16)
    t_selE = din("t_selE", [64, 32, 128], BF16); t_ovl = din("t_ovl", [128, 2, 64], BF16)
    t_frc = din("t_frc", [128, 32, 64]); t_triu = din("t_triu", [128, 128]); t_dmask = din("t_dmask", [128, 128])
    t_oneh = din("t_oneh", [32, 32, 128]); t_trow = din("t_trow", [1, 512])
    t_mskU = din("t_mskU", [128, 7, 128]); t_mskL = din("t_mskL", [128, 7, 128])
    t_kb = din("t_kb", [128, 8, 32]); t_cb = din("t_cb", [128, 8, 2, 8]); t_nsl = din("t_nsl", [1, 8])

    out_h = nc.dram_tensor("out_h", [2048, D], F32, kind="ExternalOutput").ap()

    modv = dscr("modv", [8, D], F32, dbgout=True, ph=0)
    w_fm = dscr("w_fm", [N_FM, 128, 32, 128], BF16)
    w_tm = dscr("w_tm", [4, 128, 32, 512], BF16)
    gqT = dscr("gqT", [8, 128, T], BF16, True, ph=1, last=2); gkT = dscr("gkT", [8, 128, T], BF16, True, ph=1, last=2)
    gk_tm = dscr("gk_tm", [8, T, 128], BF16, True, ph=1, last=2); gv_tm = dscr("gv_tm", [8, T, 128], BF16, True, ph=1, last=2)
    sz_tm = dscr("sz_tm", [T, 1024], BF16, True, ph=1, last=2)
    nqT = dscr("nqT", [8, 128, T], BF16, True, ph=1, last=3)
    kvT = dscr("kvT", [8, 128, T], BF16, True, ph=1, last=3)
    vsw_tm = dscr("vsw_tm", [T, 512], BF16, True, ph=1, last=3)
    sm_tm = dscr("sm_tm", [T, 64], F32, True, ph=1, last=3)
    mixT = dscr("mixT", [16, 128, T], BF16, True)
    mixG = dscr("mixG", [16, 2 * 128, T], BF16, ph=3)
    w_out_b = dscr("w_out_b", [8, 128, 32, 512], BF16)
    w_up_b = dscr("w_up_b", [128, 128, 32, 128], BF16)
    w_dn_b = dscr("w_dn_b", [8, 16, 128, 8, 512], BF16)
    x1s = dscr("x1s", [2048, D], F32, True)

    ident = P.sb("ident", [128, 128], BF16)
    identf = P.sb("identf", [128, 128], F32)
    onesb = P.sb("onesb", [128, 128], BF16)
    onesf = P.sb("onesf", [128, 128], F32)
    P.op("pool", lambda e: e.memset(ident[:], 1.0), writes=["ident"])
    P.op("pool", lambda e: e.affine_select(ident[:], ident[:], pattern=[[-1, 128]], compare_op=ALU.is_equal,
                                           fill=0.0, base=0, channel_multiplier=1), reads=["ident"], writes=["ident"])
    P.op("pool", lambda e: e.memset(identf[:], 1.0), writes=["identf"])
    P.op("pool", lambda e: e.affine_select(identf[:], identf[:], pattern=[[-1, 128]], compare_op=ALU.is_equal,
                                           fill=0.0, base=0, channel_multiplier=1), reads=["identf"], writes=["identf"])
    P.op("pool", lambda e: e.memset(onesb[:], 1.0), writes=["onesb"])
    P.op("pool", lambda e: e.memset(onesf[:], 1.0), writes=["onesf"])

    psb = [P.ps("psb%d" % i, [128, 512], F32) for i in range(8)]

    for f in range(N_FM if start <= 1 else 0):
        P.dma("pool", w_fm[f], w_in_c[:, f * 128:(f + 1) * 128].rearrange("(kc p) f -> p kc f", p=128), "cvt",
              writes=["w_fm%d" % f])
    for g in range(3 if start <= 1 else 0):
        o = N_FM * 128 + g * 512
        P.dma("pool", w_tm[g], w_in_c[:, o:o + 512].rearrange("(kc p) f -> p kc f", p=128), "cvt", writes=["w_tm%d" % g])
    o = N_FM * 128 + 3 * 512
    if start <= 1:
      P.dma("pool", w_tm[3][:, :, 0:64], w_in_c[:, o:o + 64].rearrange("(kc p) f -> p kc f", p=128), "cvt", writes=["w_tm3"])
    cast_q = []
    if nphase >= 4:
        for n in range(8):
            cast_q.append((w_out_b[n], w_out[:, n * 512:(n + 1) * 512].rearrange("(kc p) f -> p kc f", p=128), "w_out_b%d" % n))
        for fg in range(16):
            src = w_up[:, fg * 1024:(fg + 1) * 1024].rearrange("(kc p) (c f) -> p c kc f", p=128, f=128)
            for cc_ in range(8):
                cast_q.append((w_up_b[fg * 8 + cc_], src[:, cc_], "w_up_b%d" % (fg * 8 + cc_)))
        for n in range(8):
            for fg in range(16):
                src = w_down[fg * 1024:(fg + 1) * 1024, n * 512:(n + 1) * 512].rearrange("(j p) f -> p j f", p=128)
                cast_q.append((w_dn_b[n, fg], src, "w_dn_b%d_%d" % (n, fg)))

    def cast_step(nmax):
        for _ in range(nmax):
            if not cast_q:
                return
            o_, i_, k_ = cast_q.pop(0)
            P.dma("pool", o_, i_, "cvt", writes=[k_])

    if start > 2:
        cast_step(10 ** 6)

    with ExitStack() as st:
      def _ph0():
        sT = P.sb("p0_sT", [128, 32], F32, st)
        awt = [P.sb("p0_aw%d" % i, [128, 32, 512], F32, st) for i in range(2)]
        row = [P.sb("p0_row%d" % i, [1, 512], F32, st) for i in range(2)]
        bro = [P.sb("p0_bro%d" % i, [1, 512], F32, st) for i in range(2)]
        nro = [P.sb("p0_nro%d" % i, [1, 512], F32, st) for i in range(2)]
        P.dma("sp", sT[:], cT[:, :], "p0c", writes=["sT"])
        P.op("act", lambda e: e.activation(sT[:], sT[:], AF.Silu), reads=["sT"], writes=["sT"])
        P.dma("sp", modv[6:7, :], fnw[:, :], "p0s", writes=["modv"])
        NB = 48
        for nb in range(NB):
            a = awt[nb % 2]
            kind = nb // 8
            cs = (nb % 8) * 512
            P.dma("sp", a[:], ada_w[:, nb * 512:(nb + 1) * 512].rearrange("(kc p) f -> p kc f", p=128), "p0w%d" % (nb % 2),
                  writes=["aw%d" % (nb % 2)])
            P.dma("sp", bro[nb % 2][:], ada_b[:, nb * 512:(nb + 1) * 512], "p0b%d" % (nb % 2), writes=["bro%d" % (nb % 2)])
            if kind in (1, 4):
                P.dma("sp", nro[nb % 2][:], (n1w if kind == 1 else n2w)[:, cs:cs + 512], "p0n%d" % (nb % 2), writes=["nro%d" % (nb % 2)])
            pb = psb[nb % 2]
            for kc in range(32):
                P.op("pe", lambda e, a=a, pb=pb, kc=kc: e.matmul(pb[0:1, :], sT[:, kc:kc + 1], a[:, kc, :], start=(kc == 0), stop=(kc == 31)),
                     reads=["sT", "aw%d" % (nb % 2)], writes=["ps%d" % (nb % 2)])
            r = row[nb % 2]
            br_ = bro[nb % 2]; nr_ = nro[nb % 2]
            P.op("dve", lambda e, r=r, pb=pb, br_=br_: e.tensor_tensor(r[:], pb[0:1, :], br_[:], op=ALU.add),
                 reads=["ps%d" % (nb % 2), "bro%d" % (nb % 2)], writes=["row%d" % (nb % 2)])
            if kind in (1, 4):
                P.op("dve", lambda e, r=r, nr_=nr_: e.scalar_tensor_tensor(out=r[:], in0=r[:], scalar=1.0, in1=nr_[:], op0=ALU.add, op1=ALU.mult),
                     reads=["row%d" % (nb % 2), "nro%d" % (nb % 2)], writes=["row%d" % (nb % 2)])
            dst = {0: 1, 1: 0, 2: 2, 3: 4, 4: 3, 5: 5}[kind]
            P.dma("sp", modv[dst:dst + 1, cs:cs + 512], r[:], "p0s", reads=["row%d" % (nb % 2)], writes=["modv"])
      if start <= 0:
        _ph0()
    P.barrier()
    if nphase < 1:
        P.finish(); P.emit(); return nc

    with ExitStack() as st:
      def _ph1():
        w1b = P.sb("p1_w1b", [128, D], F32, st)
        sh1b = P.sb("p1_sh1b", [128, D], F32, st)
        hT = P.sb("p1_hT", [128, 32, 1024], BF16, st)
        xt = [P.sb("p1_x%d" % i, [128, D], F32, st) for i in range(2)]
        hb = P.sb("p1_hb", [128, D], BF16, st)
        wfb = [P.sb("p1_wf%d" % i, [128, 32, 128], BF16, st) for i in range(3)]
        wtb = [P.sb("p1_wt%d" % i, [128, 32, 256], BF16, st) for i in range(1)]
        cw = P.sb("p1_cw", [128, 96], F32, st)
        halo = P.sb("p1_halo", [128, 24, 3], F32, st)
        raw = [P.sb("p1_raw%d" % i, [128, 515], F32, st) for i in range(2)]
        acc = [P.sb("p1_acc%d" % i, [128, 512], F32, st) for i in range(2)]
        sq = [P.sb("p1_sq%d" % i, [128, 512], BF16, st) for i in range(2)]
        rin = [P.sb("p1_rin%d" % i, [128, 512], F32, st) for i in range(2)]
        ob = [P.sb("p1_ob%d" % i, [128, 512], BF16, st) for i in range(3)]
        tmb = [P.sb("p1_tm%d" % i, [128, 512], BF16, st) for i in range(2)]
        smf = [P.sb("p1_smf%d" % i, [128, 64], F32, st) for i in range(2)]
        stat = P.sb("p1_stat", [128, 8], F32, st)
        dtbb = P.sb("p1_dtbb", [128, 8], F32, st)
        nab = P.sb("p1_nab", [128, 8], F32, st)
        P.dma("sp", w1b[:], modv[0:1, :].partition_broadcast(128), "p1c", reads=["modv"], writes=["w1b"])
        P.dma("sp", sh1b[:], modv[1:2, :].partition_broadcast(128), "p1c", reads=["modv"], writes=["sh1b"])
        P.dma("sp", cw[:], convw[:, :], "p1c", writes=["cw"])
        P.dma("sp", dtbb[:], dtb[0:1, :].partition_broadcast(128), "p1c", writes=["dtbb"])
        P.dma("sp", nab[:], alog[0:1, :].partition_broadcast(128), "p1c", writes=["nab"])
        P.op("act", lambda e: e.activation(nab[:], nab[:], AF.Exp), reads=["nab"], writes=["nab"])
        P.op("dve", lambda e: e.tensor_scalar(nab[:], nab[:], -1.0, None, op0=ALU.mult), reads=["nab"], writes=["nab"])
        P.op("pool", lambda e: e.memset(halo[:], 0.0), writes=["halo"])
        psT = [psb[0], psb[1]]
        psM = Rot([(psb[2], "psb2"), (psb[3], "psb3"), (psb[4], "psb4"), (psb[5], "psb5")])
        psN = Rot([(psb[6], "psb6"), (psb[7], "psb7")])
        rawR = Rot(list(zip(raw, ["raw0", "raw1"]))); accR = Rot(list(zip(acc, ["acc0", "acc1"])))
        sqR = Rot(list(zip(sq, ["sq0", "sq1"]))); rinR = Rot(list(zip(rin, ["rin0", "rin1"])))
        obR = Rot(list(zip(ob, ["ob0", "ob1", "ob2"]))); tmR = Rot(list(zip(tmb, ["tm0", "tm1"])))
        smR = Rot(list(zip(smf, ["smf0", "smf1"])))
        evq = Rot(["act", "dve"])

        def load_x(sbk, i):
            j = (sbk * 8 + i)
            P.dma("sp", xt[j % 2][:], x_b[j * 128:(j + 1) * 128, :], "p1x%d" % (j % 2), writes=["xt%d" % (j % 2)])

        load_x(0, 0)
        for sbk in range(4):
            t0s = sbk * 1024
            for i in range(8):
                j = sbk * 8 + i
                if j + 1 < 32:
                    load_x((j + 1) // 8, (j + 1) % 8)
                x = xt[j % 2]; xk = "xt%d" % (j % 2)
                P.op("act", lambda e, x=x: e.activation(hb[:], x[:], AF.Square, accum_out=stat[:, 0:1]), reads=[xk], writes=["hb", "st0"])
                P.op("act", lambda e: e.activation(stat[:, 1:2], stat[:, 0:1], AF.Sqrt, bias=1e-6, scale=1.0 / D), reads=["st0"], writes=["st1"])
                P.op("dve", lambda e: e.reciprocal(stat[:, 2:3], stat[:, 1:2]), reads=["st1"], writes=["st2"])
                P.op("dve", lambda e, x=x: e.scalar_tensor_tensor(out=x[:], in0=x[:], scalar=stat[:, 2:3], in1=w1b[:], op0=ALU.mult, op1=ALU.mult),
                     reads=[xk, "st2", "w1b"], writes=[xk])
                P.op("pool", lambda e, x=x: e.tensor_tensor(hb[:], x[:], sh1b[:], op=ALU.add), reads=[xk, "sh1b"], writes=["hb"])
                for g in range(4):
                    pt = psT[g % 2]; pk = "psb%d" % (g % 2)
                    ptb = pt[:].bitcast(BF16)
                    for u in range(8):
                        kc = g * 8 + u
                        P.op("pe", lambda e, ptb=ptb, u=u, kc=kc: e.transpose(ptb[:, u * 128:(u + 1) * 128], hb[:, kc * 128:(kc + 1) * 128], ident[:]),
                             reads=["hb", "ident"], writes=[pk])
                    q_ = evq.next()
                    dstap = hT[:, g * 8:(g + 1) * 8, i * 128:(i + 1) * 128]
                    srcap = ptb[:, 0:1024].rearrange("p (u t) -> p u t", u=8)
                    if q_ == "act":
                        P.op("act", lambda e, d=dstap, s=srcap: e.copy(d, s), reads=[pk], writes=["hT%d_%d" % (i, g)])
                    else:
                        P.op("dve", lambda e, d=dstap, s=srcap: e.tensor_copy(d, s), reads=[pk], writes=["hT%d_%d" % (i, g)])
            hT_keys = ["hT%d_%d" % (i, g) for i in range(8) for g in range(4)]
            hTh = [[("hT%d_%d" % (i, g)) for i in range(th * 4, th * 4 + 4) for g in range(4)] for th in range(2)]

            def load_wf(f):
                P.dma("sp", wfb[f % 3][:], w_fm[f], "p1wf%d" % (f % 3), reads=["w_fm%d" % f], writes=["wf%d" % (f % 3)])

            load_wf(0); load_wf(1)
            for f in range(N_FM):
                if f + 2 < N_FM:
                    load_wf(f + 2)
                w = wfb[f % 3]; wk = "wf%d" % (f % 3)
                for th in range(2):
                    pm, pmk = psM.next()
                    for kc in range(32):
                        P.op("pe", lambda e, pm=pm, w=w, kc=kc, th=th: e.matmul(pm[:, :], w[:, kc, :], hT[:, kc, th * 512:(th + 1) * 512],
                                                                              start=(kc == 0), stop=(kc == 31)),
                             reads=[wk] + (hTh[th] if kc in (0, 31) else []), writes=[pmk])
                    tok0 = t0s + th * 512
                    if f < 24:
                        kind = f // 8
                        hd = f % 8
                        rw, rk = rawR.next(); ac, ak = accR.next()
                        P.op("pool", lambda e, rw=rw, f=f: e.tensor_copy(rw[:, 0:3], halo[:, f, :]), reads=["halo%d" % f], writes=[rk + "h"])
                        P.op("act", lambda e, rw=rw, pm=pm: e.copy(rw[:, 3:515], pm[:, :]), reads=[pmk], writes=[rk])
                        P.op("pool", lambda e, rw=rw, f=f: e.tensor_copy(halo[:, f, :], rw[:, 512:515]), reads=[rk], writes=["halo%d" % f])
                        P.op("dve", lambda e, rw=rw, ac=ac, f=f: e.tensor_scalar(ac[:], rw[:, 3:515], cw[:, f * 4 + 3:f * 4 + 4], None, op0=ALU.mult),
                             reads=[rk, "cw"], writes=[ak])
                        for jj in (2, 1, 0):
                            P.op("dve", lambda e, rw=rw, ac=ac, f=f, jj=jj: e.scalar_tensor_tensor(out=ac[:], in0=rw[:, jj:jj + 512], scalar=cw[:, f * 4 + jj:f * 4 + jj + 1],
                                                                                                in1=ac[:], op0=ALU.mult, op1=ALU.add),
                                 reads=[rk, rk + "h", ak, "cw"], writes=[ak])
                        o_, ok = obR.next()
                        if kind == 2:
                            P.op("act", lambda e, ac=ac, o_=o_: e.activation(o_[:], ac[:], AF.Silu), reads=[ak], writes=[ok])
                        else:
                            P.op("act", lambda e, ac=ac: e.activation(ac[:], ac[:], AF.Silu), reads=[ak], writes=[ak])
                            s_, sk = sqR.next(); ri, rik = rinR.next()
                            P.op("pool", lambda e, ac=ac, s_=s_: e.tensor_tensor(s_[:], ac[:], ac[:], op=ALU.mult), reads=[ak], writes=[sk])
                            pn, pnk = psN.next()
                            P.op("pe", lambda e, pn=pn, s_=s_: e.matmul(pn[:, :], onesb[:], s_[:], start=True, stop=True), reads=[sk, "onesb"], writes=[pnk])
                            P.op("act", lambda e, pn=pn, ri=ri: e.activation(ri[:], pn[:, :], AF.Sqrt, bias=1e-6, scale=1.0), reads=[pnk], writes=[rik])
                            P.op("dve", lambda e, ri=ri: e.reciprocal(ri[:], ri[:]), reads=[rik], writes=[rik])
                            scl = (128.0 ** -0.5) if kind == 0 else 1.0
                            P.op("dve", lambda e, ac=ac, ri=ri, o_=o_, scl=scl: e.scalar_tensor_tensor(out=o_[:], in0=ac[:], scalar=scl, in1=ri[:], op0=ALU.mult, op1=ALU.mult),
                                 reads=[ak, rik], writes=[ok])
                        if kind == 0:
                            P.dma("sp", gqT[hd, :, tok0:tok0 + 512], o_[:], "p1o", reads=[ok], writes=["gqT"])
                        elif kind == 1:
                            P.dma("sp", gkT[hd, :, tok0:tok0 + 512], o_[:], "p1o", reads=[ok], writes=["gkT"])
                        if kind >= 1:
                            pt = psT[(f + th) % 2]; pk = "psb%d" % ((f + th) % 2)
                            ptb = pt[:].bitcast(BF16)
                            for u in range(4):
                                P.op("pe", lambda e, ptb=ptb, u=u, o_=o_: e.transpose(ptb[:, u * 128:(u + 1) * 128], o_[:, u * 128:(u + 1) * 128], ident[:]),
                                     reads=[ok, "ident"], writes=[pk])
                            tm_, tk = tmR.next()
                            P.op("act", lambda e, tm_=tm_, ptb=ptb: e.copy(tm_[:], ptb[:, 0:512]), reads=[pk], writes=[tk])
                            dstt = (gk_tm if kind == 1 else gv_tm)[hd, tok0:tok0 + 512, :].rearrange("(u p) d -> p u d", p=128)
                            P.dma("sp", dstt, tm_[:].rearrange("p (u d) -> p u d", u=4), "p1o", reads=[tk], writes=["gtm"])
                    else:
                        o_, ok = obR.next()
                        if f < 32:
                            P.op("act", lambda e, o_=o_, pm=pm: e.activation(o_[:], pm[:, :], AF.Copy, scale=128.0 ** -0.5), reads=[pmk], writes=[ok])
                            P.dma("sp", nqT[f - 24, :, tok0:tok0 + 512], o_[:], "p1o", reads=[ok], writes=["nqT"])
                        else:
                            P.op("act", lambda e, o_=o_, pm=pm: e.copy(o_[:], pm[:, :]), reads=[pmk], writes=[ok])
                            P.dma("sp", kvT[f - 32, :, tok0:tok0 + 512], o_[:], "p1o", reads=[ok], writes=["kvT"])
            for g2 in range(7):
                wt = wtb[0]
                g = g2 // 2 if g2 < 6 else 3
                hf = g2 % 2
                ncol = 256 if g < 3 else 64
                c0 = hf * 256 if g < 3 else 0
                P.dma("sp", wt[:, :, 0:ncol], w_tm[g][:, :, c0:c0 + ncol], "p1wt", reads=["w_tm%d" % g], writes=["wt"])
                for i in range(8):
                    pm, pmk = psM.next()
                    for kc in range(32):
                        P.op("pe", lambda e, pm=pm, wt=wt, kc=kc, i=i, ncol=ncol: e.matmul(pm[:, 0:ncol], hT[:, kc, i * 128:(i + 1) * 128], wt[:, kc, 0:ncol],
                                                                                        start=(kc == 0), stop=(kc == 31)),
                             reads=["wt"] + (["hT%d_%d" % (i, gg) for gg in range(4)] if kc in (0, 31) else []), writes=[pmk])
                    tok0 = t0s + i * 128
                    if g < 2:
                        o_, ok = obR.next()
                        P.op("act", lambda e, o_=o_, pm=pm: e.activation(o_[:, 0:256], pm[:, 0:256], AF.Silu), reads=[pmk], writes=[ok])
                        P.dma("sp", sz_tm[tok0:tok0 + 128, g * 512 + c0:g * 512 + c0 + 256], o_[:, 0:256], "p1o", reads=[ok], writes=["sz_tm"])
                    elif g == 2:
                        o_, ok = obR.next()
                        P.op("act", lambda e, o_=o_, pm=pm: e.copy(o_[:, 0:256], pm[:, 0:256]), reads=[pmk], writes=[ok])
                        P.dma("sp", vsw_tm[tok0:tok0 + 128, c0:c0 + 256], o_[:, 0:256], "p1o", reads=[ok], writes=["vsw_tm"])
                    else:
                        s_, sk = smR.next()
                        P.op("act", lambda e, s_=s_, pm=pm: e.activation(s_[:, 0:8], pm[:, 0:8], AF.Sigmoid), reads=[pmk], writes=[sk + "a"])
                        P.op("act", lambda e, s_=s_, pm=pm: e.activation(s_[:, 16:40], pm[:, 16:40], AF.Sigmoid), reads=[pmk], writes=[sk + "c"])
                        P.op("dve", lambda e, s_=s_, pm=pm: e.tensor_tensor(s_[:, 8:16], pm[:, 8:16], dtbb[:], op=ALU.add), reads=[pmk, "dtbb"], writes=[sk + "b"])
                        P.op("act", lambda e, s_=s_: e.activation(s_[:, 8:16], s_[:, 8:16], AF.Exp), reads=[sk + "b"], writes=[sk + "b"])
                        P.op("act", lambda e, s_=s_: e.activation(s_[:, 8:16], s_[:, 8:16], AF.Ln, bias=1.0, scale=1.0), reads=[sk + "b"], writes=[sk + "b"])
                        P.op("dve", lambda e, s_=s_: e.tensor_tensor(s_[:, 8:16], s_[:, 8:16], nab[:], op=ALU.mult), reads=[sk + "b", "nab"], writes=[sk + "b"])
                        P.op("pool", lambda e, s_=s_: e.memset(s_[:, 40:64], 0.0), writes=[sk + "d"])
                        P.dma("sp", sm_tm[tok0:tok0 + 128, :], s_[:], "p1o", reads=[sk + "a", sk + "b", sk + "c", sk + "d"], writes=["sm_tm"])
      if start <= 1:
        _ph1()
    P.barrier()
    if nphase < 2:
        P.finish(); P.emit(); return nc

    with ExitStack() as st:
      def _ph2():
        sm_all = P.sb("p2_sm", [128, 32, 64], F32, st)
        triu = P.sb("p2_triu", [128, 128], F32, st)
        dmask = P.sb("p2_dmask", [128, 128], F32, st)
        oneh = P.sb("p2_oneh", [32, 32, 128], F32, st)
        gnwb = P.sb("p2_gnwb", [128, 128], F32, st)
        mskU = P.sb("p2_mskU", [128, 7, 128], F32, st)
        mskL = P.sb("p2_mskL", [128, 7, 128], F32, st)
        P.dma("sp", mskU[:], t_mskU[:, :, :], "p2c", writes=["mskU"])
        P.dma("sp", mskL[:], t_mskL[:, :, :], "p2c", writes=["mskL"])
        P.dma("sp", sm_all[:], sm_tm.rearrange("(c p) f -> p c f", p=128), "p2c", reads=["sm_tm"], writes=["sm_all"])
        P.dma("sp", triu[:], t_triu[:, :], "p2c", writes=["triu"])
        P.dma("sp", dmask[:], t_dmask[:, :], "p2c", writes=["dmask"])
        P.dma("sp", oneh[:], t_oneh[:, :, :], "p2c", writes=["oneh"])
        P.dma("sp", gnwb[:], gnw[0:1, :].partition_broadcast(128), "p2c", writes=["gnwb"])
        HB = []
        for par in range(2):
            hbuf = {}
            for nm in ("qT", "kT"):
                hbuf[nm] = P.sb("p2_%s%d" % (nm, par), [128, T], BF16, st)
            for nm in ("ktm", "vtm", "sz", "XT", "QKD", "QdT", "Kd"):
                hbuf[nm] = P.sb("p2_%s%d" % (nm, par), [128, 32, 128], BF16, st)
            hbuf["mix"] = P.sb("p2_mix%d" % par, [128, T], BF16, st) if par == 0 else HB[0]["mix"]
            hbuf["sc"] = P.sb("p2_sc%d" % par, [128, 6, 32], F32, st)
            hbuf["gamT"] = P.sb("p2_gamT%d" % par, [32, 128], F32, st)
            HB.append(hbuf)
        S = P.sb("p2_S", [128, 128], F32, st)
        Sb = P.sb("p2_Sb", [128, 128], BF16, st)
        tA = [dict(dd=P.sb("p2_dd%d" % i, [128, 128], F32, st), DT=P.sb("p2_DT%d" % i, [128, 128], F32, st),
                   Eg=P.sb("p2_Eg%d" % i, [128, 128], F32, st), Mf=P.sb("p2_Mf%d" % i, [128, 128], F32, st),
                   Lf=P.sb("p2_Lf%d" % i, [128, 128], F32, st), Cu=P.sb("p2_Cu%d" % i, [128, 128], F32, st), Cl=P.sb("p2_Cl%d" % i, [128, 128], F32, st),
                   T1=P.sb("p2_T1%d" % i, [128, 128], F32, st), T1p=P.sb("p2_T1p%d" % i, [128, 128], F32, st),
                   U=[P.sb("p2_U%d_%d" % (i, j), [128, 128], F32, st) for j in range(2)],
                   L=[P.sb("p2_L%d_%d" % (i, j), [128, 128], F32, st) for j in range(2)]) for i in range(2)]
        tB = [dict(R=P.sb("p2_R%d" % i, [128, 128], BF16, st), vn=P.sb("p2_vn%d" % i, [128, 128], BF16, st),
                   j1=P.sb("p2_j1%d" % i, [128, 128], F32, st), j2=P.sb("p2_j2%d" % i, [128, 128], F32, st),
                   om=P.sb("p2_om%d" % i, [128, 128], BF16, st), st=P.sb("p2_st%d" % i, [128, 4], F32, st)) for i in range(2)]

        def head_load(hd):
            par = hd % 2; hb_ = HB[par]; k = "h%d" % par
            P.dma("sp", hb_["qT"][:], gqT[hd], "p2l%d" % par, reads=["gqT"], writes=[k + "qT"])
            P.dma("sp", hb_["kT"][:], gkT[hd], "p2l%d" % par, reads=["gkT"], writes=[k + "kT"])
            P.dma("sp", hb_["ktm"][:], gk_tm[hd].rearrange("(c p) d -> p c d", p=128), "p2l%d" % par, reads=["gtm"], writes=[k + "ktm"])
            P.dma("sp", hb_["vtm"][:], gv_tm[hd].rearrange("(c p) d -> p c d", p=128), "p2l%d" % par, reads=["gtm"], writes=[k + "vtm"])
            P.dma("sp", hb_["sz"][:], sz_tm[:, hd * 128:(hd + 1) * 128].rearrange("(c p) d -> p c d", p=128), "p2l%d" % par,
                  reads=["sz_tm"], writes=[k + "sz"])

        def head_pre(hd):
            par = hd % 2; hb_ = HB[par]; k = "h%d" % par
            sc = hb_["sc"]
            g_h = sc[:, 5, :]
            P.op("dve", lambda e: e.tensor_copy(sc[:, 5, :], sm_all[:, :, 8 + hd]), reads=["sm_all"], writes=[k + "gh"])
            P.op("pe", lambda e: e.matmul(psb[0][:, 0:32], triu[:], g_h, start=True, stop=True), reads=[k + "gh", "triu"], writes=["ps0a"])
            P.op("pe", lambda e: e.matmul(psb[0][:, 32:64], onesf[:], g_h, start=True, stop=True), reads=[k + "gh", "onesf"], writes=["ps0b"])
            if gdbg == "pre0":
                return
            P.op("act", lambda e: e.copy(sc[:, 0, :], psb[0][:, 0:32]), reads=["ps0a"], writes=[k + "gam"])
            P.op("dve", lambda e: e.tensor_scalar(sc[:, 1, :], psb[0][:, 0:32], -1.0, None, op0=ALU.mult), reads=["ps0a"], writes=[k + "ngam"])
            P.op("act", lambda e: e.activation(sc[:, 2, :], psb[0][:, 0:32], AF.Exp), reads=["ps0a"], writes=[k + "negeg"])
            P.op("dve", lambda e: e.tensor_scalar(sc[:, 2, :], sc[:, 2, :], -1.0, None, op0=ALU.mult), reads=[k + "negeg"], writes=[k + "negeg"])
            P.op("dve", lambda e: e.tensor_tensor(sc[:, 3, :], psb[0][:, 32:64], sc[:, 0, :], op=ALU.subtract), reads=["ps0b", k + "gam"], writes=[k + "kdec"])
            P.op("act", lambda e: e.activation(sc[:, 3, :], sc[:, 3, :], AF.Exp), reads=[k + "kdec"], writes=[k + "kdec"])
            P.op("act", lambda e: e.activation(sc[:, 4, :], psb[0][:, 32:64], AF.Exp), reads=["ps0b"], writes=[k + "egl"])
            if gdbg == "pre1":
                return
            P.op("pe", lambda e: e.matmul(psb[0][0:32, 128:256], sc[:, 0, :], identf[:], start=True, stop=True), reads=[k + "gam", "identf"], writes=["ps0c"])
            P.op("act", lambda e: e.copy(hb_["gamT"][:], psb[0][0:32, 128:256]), reads=["ps0c"], writes=[k + "gamT"])

        def GA(hd, c):
            par = hd % 2; hb_ = HB[par]; k = "h%d" % par
            pc = c % 2; ta = tA[pc]; a_ = "A%d" % pc
            sc = hb_["sc"]
            cs = slice(c * 128, (c + 1) * 128)
            qTc = hb_["qT"][:, cs]; kTc = hb_["kT"][:, cs]
            bA = 1 + 2 * pc; bB = 2 + 2 * pc
            pG = psb[bA][:, 0:128]; pGk = "ps%dG" % bA
            pKK = psb[bA][:, 128:256]; pQK = psb[bA][:, 256:384]; pKk = "ps%dK" % bA
            pP = psb[bB][:, 0:128]; pQ = psb[bB][:, 128:256]
            pPk = "ps%dP" % bB; pQk = "ps%dQ" % bB
            pY = psb[bB][:, 256:384]; pYk = "ps%dY" % bB
            pLt = psb[bA][:, 384:512]; pLk = "ps%dL" % bA
            P.op("pe", lambda e: e.matmul(pG, oneh[:, c, :], hb_["gamT"][:], start=True, stop=True), reads=["oneh", k + "gamT"], writes=[pGk])
            yield
            P.op("dve", lambda e: e.scalar_tensor_tensor(out=ta["dd"][:], in0=pG, scalar=sc[:, 1, c:c + 1], in1=dmask[:], op0=ALU.add, op1=ALU.add),
                 reads=[pGk, k + "ngam", "dmask"], writes=[a_ + "dd"])
            yield
            P.op("act", lambda e: e.activation(ta["DT"][:], ta["dd"][:], AF.Exp), reads=[a_ + "dd"], writes=[a_ + "DT"])
            yield
            P.op("act", lambda e: e.activation(ta["Eg"][:], pG, AF.Exp), reads=[pGk], writes=[a_ + "Eg"])
            yield
            P.op("pool", lambda e: e.tensor_tensor(hb_["QdT"][:, c, :], qTc, ta["Eg"][:], op=ALU.mult), reads=[k + "qT", a_ + "Eg"], writes=[k + "QdT%d" % c])
            yield
            P.op("pool", lambda e: e.tensor_scalar(hb_["Kd"][:, c, :], hb_["ktm"][:, c, :], sc[:, 3, c:c + 1], None, op0=ALU.mult),
                 reads=[k + "ktm", k + "kdec"], writes=[k + "Kd%d" % c])
            yield
            P.op("pe", lambda e: e.matmul(pKK, kTc, kTc, start=True, stop=True), reads=[k + "kT"], writes=[pKk + "a"])
            yield
            P.op("pe", lambda e: e.matmul(pQK, kTc, qTc, start=True, stop=True), reads=[k + "kT", k + "qT"], writes=[pKk + "b"])
            yield
            P.op("dve", lambda e: e.scalar_tensor_tensor(out=ta["Mf"][:], in0=pKK, scalar=sm_all[:, c, hd:hd + 1], in1=ta["DT"][:], op0=ALU.mult, op1=ALU.mult),
                 reads=[pKk + "a", "sm_all", a_ + "DT"], writes=[a_ + "Mf"])
            yield
            P.op("dve", lambda e: e.tensor_tensor(hb_["QKD"][:, c, :], pQK, ta["DT"][:], op=ALU.mult), reads=[pKk + "b", a_ + "DT"], writes=[k + "QKD%d" % c])
            yield
            Mf, Lf, Cu, Cl, T1, T1p, U, L = ta["Mf"], ta["Lf"], ta["Cu"], ta["Cl"], ta["T1"], ta["T1p"], ta["U"], ta["L"]
            pT1 = psb[bB][:, 0:128]; pT1p = psb[bB][:, 128:256]; pT2 = psb[bB][:, 256:384]; pT2p = psb[bB][:, 384:512]
            kb_ = "ps%d" % bB
            P.op("pe", lambda e: e.matmul(pLt, Mf[:], identf[:], start=True, stop=True), reads=[a_ + "Mf", "identf"], writes=[pLk])
            yield
            P.op("act", lambda e: e.copy(Lf[:], pLt), reads=[pLk], writes=[a_ + "Lf"])
            yield
            P.op("pool", lambda e: e.tensor_tensor(Cu[:], Mf[:], mskU[:, 0, :], op=ALU.mult), reads=[a_ + "Mf", "mskU"], writes=[a_ + "Cu"])
            yield
            P.op("pool", lambda e: e.tensor_tensor(Cl[:], Lf[:], mskL[:, 0, :], op=ALU.mult), reads=[a_ + "Lf", "mskL"], writes=[a_ + "Cl"])
            yield
            P.op("dve", lambda e: e.tensor_tensor(U[0][:], identf[:], Cu[:], op=ALU.subtract), reads=["identf", a_ + "Cu"], writes=[a_ + "U0"])
            yield
            P.op("dve", lambda e: e.tensor_tensor(L[0][:], identf[:], Cl[:], op=ALU.subtract), reads=["identf", a_ + "Cl"], writes=[a_ + "L0"])
            yield
            cur = 0
            for lvl in range(1, 7):
                nx = 1 - cur
                lastl = (lvl == 6)
                P.op("pool", lambda e, lvl=lvl: e.tensor_tensor(Cl[:], Lf[:], mskL[:, lvl, :], op=ALU.mult), reads=[a_ + "Lf", "mskL"], writes=[a_ + "Cl"])
                yield
                P.op("pe", lambda e, cur=cur: e.matmul(pT1, Cl[:], U[cur][:], start=True, stop=True), reads=[a_ + "Cl", a_ + "U%d" % cur], writes=[kb_ + "a"])
                yield
                P.op("act", lambda e: e.copy(T1[:], pT1), reads=[kb_ + "a"], writes=[a_ + "T1"])
                yield
                if not lastl:
                    P.op("pool", lambda e, lvl=lvl: e.tensor_tensor(Cu[:], Mf[:], mskU[:, lvl, :], op=ALU.mult), reads=[a_ + "Mf", "mskU"], writes=[a_ + "Cu"])
                    yield
                    P.op("pe", lambda e, cur=cur: e.matmul(pT1p, Cu[:], L[cur][:], start=True, stop=True), reads=[a_ + "Cu", a_ + "L%d" % cur], writes=[kb_ + "b"])
                    yield
                    P.op("dve", lambda e: e.tensor_copy(T1p[:], pT1p), reads=[kb_ + "b"], writes=[a_ + "T1p"])
                    yield
                P.op("pe", lambda e, cur=cur: e.matmul(pT2, L[cur][:], T1[:], start=True, stop=True), reads=[a_ + "L%d" % cur, a_ + "T1"], writes=[kb_ + "c"])
                yield
                if not lastl:
                    P.op("pe", lambda e, cur=cur: e.matmul(pT2p, U[cur][:], T1p[:], start=True, stop=True), reads=[a_ + "U%d" % cur, a_ + "T1p"], writes=[kb_ + "d"])
                    yield
                    P.op("dve", lambda e, cur=cur, nx=nx: e.tensor_tensor(U[nx][:], U[cur][:], pT2, op=ALU.subtract), reads=[kb_ + "c", a_ + "U%d" % cur], writes=[a_ + "U%d" % nx])
                    yield
                    P.op("dve", lambda e, cur=cur, nx=nx: e.tensor_tensor(L[nx][:], L[cur][:], pT2p, op=ALU.subtract), reads=[kb_ + "d", a_ + "L%d" % cur], writes=[a_ + "L%d" % nx])
                    yield
                else:
                    P.op("dve", lambda e, cur=cur: e.tensor_tensor(hb_["XT"][:, c, :], U[cur][:], pT2, op=ALU.subtract), reads=[kb_ + "c", a_ + "U%d" % cur], writes=[k + "XT%d" % c])
                    yield
                cur = nx

        def GB(hd, c):
            par = hd % 2; hb_ = HB[par]; k = "h%d" % par
            pc = c % 2; tb = tB[pc]; b_ = "B%d" % pc
            sc = hb_["sc"]
            cs = slice(c * 128, (c + 1) * 128)
            kTc = hb_["kT"][:, cs]
            pKS = psb[5][:, 0:128]; pVN = psb[5][:, 128:256]
            pO = psb[6][:, 0:128] if pc == 0 else psb[0][:, 256:384]
            pOk = "ps6_O" if pc == 0 else "ps0_O"
            pD = psb[7][:, 0:128]
            pOt = psb[7][:].bitcast(BF16)[:, 512:640]
            if c == 0:
                P.op("pool", lambda e: e.memset(S[:], 0.0), writes=["S"])
                yield
                P.op("pool", lambda e: e.memset(Sb[:], 0.0), writes=["Sb"])
                yield
            P.op("pe", lambda e: e.matmul(pKS, kTc, Sb[:], start=True, stop=True), reads=[k + "kT", "Sb"], writes=["ps5a"])
            yield
            P.op("dve", lambda e: e.scalar_tensor_tensor(out=tb["R"][:], in0=pKS, scalar=sc[:, 2, c:c + 1], in1=hb_["vtm"][:, c, :], op0=ALU.mult, op1=ALU.add),
                 reads=["ps5a", k + "negeg", k + "vtm"], writes=[b_ + "R"])
            yield
            P.op("pe", lambda e: e.matmul(pVN, hb_["XT"][:, c, :], tb["R"][:], start=True, stop=True), reads=[k + "XT%d" % c, b_ + "R"], writes=["ps5b"])
            yield
            P.op("act", lambda e: e.activation(tb["vn"][:], pVN, AF.Copy, scale=sm_all[:, c, hd:hd + 1]), reads=["ps5b", "sm_all"], writes=[b_ + "vn"])
            yield
            P.op("pe", lambda e: e.matmul(pO, hb_["QdT"][:, c, :], Sb[:], start=True, stop=False), reads=[k + "QdT%d" % c, "Sb"], writes=[pOk])
            yield
            P.op("pe", lambda e: e.matmul(pO, hb_["QKD"][:, c, :], tb["vn"][:], start=False, stop=True), reads=[k + "QKD%d" % c, b_ + "vn"], writes=[pOk])
            yield
            P.op("pe", lambda e: e.matmul(pD, hb_["Kd"][:, c, :], tb["vn"][:], start=True, stop=True), reads=[k + "Kd%d" % c, b_ + "vn"], writes=["ps7d"])
            yield
            P.op("dve", lambda e: e.scalar_tensor_tensor(out=S[:], in0=S[:], scalar=sc[:, 4, c:c + 1], in1=pD, op0=ALU.mult, op1=ALU.add),
                 reads=["S", k + "egl", "ps7d"], writes=["S"])
            yield
            P.op("act", lambda e: e.copy(Sb[:], S[:]), reads=["S"], writes=["Sb"])
            yield
            P.op("act", lambda e: e.activation(tb["j1"][:], pO, AF.Square, accum_out=tb["st"][:, 0:1]), reads=[pOk], writes=[b_ + "j1", b_ + "s0"])
            yield
            P.op("act", lambda e: e.activation(tb["st"][:, 1:2], tb["st"][:, 0:1], AF.Sqrt, bias=1e-6, scale=1.0 / 128.0), reads=[b_ + "s0"], writes=[b_ + "s1"])
            yield
            P.op("dve", lambda e: e.reciprocal(tb["st"][:, 2:3], tb["st"][:, 1:2]), reads=[b_ + "s1"], writes=[b_ + "s2"])
            yield
            P.op("dve", lambda e: e.scalar_tensor_tensor(out=tb["j2"][:], in0=pO, scalar=tb["st"][:, 2:3], in1=gnwb[:], op0=ALU.mult, op1=ALU.mult),
                 reads=[pOk, b_ + "s2", "gnwb"], writes=[b_ + "j2"])
            yield
            P.op("pool", lambda e: e.tensor_tensor(tb["om"][:], tb["j2"][:], hb_["sz"][:, c, :], op=ALU.mult), reads=[b_ + "j2", k + "sz"], writes=[b_ + "om"])
            yield
            P.op("pe", lambda e: e.transpose(pOt, tb["om"][:], ident[:]), reads=[b_ + "om", "ident"], writes=["ps7t"])
            yield
            P.op("act", lambda e: e.copy(hb_["mix"][:, cs], pOt), reads=["ps7t"], writes=["mix%d" % c])
            yield
            if c == 31:
                P.dma("sp", mixT[hd], hb_["mix"][:], "p2o", reads=["mix%d" % cc_ for cc_ in range(32)], writes=["mixT"])
                yield

        head_load(0)
        if gdbg == "load":
            pass
        elif gdbg in ("pre", "pre0", "pre1"):
            head_pre(0)
        elif gdbg == "ga1":
            head_pre(0); list(GA(0, 0))
        elif gdbg == "ga":
            head_pre(0)
            for c in range(32):
                list(GA(0, c))
        elif gdbg == "gb1":
            head_pre(0)
            for c in range(32):
                list(GA(0, c))
            list(GB(0, 0))
        else:
          NH = only_heads
          def dump(name, ap_, shape, dt, reads):
              t_ = nc.dram_tensor(name, list(shape), dt, kind="ExternalOutput").ap()
              P.dma("sp", t_, ap_, "dbgd", reads=reads)
          for hd in range(NH + 1):
            if dbg and gdbg == "dump" and hd == NH:
                hb_ = HB[(NH - 1) % 2]; k = "h%d" % ((NH - 1) % 2)
                allk = [k + "%s%d" % (nm, c_) for nm in ("XT", "QKD", "QdT", "Kd") for c_ in range(32)]
                for nm in ("XT", "QKD", "QdT", "Kd", "ktm", "vtm", "sz"):
                    dump("d_" + nm, hb_[nm][:], [128, 32, 128], BF16, allk + [k + "ktm", k + "vtm", k + "sz"])
                dump("d_sc", hb_["sc"][:], [128, 6, 32], F32, [k + x for x in ("gam", "ngam", "negeg", "kdec", "egl", "gh")])
                dump("d_gamT", hb_["gamT"][:], [32, 128], F32, [k + "gamT"])
                dump("d_kT", hb_["kT"][:], [128, T], BF16, [k + "kT"])
            if hd < NH:
                head_pre(hd)
            for c0 in range(0, 32, 2):
                gens = []
                if hd < NH:
                    cast_step(3 if hd == 0 else 2)
                    gens += [GA(hd, c0), GA(hd, c0 + 1)]
                if hd > 0:
                    def gbchain(h_=hd - 1, c_=c0):
                        yield from GB(h_, c_)
                        yield from GB(h_, c_ + 1)
                    gens.append(gbchain())
                while gens:
                    for g_ in list(gens):
                        try:
                            next(g_)
                        except StopIteration:
                            gens.remove(g_)
            if hd + 1 < NH:
                head_load(hd + 1)
      if start <= 2:
        _ph2()
        cast_step(10 ** 6)
    P.barrier()
    if nphase < 3:
        P.finish(); P.emit(); return nc

    with ExitStack() as st:
      def _ph3():
        sm_all = P.sb("p3_sm", [128, 32, 64], F32, st)
        winT = P.sb("p3_winT", [128, 8, 512], BF16, st)
        cmT = P.sb("p3_cmT", [128, 2, T], BF16, st)
        selE = P.sb("p3_selE", [64, 32, 128], BF16, st)
        frc = P.sb("p3_frc", [128, 32, 64], F32, st)
        kbt = P.sb("p3_kb", [128, 8, 32], F32, st)
        cbt = P.sb("p3_cb", [128, 8, 2, 8], F32, st)
        nslt = P.sb("p3_nsl", [1, 8], F32, st)
        trow = P.sb("p3_trow", [1, 512], F32, st)
        zer = P.sb("p3_zer", [128, 512], BF16, st)
        w1b = [P.sb("p3_w1%d" % i, [128, 32, 128], BF16, st) for i in range(2)]
        w2b = [P.sb("p3_w2%d" % i, [128, 128], BF16, st) for i in range(2)]
        posf = [P.sb("p3_pos%d" % i, [32, 128], F32, st) for i in range(2)]
        posT = [P.sb("p3_posT%d" % i, [128, 32], BF16, st) for i in range(2)]
        cvec = [P.sb("p3_cvec%d" % i, [128, 1], F32, st) for i in range(2)]
        kT4 = [P.sb("p3_kT%d" % i, [128, T], BF16, st) for i in range(4)]
        kcD = P.sb("p3_kcD", [128, 16, 256], BF16, st)
        hcm = P.sb("p3_hcm", [128, 256], BF16, st)
        kcmpT = P.sb("p3_kcmpT", [128, 256], BF16, st)
        VC = P.sb("p3_VC", [128, 2, 193], BF16, st)
        VS = P.sb("p3_VS", [128, 32, 129], BF16, st)
        VW = P.sb("p3_VW", [128, 32, 129], BF16, st)
        negK = P.sb("p3_negK", [1, 4], F32, st)
        kmx = P.sb("p3_kmx", [1, 32], F32, st)
        sqt = [P.sb("p3_sq%d" % i, [128, 512], BF16, st) for i in range(2)]
        qsb = [P.sb("p3_q%d" % i, [128, 4, 512], BF16, st) for i in range(2)]
        qn = P.sb("p3_qn", [1, 512], F32, st)
        srow = P.sb("p3_srow", [1, 512], F32, st)
        rrow = P.sb("p3_rrow", [1, 3, 512], BF16, st)
        PT = [P.sb("p3_PT%d" % i, [128, 512], BF16, st) for i in range(3)]
        oacc = P.sb("p3_oacc", [128, 4, 4, 128], F32, st)
        imp = P.sb("p3_imp", [128, 4, 64], F32, st)
        impp = P.sb("p3_impp", [128, 64], F32, st)
        impq = P.sb("p3_impq", [128, 64], F32, st)
        mx8 = P.sb("p3_mx8", [128, 16], F32, st)
        nsel = P.sb("p3_nsel", [128, 64], BF16, st)
        negselT = P.sb("p3_nselT", [64, 512], BF16, st)
        stt = P.sb("p3_stt", [128, 16], F32, st)
        ofin = P.sb("p3_ofin", [128, 128], BF16, st)
        mstage = [P.sb("p3_ms%d" % i, [128, 512], BF16, st) for i in range(2)]
        P.dma("sp", sm_all[:], sm_tm.rearrange("(c p) f -> p c f", p=128), "x", reads=["sm_tm"], writes=["sm_all"])
        for nm_, t_, src_ in (("winT", winT, t_winT), ("cmT", cmT, t_cmT), ("selE", selE, t_selE), ("frc", frc, t_frc),
                              ("kbt", kbt, t_kb), ("cbt", cbt, t_cb), ("nslt", nslt, t_nsl), ("trow", trow, t_trow)):
            P.dma("sp", t_[:], src_, "x", writes=[nm_])
        P.op("pool", lambda e: e.memset(zer[:], 0.0), writes=["zer"])
        for i, (w1_, w2_, pos_) in enumerate(((w1_k, w2_k, pos_k), (w1_v, w2_v, pos_v))):
            P.dma("pool", w1b[i][:], w1_.rearrange("(j d) o -> d j o", d=128), "x", writes=["w1b%d" % i])
            P.dma("pool", w2b[i][:], w2_[:, :], "x", writes=["w2b%d" % i])
            P.dma("sp", posf[i][:], pos_[:, :], "x", writes=["posf%d" % i])
            P.op("pe", lambda e, i=i: e.matmul(psb[6][:, 0:32], posf[i][:], identf[0:32, 0:32], start=True, stop=True), reads=["posf%d" % i, "identf"], writes=["ps6"])
            P.op("act", lambda e, i=i: e.copy(posT[i][:], psb[6][:, 0:32]), reads=["ps6"], writes=["posT%d" % i])
            for j in range(32):
                P.op("pe", lambda e, i=i, j=j: e.matmul(psb[6][:, 64:65], w1b[i][:, j, :], posT[i][:, j:j + 1], start=(j == 0), stop=(j == 31)),
                     reads=["w1b%d" % i, "posT%d" % i], writes=["ps6"])
            P.op("act", lambda e, i=i: e.copy(cvec[i][:], psb[6][:, 64:65]), reads=["ps6"], writes=["cvec%d" % i])

        psS = Rot([(psb[0], "ps0"), (psb[1], "ps1")])
        PTR = Rot([(PT[0], "PT0"), (PT[1], "PT1"), (PT[2], "PT2")])
        sqR = Rot([(sqt[0], "sq0"), (sqt[1], "sq1")])

        def zero_bank(b):
            P.op("pe", lambda e, b=b: e.matmul(psb[b][:, :], zer[:, 0:128], zer[:, :], start=True, stop=True, skip_group_check=True),
                 reads=["zer"], writes=["ps%d" % b])

        def do_group(gl):
            gk = "g"
            for i in range(4):
                P.dma("sp", kT4[i][:], kvT[2 * i + gl], "x", reads=["kvT"], writes=["kT4_%d" % i])
            P.dma("sp", VS[:, :, 0:128], vsw_tm[:, gl * 128:(gl + 1) * 128].rearrange("(c p) d -> p c d", p=128), "x", reads=["vsw_tm"], writes=["VSd"])
            P.dma("sp", VW[:, :, 0:128], vsw_tm[:, 256 + gl * 128:256 + (gl + 1) * 128].rearrange("(c p) d -> p c d", p=128), "x", reads=["vsw_tm"], writes=["VWd"])
            P.op("pool", lambda e: e.memset(VS[:, :, 128:129], 1.0), writes=["VS1"])
            P.op("pool", lambda e: e.memset(VW[:, :, 128:129], 1.0), writes=["VW1"])
            for i in range(2):
                src = kT4[i]
                P.op("dve", lambda e, src=src: e.tensor_copy(kcD[:], src[:].rearrange("p (n r) -> p r n", r=16)), reads=["kT4_%d" % i], writes=["kcD"])
                for j in range(32):
                    P.op("pe", lambda e, i=i, j=j: e.matmul(psb[6][:, 0:255], w1b[i][:, j, :], kcD[:, j % 16, j // 16:j // 16 + 255], start=(j == 0), stop=(j == 31)),
                         reads=["w1b%d" % i, "kcD"], writes=["ps6"])
                P.op("pool", lambda e: e.memset(hcm[:, 255:256], 0.0), writes=["hcm1"])
                P.op("act", lambda e, i=i: e.activation(hcm[:, 0:255], psb[6][:, 0:255], AF.Silu, bias=cvec[i][:, 0:1], scale=1.0), reads=["ps6", "cvec%d" % i], writes=["hcm"])
                if i == 0:
                    P.op("pe", lambda e: e.matmul(psb[7][:, 0:256], w2b[0][:], hcm[:], start=True, stop=True), reads=["w2b0", "hcm", "hcm1"], writes=["ps7"])
                    P.op("act", lambda e: e.copy(kcmpT[:], psb[7][:, 0:256]), reads=["ps7"], writes=["kcmpT"])
                else:
                    for nc_ in range(2):
                        P.op("pe", lambda e, nc_=nc_: e.matmul(psb[7][:, nc_ * 128:(nc_ + 1) * 128], hcm[:, nc_ * 128:(nc_ + 1) * 128], w2b[1][:], start=True, stop=True),
                             reads=["w2b1", "hcm", "hcm1"], writes=["ps7"])
                    P.op("act", lambda e: e.copy(VC[:, :, 0:128], psb[7][:, 0:256].rearrange("p (c d) -> p c d", c=2)), reads=["ps7"], writes=["VCd"])
                    P.op("pool", lambda e: e.memset(VC[:, :, 128:129], 1.0), writes=["VC1"])
                    P.dma("sp", VC[:, :, 129:193], t_ovl[:, :, :], "x", writes=["VCo"])
            for br, (src, ncols) in enumerate(((kcmpT, 256), (kT4[2], T), (kT4[3], T))):
                nch = max(1, ncols // 512)
                for cc_ in range(nch):
                    w_ = min(512, ncols)
                    s_, sk = sqR.next()
                    P.op("pool", lambda e, s_=s_, src=src, cc_=cc_, w_=w_: e.tensor_tensor(s_[:, 0:w_], src[:, cc_ * 512:cc_ * 512 + w_], src[:, cc_ * 512:cc_ * 512 + w_], op=ALU.mult),
                         reads=["kcmpT", "kT4_2", "kT4_3"], writes=[sk])
                    P.op("pe", lambda e, s_=s_, w_=w_: e.matmul(psb[6][0:1, 0:w_], onesb[:, 0:1], s_[:, 0:w_], start=True, stop=True), reads=[sk, "onesb"], writes=["ps6"])
                    P.op("dve", lambda e, br=br, cc_=cc_, w_=w_: e.reduce_max(kmx[:, br * 8 + cc_:br * 8 + cc_ + 1], psb[6][0:1, 0:w_], axis=AX.X), reads=["ps6"], writes=["kmx"])
                P.op("dve", lambda e, br=br, nch=nch: e.reduce_max(negK[:, br:br + 1], kmx[:, br * 8:br * 8 + nch], axis=AX.X), reads=["kmx"], writes=["negK%d" % br])
                P.op("act", lambda e, br=br: e.activation(negK[:, br:br + 1], negK[:, br:br + 1], AF.Sqrt), reads=["negK%d" % br], writes=["negK%d" % br])
                P.op("dve", lambda e, br=br: e.tensor_scalar(negK[:, br:br + 1], negK[:, br:br + 1], -1.0, None, op0=ALU.mult), reads=["negK%d" % br], writes=["negK%d" % br])

            def do_G(G):
                q_ = qsb[G % 2]; qk_ = "q%d" % (G % 2)
                for r in range(4):
                    P.dma("sp", q_[:, r, :], nqT[gl * 4 + r][:, G * 512:(G + 1) * 512], "x", reads=["nqT"], writes=[qk_ + "_%d" % r])
                P.op("pool", lambda e: e.memset(imp[:], 0.0), writes=["imp"])

                def make_rrow(r):
                    hr = gl * 4 + r
                    s_, sk = sqR.next()
                    P.op("pool", lambda e, s_=s_: e.tensor_tensor(s_[:], q_[:, r, :], q_[:, r, :], op=ALU.mult), reads=[qk_ + "_%d" % r], writes=[sk])
                    P.op("pe", lambda e, s_=s_: e.matmul(psb[6][0:1, :], onesb[:, 0:1], s_[:], start=True, stop=True), reads=[sk, "onesb"], writes=["ps6"])
                    P.op("act", lambda e: e.activation(qn[:], psb[6][0:1, :], AF.Sqrt), reads=["ps6"], writes=["qn"])
                    P.op("dve", lambda e: e.tensor_scalar(srow[:], trow[:], nslt[0:1, hr:hr + 1], None, op0=ALU.mult), reads=["trow", "nslt"], writes=["srow"])
                    for br in range(3):
                        P.op("dve", lambda e, br=br: e.scalar_tensor_tensor(out=rrow[:, br, :], in0=qn[:], scalar=negK[0:1, br:br + 1], in1=srow[:], op0=ALU.mult, op1=ALU.add),
                             reads=["qn", "negK%d" % br, "srow"], writes=["rrow%d" % br])

                def scores(kTsrc, kkey, kc, r, br, extra):
                    ps_, psk = psS.next()
                    P.op("pe", lambda e: e.matmul(ps_[:, :], kTsrc[:, kc * 128:(kc + 1) * 128], q_[:, r, :], start=True, stop=False),
                         reads=[kkey, qk_ + "_%d" % r], writes=[psk])
                    P.op("pe", lambda e: e.matmul(ps_[:, :], onesb[0:1, :], rrow[0:1, br, :], start=False, stop=(len(extra) == 0)),
                         reads=["onesb", "rrow%d" % br], writes=[psk])
                    for ei, (l_, r_, rk_) in enumerate(extra):
                        P.op("pe", lambda e, l_=l_, r_=r_, ei=ei: e.matmul(ps_[:, :], l_, r_, start=False, stop=(ei == len(extra) - 1)), reads=rk_, writes=[psk])
                    return ps_, psk

                def pass1(r):
                    hr = gl * 4 + r
                    make_rrow(r)
                    zero_bank(2); zero_bank(3)
                    ncs = (0, 1) if G >= 4 else (0,)
                    pend = [scores(kcmpT, "kcmpT", ncs[0], r, 0, [(ident[:], cmT[:, ncs[0], G * 512:(G + 1) * 512], ["ident", "cmT"])])]
                    for ni, nc_ in enumerate(ncs):
                        if ni + 1 < len(ncs):
                            n2 = ncs[ni + 1]
                            pend.append(scores(kcmpT, "kcmpT", n2, r, 0, [(ident[:], cmT[:, n2, G * 512:(G + 1) * 512], ["ident", "cmT"])]))
                        ps_, psk = pend.pop(0)
                        pt_, ptk = PTR.next()
                        P.op("act", lambda e, ps_=ps_, pt_=pt_, nc_=nc_: e.activation(pt_[:], ps_[:, :], AF.Exp, bias=cbt[:, hr, nc_, G:G + 1], scale=1.0), reads=[psk, "cbt"], writes=[ptk])
                        for m in range(4):
                            b_ = 2 + m // 2; o_ = (m % 2) * 193
                            P.op("pe", lambda e, pt_=pt_, m=m, b_=b_, o_=o_, nc_=nc_: e.matmul(psb[b_][:, o_:o_ + 193], pt_[:, m * 128:(m + 1) * 128], VC[:, nc_, :], start=False, stop=(nc_ == ncs[-1]), skip_group_check=True),
                                 reads=[ptk, "VCd", "VC1", "VCo"], writes=["ps%d" % b_])
                    for m in range(4):
                        b_ = 2 + m // 2; o_ = (m % 2) * 193; qt = G * 4 + m
                        po = psb[b_]
                        P.op("dve", lambda e, po=po, o_=o_: e.tensor_scalar(stt[:, 0:1], po[:, o_ + 128:o_ + 129], 1e-30, None, op0=ALU.max), reads=["ps%d" % b_], writes=["stt0"])
                        P.op("dve", lambda e: e.reciprocal(stt[:, 1:2], stt[:, 0:1]), reads=["stt0"], writes=["stt1"])
                        P.op("dve", lambda e, po=po, o_=o_, m=m: e.scalar_tensor_tensor(out=imp[:, m, :], in0=po[:, o_ + 129:o_ + 193], scalar=stt[:, 1:2], in1=imp[:, m, :], op0=ALU.mult, op1=ALU.add),
                             reads=["ps%d" % b_, "stt1", "imp"], writes=["imp"])
                        P.op("dve", lambda e, qt=qt, hr=hr: e.tensor_tensor(stt[:, 2:3], stt[:, 1:2], sm_all[:, qt, 16 + hr * 3:16 + hr * 3 + 1], op=ALU.mult), reads=["stt1", "sm_all"], writes=["stt2"])
                        P.op("act", lambda e, po=po, o_=o_, r=r, m=m: e.activation(oacc[:, r, m, :], po[:, o_:o_ + 128], AF.Copy, scale=stt[:, 2:3]), reads=["ps%d" % b_, "stt2"], writes=["oacc%d_%d" % (r, m)])
                def select(m):
                    qt = G * 4 + m
                    P.op("dve", lambda e, m=m, qt=qt: e.tensor_tensor(impp[:], imp[:, m, :], frc[:, qt, :], op=ALU.add), reads=["imp", "frc"], writes=["impp"])
                    P.op("dve", lambda e: e.max(out=mx8[:, 0:8], in_=impp[:]), reads=["impp"], writes=["mx8a"])
                    P.op("dve", lambda e: e.match_replace(out=impq[:], in_to_replace=mx8[:, 0:8], in_values=impp[:], imm_value=-3e38), reads=["impp", "mx8a"], writes=["impq"])
                    P.op("dve", lambda e: e.max(out=mx8[:, 8:16], in_=impq[:]), reads=["impq"], writes=["mx8b"])
                    P.op("dve", lambda e: e.tensor_scalar(impq[:], impp[:], mx8[:, 15:16], None, op0=ALU.is_ge), reads=["impp", "mx8b"], writes=["impq"])
                    P.op("dve", lambda e: e.tensor_scalar(nsel[:], impq[:], 1.0, -NEG, op0=ALU.subtract, op1=ALU.mult), reads=["impq"], writes=["nsel"])
                    P.op("pe", lambda e: e.transpose(psb[7][:].bitcast(BF16)[0:64, 0:128], nsel[:], ident[:]), reads=["nsel", "ident"], writes=["ps7"])
                    P.op("act", lambda e, m=m: e.copy(negselT[:, m * 128:(m + 1) * 128], psb[7][:].bitcast(BF16)[0:64, 0:128]), reads=["ps7"], writes=["nselT%d" % m])
                def pass2(r):
                    hr = gl * 4 + r
                    make_rrow(r)
                    for b_ in (2, 3, 4, 5):
                        zero_bank(b_)
                    last = 4 * G + 3
                    def sel_scores(kc):
                        extra = [(selE[:, kc, :], negselT[:, :], ["selE"] + ["nselT%d" % m for m in range(4)])]
                        if kc >= 4 * G:
                            extra.append((ident[:], winT[:, 4 + kc - 4 * G, :], ["ident", "winT"]))
                        return scores(kT4[2], "kT4_2", kc, r, 1, extra)

                    def win_scores(kc):
                        return scores(kT4[3], "kT4_3", kc, r, 2, [(ident[:], winT[:, 4 + kc - 4 * G, :], ["ident", "winT"])])

                    wlist = list(range(max(0, 4 * G - 4), last + 1))
                    pend = [sel_scores(0)]
                    for kc in range(0, last + 1):
                        if kc + 1 <= last:
                            pend.append(sel_scores(kc + 1))
                        else:
                            pend.append(win_scores(wlist[0]))
                        ps_, psk = pend.pop(0)
                        pt_, ptk = PTR.next()
                        rel_ = kc - 4 * G + 28
                        P.op("act", lambda e, ps_=ps_, pt_=pt_, rel_=rel_: e.activation(pt_[:], ps_[:, :], AF.Exp, bias=kbt[:, hr, rel_:rel_ + 1], scale=1.0), reads=[psk, "kbt"], writes=[ptk])
                        for m in range(4):
                            b_ = 2 + m // 2; o_ = (m % 2) * 129
                            P.op("pe", lambda e, pt_=pt_, m=m, b_=b_, o_=o_, kc=kc: e.matmul(psb[b_][:, o_:o_ + 129], pt_[:, m * 128:(m + 1) * 128], VS[:, kc, :], start=False, stop=(kc == last), skip_group_check=True),
                                 reads=[ptk, "VSd", "VS1"], writes=["ps%d" % b_])
                    for wi, kc in enumerate(wlist):
                        if wi + 1 < len(wlist):
                            pend.append(win_scores(wlist[wi + 1]))
                        ps_, psk = pend.pop(0)
                        pt_, ptk = PTR.next()
                        rel_ = kc - 4 * G + 28
                        P.op("act", lambda e, ps_=ps_, pt_=pt_, rel_=rel_: e.activation(pt_[:], ps_[:, :], AF.Exp, bias=kbt[:, hr, rel_:rel_ + 1], scale=1.0), reads=[psk, "kbt"], writes=[ptk])
                        for m in range(4):
                            b_ = 4 + m // 2; o_ = (m % 2) * 129
                            P.op("pe", lambda e, pt_=pt_, m=m, b_=b_, o_=o_, kc=kc: e.matmul(psb[b_][:, o_:o_ + 129], pt_[:, m * 128:(m + 1) * 128], VW[:, kc, :], start=False, stop=(kc == last), skip_group_check=True),
                                 reads=[ptk, "VWd", "VW1"], writes=["ps%d" % b_])
                    ms_ = mstage[(G * 4 + r) % 2]; msk = "ms%d" % ((G * 4 + r) % 2)
                    for m in range(4):
                        qt = G * 4 + m
                        oa = oacc[:, r, m, :]; oak = "oacc%d_%d" % (r, m)
                        for bi, (bb, gcol) in enumerate(((2 + m // 2, 1), (4 + m // 2, 2))):
                            po = psb[bb]; o_ = (m % 2) * 129
                            P.op("dve", lambda e, po=po, o_=o_, bi=bi: e.reciprocal(stt[:, 4 + bi:5 + bi], po[:, o_ + 128:o_ + 129]), reads=["ps%d" % bb], writes=["stt%d" % (4 + bi)])
                            P.op("dve", lambda e, bi=bi, qt=qt, gcol=gcol: e.tensor_tensor(stt[:, 6 + bi:7 + bi], stt[:, 4 + bi:5 + bi], sm_all[:, qt, 16 + hr * 3 + gcol:16 + hr * 3 + gcol + 1], op=ALU.mult),
                                 reads=["stt%d" % (4 + bi), "sm_all"], writes=["stt%d" % (6 + bi)])
                            P.op("dve", lambda e, po=po, o_=o_, bi=bi, oa=oa: e.scalar_tensor_tensor(out=oa, in0=po[:, o_:o_ + 128], scalar=stt[:, 6 + bi:7 + bi], in1=oa, op0=ALU.mult, op1=ALU.add),
                                 reads=["ps%d" % bb, "stt%d" % (6 + bi), oak], writes=[oak])
                        P.op("act", lambda e, oa=oa: e.copy(ofin[:], oa), reads=[oak], writes=["ofin"])
                        P.op("pe", lambda e: e.transpose(psb[7][:].bitcast(BF16)[:, 256:384], ofin[:], ident[:]), reads=["ofin", "ident"], writes=["ps7"])
                        P.op("act", lambda e, ms_=ms_, m=m: e.copy(ms_[:, m * 128:(m + 1) * 128], psb[7][:].bitcast(BF16)[:, 256:384]), reads=["ps7"], writes=[msk + "_%d" % m])
                    P.dma("sp", mixT[8 + hr][:, G * 512:(G + 1) * 512], ms_[:], "x", reads=[msk + "_%d" % m for m in range(4)], writes=["mixT"])
                for r in range(4):
                    pass1(r)
                for m in range(4):
                    select(m)
                for r in range(4):
                    pass2(r)

            for G in range(8):
                do_G(G)

        for gl in range(2):
            do_group(gl)
      if start <= 3:
        _ph3()
    P.barrier()
    if nphase < 4:
        P.finish(); P.emit(); return nc

    if start <= 3:
        rg = [[0, 1], [2, 3], [4, 5], [6, 7]]
        for j in range(16):
            P.cc(lambda e, j=j: e.collective_compute("AllGather", ALU.bypass, replica_groups=rg, ins=[mixT[j]], outs=[mixG[j]]),
                 "ccx", reads=["mixT"], writes=["mixG"])
        P.barrier()

    def _ph4():
        def gsrc(kc):
            if kc < 16:
                r_, j_ = kc // 8, kc % 8
            else:
                r_, j_ = (kc - 16) // 8, 8 + (kc - 16) % 8
            return mixG[j_][r_ * 128:(r_ + 1) * 128, :]

        selt = P.sb("p4_sel", [128, 2], F32)
        rst = P.sb("p4_rst", [128, 4, 4], F32)
        P.dma("sp", selt[:], selv[:, :], "x", writes=["selt"])
        for tb in range(4):
            tok0 = tb * 512
            with ExitStack() as st:
                mixsel = P.sb("p4_mixsel", [128, 32, 512], BF16, st)
                mA = [P.sb("p4_mA%d" % i, [128, 4, 512], BF16, st) for i in range(2)]
                mB = [P.sb("p4_mB%d" % i, [128, 4, 512], BF16, st) for i in range(2)]
                wo = [P.sb("p4_wo%d" % i, [128, 32, 512], BF16, st) for i in range(2)]
                g1b = P.sb("p4_g1b", [128, D], F32, st)
                xp = [P.sb("p4_xp%d" % i, [128, 512], F32, st) for i in range(3)]
                yp = [P.sb("p4_yp%d" % i, [128, 512], F32, st) for i in range(3)]
                jk = P.sb("p4_jk", [128, 512], BF16, st)
                ss1 = P.sb("p4_ss1", [128, 4, 8], F32, st)
                P.dma("sp", g1b[:], modv[2:3, :].partition_broadcast(128), "x", reads=["modv"], writes=["g1b"])
                for q4 in range(8):
                    a_ = mA[q4 % 2]; b_ = mB[q4 % 2]
                    for u in range(4):
                        kc = q4 * 4 + u
                        P.dma("sp", a_[:, u, :], gsrc(kc)[:, tok0:tok0 + 512], "x", reads=["mixG"], writes=["mA%d" % (q4 % 2)])
                        P.dma("sp", b_[:, u, :], gsrc(kc)[:, 2048 + tok0:2048 + tok0 + 512], "x", reads=["mixG"], writes=["mB%d" % (q4 % 2)])
                    dst = mixsel[:, q4 * 4:(q4 + 1) * 4, :]
                    P.op("dve", lambda e, a_=a_, dst=dst: e.tensor_scalar(dst, a_[:], selt[:, 0:1], None, op0=ALU.mult), reads=["mA%d" % (q4 % 2), "selt"], writes=["mixsel%d" % q4])
                    P.op("dve", lambda e, b_=b_, dst=dst: e.scalar_tensor_tensor(out=dst, in0=b_[:], scalar=selt[:, 1:2], in1=dst, op0=ALU.mult, op1=ALU.add),
                         reads=["mB%d" % (q4 % 2), "selt", "mixsel%d" % q4], writes=["mixsel%d" % q4])
                msk_all = ["mixsel%d" % q4 for q4 in range(8)]
                P.dma("sp", wo[0][:], w_out_b[0], "x", reads=["w_out_b0"], writes=["wo0"])
                it = 0
                for n in range(8):
                    if n + 1 < 8:
                        P.dma("sp", wo[(n + 1) % 2][:], w_out_b[n + 1], "x", reads=["w_out_b%d" % (n + 1)], writes=["wo%d" % ((n + 1) % 2)])
                    w_ = wo[n % 2]; wk = "wo%d" % (n % 2)
                    for m in range(4):
                        b = 2 + (it % 4); it += 1
                        x_ = xp[it % 3]; xk = "xp%d" % (it % 3); y_ = yp[it % 3]; yk = "yp%d" % (it % 3)
                        P.dma("sp", x_[:], x_h[tok0 + m * 128:tok0 + (m + 1) * 128, n * 512:(n + 1) * 512], "x", writes=[xk])
                        for kc in range(32):
                            P.op("pe", lambda e, b=b, w_=w_, kc=kc, m=m: e.matmul(psb[b][:, :], mixsel[:, kc, m * 128:(m + 1) * 128], w_[:, kc, :], start=(kc == 0), stop=(kc == 31)),
                                 reads=[wk] + (msk_all if kc in (0, 31) else []), writes=["ps%d" % b])
                        P.op("dve", lambda e, b=b, y_=y_, n=n: e.tensor_tensor(y_[:], psb[b][:, :], g1b[:, n * 512:(n + 1) * 512], op=ALU.mult), reads=["ps%d" % b, "g1b"], writes=[yk])
                        P.op("pool", lambda e, y_=y_, x_=x_: e.tensor_tensor(y_[:], y_[:], x_[:], op=ALU.add), reads=[yk, xk], writes=[yk])
                        P.op("act", lambda e, y_=y_, m=m, n=n: e.activation(jk[:], y_[:], AF.Square, accum_out=ss1[:, m, n:n + 1]), reads=[yk], writes=["jk", "ss1_%d_%d" % (m, n)])
                        P.dma("sp", x1s[tok0 + m * 128:tok0 + (m + 1) * 128, n * 512:(n + 1) * 512], y_[:], "x", reads=[yk], writes=["x1s"])
                for m in range(4):
                    P.op("dve", lambda e, m=m: e.reduce_sum(rst[:, m, 0:1], ss1[:, m, :], axis=AX.X), reads=["ss1_%d_%d" % (m, n) for n in range(8)], writes=["rst%d" % m])
                    P.op("act", lambda e, m=m: e.activation(rst[:, m, 1:2], rst[:, m, 0:1], AF.Sqrt, bias=1e-6, scale=1.0 / D), reads=["rst%d" % m], writes=["rst%d" % m])
                    P.op("dve", lambda e, m=m: e.reciprocal(rst[:, m, 2:3], rst[:, m, 1:2]), reads=["rst%d" % m], writes=["rst%d" % m])
            P.barrier()
            with ExitStack() as st, ExitStack() as sth:
                hidT = P.sb("p4_hidT", [128, 128, 512], BF16, st)
                h2T = P.sb("p4_h2T", [128, 32, 512], BF16, sth)
                with ExitStack() as st2:
                    w2b = P.sb("p4_w2b", [128, 2048], F32, st2)
                    sh2b = P.sb("p4_sh2b", [128, 2048], F32, st2)
                    xr = P.sb("p4_xr", [128, D], F32, st2)
                    hb2 = P.sb("p4_hb2", [128, D], BF16, st2)
                    for m in range(4):
                        P.dma("sp", xr[:], x1s[tok0 + m * 128:tok0 + (m + 1) * 128, :], "x", reads=["x1s"], writes=["xr"])
                        for hf in range(2):
                            cs_ = slice(hf * 2048, (hf + 1) * 2048)
                            P.dma("sp", w2b[:], modv[3:4, cs_].partition_broadcast(128), "x", reads=["modv"], writes=["w2b"])
                            P.dma("sp", sh2b[:], modv[4:5, cs_].partition_broadcast(128), "x", reads=["modv"], writes=["sh2b"])
                            P.op("dve", lambda e, m=m, cs_=cs_: e.scalar_tensor_tensor(out=xr[:, cs_], in0=xr[:, cs_], scalar=rst[:, m, 2:3], in1=w2b[:], op0=ALU.mult, op1=ALU.mult),
                                 reads=["xr", "rst%d" % m, "w2b"], writes=["xr"])
                            P.op("pool", lambda e, cs_=cs_: e.tensor_tensor(hb2[:, cs_], xr[:, cs_], sh2b[:], op=ALU.add), reads=["xr", "sh2b"], writes=["hb2"])
                        for g in range(4):
                            pk = "ps%d" % (g % 2)
                            ptb = psb[g % 2][:].bitcast(BF16)
                            for u in range(8):
                                kc = g * 8 + u
                                P.op("pe", lambda e, ptb=ptb, u=u, kc=kc: e.transpose(ptb[:, u * 128:(u + 1) * 128], hb2[:, kc * 128:(kc + 1) * 128], ident[:]),
                                     reads=["hb2", "ident"], writes=[pk])
                            dstap = h2T[:, g * 8:(g + 1) * 8, m * 128:(m + 1) * 128]
                            srcap = ptb[:, 0:1024].rearrange("p (u t) -> p u t", u=8)
                            P.op("act", lambda e, d=dstap, s_=srcap: e.copy(d, s_), reads=[pk], writes=["h2T%d_%d" % (m, g)])
                P.barrier()
                st3 = ExitStack()
                wu = [P.sb("p4_wu%d" % i, [128, 32, 128], BF16, st3) for i in range(3)]
                rl = [P.sb("p4_rl%d" % i, [128, 512], F32, st3) for i in range(2)]
                P.dma("sp", wu[0][:], w_up_b[0], "x", reads=["w_up_b0"], writes=["wu0"])
                P.dma("sp", wu[1][:], w_up_b[1], "x", reads=["w_up_b1"], writes=["wu1"])
                for fc in range(128):
                    if fc + 2 < 128:
                        P.dma("sp", wu[(fc + 2) % 3][:], w_up_b[fc + 2], "x", reads=["w_up_b%d" % (fc + 2)], writes=["wu%d" % ((fc + 2) % 3)])
                    w_ = wu[fc % 3]; wk = "wu%d" % (fc % 3)
                    b = 2 + fc % 4
                    for kc in range(32):
                        P.op("pe", lambda e, b=b, w_=w_, kc=kc: e.matmul(psb[b][:, :], w_[:, kc, :], h2T[:, kc, :], start=(kc == 0), stop=(kc == 31)),
                             reads=[wk, "h2T"], writes=["ps%d" % b])
                    r_ = rl[fc % 2]; rk = "rl%d" % (fc % 2)
                    P.op("act", lambda e, b=b, r_=r_: e.activation(r_[:], psb[b][:, :], AF.Relu), reads=["ps%d" % b], writes=[rk])
                    P.op("dve", lambda e, r_=r_, fc=fc: e.tensor_tensor(hidT[:, fc, :], r_[:], r_[:], op=ALU.mult), reads=[rk], writes=["hidT"])
                P.barrier()
                st3.close()
                sth.close()
                wd = [P.sb("p4_wd%d" % i, [128, 8, 512], BF16, st) for i in range(3)]
                g2b = P.sb("p4_g2b", [128, D], F32, st)
                xp2 = [P.sb("p4_xq%d" % i, [128, 512], F32, st) for i in range(3)]
                yp2 = [P.sb("p4_yq%d" % i, [128, 512], F32, st) for i in range(3)]
                jk2 = P.sb("p4_jk2", [128, 512], BF16, st)
                ss2 = P.sb("p4_ss2", [128, 4, 8], F32, st)
                P.dma("sp", g2b[:], modv[5:6, :].partition_broadcast(128), "x", reads=["modv"], writes=["g2b"])
                seq = [(n, fg) for n in range(8) for fg in range(16)]
                for i_ in range(2):
                    n_, fg_ = seq[i_]
                    P.dma("sp", wd[i_ % 3][:], w_dn_b[n_, fg_], "x", reads=["w_dn_b%d_%d" % (n_, fg_)], writes=["wd%d" % (i_ % 3)])
                it = 0
                for si, (n, fg) in enumerate(seq):
                    if si + 2 < len(seq):
                        n_, fg_ = seq[si + 2]
                        P.dma("sp", wd[(si + 2) % 3][:], w_dn_b[n_, fg_], "x", reads=["w_dn_b%d_%d" % (n_, fg_)], writes=["wd%d" % ((si + 2) % 3)])
                    w_ = wd[si % 3]; wk = "wd%d" % (si % 3)
                    for m in range(4):
                        b = 2 + m
                        for j in range(8):
                            P.op("pe", lambda e, b=b, w_=w_, j=j, m=m, fg=fg: e.matmul(psb[b][:, :], hidT[:, fg * 8 + j, m * 128:(m + 1) * 128], w_[:, j, :],
                                                                                  start=(fg == 0 and j == 0), stop=(fg == 15 and j == 7)),
                                 reads=[wk, "hidT"], writes=["ps%d" % b])
                    if fg == 15:
                        for m in range(4):
                            b = 2 + m; it += 1
                            x_ = xp2[it % 3]; xk = "xq%d" % (it % 3); y_ = yp2[it % 3]; yk = "yq%d" % (it % 3)
                            rows = slice(tok0 + m * 128, tok0 + (m + 1) * 128); cols = slice(n * 512, (n + 1) * 512)
                            P.dma("sp", x_[:], x1s[rows, cols], "x", reads=["x1s"], writes=[xk])
                            P.op("dve", lambda e, b=b, y_=y_, n=n: e.tensor_tensor(y_[:], psb[b][:, :], g2b[:, n * 512:(n + 1) * 512], op=ALU.mult), reads=["ps%d" % b, "g2b"], writes=[yk])
                            P.op("pool", lambda e, y_=y_, x_=x_: e.tensor_tensor(y_[:], y_[:], x_[:], op=ALU.add), reads=[yk, xk], writes=[yk])
                            P.op("act", lambda e, y_=y_, m=m, n=n: e.activation(jk2[:], y_[:], AF.Square, accum_out=ss2[:, m, n:n + 1]), reads=[yk], writes=["jk2", "ss2_%d_%d" % (m, n)])
                            P.dma("sp", x1s[rows, cols], y_[:], "x", reads=[yk, "x1s"], writes=["x1s"])
                for m in range(4):
                    P.op("dve", lambda e, m=m: e.reduce_sum(rst[:, m, 0:1], ss2[:, m, :], axis=AX.X), reads=["ss2_%d_%d" % (m, n) for n in range(8)], writes=["rst%d" % m])
                    P.op("act", lambda e, m=m: e.activation(rst[:, m, 1:2], rst[:, m, 0:1], AF.Sqrt, bias=1e-6, scale=1.0 / D), reads=["rst%d" % m], writes=["rst%d" % m])
                    P.op("dve", lambda e, m=m: e.reciprocal(rst[:, m, 2:3], rst[:, m, 1:2]), reads=["rst%d" % m], writes=["rst%d" % m])
            P.barrier()
            with ExitStack() as st:
                fnb = P.sb("p4_fnb", [128, D], F32, st)
                xo = [P.sb("p4_xo%d" % i, [128, D], F32, st) for i in range(2)]
                P.dma("sp", fnb[:], modv[6:7, :].partition_broadcast(128), "x", reads=["modv"], writes=["fnb"])
                for m in range(4):
                    x_ = xo[m % 2]; xk = "xo%d" % (m % 2)
                    rows = slice(tok0 + m * 128, tok0 + (m + 1) * 128)
                    P.dma("sp", x_[:], x1s[rows, :], "x", reads=["x1s"], writes=[xk])
                    P.op("dve", lambda e, x_=x_, m=m: e.scalar_tensor_tensor(out=x_[:], in0=x_[:], scalar=rst[:, m, 2:3], in1=fnb[:], op0=ALU.mult, op1=ALU.mult),
                         reads=[xk, "rst%d" % m, "fnb"], writes=[xk])
                    P.dma("sp", out_h[rows, :], x_[:], "x", reads=[xk], writes=["out_h"])
            P.barrier()

    _ph4()
    P.finish()
    P.emit()
    return nc


def core_inputs(inp, b, hh, consts):
    f32 = np.float32
    d = {}
    d["x_b"] = np.ascontiguousarray(inp["x"][b])
    d["x_h"] = np.ascontiguousarray(inp["x"][b, hh * 2048:(hh + 1) * 2048])
    d["cT"] = np.ascontiguousarray(inp["c"][b].reshape(32, 128).T)
    d["ada_w"] = inp["ada_w"][0]
    d["ada_b"] = inp["ada_b"][0][None, :]
    d["n1w"] = inp["norm1_w"][0][None, :]
    d["n2w"] = inp["norm2_w"][0][None, :]
    d["fnw"] = inp["final_norm_w"][None, :]
    cols = w_in_cols(hh)
    wc = np.zeros((D, W_IN_COLS), f32)
    wc[:, :cols.size] = inp["w_in"][0][:, cols]
    d["w_in_c"] = wc
    cwv = inp["gdn_conv_w"][0]
    cw = np.zeros((128, 24, 4), f32)
    for f in range(24):
        kind, hd = f // 8, f % 8
        ch = kind * 2048 + (8 * hh + hd) * 128 + np.arange(128)
        cw[:, f, :] = cwv[:, ch].T
    d["convw"] = cw.reshape(128, 96)
    d["alog"] = inp["gdn_a_log"][0][None, 8 * hh:8 * hh + 8].astype(f32)
    d["dtb"] = inp["gdn_dt_bias"][0][None, 8 * hh:8 * hh + 8].astype(f32)
    d["gnw"] = inp["gdn_norm_w"][0][None, :]
    for s in ("k", "v"):
        d["pos_" + s] = inp["cmp_pos_" + s][0]
        d["w1_" + s] = inp["cmp_w1_" + s][0]
        d["w2_" + s] = inp["cmp_w2_" + s][0]
    d["w_out"] = inp["w_out"][0]
    d["w_up"] = inp["w_up"][0]
    d["w_down"] = inp["w_down"][0]
    sv = np.zeros((128, 2), f32)
    sv[:, hh] = 1.0
    d["selv"] = sv
    for k, v in consts.items():
        d["t_" + k] = v
    kb, cb, nsl = alibi_tables(hh)
    d["t_kb"] = kb
    d["t_cb"] = cb
    d["t_nsl"] = nsl
    return {k: np.ascontiguousarray(v) for k, v in d.items()}


def kernel(**inputs):
    inp = {k: np.asarray(v) for k, v in inputs.items()}
    consts = const_tables()
    nc = build_program()
    in_maps = [core_inputs(inp, cid // 2, cid % 2, consts) for cid in range(8)]
    res = run_bass_kernel_spmd(nc, in_maps, core_ids=list(range(8)))
    out = np.zeros((4, T, D), np.float32)
    for cid in range(8):
        b, hh = cid // 2, cid % 2
        out[b, hh * 2048:(hh + 1) * 2048] = res.results[cid]["out_h"]
    return out
```

```python
import numpy as np
import ml_dtypes
from contextlib import ExitStack
import concourse.bass as bass
import concourse.mybir as mybir
from concourse.bass_utils import run_bass_kernel_spmd

F32 = mybir.dt.float32
BF16 = mybir.dt.bfloat16
I32 = mybir.dt.int32
ALU = mybir.AluOpType
AF = mybir.ActivationFunctionType
AX = mybir.AxisListType

ENGS = ("pe", "act", "dve", "pool", "sp")

T = 4096
D = 4096
DFF = 16384
NEG = -30000.0


class Prog:
    def __init__(self, nc):
        self.nc = nc
        self.ops = {e: [] for e in ENGS}
        self.nops = {e: 0 for e in ENGS}
        self.dcount = {}
        self.res = {}
        self.waited = {}
        self.awaited = {e: set() for e in ENGS}
        self.bankacc = {}
        self.dring = {}
        self.stack = ExitStack()

    def sb(self, name, shape, dt, stack=None):
        self.uid = getattr(self, "uid", 0) + 1
        name = "%s_u%d" % (name, self.uid)
        return (stack or self.stack).enter_context(self.nc.sbuf_tensor(name, list(shape), dt))

    def ps(self, name, shape, dt=F32, stack=None):
        return (stack or self.stack).enter_context(self.nc.psum_tensor(name, list(shape), dt))

    def _deps(self, eng, reads, writes):
        deps = {}
        for r in reads:
            ent = self.res.get(r)
            if ent is not None and ent[0] is not None:
                k, i = ent[0]
                if deps.get(k, -1) < i:
                    deps[k] = i
        for w in writes:
            ent = self.res.get(w)
            if ent is not None:
                if ent[0] is not None:
                    k, i = ent[0]
                    if deps.get(k, -1) < i:
                        deps[k] = i
                for k, i in ent[1].items():
                    if deps.get(k, -1) < i:
                        deps[k] = i
        waits = []
        for k, i in deps.items():
            if k == eng and eng == "pe":
                continue
            if self.waited.get((eng, k), -1) >= i:
                continue
            self.waited[(eng, k)] = i
            waits.append((k, i))
            if k in self.awaited:
                self.awaited[k].add(i)
        return waits

    def _update(self, tok, reads, writes):
        k, i = tok
        for w in writes:
            self.res[w] = [tok, {}]
        for r in reads:
            ent = self.res.get(r)
            if ent is None:
                ent = self.res[r] = [None, {}]
            if ent[1].get(k, -1) < i:
                ent[1][k] = i

    @staticmethod
    def _bank(key):
        if key.startswith("psb"):
            return int(key[3])
        if key.startswith("ps"):
            return int(key[2])
        return None

    def _bank_deps(self, eng, reads, writes, waits):
        banks = set()
        for k_ in list(reads) + list(writes):
            b = self._bank(k_)
            if b is not None:
                banks.add(b)
        for b in banks:
            acc = self.bankacc.setdefault(b, {})
            for k, i in acc.items():
                if k == eng:
                    continue
                if self.waited.get((eng, k), -1) >= i:
                    continue
                self.waited[(eng, k)] = i
                waits.append((k, i))
                self.awaited[k].add(i)
        return banks

    def op(self, eng, fn, reads=(), writes=()):
        waits = self._deps(eng, reads, writes)
        banks = self._bank_deps(eng, reads, writes, waits)
        for b in banks:
            self.bankacc[b][eng] = self.nops[eng] + 1
        self.nops[eng] += 1
        tok = (eng, self.nops[eng])
        self.ops[eng].append([waits, fn, "c", self.nops[eng]])
        self._update(tok, reads, writes)
        return tok

    NRING = {"sp": 40, "pool": 16, "act": 8}

    def dma(self, q, out, in_, sem, reads=(), writes=(), **kw):
        waits = self._deps(q, reads, writes)
        n = self.dring.get(q, 0)
        self.dring[q] = n + 1
        sem = "%s_r%d" % (q, n % self.NRING[q])
        prev = self.dcount.get(sem, 0)
        if prev and self.waited.get((q, sem), -1) < prev:
            self.waited[(q, sem)] = prev
            waits.append((sem, prev))
        self.dcount[sem] = prev + 16
        tok = (sem, self.dcount[sem])
        self.ops[q].append([waits, lambda e: e.dma_start(out=out, in_=in_, **kw), "d", sem])
        self._update(tok, reads, writes)
        return tok

    def cc(self, fn, sem, reads=(), writes=()):
        waits = self._deps("pool", reads, writes)
        self.dcount[sem] = self.dcount.get(sem, 0) + 1
        tok = (sem, self.dcount[sem])
        self.ops["pool"].append([waits, fn, "k", sem])
        self._update(tok, reads, writes)
        return tok

    def barrier(self):
        for e in ENGS:
            waits = []
            for k in ENGS:
                if k == e or self.nops[k] == 0:
                    continue
                i = self.nops[k]
                if self.waited.get((e, k), -1) >= i:
                    continue
                self.waited[(e, k)] = i
                self.awaited[k].add(i)
                waits.append((k, i))
            for k, c in self.dcount.items():
                if self.waited.get((e, k), -1) >= c:
                    continue
                self.waited[(e, k)] = c
                waits.append((k, c))
            if waits:
                self.ops[e].append([waits, None, "w", None])
        self.res = {}

    def finish(self):
        waits = [(k, c) for k, c in self.dcount.items()]
        self.ops["sp"].append([waits, None, "w", None])

    def emit(self):
        nc = self.nc
        sems = {}
        for e in ENGS:
            if self.nops[e]:
                sems[e] = self.stack.enter_context(nc.semaphore("s_" + e))
        for k in self.dcount:
            sems[k] = self.stack.enter_context(nc.semaphore("d_" + k))
        vmap = {}
        for e in ENGS:
            aw = sorted(self.awaited[e])
            vmap[e] = {idx: n + 1 for n, idx in enumerate(aw)}

        def val(k, i):
            return vmap[k][i] if k in vmap else i

        def run(e, engobj):
            for waits, fn, kind, extra in self.ops[e]:
                for k, i in waits:
                    engobj.wait_ge(sems[k], val(k, i))
                if kind == "w":
                    continue
                ins = fn(engobj)
                if kind == "c":
                    if extra in vmap[e]:
                        ins.then_inc(sems[e], 1)
                elif kind == "d":
                    ins.then_inc(sems[extra], 16)
                else:
                    ins.then_inc(sems[extra], 1)

        with nc.Block() as block:
            @block.tensor
            def _(t):
                run("pe", t)

            @block.scalar
            def _(t):
                run("act", t)

            @block.vector
            def _(t):
                run("dve", t)

            @block.gpsimd
            def _(t):
                run("pool", t)

            @block.sync
            def _(t):
                run("sp", t)


class Rot:
    def __init__(self, items):
        self.items = items
        self.i = 0

    def next(self):
        it = self.items[self.i % len(self.items)]
        self.i += 1
        return it


def const_tables():
    c = {}
    p = np.arange(128)
    q = np.arange(512)
    win = np.zeros((128, 8, 512), np.float32)
    for j in range(8):
        dist = q[None, :] - (128 * (j - 4) + p[:, None])
        win[:, j, :] = np.where((dist >= 0) & (dist < 512), 0.0, NEG)
    c["winT"] = win.astype(ml_dtypes.bfloat16)
    n = (np.arange(2)[None, :, None] * 128 + p[:, None, None])
    t = np.arange(T)[None, None, :]
    c["cmT"] = np.where((t >= 16 * n + 31) & (n < 255), 0.0, NEG).astype(ml_dtypes.bfloat16)
    s = np.arange(64)[:, None, None]
    kc = np.arange(32)[None, :, None]
    m = np.arange(128)[None, None, :]
    c["selE"] = (s == 2 * kc + m // 64).astype(np.float32).astype(ml_dtypes.bfloat16)
    n = (np.arange(2)[None, :, None] * 128 + p[:, None, None])
    sb = np.arange(64)[None, None, :]
    ov = ((16 * n <= 64 * sb + 63) & (16 * n + 31 >= 64 * sb) & (n < 255)).astype(np.float32)
    c["ovl"] = ov.astype(ml_dtypes.bfloat16)
    tt = (np.arange(32)[None, :, None] * 128 + p[:, None, None])
    cur = tt // 64
    blk = np.arange(64)[None, None, :]
    frc = np.zeros((128, 32, 64), np.float32)
    forced = (blk == 0) | ((cur - blk) < 2)
    frc = np.where(forced, 1e30 * (1.0 + 0.25 * (blk % 4)), frc)
    frc = np.where(blk <= cur, frc, -1e30)
    c["frc"] = frc.astype(np.float32)
    c["triu"] = (p[:, None] <= p[None, :]).astype(np.float32)
    c["dmask"] = np.where(p[None, :] >= p[:, None], 0.0, NEG).astype(np.float32)
    oh = (np.arange(32)[:, None, None] == np.arange(32)[None, :, None]).astype(np.float32)
    c["oneh"] = np.broadcast_to(oh, (32, 32, 128)).copy().astype(np.float32)
    c["trow"] = np.arange(512, dtype=np.float32)[None, :]
    a_ = p[:, None]; b_ = p[None, :]
    mu = np.zeros((128, 7, 128), np.float32)
    for l in range(7):
        sz_ = 2 ** l
        mu[:, l, :] = ((a_ // (2 * sz_) == b_ // (2 * sz_)) & ((a_ % (2 * sz_)) < sz_) & ((b_ % (2 * sz_)) >= sz_)).astype(np.float32)
    c["mskU"] = mu
    c["mskL"] = np.ascontiguousarray(mu.transpose(2, 1, 0))
    return c


def alibi_tables(hh):
    slopes = 2.0 ** (-8.0 * np.arange(1, 17, dtype=np.float64) / 16.0)
    p = np.arange(128, dtype=np.float64)
    hs = slopes[8 * hh:8 * hh + 8]
    rel = np.arange(32, dtype=np.float64)
    kb = hs[None, :, None] * (128.0 * (rel[None, None, :] - 28.0) + p[:, None, None])
    ncx = np.arange(2, dtype=np.float64)
    G = np.arange(8, dtype=np.float64)
    cb = hs[None, :, None, None] * (16.0 * (ncx[None, None, :, None] * 128 + p[:, None, None, None]) + 31.0
                                    - 512.0 * G[None, None, None, :])
    nsl = -hs[None, :]
    return kb.astype(np.float32), cb.astype(np.float32), nsl.astype(np.float32)


def w_in_cols(hh):
    GDN_DK, NSA_DQ, DKV = 2048, 2048, 512
    sizes = (GDN_DK, GDN_DK, GDN_DK, GDN_DK, 16, 16, NSA_DQ, DKV, DKV, DKV, DKV, DKV, DKV, 48)
    off = np.concatenate([[0], np.cumsum(sizes)])
    gq, gk, gv, gz, gb, ga, nq, nkc, nvc, nks, nvs, nkw, nvw, ngate = [int(o) for o in off[:-1]]
    h8 = np.arange(1024) + 1024 * hh
    g2 = np.arange(256) + 256 * hh
    cols = []
    for base in (gq, gk, gv):
        cols.append(base + h8)
    cols.append(nq + h8)
    for base in (nkc, nvc, nks, nkw):
        cols.append(base + g2)
    cols.append(gz + h8)
    cols.append(nvs + g2)
    cols.append(nvw + g2)
    cols.append(gb + 8 * hh + np.arange(8))
    cols.append(ga + 8 * hh + np.arange(8))
    cols.append(ngate + 24 * hh + np.arange(24))
    return np.concatenate(cols)


N_FM = 40
W_IN_COLS = 40 * 128 + 3 * 512 + 64


def build_program(dbg=None, nphase=99, start=0, only_heads=8, gdn_chunks=32, gdbg=None, lite=False):
    nc = bass.Bass("TRN2", target_bir_lowering=False)
    P = Prog(nc)
    ck = "ExternalOutput" if dbg else "Internal"

    need = {"x_b": (1,), "ada_w": (0,), "w_in_c": (0, 1), "w_out": (4,), "w_up": (4,), "w_down": (4,), "x_h": (4,),
            "w1_k": (3,), "w1_v": (3,)}

    def din(name, shape, dt=F32):
        if lite and name in need and not any(start <= ph_ <= nphase - 0 for ph_ in need[name]):
            shape = [1, 1]
        return nc.dram_tensor(name, list(shape), dt, kind="ExternalInput").ap()

    def dscr(name, shape, dt, dbgout=False, ph=99, last=99):
        if ph < start <= last:
            return nc.dram_tensor(name, list(shape), dt, kind="ExternalInput").ap()
        return nc.dram_tensor(name, list(shape), dt, kind=("ExternalOutput" if (dbg and dbgout) else "Internal")).ap()

    x_b = din("x_b", [T, D]); x_h = din("x_h", [2048, D]); cT = din("cT", [128, 32])
    ada_w = din("ada_w", [D, 6 * D]); ada_b = din("ada_b", [1, 6 * D])
    n1w = din("n1w", [1, D]); n2w = din("n2w", [1, D]); fnw = din("fnw", [1, D])
    w_in_c = din("w_in_c", [D, W_IN_COLS])
    convw = din("convw", [128, 24 * 4]); alog = din("alog", [1, 8]); dtb = din("dtb", [1, 8]); gnw = din("gnw", [1, 128])
    pos_k = din("pos_k", [32, 128]); w1_k = din("w1_k", [4096, 128]); w2_k = din("w2_k", [128, 128])
    pos_v = din("pos_v", [32, 128]); w1_v = din("w1_v", [4096, 128]); w2_v = din("w2_v", [128, 128])
    w_out = din("w_out", [D, D]); w_up = din("w_up", [D, DFF]); w_down = din("w_down", [DFF, D])
    selv = din("selv", [128, 2])
    t_winT = din("t_winT", [128, 8, 512], BF16); t_cmT = din("t_cmT", [128, 2, T], BF16)
    t_selE = din("t_selE", [64, 32, 128], BF16); t_ovl = din("t_ovl", [128, 2, 64], BF16)
    t_frc = din("t_frc", [128, 32, 64]); t_triu = din("t_triu", [128, 128]); t_dmask = din("t_dmask", [128, 128])
    t_oneh = din("t_oneh", [32, 32, 128]); t_trow = din("t_trow", [1, 512])
    t_mskU = din("t_mskU", [128, 7, 128]); t_mskL = din("t_mskL", [128, 7, 128])
    t_kb = din("t_kb", [128, 8, 32]); t_cb = din("t_cb", [128, 8, 2, 8]); t_nsl = din("t_nsl", [1, 8])

    out_h = nc.dram_tensor("out_h", [2048, D], F32, kind="ExternalOutput").ap()

    modv = dscr("modv", [8, D], F32, dbgout=True, ph=0)
    w_fm = dscr("w_fm", [N_FM, 128, 32, 128], BF16)
    w_tm = dscr("w_tm", [4, 128, 32, 512], BF16)
    gqT = dscr("gqT", [8, 128, T], BF16, True, ph=1, last=2); gkT = dscr("gkT", [8, 128, T], BF16, True, ph=1, last=2)
    gk_tm = dscr("gk_tm", [8, T, 128], BF16, True, ph=1, last=2); gv_tm = dscr("gv_tm", [8, T, 128], BF16, True, ph=1, last=2)
    sz_tm = dscr("sz_tm", [T, 1024], BF16, True, ph=1, last=2)
    nqT = dscr("nqT", [8, 128, T], BF16, True, ph=1, last=3)
    kvT = dscr("kvT", [8, 128, T], BF16, True, ph=1, last=3)
    vsw_tm = dscr("vsw_tm", [T, 512], BF16, True, ph=1, last=3)
    sm_tm = dscr("sm_tm", [T, 64], F32, True, ph=1, last=3)
    mixT = dscr("mixT", [16, 128, T], BF16, True)
    mixG = dscr("mixG", [16, 2 * 128, T], BF16, ph=3)
    w_out_b = dscr("w_out_b", [8, 128, 32, 512], BF16)
    w_up_b = dscr("w_up_b", [128, 128, 32, 128], BF16)
    w_dn_b = dscr("w_dn_b", [8, 16, 128, 8, 512], BF16)
    x1s = dscr("x1s", [2048, D], F32, True)

    ident = P.sb("ident", [128, 128], BF16)
    identf = P.sb("identf", [128, 128], F32)
    onesb = P.sb("onesb", [128, 128], BF16)
    onesf = P.sb("onesf", [128, 128], F32)
    P.op("pool", lambda e: e.memset(ident[:], 1.0), writes=["ident"])
    P.op("pool", lambda e: e.affine_select(ident[:], ident[:], pattern=[[-1, 128]], compare_op=ALU.is_equal,
                                           fill=0.0, base=0, channel_multiplier=1), reads=["ident"], writes=["ident"])
    P.op("pool", lambda e: e.memset(identf[:], 1.0), writes=["identf"])
    P.op("pool", lambda e: e.affine_select(identf[:], identf[:], pattern=[[-1, 128]], compare_op=ALU.is_equal,
                                           fill=0.0, base=0, channel_multiplier=1), reads=["identf"], writes=["identf"])
    P.op("pool", lambda e: e.memset(onesb[:], 1.0), writes=["onesb"])
    P.op("pool", lambda e: e.memset(onesf[:], 1.0), writes=["onesf"])

    psb = [P.ps("psb%d" % i, [128, 512], F32) for i in range(8)]

    for f in range(N_FM if start <= 1 else 0):
        P.dma("pool", w_fm[f], w_in_c[:, f * 128:(f + 1) * 128].rearrange("(kc p) f -> p kc f", p=128), "cvt",
              writes=["w_fm%d" % f])
    for g in range(3 if start <= 1 else 0):
        o = N_FM * 128 + g * 512
        P.dma("pool", w_tm[g], w_in_c[:, o:o + 512].rearrange("(kc p) f -> p kc f", p=128), "cvt", writes=["w_tm%d" % g])
    o = N_FM * 128 + 3 * 512
    if start <= 1:
      P.dma("pool", w_tm[3][:, :, 0:64], w_in_c[:, o:o + 64].rearrange("(kc p) f -> p kc f", p=128), "cvt", writes=["w_tm3"])
    cast_q = []
    if nphase >= 4:
        for n in range(8):
            cast_q.append((w_out_b[n], w_out[:, n * 512:(n + 1) * 512].rearrange("(kc p) f -> p kc f", p=128), "w_out_b%d" % n))
        for fg in range(16):
            src = w_up[:, fg * 1024:(fg + 1) * 1024].rearrange("(kc p) (c f) -> p c kc f", p=128, f=128)
            for cc_ in range(8):
                cast_q.append((w_up_b[fg * 8 + cc_], src[:, cc_], "w_up_b%d" % (fg * 8 + cc_)))
        for n in range(8):
            for fg in range(16):
                src = w_down[fg * 1024:(fg + 1) * 1024, n * 512:(n + 1) * 512].rearrange("(j p) f -> p j f", p=128)
                cast_q.append((w_dn_b[n, fg], src, "w_dn_b%d_%d" % (n, fg)))

    def cast_step(nmax):
        for _ in range(nmax):
            if not cast_q:
                return
            o_, i_, k_ = cast_q.pop(0)
            P.dma("pool", o_, i_, "cvt", writes=[k_])

    if start > 2:
        cast_step(10 ** 6)

    with ExitStack() as st:
      def _ph0():
        sT = P.sb("p0_sT", [128, 32], F32, st)
        awt = [P.sb("p0_aw%d" % i, [128, 32, 512], F32, st) for i in range(2)]
        row = [P.sb("p0_row%d" % i, [1, 512], F32, st) for i in range(2)]
        bro = [P.sb("p0_bro%d" % i, [1, 512], F32, st) for i in range(2)]
        nro = [P.sb("p0_nro%d" % i, [1, 512], F32, st) for i in range(2)]
        P.dma("sp", sT[:], cT[:, :], "p0c", writes=["sT"])
        P.op("act", lambda e: e.activation(sT[:], sT[:], AF.Silu), reads=["sT"], writes=["sT"])
        P.dma("sp", modv[6:7, :], fnw[:, :], "p0s", writes=["modv"])
        NB = 48
        for nb in range(NB):
            a = awt[nb % 2]
            kind = nb // 8
            cs = (nb % 8) * 512
            P.dma("sp", a[:], ada_w[:, nb * 512:(nb + 1) * 512].rearrange("(kc p) f -> p kc f", p=128), "p0w%d" % (nb % 2),
                  writes=["aw%d" % (nb % 2)])
            P.dma("sp", bro[nb % 2][:], ada_b[:, nb * 512:(nb + 1) * 512], "p0b%d" % (nb % 2), writes=["bro%d" % (nb % 2)])
            if kind in (1, 4):
                P.dma("sp", nro[nb % 2][:], (n1w if kind == 1 else n2w)[:, cs:cs + 512], "p0n%d" % (nb % 2), writes=["nro%d" % (nb % 2)])
            pb = psb[nb % 2]
            for kc in range(32):
                P.op("pe", lambda e, a=a, pb=pb, kc=kc: e.matmul(pb[0:1, :], sT[:, kc:kc + 1], a[:, kc, :], start=(kc == 0), stop=(kc == 31)),
                     reads=["sT", "aw%d" % (nb % 2)], writes=["ps%d" % (nb % 2)])
            r = row[nb % 2]
            br_ = bro[nb % 2]; nr_ = nro[nb % 2]
            P.op("dve", lambda e, r=r, pb=pb, br_=br_: e.tensor_tensor(r[:], pb[0:1, :], br_[:], op=ALU.add),
                 reads=["ps%d" % (nb % 2), "bro%d" % (nb % 2)], writes=["row%d" % (nb % 2)])
            if kind in (1, 4):
                P.op("dve", lambda e, r=r, nr_=nr_: e.scalar_tensor_tensor(out=r[:], in0=r[:], scalar=1.0, in1=nr_[:], op0=ALU.add, op1=ALU.mult),
                     reads=["row%d" % (nb % 2), "nro%d" % (nb % 2)], writes=["row%d" % (nb % 2)])
            dst = {0: 1, 1: 0, 2: 2, 3: 4, 4: 3, 5: 5}[kind]
            P.dma("sp", modv[dst:dst + 1, cs:cs + 512], r[:], "p0s", reads=["row%d" % (nb % 2)], writes=["modv"])
      if start <= 0:
        _ph0()
    P.barrier()
    if nphase < 1:
        P.finish(); P.emit(); return nc

    with ExitStack() as st:
      def _ph1():
        w1b = P.sb("p1_w1b", [128, D], F32, st)
        sh1b = P.sb("p1_sh1b", [128, D], F32, st)
        hT = P.sb("p1_hT", [128, 32, 1024], BF16, st)
        xt = [P.sb("p1_x%d" % i, [128, D], F32, st) for i in range(2)]
        hb = P.sb("p1_hb", [128, D], BF16, st)
        wfb = [P.sb("p1_wf%d" % i, [128, 32, 128], BF16, st) for i in range(3)]
        wtb = [P.sb("p1_wt%d" % i, [128, 32, 256], BF16, st) for i in range(1)]
        cw = P.sb("p1_cw", [128, 96], F32, st)
        halo = P.sb("p1_halo", [128, 24, 3], F32, st)
        raw = [P.sb("p1_raw%d" % i, [128, 515], F32, st) for i in range(2)]
        acc = [P.sb("p1_acc%d" % i, [128, 512], F32, st) for i in range(2)]
        sq = [P.sb("p1_sq%d" % i, [128, 512], BF16, st) for i in range(2)]
        rin = [P.sb("p1_rin%d" % i, [128, 512], F32, st) for i in range(2)]
        ob = [P.sb("p1_ob%d" % i, [128, 512], BF16, st) for i in range(3)]
        tmb = [P.sb("p1_tm%d" % i, [128, 512], BF16, st) for i in range(2)]
        smf = [P.sb("p1_smf%d" % i, [128, 64], F32, st) for i in range(2)]
        stat = P.sb("p1_stat", [128, 8], F32, st)
        dtbb = P.sb("p1_dtbb", [128, 8], F32, st)
        nab = P.sb("p1_nab", [128, 8], F32, st)
        P.dma("sp", w1b[:], modv[0:1, :].partition_broadcast(128), "p1c", reads=["modv"], writes=["w1b"])
        P.dma("sp", sh1b[:], modv[1:2, :].partition_broadcast(128), "p1c", reads=["modv"], writes=["sh1b"])
        P.dma("sp", cw[:], convw[:, :], "p1c", writes=["cw"])
        P.dma("sp", dtbb[:], dtb[0:1, :].partition_broadcast(128), "p1c", writes=["dtbb"])
        P.dma("sp", nab[:], alog[0:1, :].partition_broadcast(128), "p1c", writes=["nab"])
        P.op("act", lambda e: e.activation(nab[:], nab[:], AF.Exp), reads=["nab"], writes=["nab"])
        P.op("dve", lambda e: e.tensor_scalar(nab[:], nab[:], -1.0, None, op0=ALU.mult), reads=["nab"], writes=["nab"])
        P.op("pool", lambda e: e.memset(halo[:], 0.0), writes=["halo"])
        psT = [psb[0], psb[1]]
        psM = Rot([(psb[2], "psb2"), (psb[3], "psb3"), (psb[4], "psb4"), (psb[5], "psb5")])
        psN = Rot([(psb[6], "psb6"), (psb[7], "psb7")])
        rawR = Rot(list(zip(raw, ["raw0", "raw1"]))); accR = Rot(list(zip(acc, ["acc0", "acc1"])))
        sqR = Rot(list(zip(sq, ["sq0", "sq1"]))); rinR = Rot(list(zip(rin, ["rin0", "rin1"])))
        obR = Rot(list(zip(ob, ["ob0", "ob1", "ob2"]))); tmR = Rot(list(zip(tmb, ["tm0", "tm1"])))
        smR = Rot(list(zip(smf, ["smf0", "smf1"])))
        evq = Rot(["act", "dve"])

        def load_x(sbk, i):
            j = (sbk * 8 + i)
            P.dma("sp", xt[j % 2][:], x_b[j * 128:(j + 1) * 128, :], "p1x%d" % (j % 2), writes=["xt%d" % (j % 2)])

        load_x(0, 0)
        for sbk in range(4):
            t0s = sbk * 1024
            for i in range(8):
                j = sbk * 8 + i
                if j + 1 < 32:
                    load_x((j + 1) // 8, (j + 1) % 8)
                x = xt[j % 2]; xk = "xt%d" % (j % 2)
                P.op("act", lambda e, x=x: e.activation(hb[:], x[:], AF.Square, accum_out=stat[:, 0:1]), reads=[xk], writes=["hb", "st0"])
                P.op("act", lambda e: e.activation(stat[:, 1:2], stat[:, 0:1], AF.Sqrt, bias=1e-6, scale=1.0 / D), reads=["st0"], writes=["st1"])
                P.op("dve", lambda e: e.reciprocal(stat[:, 2:3], stat[:, 1:2]), reads=["st1"], writes=["st2"])
                P.op("dve", lambda e, x=x: e.scalar_tensor_tensor(out=x[:], in0=x[:], scalar=stat[:, 2:3], in1=w1b[:], op0=ALU.mult, op1=ALU.mult),
                     reads=[xk, "st2", "w1b"], writes=[xk])
                P.op("pool", lambda e, x=x: e.tensor_tensor(hb[:], x[:], sh1b[:], op=ALU.add), reads=[xk, "sh1b"], writes=["hb"])
                for g in range(4):
                    pt = psT[g % 2]; pk = "psb%d" % (g % 2)
                    ptb = pt[:].bitcast(BF16)
                    for u in range(8):
                        kc = g * 8 + u
                        P.op("pe", lambda e, ptb=ptb, u=u, kc=kc: e.transpose(ptb[:, u * 128:(u + 1) * 128], hb[:, kc * 128:(kc + 1) * 128], ident[:]),
                             reads=["hb", "ident"], writes=[pk])
                    q_ = evq.next()
                    dstap = hT[:, g * 8:(g + 1) * 8, i * 128:(i + 1) * 128]
                    srcap = ptb[:, 0:1024].rearrange("p (u t) -> p u t", u=8)
                    if q_ == "act":
                        P.op("act", lambda e, d=dstap, s=srcap: e.copy(d, s), reads=[pk], writes=["hT%d_%d" % (i, g)])
                    else:
                        P.op("dve", lambda e, d=dstap, s=srcap: e.tensor_copy(d, s), reads=[pk], writes=["hT%d_%d" % (i, g)])
            hT_keys = ["hT%d_%d" % (i, g) for i in range(8) for g in range(4)]
            hTh = [[("hT%d_%d" % (i, g)) for i in range(th * 4, th * 4 + 4) for g in range(4)] for th in range(2)]

            def load_wf(f):
                P.dma("sp", wfb[f % 3][:], w_fm[f], "p1wf%d" % (f % 3), reads=["w_fm%d" % f], writes=["wf%d" % (f % 3)])

            post_q = []

            def post_tile(f, th, pm, pmk, tok0):
                if f < 24:
                    kind = f // 8
                    hd = f % 8
                    rw, rk = rawR.next(); ac, ak = accR.next()
                    P.op("pool", lambda e, rw=rw, f=f: e.tensor_copy(rw[:, 0:3], halo[:, f, :]), reads=["halo%d" % f], writes=[rk + "h"])
                    P.op("act", lambda e, rw=rw, pm=pm: e.copy(rw[:, 3:515], pm[:, :]), reads=[pmk], writes=[rk])
                    P.op("pool", lambda e, rw=rw, f=f: e.tensor_copy(halo[:, f, :], rw[:, 512:515]), reads=[rk], writes=["halo%d" % f])
                    P.op("dve", lambda e, rw=rw, ac=ac, f=f: e.tensor_scalar(ac[:], rw[:, 3:515], cw[:, f * 4 + 3:f * 4 + 4], None, op0=ALU.mult),
                         reads=[rk, "cw"], writes=[ak])
                    for jj in (2, 1, 0):
                        P.op("dve", lambda e, rw=rw, ac=ac, f=f, jj=jj: e.scalar_tensor_tensor(out=ac[:], in0=rw[:, jj:jj + 512], scalar=cw[:, f * 4 + jj:f * 4 + jj + 1],
                                                                                            in1=ac[:], op0=ALU.mult, op1=ALU.add),
                             reads=[rk, rk + "h", ak, "cw"], writes=[ak])
                    o_, ok = obR.next()
                    if kind == 2:
                        P.op("act", lambda e, ac=ac, o_=o_: e.activation(o_[:], ac[:], AF.Silu), reads=[ak], writes=[ok])
                    else:
                        P.op("act", lambda e, ac=ac: e.activation(ac[:], ac[:], AF.Silu), reads=[ak], writes=[ak])
                        s_, sk = sqR.next(); ri, rik = rinR.next()
                        P.op("pool", lambda e, ac=ac, s_=s_: e.tensor_tensor(s_[:], ac[:], ac[:], op=ALU.mult), reads=[ak], writes=[sk])
                        pn, pnk = psN.next()
                        P.op("pe", lambda e, pn=pn, s_=s_: e.matmul(pn[:, :], onesb[:], s_[:], start=True, stop=True), reads=[sk, "onesb"], writes=[pnk])
                        P.op("act", lambda e, pn=pn, ri=ri: e.activation(ri[:], pn[:, :], AF.Sqrt, bias=1e-6, scale=1.0), reads=[pnk], writes=[rik])
                        P.op("dve", lambda e, ri=ri: e.reciprocal(ri[:], ri[:]), reads=[rik], writes=[rik])
                        scl = (128.0 ** -0.5) if kind == 0 else 1.0
                        P.op("dve", lambda e, ac=ac, ri=ri, o_=o_, scl=scl: e.scalar_tensor_tensor(out=o_[:], in0=ac[:], scalar=scl, in1=ri[:], op0=ALU.mult, op1=ALU.mult),
                             reads=[ak, rik], writes=[ok])
                    if kind == 0:
                        P.dma("sp", gqT[hd, :, tok0:tok0 + 512], o_[:], "p1o", reads=[ok], writes=["gqT"])
                    elif kind == 1:
                        P.dma("sp", gkT[hd, :, tok0:tok0 + 512], o_[:], "p1o", reads=[ok], writes=["gkT"])
                    if kind >= 1:
                        pt = psT[(f + th) % 2]; pk = "psb%d" % ((f + th) % 2)
                        ptb = pt[:].bitcast(BF16)
                        for u in range(4):
                            P.op("pe", lambda e, ptb=ptb, u=u, o_=o_: e.transpose(ptb[:, u * 128:(u + 1) * 128], o_[:, u * 128:(u + 1) * 128], ident[:]),
                                 reads=[ok, "ident"], writes=[pk])
                        tm_, tk = tmR.next()
                        P.op("act", lambda e, tm_=tm_, ptb=ptb: e.copy(tm_[:], ptb[:, 0:512]), reads=[pk], writes=[tk])
                        dstt = (gk_tm if kind == 1 else gv_tm)[hd, tok0:tok0 + 512, :].rearrange("(u p) d -> p u d", p=128)
                        P.dma("sp", dstt, tm_[:].rearrange("p (u d) -> p u d", u=4), "p1o", reads=[tk], writes=["gtm"])
                else:
                    o_, ok = obR.next()
                    if f < 32:
                        P.op("act", lambda e, o_=o_, pm=pm: e.activation(o_[:], pm[:, :], AF.Copy, scale=128.0 ** -0.5), reads=[pmk], writes=[ok])
                        P.dma("sp", nqT[f - 24, :, tok0:tok0 + 512], o_[:], "p1o", reads=[ok], writes=["nqT"])
                    else:
                        P.op("act", lambda e, o_=o_, pm=pm: e.copy(o_[:], pm[:, :]), reads=[pmk], writes=[ok])
                        P.dma("sp", kvT[f - 32, :, tok0:tok0 + 512], o_[:], "p1o", reads=[ok], writes=["kvT"])

            load_wf(0); load_wf(1)
            for f in range(N_FM):
                if f + 2 < N_FM:
                    load_wf(f + 2)
                w = wfb[f % 3]; wk = "wf%d" % (f % 3)
                for th in range(2):
                    pm, pmk = psM.next()
                    for kc in range(32):
                        P.op("pe", lambda e, pm=pm, w=w, kc=kc, th=th: e.matmul(pm[:, :], w[:, kc, :], hT[:, kc, th * 512:(th + 1) * 512],
                                                                              start=(kc == 0), stop=(kc == 31)),
                             reads=[wk] + (hTh[th] if kc in (0, 31) else []), writes=[pmk])
                    tok0 = t0s + th * 512
                    post_q.append((f, th, pm, pmk, tok0))
                    if len(post_q) > 1:
                        post_tile(*post_q.pop(0))
            while post_q:
                post_tile(*post_q.pop(0))
            for g2 in range(7):
                wt = wtb[0]
                g = g2 // 2 if g2 < 6 else 3
                hf = g2 % 2
                ncol = 256 if g < 3 else 64
                c0 = hf * 256 if g < 3 else 0
                P.dma("sp", wt[:, :, 0:ncol], w_tm[g][:, :, c0:c0 + ncol], "p1wt", reads=["w_tm%d" % g], writes=["wt"])
                for i in range(8):
                    pm, pmk = psM.next()
                    for kc in range(32):
                        P.op("pe", lambda e, pm=pm, wt=wt, kc=kc, i=i, ncol=ncol: e.matmul(pm[:, 0:ncol], hT[:, kc, i * 128:(i + 1) * 128], wt[:, kc, 0:ncol],
                                                                                        start=(kc == 0), stop=(kc == 31)),
                             reads=["wt"] + (["hT%d_%d" % (i, gg) for gg in range(4)] if kc in (0, 31) else []), writes=[pmk])
                    tok0 = t0s + i * 128
                    if g < 2:
                        o_, ok = obR.next()
                        P.op("act", lambda e, o_=o_, pm=pm: e.activation(o_[:, 0:256], pm[:, 0:256], AF.Silu), reads=[pmk], writes=[ok])
                        P.dma("sp", sz_tm[tok0:tok0 + 128, g * 512 + c0:g * 512 + c0 + 256], o_[:, 0:256], "p1o", reads=[ok], writes=["sz_tm"])
                    elif g == 2:
                        o_, ok = obR.next()
                        P.op("act", lambda e, o_=o_, pm=pm: e.copy(o_[:, 0:256], pm[:, 0:256]), reads=[pmk], writes=[ok])
                        P.dma("sp", vsw_tm[tok0:tok0 + 128, c0:c0 + 256], o_[:, 0:256], "p1o", reads=[ok], writes=["vsw_tm"])
                    else:
                        s_, sk = smR.next()
                        P.op("act", lambda e, s_=s_, pm=pm: e.activation(s_[:, 0:8], pm[:, 0:8], AF.Sigmoid), reads=[pmk], writes=[sk + "a"])
                        P.op("act", lambda e, s_=s_, pm=pm: e.activation(s_[:, 16:40], pm[:, 16:40], AF.Sigmoid), reads=[pmk], writes=[sk + "c"])
                        P.op("dve", lambda e, s_=s_, pm=pm: e.tensor_tensor(s_[:, 8:16], pm[:, 8:16], dtbb[:], op=ALU.add), reads=[pmk, "dtbb"], writes=[sk + "b"])
                        P.op("act", lambda e, s_=s_: e.activation(s_[:, 8:16], s_[:, 8:16], AF.Exp), reads=[sk + "b"], writes=[sk + "b"])
                        P.op("act", lambda e, s_=s_: e.activation(s_[:, 8:16], s_[:, 8:16], AF.Ln, bias=1.0, scale=1.0), reads=[sk + "b"], writes=[sk + "b"])
                        P.op("dve", lambda e, s_=s_: e.tensor_tensor(s_[:, 8:16], s_[:, 8:16], nab[:], op=ALU.mult), reads=[sk + "b", "nab"], writes=[sk + "b"])
                        P.op("pool", lambda e, s_=s_: e.memset(s_[:, 40:64], 0.0), writes=[sk + "d"])
                        P.dma("sp", sm_tm[tok0:tok0 + 128, :], s_[:], "p1o", reads=[sk + "a", sk + "b", sk + "c", sk + "d"], writes=["sm_tm"])
      if start <= 1:
        _ph1()
    P.barrier()
    if nphase < 2:
        P.finish(); P.emit(); return nc

    with ExitStack() as st:
      def _ph2():
        sm_all = P.sb("p2_sm", [128, 32, 64], F32, st)
        triu = P.sb("p2_triu", [128, 128], F32, st)
        dmask = P.sb("p2_dmask", [128, 128], F32, st)
        oneh = P.sb("p2_oneh", [32, 32, 128], F32, st)
        gnwb = P.sb("p2_gnwb", [128, 128], F32, st)
        mskU = P.sb("p2_mskU", [128, 7, 128], F32, st)
        mskL = P.sb("p2_mskL", [128, 7, 128], F32, st)
        P.dma("sp", mskU[:], t_mskU[:, :, :], "p2c", writes=["mskU"])
        P.dma("sp", mskL[:], t_mskL[:, :, :], "p2c", writes=["mskL"])
        P.dma("sp", sm_all[:], sm_tm.rearrange("(c p) f -> p c f", p=128), "p2c", reads=["sm_tm"], writes=["sm_all"])
        P.dma("sp", triu[:], t_triu[:, :], "p2c", writes=["triu"])
        P.dma("sp", dmask[:], t_dmask[:, :], "p2c", writes=["dmask"])
        P.dma("sp", oneh[:], t_oneh[:, :, :], "p2c", writes=["oneh"])
        P.dma("sp", gnwb[:], gnw[0:1, :].partition_broadcast(128), "p2c", writes=["gnwb"])
        HB = []
        for par in range(2):
            hbuf = {}
            for nm in ("qT", "kT"):
                hbuf[nm] = P.sb("p2_%s%d" % (nm, par), [128, T], BF16, st)
            for nm in ("ktm", "vtm", "sz", "XT", "QKD", "QdT", "Kd"):
                hbuf[nm] = P.sb("p2_%s%d" % (nm, par), [128, 32, 128], BF16, st)
            hbuf["mix"] = P.sb("p2_mix%d" % par, [128, T], BF16, st) if par == 0 else HB[0]["mix"]
            hbuf["sc"] = P.sb("p2_sc%d" % par, [128, 6, 32], F32, st)
            hbuf["gamT"] = P.sb("p2_gamT%d" % par, [32, 128], F32, st)
            HB.append(hbuf)
        S = P.sb("p2_S", [128, 128], F32, st)
        Sb = P.sb("p2_Sb", [128, 128], BF16, st)
        tA = [dict(dd=P.sb("p2_dd%d" % i, [128, 128], F32, st), DT=P.sb("p2_DT%d" % i, [128, 128], F32, st),
                   Eg=P.sb("p2_Eg%d" % i, [128, 128], F32, st), Mf=P.sb("p2_Mf%d" % i, [128, 128], F32, st),
                   Lf=P.sb("p2_Lf%d" % i, [128, 128], F32, st), Cu=P.sb("p2_Cu%d" % i, [128, 128], F32, st), Cl=P.sb("p2_Cl%d" % i, [128, 128], F32, st),
                   T1=P.sb("p2_T1%d" % i, [128, 128], F32, st), T1p=P.sb("p2_T1p%d" % i, [128, 128], F32, st),
                   U=[P.sb("p2_U%d_%d" % (i, j), [128, 128], F32, st) for j in range(2)],
                   L=[P.sb("p2_L%d_%d" % (i, j), [128, 128], F32, st) for j in range(2)]) for i in range(2)]
        tB = [dict(R=P.sb("p2_R%d" % i, [128, 128], BF16, st), vn=P.sb("p2_vn%d" % i, [128, 128], BF16, st),
                   j1=P.sb("p2_j1%d" % i, [128, 128], F32, st), j2=P.sb("p2_j2%d" % i, [128, 128], F32, st),
                   om=P.sb("p2_om%d" % i, [128, 128], BF16, st), st=P.sb("p2_st%d" % i, [128, 4], F32, st)) for i in range(2)]

        def head_load(hd):
            par = hd % 2; hb_ = HB[par]; k = "h%d" % par
            P.dma("sp", hb_["qT"][:], gqT[hd], "p2l%d" % par, reads=["gqT"], writes=[k + "qT"])
            P.dma("sp", hb_["kT"][:], gkT[hd], "p2l%d" % par, reads=["gkT"], writes=[k + "kT"])
            P.dma("sp", hb_["ktm"][:], gk_tm[hd].rearrange("(c p) d -> p c d", p=128), "p2l%d" % par, reads=["gtm"], writes=[k + "ktm"])
            P.dma("sp", hb_["vtm"][:], gv_tm[hd].rearrange("(c p) d -> p c d", p=128), "p2l%d" % par, reads=["gtm"], writes=[k + "vtm"])
            P.dma("sp", hb_["sz"][:], sz_tm[:, hd * 128:(hd + 1) * 128].rearrange("(c p) d -> p c d", p=128), "p2l%d" % par,
                  reads=["sz_tm"], writes=[k + "sz"])

        def head_pre(hd):
            par = hd % 2; hb_ = HB[par]; k = "h%d" % par
            sc = hb_["sc"]
            g_h = sc[:, 5, :]
            P.op("dve", lambda e: e.tensor_copy(sc[:, 5, :], sm_all[:, :, 8 + hd]), reads=["sm_all"], writes=[k + "gh"])
            P.op("pe", lambda e: e.matmul(psb[0][:, 0:32], triu[:], g_h, start=True, stop=True), reads=[k + "gh", "triu"], writes=["ps0a"])
            P.op("pe", lambda e: e.matmul(psb[0][:, 32:64], onesf[:], g_h, start=True, stop=True), reads=[k + "gh", "onesf"], writes=["ps0b"])
            if gdbg == "pre0":
                return
            P.op("act", lambda e: e.copy(sc[:, 0, :], psb[0][:, 0:32]), reads=["ps0a"], writes=[k + "gam"])
            P.op("dve", lambda e: e.tensor_scalar(sc[:, 1, :], psb[0][:, 0:32], -1.0, None, op0=ALU.mult), reads=["ps0a"], writes=[k + "ngam"])
            P.op("act", lambda e: e.activation(sc[:, 2, :], psb[0][:, 0:32], AF.Exp), reads=["ps0a"], writes=[k + "negeg"])
            P.op("dve", lambda e: e.tensor_scalar(sc[:, 2, :], sc[:, 2, :], -1.0, None, op0=ALU.mult), reads=[k + "negeg"], writes=[k + "negeg"])
            P.op("dve", lambda e: e.tensor_tensor(sc[:, 3, :], psb[0][:, 32:64], sc[:, 0, :], op=ALU.subtract), reads=["ps0b", k + "gam"], writes=[k + "kdec"])
            P.op("act", lambda e: e.activation(sc[:, 3, :], sc[:, 3, :], AF.Exp), reads=[k + "kdec"], writes=[k + "kdec"])
            P.op("act", lambda e: e.activation(sc[:, 4, :], psb[0][:, 32:64], AF.Exp), reads=["ps0b"], writes=[k + "egl"])
            if gdbg == "pre1":
                return
            P.op("pe", lambda e: e.matmul(psb[0][0:32, 128:256], sc[:, 0, :], identf[:], start=True, stop=True), reads=[k + "gam", "identf"], writes=["ps0c"])
            P.op("act", lambda e: e.copy(hb_["gamT"][:], psb[0][0:32, 128:256]), reads=["ps0c"], writes=[k + "gamT"])

        def GA(hd, c):
            par = hd % 2; hb_ = HB[par]; k = "h%d" % par
            pc = c % 2; ta = tA[pc]; a_ = "A%d" % pc
            sc = hb_["sc"]
            cs = slice(c * 128, (c + 1) * 128)
            qTc = hb_["qT"][:, cs]; kTc = hb_["kT"][:, cs]
            bA = 1 + 2 * pc; bB = 2 + 2 * pc
            pG = psb[bA][:, 0:128]; pGk = "ps%dG" % bA
            pKK = psb[bA][:, 128:256]; pQK = psb[bA][:, 256:384]; pKk = "ps%dK" % bA
            pP = psb[bB][:, 0:128]; pQ = psb[bB][:, 128:256]
            pPk = "ps%dP" % bB; pQk = "ps%dQ" % bB
            pY = psb[bB][:, 256:384]; pYk = "ps%dY" % bB
            pLt = psb[bA][:, 384:512]; pLk = "ps%dL" % bA
            P.op("pe", lambda e: e.matmul(pG, oneh[:, c, :], hb_["gamT"][:], start=True, stop=True), reads=["oneh", k + "gamT"], writes=[pGk])
            yield
            P.op("dve", lambda e: e.scalar_tensor_tensor(out=ta["dd"][:], in0=pG, scalar=sc[:, 1, c:c + 1], in1=dmask[:], op0=ALU.add, op1=ALU.add),
                 reads=[pGk, k + "ngam", "dmask"], writes=[a_ + "dd"])
            yield
            P.op("act", lambda e: e.activation(ta["DT"][:], ta["dd"][:], AF.Exp), reads=[a_ + "dd"], writes=[a_ + "DT"])
            yield
            P.op("act", lambda e: e.activation(ta["Eg"][:], pG, AF.Exp), reads=[pGk], writes=[a_ + "Eg"])
            yield
            P.op("pool", lambda e: e.tensor_tensor(hb_["QdT"][:, c, :], qTc, ta["Eg"][:], op=ALU.mult), reads=[k + "qT", a_ + "Eg"], writes=[k + "QdT%d" % c])
            yield
            P.op("pool", lambda e: e.tensor_scalar(hb_["Kd"][:, c, :], hb_["ktm"][:, c, :], sc[:, 3, c:c + 1], None, op0=ALU.mult),
                 reads=[k + "ktm", k + "kdec"], writes=[k + "Kd%d" % c])
            yield
            P.op("pe", lambda e: e.matmul(pKK, kTc, kTc, start=True, stop=True), reads=[k + "kT"], writes=[pKk + "a"])
            yield
            P.op("pe", lambda e: e.matmul(pQK, kTc, qTc, start=True, stop=True), reads=[k + "kT", k + "qT"], writes=[pKk + "b"])
            yield
            P.op("dve", lambda e: e.scalar_tensor_tensor(out=ta["Mf"][:], in0=pKK, scalar=sm_all[:, c, hd:hd + 1], in1=ta["DT"][:], op0=ALU.mult, op1=ALU.mult),
                 reads=[pKk + "a", "sm_all", a_ + "DT"], writes=[a_ + "Mf"])
            yield
            P.op("dve", lambda e: e.tensor_tensor(hb_["QKD"][:, c, :], pQK, ta["DT"][:], op=ALU.mult), reads=[pKk + "b", a_ + "DT"], writes=[k + "QKD%d" % c])
            yield
            Mf, Lf, Cu, Cl, T1, T1p, U, L = ta["Mf"], ta["Lf"], ta["Cu"], ta["Cl"], ta["T1"], ta["T1p"], ta["U"], ta["L"]
            pT1 = psb[bB][:, 0:128]; pT1p = psb[bB][:, 128:256]; pT2 = psb[bB][:, 256:384]; pT2p = psb[bB][:, 384:512]
            kb_ = "ps%d" % bB
            P.op("pe", lambda e: e.matmul(pLt, Mf[:], identf[:], start=True, stop=True), reads=[a_ + "Mf", "identf"], writes=[pLk])
            yield
            P.op("act", lambda e: e.copy(Lf[:], pLt), reads=[pLk], writes=[a_ + "Lf"])
            yield
            P.op("pool", lambda e: e.tensor_tensor(Cu[:], Mf[:], mskU[:, 0, :], op=ALU.mult), reads=[a_ + "Mf", "mskU"], writes=[a_ + "Cu"])
            yield
            P.op("pool", lambda e: e.tensor_tensor(Cl[:], Lf[:], mskL[:, 0, :], op=ALU.mult), reads=[a_ + "Lf", "mskL"], writes=[a_ + "Cl"])
            yield
            P.op("dve", lambda e: e.tensor_tensor(U[0][:], identf[:], Cu[:], op=ALU.subtract), reads=["identf", a_ + "Cu"], writes=[a_ + "U0"])
            yield
            P.op("dve", lambda e: e.tensor_tensor(L[0][:], identf[:], Cl[:], op=ALU.subtract), reads=["identf", a_ + "Cl"], writes=[a_ + "L0"])
            yield
            cur = 0
            for lvl in range(1, 7):
                nx = 1 - cur
                lastl = (lvl == 6)
                P.op("pool", lambda e, lvl=lvl: e.tensor_tensor(Cl[:], Lf[:], mskL[:, lvl, :], op=ALU.mult), reads=[a_ + "Lf", "mskL"], writes=[a_ + "Cl"])
                yield
                P.op("pe", lambda e, cur=cur: e.matmul(pT1, Cl[:], U[cur][:], start=True, stop=True), reads=[a_ + "Cl", a_ + "U%d" % cur], writes=[kb_ + "a"])
                yield
                P.op("act", lambda e: e.copy(T1[:], pT1), reads=[kb_ + "a"], writes=[a_ + "T1"])
                yield
                if not lastl:
                    P.op("pool", lambda e, lvl=lvl: e.tensor_tensor(Cu[:], Mf[:], mskU[:, lvl, :], op=ALU.mult), reads=[a_ + "Mf", "mskU"], writes=[a_ + "Cu"])
                    yield
                    P.op("pe", lambda e, cur=cur: e.matmul(pT1p, Cu[:], L[cur][:], start=True, stop=True), reads=[a_ + "Cu", a_ + "L%d" % cur], writes=[kb_ + "b"])
                    yield
                    P.op("dve", lambda e: e.tensor_copy(T1p[:], pT1p), reads=[kb_ + "b"], writes=[a_ + "T1p"])
                    yield
                P.op("pe", lambda e, cur=cur: e.matmul(pT2, L[cur][:], T1[:], start=True, stop=True), reads=[a_ + "L%d" % cur, a_ + "T1"], writes=[kb_ + "c"])
                yield
                if not lastl:
                    P.op("pe", lambda e, cur=cur: e.matmul(pT2p, U[cur][:], T1p[:], start=True, stop=True), reads=[a_ + "U%d" % cur, a_ + "T1p"], writes=[kb_ + "d"])
                    yield
                    P.op("dve", lambda e, cur=cur, nx=nx: e.tensor_tensor(U[nx][:], U[cur][:], pT2, op=ALU.subtract), reads=[kb_ + "c", a_ + "U%d" % cur], writes=[a_ + "U%d" % nx])
                    yield
                    P.op("dve", lambda e, cur=cur, nx=nx: e.tensor_tensor(L[nx][:], L[cur][:], pT2p, op=ALU.subtract), reads=[kb_ + "d", a_ + "L%d" % cur], writes=[a_ + "L%d" % nx])
                    yield
                else:
                    P.op("dve", lambda e, cur=cur: e.tensor_tensor(hb_["XT"][:, c, :], U[cur][:], pT2, op=ALU.subtract), reads=[kb_ + "c", a_ + "U%d" % cur], writes=[k + "XT%d" % c])
                    yield
                cur = nx

        def GB(hd, c):
            par = hd % 2; hb_ = HB[par]; k = "h%d" % par
            pc = c % 2; tb = tB[pc]; b_ = "B%d" % pc
            sc = hb_["sc"]
            cs = slice(c * 128, (c + 1) * 128)
            kTc = hb_["kT"][:, cs]
            pKS = psb[5][:, 0:128]; pVN = psb[5][:, 128:256]
            pO = psb[6][:, 0:128] if pc == 0 else psb[0][:, 256:384]
            pOk = "ps6_O" if pc == 0 else "ps0_O"
            pD = psb[7][:, 0:128]
            pOt = psb[7][:].bitcast(BF16)[:, 512:640]
            if c == 0:
                P.op("pool", lambda e: e.memset(S[:], 0.0), writes=["S"])
                yield
                P.op("pool", lambda e: e.memset(Sb[:], 0.0), writes=["Sb"])
                yield
            P.op("pe", lambda e: e.matmul(pKS, kTc, Sb[:], start=True, stop=True), reads=[k + "kT", "Sb"], writes=["ps5a"])
            yield
            P.op("dve", lambda e: e.scalar_tensor_tensor(out=tb["R"][:], in0=pKS, scalar=sc[:, 2, c:c + 1], in1=hb_["vtm"][:, c, :], op0=ALU.mult, op1=ALU.add),
                 reads=["ps5a", k + "negeg", k + "vtm"], writes=[b_ + "R"])
            yield
            P.op("pe", lambda e: e.matmul(pVN, hb_["XT"][:, c, :], tb["R"][:], start=True, stop=True), reads=[k + "XT%d" % c, b_ + "R"], writes=["ps5b"])
            yield
            P.op("act", lambda e: e.activation(tb["vn"][:], pVN, AF.Copy, scale=sm_all[:, c, hd:hd + 1]), reads=["ps5b", "sm_all"], writes=[b_ + "vn"])
            yield
            P.op("pe", lambda e: e.matmul(pO, hb_["QdT"][:, c, :], Sb[:], start=True, stop=False), reads=[k + "QdT%d" % c, "Sb"], writes=[pOk])
            yield
            P.op("pe", lambda e: e.matmul(pO, hb_["QKD"][:, c, :], tb["vn"][:], start=False, stop=True), reads=[k + "QKD%d" % c, b_ + "vn"], writes=[pOk])
            yield
            P.op("pe", lambda e: e.matmul(pD, hb_["Kd"][:, c, :], tb["vn"][:], start=True, stop=True), reads=[k + "Kd%d" % c, b_ + "vn"], writes=["ps7d"])
            yield
            P.op("dve", lambda e: e.scalar_tensor_tensor(out=S[:], in0=S[:], scalar=sc[:, 4, c:c + 1], in1=pD, op0=ALU.mult, op1=ALU.add),
                 reads=["S", k + "egl", "ps7d"], writes=["S"])
            yield
            P.op("act", lambda e: e.copy(Sb[:], S[:]), reads=["S"], writes=["Sb"])
            yield
            P.op("act", lambda e: e.activation(tb["j1"][:], pO, AF.Square, accum_out=tb["st"][:, 0:1]), reads=[pOk], writes=[b_ + "j1", b_ + "s0"])
            yield
            P.op("act", lambda e: e.activation(tb["st"][:, 1:2], tb["st"][:, 0:1], AF.Sqrt, bias=1e-6, scale=1.0 / 128.0), reads=[b_ + "s0"], writes=[b_ + "s1"])
            yield
            P.op("dve", lambda e: e.reciprocal(tb["st"][:, 2:3], tb["st"][:, 1:2]), reads=[b_ + "s1"], writes=[b_ + "s2"])
            yield
            P.op("dve", lambda e: e.scalar_tensor_tensor(out=tb["j2"][:], in0=pO, scalar=tb["st"][:, 2:3], in1=gnwb[:], op0=ALU.mult, op1=ALU.mult),
                 reads=[pOk, b_ + "s2", "gnwb"], writes=[b_ + "j2"])
            yield
            P.op("pool", lambda e: e.tensor_tensor(tb["om"][:], tb["j2"][:], hb_["sz"][:, c, :], op=ALU.mult), reads=[b_ + "j2", k + "sz"], writes=[b_ + "om"])
            yield
            P.op("pe", lambda e: e.transpose(pOt, tb["om"][:], ident[:]), reads=[b_ + "om", "ident"], writes=["ps7t"])
            yield
            P.op("act", lambda e: e.copy(hb_["mix"][:, cs], pOt), reads=["ps7t"], writes=["mix%d" % c])
            yield
            if c == 31:
                P.dma("sp", mixT[hd], hb_["mix"][:], "p2o", reads=["mix%d" % cc_ for cc_ in range(32)], writes=["mixT"])
                yield

        head_load(0)
        if gdbg == "load":
            pass
        elif gdbg in ("pre", "pre0", "pre1"):
            head_pre(0)
        elif gdbg == "ga1":
            head_pre(0); list(GA(0, 0))
        elif gdbg == "ga":
            head_pre(0)
            for c in range(32):
                list(GA(0, c))
        elif gdbg == "gb1":
            head_pre(0)
            for c in range(32):
                list(GA(0, c))
            list(GB(0, 0))
        else:
          NH = only_heads
          def dump(name, ap_, shape, dt, reads):
              t_ = nc.dram_tensor(name, list(shape), dt, kind="ExternalOutput").ap()
              P.dma("sp", t_, ap_, "dbgd", reads=reads)
          for hd in range(NH + 1):
            if dbg and gdbg == "dump" and hd == NH:
                hb_ = HB[(NH - 1) % 2]; k = "h%d" % ((NH - 1) % 2)
                allk = [k + "%s%d" % (nm, c_) for nm in ("XT", "QKD", "QdT", "Kd") for c_ in range(32)]
                for nm in ("XT", "QKD", "QdT", "Kd", "ktm", "vtm", "sz"):
                    dump("d_" + nm, hb_[nm][:], [128, 32, 128], BF16, allk + [k + "ktm", k + "vtm", k + "sz"])
                dump("d_sc", hb_["sc"][:], [128, 6, 32], F32, [k + x for x in ("gam", "ngam", "negeg", "kdec", "egl", "gh")])
                dump("d_gamT", hb_["gamT"][:], [32, 128], F32, [k + "gamT"])
                dump("d_kT", hb_["kT"][:], [128, T], BF16, [k + "kT"])
            if hd < NH:
                head_pre(hd)
            for c0 in range(0, 32, 2):
                gens = []
                if hd < NH:
                    cast_step(3 if hd == 0 else 2)
                    gens += [GA(hd, c0), GA(hd, c0 + 1)]
                if hd > 0:
                    def gbchain(h_=hd - 1, c_=c0):
                        yield from GB(h_, c_)
                        yield from GB(h_, c_ + 1)
                    gens.append(gbchain())
                while gens:
                    for g_ in list(gens):
                        try:
                            next(g_)
                        except StopIteration:
                            gens.remove(g_)
            if hd + 1 < NH:
                head_load(hd + 1)
      if start <= 2:
        _ph2()
        cast_step(10 ** 6)
    P.barrier()
    if nphase < 3:
        P.finish(); P.emit(); return nc

    rg = [[0, 1], [2, 3], [4, 5], [6, 7]]

    def exch(js):
        if start > 3 or nphase < 4:
            return
        for j in js:
            P.cc(lambda e, j=j: e.collective_compute("AllGather", ALU.bypass, replica_groups=rg, ins=[mixT[j]], outs=[mixG[j]]),
                 "ccx", reads=["mixT"], writes=["mixG%d" % j])

    if start <= 3:
        exch(range(0, 8))

    with ExitStack() as st:
      def _ph3():
        sm_all = P.sb("p3_sm", [128, 32, 64], F32, st)
        winT = P.sb("p3_winT", [128, 8, 512], BF16, st)
        cmT = P.sb("p3_cmT", [128, 2, T], BF16, st)
        selE = P.sb("p3_selE", [64, 32, 128], BF16, st)
        frc = P.sb("p3_frc", [128, 32, 64], F32, st)
        kbt = P.sb("p3_kb", [128, 8, 32], F32, st)
        cbt = P.sb("p3_cb", [128, 8, 2, 8], F32, st)
        nslt = P.sb("p3_nsl", [1, 8], F32, st)
        trow = P.sb("p3_trow", [1, 512], F32, st)
        zer = P.sb("p3_zer", [128, 512], BF16, st)
        w1b = [P.sb("p3_w1%d" % i, [128, 32, 128], BF16, st) for i in range(2)]
        w2b = [P.sb("p3_w2%d" % i, [128, 128], BF16, st) for i in range(2)]
        posf = [P.sb("p3_pos%d" % i, [32, 128], F32, st) for i in range(2)]
        posT = [P.sb("p3_posT%d" % i, [128, 32], BF16, st) for i in range(2)]
        cvec = [P.sb("p3_cvec%d" % i, [128, 1], F32, st) for i in range(2)]
        kT4 = [P.sb("p3_kT%d" % i, [128, T], BF16, st) for i in range(4)]
        kcD = P.sb("p3_kcD", [128, 16, 256], BF16, st)
        hcm = P.sb("p3_hcm", [128, 256], BF16, st)
        kcmpT = P.sb("p3_kcmpT", [128, 256], BF16, st)
        VC = P.sb("p3_VC", [128, 2, 193], BF16, st)
        VS = P.sb("p3_VS", [128, 32, 129], BF16, st)
        VW = P.sb("p3_VW", [128, 32, 129], BF16, st)
        negK = P.sb("p3_negK", [1, 4], F32, st)
        kmx = P.sb("p3_kmx", [1, 32], F32, st)
        sqt = [P.sb("p3_sq%d" % i, [128, 512], BF16, st) for i in range(2)]
        qsb = [P.sb("p3_q%d" % i, [128, 4, 512], BF16, st) for i in range(2)]
        qn = P.sb("p3_qn", [1, 512], F32, st)
        srow = P.sb("p3_srow", [1, 512], F32, st)
        rrow = P.sb("p3_rrow", [1, 3, 512], BF16, st)
        PT = [P.sb("p3_PT%d" % i, [128, 512], BF16, st) for i in range(3)]
        oacc = P.sb("p3_oacc", [128, 4, 4, 128], F32, st)
        imp = P.sb("p3_imp", [128, 4, 64], F32, st)
        impp = P.sb("p3_impp", [128, 64], F32, st)
        impq = P.sb("p3_impq", [128, 64], F32, st)
        mx8 = P.sb("p3_mx8", [128, 16], F32, st)
        nsel = P.sb("p3_nsel", [128, 64], BF16, st)
        negselT = P.sb("p3_nselT", [64, 512], BF16, st)
        stt = P.sb("p3_stt", [128, 16], F32, st)
        ofin = P.sb("p3_ofin", [128, 128], BF16, st)
        mstage = [P.sb("p3_ms%d" % i, [128, 512], BF16, st) for i in range(2)]
        pcp = P.sb("p3_pcp", [128, 4, 386], F32, st)
        P.dma("sp", sm_all[:], sm_tm.rearrange("(c p) f -> p c f", p=128), "x", reads=["sm_tm"], writes=["sm_all"])
        for nm_, t_, src_ in (("winT", winT, t_winT), ("cmT", cmT, t_cmT), ("selE", selE, t_selE), ("frc", frc, t_frc),
                              ("kbt", kbt, t_kb), ("cbt", cbt, t_cb), ("nslt", nslt, t_nsl), ("trow", trow, t_trow)):
            P.dma("sp", t_[:], src_, "x", writes=[nm_])
        P.op("pool", lambda e: e.memset(zer[:], 0.0), writes=["zer"])
        for i, (w1_, w2_, pos_) in enumerate(((w1_k, w2_k, pos_k), (w1_v, w2_v, pos_v))):
            P.dma("pool", w1b[i][:], w1_.rearrange("(j d) o -> d j o", d=128), "x", writes=["w1b%d" % i])
            P.dma("pool", w2b[i][:], w2_[:, :], "x", writes=["w2b%d" % i])
            P.dma("sp", posf[i][:], pos_[:, :], "x", writes=["posf%d" % i])
            P.op("pe", lambda e, i=i: e.matmul(psb[6][:, 0:32], posf[i][:], identf[0:32, 0:32], start=True, stop=True), reads=["posf%d" % i, "identf"], writes=["ps6"])
            P.op("act", lambda e, i=i: e.copy(posT[i][:], psb[6][:, 0:32]), reads=["ps6"], writes=["posT%d" % i])
            for j in range(32):
                P.op("pe", lambda e, i=i, j=j: e.matmul(psb[6][:, 64:65], w1b[i][:, j, :], posT[i][:, j:j + 1], start=(j == 0), stop=(j == 31)),
                     reads=["w1b%d" % i, "posT%d" % i], writes=["ps6"])
            P.op("act", lambda e, i=i: e.copy(cvec[i][:], psb[6][:, 64:65]), reads=["ps6"], writes=["cvec%d" % i])

        psS = Rot([(psb[0], "ps0"), (psb[1], "ps1")])
        PTR = Rot([(PT[0], "PT0"), (PT[1], "PT1"), (PT[2], "PT2")])
        sqR = Rot([(sqt[0], "sq0"), (sqt[1], "sq1")])

        def zero_bank(b):
            P.op("pe", lambda e, b=b: e.matmul(psb[b][:, :], zer[:, 0:128], zer[:, :], start=True, stop=True, skip_group_check=True),
                 reads=["zer"], writes=["ps%d" % b])

        def do_group(gl):
            gk = "g"
            for i in range(4):
                P.dma("sp", kT4[i][:], kvT[2 * i + gl], "x", reads=["kvT"], writes=["kT4_%d" % i])
            P.dma("sp", VS[:, :, 0:128], vsw_tm[:, gl * 128:(gl + 1) * 128].rearrange("(c p) d -> p c d", p=128), "x", reads=["vsw_tm"], writes=["VSd"])
            P.dma("sp", VW[:, :, 0:128], vsw_tm[:, 256 + gl * 128:256 + (gl + 1) * 128].rearrange("(c p) d -> p c d", p=128), "x", reads=["vsw_tm"], writes=["VWd"])
            P.op("pool", lambda e: e.memset(VS[:, :, 128:129], 1.0), writes=["VS1"])
            P.op("pool", lambda e: e.memset(VW[:, :, 128:129], 1.0), writes=["VW1"])
            for i in range(2):
                src = kT4[i]
                P.op("dve", lambda e, src=src: e.tensor_copy(kcD[:], src[:].rearrange("p (n r) -> p r n", r=16)), reads=["kT4_%d" % i], writes=["kcD"])
                for j in range(32):
                    P.op("pe", lambda e, i=i, j=j: e.matmul(psb[6][:, 0:255], w1b[i][:, j, :], kcD[:, j % 16, j // 16:j // 16 + 255], start=(j == 0), stop=(j == 31)),
                         reads=["w1b%d" % i, "kcD"], writes=["ps6"])
                P.op("pool", lambda e: e.memset(hcm[:, 255:256], 0.0), writes=["hcm1"])
                P.op("act", lambda e, i=i: e.activation(hcm[:, 0:255], psb[6][:, 0:255], AF.Silu, bias=cvec[i][:, 0:1], scale=1.0), reads=["ps6", "cvec%d" % i], writes=["hcm"])
                if i == 0:
                    P.op("pe", lambda e: e.matmul(psb[7][:, 0:256], w2b[0][:], hcm[:], start=True, stop=True), reads=["w2b0", "hcm", "hcm1"], writes=["ps7"])
                    P.op("act", lambda e: e.copy(kcmpT[:], psb[7][:, 0:256]), reads=["ps7"], writes=["kcmpT"])
                else:
                    for nc_ in range(2):
                        P.op("pe", lambda e, nc_=nc_: e.matmul(psb[7][:, nc_ * 128:(nc_ + 1) * 128], hcm[:, nc_ * 128:(nc_ + 1) * 128], w2b[1][:], start=True, stop=True),
                             reads=["w2b1", "hcm", "hcm1"], writes=["ps7"])
                    P.op("act", lambda e: e.copy(VC[:, :, 0:128], psb[7][:, 0:256].rearrange("p (c d) -> p c d", c=2)), reads=["ps7"], writes=["VCd"])
                    P.op("pool", lambda e: e.memset(VC[:, :, 128:129], 1.0), writes=["VC1"])
                    P.dma("sp", VC[:, :, 129:193], t_ovl[:, :, :], "x", writes=["VCo"])
            for br, (src, ncols) in enumerate(((kcmpT, 256), (kT4[2], T), (kT4[3], T))):
                nch = max(1, ncols // 512)
                for cc_ in range(nch):
                    w_ = min(512, ncols)
                    s_, sk = sqR.next()
                    P.op("pool", lambda e, s_=s_, src=src, cc_=cc_, w_=w_: e.tensor_tensor(s_[:, 0:w_], src[:, cc_ * 512:cc_ * 512 + w_], src[:, cc_ * 512:cc_ * 512 + w_], op=ALU.mult),
                         reads=["kcmpT", "kT4_2", "kT4_3"], writes=[sk])
                    P.op("pe", lambda e, s_=s_, w_=w_: e.matmul(psb[6][0:1, 0:w_], onesb[:, 0:1], s_[:, 0:w_], start=True, stop=True), reads=[sk, "onesb"], writes=["ps6"])
                    P.op("dve", lambda e, br=br, cc_=cc_, w_=w_: e.reduce_max(kmx[:, br * 8 + cc_:br * 8 + cc_ + 1], psb[6][0:1, 0:w_], axis=AX.X), reads=["ps6"], writes=["kmx"])
                P.op("dve", lambda e, br=br, nch=nch: e.reduce_max(negK[:, br:br + 1], kmx[:, br * 8:br * 8 + nch], axis=AX.X), reads=["kmx"], writes=["negK%d" % br])
                P.op("act", lambda e, br=br: e.activation(negK[:, br:br + 1], negK[:, br:br + 1], AF.Sqrt), reads=["negK%d" % br], writes=["negK%d" % br])
                P.op("dve", lambda e, br=br: e.tensor_scalar(negK[:, br:br + 1], negK[:, br:br + 1], -1.0, None, op0=ALU.mult), reads=["negK%d" % br], writes=["negK%d" % br])

            def do_G(G):
                q_ = qsb[G % 2]; qk_ = "q%d" % (G % 2)
                for r in range(4):
                    P.dma("sp", q_[:, r, :], nqT[gl * 4 + r][:, G * 512:(G + 1) * 512], "x", reads=["nqT"], writes=[qk_ + "_%d" % r])
                P.op("pool", lambda e: e.memset(imp[:], 0.0), writes=["imp"])

                def make_rrow(r):
                    hr = gl * 4 + r
                    s_, sk = sqR.next()
                    P.op("pool", lambda e, s_=s_: e.tensor_tensor(s_[:], q_[:, r, :], q_[:, r, :], op=ALU.mult), reads=[qk_ + "_%d" % r], writes=[sk])
                    P.op("pe", lambda e, s_=s_: e.matmul(psb[6][0:1, :], onesb[:, 0:1], s_[:], start=True, stop=True), reads=[sk, "onesb"], writes=["ps6"])
                    P.op("act", lambda e: e.activation(qn[:], psb[6][0:1, :], AF.Sqrt), reads=["ps6"], writes=["qn"])
                    P.op("dve", lambda e: e.tensor_scalar(srow[:], trow[:], nslt[0:1, hr:hr + 1], None, op0=ALU.mult), reads=["trow", "nslt"], writes=["srow"])
                    for br in range(3):
                        P.op("dve", lambda e, br=br: e.scalar_tensor_tensor(out=rrow[:, br, :], in0=qn[:], scalar=negK[0:1, br:br + 1], in1=srow[:], op0=ALU.mult, op1=ALU.add),
                             reads=["qn", "negK%d" % br, "srow"], writes=["rrow%d" % br])

                def scores(kTsrc, kkey, kc, r, br, extra):
                    ps_, psk = psS.next()
                    P.op("pe", lambda e: e.matmul(ps_[:, :], kTsrc[:, kc * 128:(kc + 1) * 128], q_[:, r, :], start=True, stop=False),
                         reads=[kkey, qk_ + "_%d" % r], writes=[psk])
                    P.op("pe", lambda e: e.matmul(ps_[:, :], onesb[0:1, :], rrow[0:1, br, :], start=False, stop=(len(extra) == 0)),
                         reads=["onesb", "rrow%d" % br], writes=[psk])
                    for ei, (l_, r_, rk_) in enumerate(extra):
                        P.op("pe", lambda e, l_=l_, r_=r_, ei=ei: e.matmul(ps_[:, :], l_, r_, start=False, stop=(ei == len(extra) - 1)), reads=rk_, writes=[psk])
                    return ps_, psk

                def pass1(r):
                    hr = gl * 4 + r
                    make_rrow(r)
                    zero_bank(2); zero_bank(3)
                    ncs = (0, 1) if G >= 4 else (0,)
                    pend = [scores(kcmpT, "kcmpT", ncs[0], r, 0, [(ident[:], cmT[:, ncs[0], G * 512:(G + 1) * 512], ["ident", "cmT"])])]
                    for ni, nc_ in enumerate(ncs):
                        if ni + 1 < len(ncs):
                            n2 = ncs[ni + 1]
                            pend.append(scores(kcmpT, "kcmpT", n2, r, 0, [(ident[:], cmT[:, n2, G * 512:(G + 1) * 512], ["ident", "cmT"])]))
                        ps_, psk = pend.pop(0)
                        pt_, ptk = PTR.next()
                        P.op("act", lambda e, ps_=ps_, pt_=pt_, nc_=nc_: e.activation(pt_[:], ps_[:, :], AF.Exp, bias=cbt[:, hr, nc_, G:G + 1], scale=1.0), reads=[psk, "cbt"], writes=[ptk])
                        for m in range(4):
                            b_ = 2 + m // 2; o_ = (m % 2) * 193
                            P.op("pe", lambda e, pt_=pt_, m=m, b_=b_, o_=o_, nc_=nc_: e.matmul(psb[b_][:, o_:o_ + 193], pt_[:, m * 128:(m + 1) * 128], VC[:, nc_, :], start=False, stop=(nc_ == ncs[-1]), skip_group_check=True),
                                 reads=[ptk, "VCd", "VC1", "VCo"], writes=["ps%d" % b_])
                    P.op("act", lambda e: e.copy(pcp[:, 0, 0:386], psb[2][:, 0:386]), reads=["ps2"], writes=["pcp0"])
                    P.op("dve", lambda e: e.tensor_copy(pcp[:, 1, 0:386], psb[3][:, 0:386]), reads=["ps3"], writes=["pcp1"])
                    for m in range(4):
                        b_ = 2 + m // 2; o_ = (m % 2) * 193; qt = G * 4 + m
                        po = pcp[:, b_ - 2, :]
                        P.op("dve", lambda e, po=po, o_=o_: e.tensor_scalar(stt[:, 0:1], po[:, o_ + 128:o_ + 129], 1e-30, None, op0=ALU.max), reads=["pcp%d" % (b_ - 2)], writes=["stt0"])
                        P.op("dve", lambda e: e.reciprocal(stt[:, 1:2], stt[:, 0:1]), reads=["stt0"], writes=["stt1"])
                        P.op("dve", lambda e, po=po, o_=o_, m=m: e.scalar_tensor_tensor(out=imp[:, m, :], in0=po[:, o_ + 129:o_ + 193], scalar=stt[:, 1:2], in1=imp[:, m, :], op0=ALU.mult, op1=ALU.add),
                             reads=["pcp%d" % (b_ - 2), "stt1", "imp"], writes=["imp"])
                        P.op("dve", lambda e, qt=qt, hr=hr: e.tensor_tensor(stt[:, 2:3], stt[:, 1:2], sm_all[:, qt, 16 + hr * 3:16 + hr * 3 + 1], op=ALU.mult), reads=["stt1", "sm_all"], writes=["stt2"])
                        P.op("act", lambda e, po=po, o_=o_, r=r, m=m: e.activation(oacc[:, r, m, :], po[:, o_:o_ + 128], AF.Copy, scale=stt[:, 2:3]), reads=["pcp%d" % (b_ - 2), "stt2"], writes=["oacc%d_%d" % (r, m)])
                def select(m):
                    qt = G * 4 + m
                    P.op("dve", lambda e, m=m, qt=qt: e.tensor_tensor(impp[:], imp[:, m, :], frc[:, qt, :], op=ALU.add), reads=["imp", "frc"], writes=["impp"])
                    P.op("dve", lambda e: e.max(out=mx8[:, 0:8], in_=impp[:]), reads=["impp"], writes=["mx8a"])
                    P.op("dve", lambda e: e.match_replace(out=impq[:], in_to_replace=mx8[:, 0:8], in_values=impp[:], imm_value=-3e38), reads=["impp", "mx8a"], writes=["impq"])
                    P.op("dve", lambda e: e.max(out=mx8[:, 8:16], in_=impq[:]), reads=["impq"], writes=["mx8b"])
                    P.op("dve", lambda e: e.tensor_scalar(impq[:], impp[:], mx8[:, 15:16], None, op0=ALU.is_ge), reads=["impp", "mx8b"], writes=["impq"])
                    P.op("dve", lambda e: e.tensor_scalar(nsel[:], impq[:], 1.0, -NEG, op0=ALU.subtract, op1=ALU.mult), reads=["impq"], writes=["nsel"])
                    P.op("pe", lambda e: e.transpose(psb[7][:].bitcast(BF16)[0:64, 0:128], nsel[:], ident[:]), reads=["nsel", "ident"], writes=["ps7"])
                    P.op("act", lambda e, m=m: e.copy(negselT[:, m * 128:(m + 1) * 128], psb[7][:].bitcast(BF16)[0:64, 0:128]), reads=["ps7"], writes=["nselT%d" % m])
                def pass2(r):
                    hr = gl * 4 + r
                    make_rrow(r)
                    for b_ in (2, 3, 4, 5):
                        zero_bank(b_)
                    last = 4 * G + 3
                    def sel_scores(kc):
                        extra = [(selE[:, kc, :], negselT[:, :], ["selE"] + ["nselT%d" % m for m in range(4)])]
                        if kc >= 4 * G:
                            extra.append((ident[:], winT[:, 4 + kc - 4 * G, :], ["ident", "winT"]))
                        return scores(kT4[2], "kT4_2", kc, r, 1, extra)

                    def win_scores(kc):
                        return scores(kT4[3], "kT4_3", kc, r, 2, [(ident[:], winT[:, 4 + kc - 4 * G, :], ["ident", "winT"])])

                    wlist = list(range(max(0, 4 * G - 4), last + 1))
                    pend = [sel_scores(0)]
                    for kc in range(0, last + 1):
                        if kc + 1 <= last:
                            pend.append(sel_scores(kc + 1))
                        else:
                            pend.append(win_scores(wlist[0]))
                        ps_, psk = pend.pop(0)
                        pt_, ptk = PTR.next()
                        rel_ = kc - 4 * G + 28
                        P.op("act", lambda e, ps_=ps_, pt_=pt_, rel_=rel_: e.activation(pt_[:], ps_[:, :], AF.Exp, bias=kbt[:, hr, rel_:rel_ + 1], scale=1.0), reads=[psk, "kbt"], writes=[ptk])
                        for m in range(4):
                            b_ = 2 + m // 2; o_ = (m % 2) * 129
                            P.op("pe", lambda e, pt_=pt_, m=m, b_=b_, o_=o_, kc=kc: e.matmul(psb[b_][:, o_:o_ + 129], pt_[:, m * 128:(m + 1) * 128], VS[:, kc, :], start=False, stop=(kc == last), skip_group_check=True),
                                 reads=[ptk, "VSd", "VS1"], writes=["ps%d" % b_])
                    for wi, kc in enumerate(wlist):
                        if wi + 1 < len(wlist):
                            pend.append(win_scores(wlist[wi + 1]))
                        ps_, psk = pend.pop(0)
                        pt_, ptk = PTR.next()
                        rel_ = kc - 4 * G + 28
                        P.op("act", lambda e, ps_=ps_, pt_=pt_, rel_=rel_: e.activation(pt_[:], ps_[:, :], AF.Exp, bias=kbt[:, hr, rel_:rel_ + 1], scale=1.0), reads=[psk, "kbt"], writes=[ptk])
                        for m in range(4):
                            b_ = 4 + m // 2; o_ = (m % 2) * 129
                            P.op("pe", lambda e, pt_=pt_, m=m, b_=b_, o_=o_, kc=kc: e.matmul(psb[b_][:, o_:o_ + 129], pt_[:, m * 128:(m + 1) * 128], VW[:, kc, :], start=False, stop=(kc == last), skip_group_check=True),
                                 reads=[ptk, "VWd", "VW1"], writes=["ps%d" % b_])
                    ms_ = mstage[(G * 4 + r) % 2]; msk = "ms%d" % ((G * 4 + r) % 2)
                    for bb in (2, 3, 4, 5):
                        if bb % 2 == 0:
                            P.op("act", lambda e, bb=bb: e.copy(pcp[:, bb - 2, 0:258], psb[bb][:, 0:258]), reads=["ps%d" % bb], writes=["pcp%d" % (bb - 2)])
                        else:
                            P.op("dve", lambda e, bb=bb: e.tensor_copy(pcp[:, bb - 2, 0:258], psb[bb][:, 0:258]), reads=["ps%d" % bb], writes=["pcp%d" % (bb - 2)])
                    for m in range(4):
                        qt = G * 4 + m
                        oa = oacc[:, r, m, :]; oak = "oacc%d_%d" % (r, m)
                        for bi, (bb, gcol) in enumerate(((2 + m // 2, 1), (4 + m // 2, 2))):
                            po = pcp[:, bb - 2, :]; o_ = (m % 2) * 129
                            P.op("dve", lambda e, po=po, o_=o_, bi=bi: e.reciprocal(stt[:, 4 + bi:5 + bi], po[:, o_ + 128:o_ + 129]), reads=["pcp%d" % (bb - 2)], writes=["stt%d" % (4 + bi)])
                            P.op("dve", lambda e, bi=bi, qt=qt, gcol=gcol: e.tensor_tensor(stt[:, 6 + bi:7 + bi], stt[:, 4 + bi:5 + bi], sm_all[:, qt, 16 + hr * 3 + gcol:16 + hr * 3 + gcol + 1], op=ALU.mult),
                                 reads=["stt%d" % (4 + bi), "sm_all"], writes=["stt%d" % (6 + bi)])
                            P.op("dve", lambda e, po=po, o_=o_, bi=bi, oa=oa: e.scalar_tensor_tensor(out=oa, in0=po[:, o_:o_ + 128], scalar=stt[:, 6 + bi:7 + bi], in1=oa, op0=ALU.mult, op1=ALU.add),
                                 reads=["pcp%d" % (bb - 2), "stt%d" % (6 + bi), oak], writes=[oak])
                        P.op("act", lambda e, oa=oa: e.copy(ofin[:], oa), reads=[oak], writes=["ofin"])
                        P.op("pe", lambda e: e.transpose(psb[7][:].bitcast(BF16)[:, 256:384], ofin[:], ident[:]), reads=["ofin", "ident"], writes=["ps7"])
                        P.op("act", lambda e, ms_=ms_, m=m: e.copy(ms_[:, m * 128:(m + 1) * 128], psb[7][:].bitcast(BF16)[:, 256:384]), reads=["ps7"], writes=[msk + "_%d" % m])
                    P.dma("sp", mixT[8 + hr][:, G * 512:(G + 1) * 512], ms_[:], "x", reads=[msk + "_%d" % m for m in range(4)], writes=["mixT"])
                for r in range(4):
                    pass1(r)
                for m in range(4):
                    select(m)
                for r in range(4):
                    pass2(r)

            for G in range(8):
                do_G(G)

        for gl in range(2):
            do_group(gl)
            if gl == 0:
                exch(range(8, 12))
      if start <= 3:
        _ph3()
    P.barrier()
    if nphase < 4:
        P.finish(); P.emit(); return nc

    if start <= 3:
        exch(range(12, 16))
        P.barrier()

    def _ph4():
        def gsrc(kc):
            if kc < 16:
                r_, j_ = kc // 8, kc % 8
            else:
                r_, j_ = (kc - 16) // 8, 8 + (kc - 16) % 8
            return mixG[j_][r_ * 128:(r_ + 1) * 128, :]

        selt = P.sb("p4_sel", [128, 2], F32)
        rst = P.sb("p4_rst", [128, 4, 4], F32)
        P.dma("sp", selt[:], selv[:, :], "x", writes=["selt"])
        for tb in range(4):
            tok0 = tb * 512
            with ExitStack() as st:
                mixsel = P.sb("p4_mixsel", [128, 32, 512], BF16, st)
                mA = [P.sb("p4_mA%d" % i, [128, 4, 512], BF16, st) for i in range(2)]
                mB = [P.sb("p4_mB%d" % i, [128, 4, 512], BF16, st) for i in range(2)]
                wo = [P.sb("p4_wo%d" % i, [128, 32, 512], BF16, st) for i in range(2)]
                g1b = P.sb("p4_g1b", [128, D], F32, st)
                xp = [P.sb("p4_xp%d" % i, [128, 512], F32, st) for i in range(3)]
                yp = [P.sb("p4_yp%d" % i, [128, 512], F32, st) for i in range(3)]
                jk = P.sb("p4_jk", [128, 512], BF16, st)
                ss1 = P.sb("p4_ss1", [128, 4, 8], F32, st)
                P.dma("sp", g1b[:], modv[2:3, :].partition_broadcast(128), "x", reads=["modv"], writes=["g1b"])
                for q4 in range(8):
                    a_ = mA[q4 % 2]; b_ = mB[q4 % 2]
                    for u in range(4):
                        kc = q4 * 4 + u
                        P.dma("sp", a_[:, u, :], gsrc(kc)[:, tok0:tok0 + 512], "x", reads=["mixG"], writes=["mA%d" % (q4 % 2)])
                        P.dma("sp", b_[:, u, :], gsrc(kc)[:, 2048 + tok0:2048 + tok0 + 512], "x", reads=["mixG"], writes=["mB%d" % (q4 % 2)])
                    dst = mixsel[:, q4 * 4:(q4 + 1) * 4, :]
                    P.op("dve", lambda e, a_=a_, dst=dst: e.tensor_scalar(dst, a_[:], selt[:, 0:1], None, op0=ALU.mult), reads=["mA%d" % (q4 % 2), "selt"], writes=["mixsel%d" % q4])
                    P.op("dve", lambda e, b_=b_, dst=dst: e.scalar_tensor_tensor(out=dst, in0=b_[:], scalar=selt[:, 1:2], in1=dst, op0=ALU.mult, op1=ALU.add),
                         reads=["mB%d" % (q4 % 2), "selt", "mixsel%d" % q4], writes=["mixsel%d" % q4])
                msk_all = ["mixsel%d" % q4 for q4 in range(8)]
                P.dma("sp", wo[0][:], w_out_b[0], "x", reads=["w_out_b0"], writes=["wo0"])
                it = 0
                for n in range(8):
                    if n + 1 < 8:
                        P.dma("sp", wo[(n + 1) % 2][:], w_out_b[n + 1], "x", reads=["w_out_b%d" % (n + 1)], writes=["wo%d" % ((n + 1) % 2)])
                    w_ = wo[n % 2]; wk = "wo%d" % (n % 2)
                    for m in range(4):
                        b = 2 + (it % 4); it += 1
                        x_ = xp[it % 3]; xk = "xp%d" % (it % 3); y_ = yp[it % 3]; yk = "yp%d" % (it % 3)
                        P.dma("sp", x_[:], x_h[tok0 + m * 128:tok0 + (m + 1) * 128, n * 512:(n + 1) * 512], "x", writes=[xk])
                        for kc in range(32):
                            P.op("pe", lambda e, b=b, w_=w_, kc=kc, m=m: e.matmul(psb[b][:, :], mixsel[:, kc, m * 128:(m + 1) * 128], w_[:, kc, :], start=(kc == 0), stop=(kc == 31)),
                                 reads=[wk] + (msk_all if kc in (0, 31) else []), writes=["ps%d" % b])
                        P.op("dve", lambda e, b=b, y_=y_, n=n: e.tensor_tensor(y_[:], psb[b][:, :], g1b[:, n * 512:(n + 1) * 512], op=ALU.mult), reads=["ps%d" % b, "g1b"], writes=[yk])
                        P.op("pool", lambda e, y_=y_, x_=x_: e.tensor_tensor(y_[:], y_[:], x_[:], op=ALU.add), reads=[yk, xk], writes=[yk])
                        P.op("act", lambda e, y_=y_, m=m, n=n: e.activation(jk[:], y_[:], AF.Square, accum_out=ss1[:, m, n:n + 1]), reads=[yk], writes=["jk", "ss1_%d_%d" % (m, n)])
                        P.dma("sp", x1s[tok0 + m * 128:tok0 + (m + 1) * 128, n * 512:(n + 1) * 512], y_[:], "x", reads=[yk], writes=["x1s"])
                for m in range(4):
                    P.op("dve", lambda e, m=m: e.reduce_sum(rst[:, m, 0:1], ss1[:, m, :], axis=AX.X), reads=["ss1_%d_%d" % (m, n) for n in range(8)], writes=["rst%d" % m])
                    P.op("act", lambda e, m=m: e.activation(rst[:, m, 1:2], rst[:, m, 0:1], AF.Sqrt, bias=1e-6, scale=1.0 / D), reads=["rst%d" % m], writes=["rst%d" % m])
                    P.op("dve", lambda e, m=m: e.reciprocal(rst[:, m, 2:3], rst[:, m, 1:2]), reads=["rst%d" % m], writes=["rst%d" % m])
            P.barrier()
            with ExitStack() as st, ExitStack() as sth:
                hidT = P.sb("p4_hidT", [128, 128, 512], BF16, st)
                h2T = P.sb("p4_h2T", [128, 32, 512], BF16, sth)
                with ExitStack() as st2:
                    w2b = P.sb("p4_w2b", [128, 2048], F32, st2)
                    sh2b = P.sb("p4_sh2b", [128, 2048], F32, st2)
                    xr = P.sb("p4_xr", [128, D], F32, st2)
                    hb2 = P.sb("p4_hb2", [128, D], BF16, st2)
                    for m in range(4):
                        P.dma("sp", xr[:], x1s[tok0 + m * 128:tok0 + (m + 1) * 128, :], "x", reads=["x1s"], writes=["xr"])
                        for hf in range(2):
                            cs_ = slice(hf * 2048, (hf + 1) * 2048)
                            P.dma("sp", w2b[:], modv[3:4, cs_].partition_broadcast(128), "x", reads=["modv"], writes=["w2b"])
                            P.dma("sp", sh2b[:], modv[4:5, cs_].partition_broadcast(128), "x", reads=["modv"], writes=["sh2b"])
                            P.op("dve", lambda e, m=m, cs_=cs_: e.scalar_tensor_tensor(out=xr[:, cs_], in0=xr[:, cs_], scalar=rst[:, m, 2:3], in1=w2b[:], op0=ALU.mult, op1=ALU.mult),
                                 reads=["xr", "rst%d" % m, "w2b"], writes=["xr"])
                            P.op("pool", lambda e, cs_=cs_: e.tensor_tensor(hb2[:, cs_], xr[:, cs_], sh2b[:], op=ALU.add), reads=["xr", "sh2b"], writes=["hb2"])
                        for g in range(4):
                            pk = "ps%d" % (g % 2)
                            ptb = psb[g % 2][:].bitcast(BF16)
                            for u in range(8):
                                kc = g * 8 + u
                                P.op("pe", lambda e, ptb=ptb, u=u, kc=kc: e.transpose(ptb[:, u * 128:(u + 1) * 128], hb2[:, kc * 128:(kc + 1) * 128], ident[:]),
                                     reads=["hb2", "ident"], writes=[pk])
                            dstap = h2T[:, g * 8:(g + 1) * 8, m * 128:(m + 1) * 128]
                            srcap = ptb[:, 0:1024].rearrange("p (u t) -> p u t", u=8)
                            P.op("act", lambda e, d=dstap, s_=srcap: e.copy(d, s_), reads=[pk], writes=["h2T%d_%d" % (m, g)])
                P.barrier()
                st3 = ExitStack()
                wu = [P.sb("p4_wu%d" % i, [128, 32, 128], BF16, st3) for i in range(3)]
                rl = [P.sb("p4_rl%d" % i, [128, 512], F32, st3) for i in range(2)]
                P.dma("sp", wu[0][:], w_up_b[0], "x", reads=["w_up_b0"], writes=["wu0"])
                P.dma("sp", wu[1][:], w_up_b[1], "x", reads=["w_up_b1"], writes=["wu1"])
                for fc in range(128):
                    if fc + 2 < 128:
                        P.dma("sp", wu[(fc + 2) % 3][:], w_up_b[fc + 2], "x", reads=["w_up_b%d" % (fc + 2)], writes=["wu%d" % ((fc + 2) % 3)])
                    w_ = wu[fc % 3]; wk = "wu%d" % (fc % 3)
                    b = 2 + fc % 4
                    for kc in range(32):
                        P.op("pe", lambda e, b=b, w_=w_, kc=kc: e.matmul(psb[b][:, :], w_[:, kc, :], h2T[:, kc, :], start=(kc == 0), stop=(kc == 31)),
                             reads=[wk, "h2T"], writes=["ps%d" % b])
                    r_ = rl[fc % 2]; rk = "rl%d" % (fc % 2)
                    P.op("act", lambda e, b=b, r_=r_: e.activation(r_[:], psb[b][:, :], AF.Relu), reads=["ps%d" % b], writes=[rk])
                    P.op("dve", lambda e, r_=r_, fc=fc: e.tensor_tensor(hidT[:, fc, :], r_[:], r_[:], op=ALU.mult), reads=[rk], writes=["hidT"])
                P.barrier()
                st3.close()
                sth.close()
                wd = [P.sb("p4_wd%d" % i, [128, 8, 512], BF16, st) for i in range(3)]
                g2b = P.sb("p4_g2b", [128, D], F32, st)
                xp2 = [P.sb("p4_xq%d" % i, [128, 512], F32, st) for i in range(3)]
                yp2 = [P.sb("p4_yq%d" % i, [128, 512], F32, st) for i in range(3)]
                jk2 = P.sb("p4_jk2", [128, 512], BF16, st)
                ss2 = P.sb("p4_ss2", [128, 4, 8], F32, st)
                P.dma("sp", g2b[:], modv[5:6, :].partition_broadcast(128), "x", reads=["modv"], writes=["g2b"])
                seq = [(n, fg) for n in range(8) for fg in range(16)]
                for i_ in range(2):
                    n_, fg_ = seq[i_]
                    P.dma("sp", wd[i_ % 3][:], w_dn_b[n_, fg_], "x", reads=["w_dn_b%d_%d" % (n_, fg_)], writes=["wd%d" % (i_ % 3)])
                it = 0
                for si, (n, fg) in enumerate(seq):
                    if si + 2 < len(seq):
                        n_, fg_ = seq[si + 2]
                        P.dma("sp", wd[(si + 2) % 3][:], w_dn_b[n_, fg_], "x", reads=["w_dn_b%d_%d" % (n_, fg_)], writes=["wd%d" % ((si + 2) % 3)])
                    w_ = wd[si % 3]; wk = "wd%d" % (si % 3)
                    for m in range(4):
                        b = 2 + m
                        for j in range(8):
                            P.op("pe", lambda e, b=b, w_=w_, j=j, m=m, fg=fg: e.matmul(psb[b][:, :], hidT[:, fg * 8 + j, m * 128:(m + 1) * 128], w_[:, j, :],
                                                                                  start=(fg == 0 and j == 0), stop=(fg == 15 and j == 7)),
                                 reads=[wk, "hidT"], writes=["ps%d" % b])
                    if fg == 15:
                        for m in range(4):
                            b = 2 + m; it += 1
                            x_ = xp2[it % 3]; xk = "xq%d" % (it % 3); y_ = yp2[it % 3]; yk = "yq%d" % (it % 3)
                            rows = slice(tok0 + m * 128, tok0 + (m + 1) * 128); cols = slice(n * 512, (n + 1) * 512)
                            P.dma("sp", x_[:], x1s[rows, cols], "x", reads=["x1s"], writes=[xk])
                            P.op("dve", lambda e, b=b, y_=y_, n=n: e.tensor_tensor(y_[:], psb[b][:, :], g2b[:, n * 512:(n + 1) * 512], op=ALU.mult), reads=["ps%d" % b, "g2b"], writes=[yk])
                            P.op("pool", lambda e, y_=y_, x_=x_: e.tensor_tensor(y_[:], y_[:], x_[:], op=ALU.add), reads=[yk, xk], writes=[yk])
                            P.op("act", lambda e, y_=y_, m=m, n=n: e.activation(jk2[:], y_[:], AF.Square, accum_out=ss2[:, m, n:n + 1]), reads=[yk], writes=["jk2", "ss2_%d_%d" % (m, n)])
                            P.dma("sp", x1s[rows, cols], y_[:], "x", reads=[yk, "x1s"], writes=["x1s"])
                for m in range(4):
                    P.op("dve", lambda e, m=m: e.reduce_sum(rst[:, m, 0:1], ss2[:, m, :], axis=AX.X), reads=["ss2_%d_%d" % (m, n) for n in range(8)], writes=["rst%d" % m])
                    P.op("act", lambda e, m=m: e.activation(rst[:, m, 1:2], rst[:, m, 0:1], AF.Sqrt, bias=1e-6, scale=1.0 / D), reads=["rst%d" % m], writes=["rst%d" % m])
                    P.op("dve", lambda e, m=m: e.reciprocal(rst[:, m, 2:3], rst[:, m, 1:2]), reads=["rst%d" % m], writes=["rst%d" % m])
            P.barrier()
            with ExitStack() as st:
                fnb = P.sb("p4_fnb", [128, D], F32, st)
                xo = [P.sb("p4_xo%d" % i, [128, D], F32, st) for i in range(2)]
                P.dma("sp", fnb[:], modv[6:7, :].partition_broadcast(128), "x", reads=["modv"], writes=["fnb"])
                for m in range(4):
                    x_ = xo[m % 2]; xk = "xo%d" % (m % 2)
                    rows = slice(tok0 + m * 128, tok0 + (m + 1) * 128)
                    P.dma("sp", x_[:], x1s[rows, :], "x", reads=["x1s"], writes=[xk])
                    P.op("dve", lambda e, x_=x_, m=m: e.scalar_tensor_tensor(out=x_[:], in0=x_[:], scalar=rst[:, m, 2:3], in1=fnb[:], op0=ALU.mult, op1=ALU.mult),
                         reads=[xk, "rst%d" % m, "fnb"], writes=[xk])
                    P.dma("sp", out_h[rows, :], x_[:], "x", reads=[xk], writes=["out_h"])
            P.barrier()

    _ph4()
    P.finish()
    P.emit()
    return nc


def core_inputs(inp, b, hh, consts):
    f32 = np.float32
    d = {}
    d["x_b"] = np.ascontiguousarray(inp["x"][b])
    d["x_h"] = np.ascontiguousarray(inp["x"][b, hh * 2048:(hh + 1) * 2048])
    d["cT"] = np.ascontiguousarray(inp["c"][b].reshape(32, 128).T)
    d["ada_w"] = inp["ada_w"][0]
    d["ada_b"] = inp["ada_b"][0][None, :]
    d["n1w"] = inp["norm1_w"][0][None, :]
    d["n2w"] = inp["norm2_w"][0][None, :]
    d["fnw"] = inp["final_norm_w"][None, :]
    cols = w_in_cols(hh)
    wc = np.zeros((D, W_IN_COLS), f32)
    wc[:, :cols.size] = inp["w_in"][0][:, cols]
    d["w_in_c"] = wc
    cwv = inp["gdn_conv_w"][0]
    cw = np.zeros((128, 24, 4), f32)
    for f in range(24):
        kind, hd = f // 8, f % 8
        ch = kind * 2048 + (8 * hh + hd) * 128 + np.arange(128)
        cw[:, f, :] = cwv[:, ch].T
    d["convw"] = cw.reshape(128, 96)
    d["alog"] = inp["gdn_a_log"][0][None, 8 * hh:8 * hh + 8].astype(f32)
    d["dtb"] = inp["gdn_dt_bias"][0][None, 8 * hh:8 * hh + 8].astype(f32)
    d["gnw"] = inp["gdn_norm_w"][0][None, :]
    for s in ("k", "v"):
        d["pos_" + s] = inp["cmp_pos_" + s][0]
        d["w1_" + s] = inp["cmp_w1_" + s][0]
        d["w2_" + s] = inp["cmp_w2_" + s][0]
    d["w_out"] = inp["w_out"][0]
    d["w_up"] = inp["w_up"][0]
    d["w_down"] = inp["w_down"][0]
    sv = np.zeros((128, 2), f32)
    sv[:, hh] = 1.0
    d["selv"] = sv
    for k, v in consts.items():
        d["t_" + k] = v
    kb, cb, nsl = alibi_tables(hh)
    d["t_kb"] = kb
    d["t_cb"] = cb
    d["t_nsl"] = nsl
    return {k: np.ascontiguousarray(v) for k, v in d.items()}


def kernel(**inputs):
    inp = {k: np.asarray(v) for k, v in inputs.items()}
    consts = const_tables()
    nc = build_program()
    in_maps = [core_inputs(inp, cid // 2, cid % 2, consts) for cid in range(8)]
    res = run_bass_kernel_spmd(nc, in_maps, core_ids=list(range(8)))
    out = np.zeros((4, T, D), np.float32)
    for cid in range(8):
        b, hh = cid // 2, cid % 2
        out[b, hh * 2048:(hh + 1) * 2048] = res.results[cid]["out_h"]
    return out
```

```python
import numpy as np
import ml_dtypes
from contextlib import ExitStack
import concourse.bass as bass
import concourse.mybir as mybir
from concourse.bass_utils import run_bass_kernel_spmd

F32 = mybir.dt.float32
BF16 = mybir.dt.bfloat16
I32 = mybir.dt.int32
ALU = mybir.AluOpType
AF = mybir.ActivationFunctionType
AX = mybir.AxisListType

ENGS = ("pe", "act", "dve", "pool", "sp")

T = 4096
D = 4096
DFF = 16384
NEG = -30000.0


class Prog:
    def __init__(self, nc):
        self.nc = nc
        self.ops = {e: [] for e in ENGS}
        self.nops = {e: 0 for e in ENGS}
        self.dcount = {}
        self.res = {}
        self.waited = {}
        self.awaited = {e: set() for e in ENGS}
        self.bankacc = {}
        self.dring = {}
        self.stack = ExitStack()

    def sb(self, name, shape, dt, stack=None):
        self.uid = getattr(self, "uid", 0) + 1
        name = "%s_u%d" % (name, self.uid)
        return (stack or self.stack).enter_context(self.nc.sbuf_tensor(name, list(shape), dt))

    def ps(self, name, shape, dt=F32, stack=None):
        return (stack or self.stack).enter_context(self.nc.psum_tensor(name, list(shape), dt))

    def _deps(self, eng, reads, writes):
        deps = {}
        for r in reads:
            ent = self.res.get(r)
            if ent is not None and ent[0] is not None:
                k, i = ent[0]
                if deps.get(k, -1) < i:
                    deps[k] = i
        for w in writes:
            ent = self.res.get(w)
            if ent is not None:
                if ent[0] is not None:
                    k, i = ent[0]
                    if deps.get(k, -1) < i:
                        deps[k] = i
                for k, i in ent[1].items():
                    if deps.get(k, -1) < i:
                        deps[k] = i
        waits = []
        for k, i in deps.items():
            if k == eng and eng == "pe":
                continue
            if self.waited.get((eng, k), -1) >= i:
                continue
            self.waited[(eng, k)] = i
            waits.append((k, i))
            if k in self.awaited:
                self.awaited[k].add(i)
        return waits

    def _update(self, tok, reads, writes):
        k, i = tok
        for w in writes:
            self.res[w] = [tok, {}]
        for r in reads:
            ent = self.res.get(r)
            if ent is None:
                ent = self.res[r] = [None, {}]
            if ent[1].get(k, -1) < i:
                ent[1][k] = i

    @staticmethod
    def _bank(key):
        if key.startswith("psb"):
            return int(key[3])
        if key.startswith("ps"):
            return int(key[2])
        return None

    def _bank_deps(self, eng, reads, writes, waits):
        banks = set()
        for k_ in list(reads) + list(writes):
            b = self._bank(k_)
            if b is not None:
                banks.add(b)
        for b in banks:
            acc = self.bankacc.setdefault(b, {})
            for k, i in acc.items():
                if k == eng:
                    continue
                if self.waited.get((eng, k), -1) >= i:
                    continue
                self.waited[(eng, k)] = i
                waits.append((k, i))
                self.awaited[k].add(i)
        return banks

    def op(self, eng, fn, reads=(), writes=()):
        waits = self._deps(eng, reads, writes)
        banks = self._bank_deps(eng, reads, writes, waits)
        for b in banks:
            self.bankacc[b][eng] = self.nops[eng] + 1
        self.nops[eng] += 1
        tok = (eng, self.nops[eng])
        self.ops[eng].append([waits, fn, "c", self.nops[eng]])
        self._update(tok, reads, writes)
        return tok

    NRING = {"sp": 40, "pool": 16, "act": 8}

    def dma(self, q, out, in_, sem, reads=(), writes=(), **kw):
        waits = self._deps(q, reads, writes)
        n = self.dring.get(q, 0)
        self.dring[q] = n + 1
        sem = "%s_r%d" % (q, n % self.NRING[q])
        prev = self.dcount.get(sem, 0)
        if prev and self.waited.get((q, sem), -1) < prev:
            self.waited[(q, sem)] = prev
            waits.append((sem, prev))
        self.dcount[sem] = prev + 16
        tok = (sem, self.dcount[sem])
        self.ops[q].append([waits, lambda e: e.dma_start(out=out, in_=in_, **kw), "d", sem])
        self._update(tok, reads, writes)
        return tok

    def cc(self, fn, sem, reads=(), writes=()):
        waits = self._deps("pool", reads, writes)
        self.dcount[sem] = self.dcount.get(sem, 0) + 1
        tok = (sem, self.dcount[sem])
        self.ops["pool"].append([waits, fn, "k", sem])
        self._update(tok, reads, writes)
        return tok

    def barrier(self):
        for e in ENGS:
            waits = []
            for k in ENGS:
                if k == e or self.nops[k] == 0:
                    continue
                i = self.nops[k]
                if self.waited.get((e, k), -1) >= i:
                    continue
                self.waited[(e, k)] = i
                self.awaited[k].add(i)
                waits.append((k, i))
            for k, c in self.dcount.items():
                if self.waited.get((e, k), -1) >= c:
                    continue
                self.waited[(e, k)] = c
                waits.append((k, c))
            if waits:
                self.ops[e].append([waits, None, "w", None])
        self.res = {}

    def finish(self):
        waits = [(k, c) for k, c in self.dcount.items()]
        self.ops["sp"].append([waits, None, "w", None])

    def emit(self):
        nc = self.nc
        sems = {}
        for e in ENGS:
            if self.nops[e]:
                sems[e] = self.stack.enter_context(nc.semaphore("s_" + e))
        for k in self.dcount:
            sems[k] = self.stack.enter_context(nc.semaphore("d_" + k))
        vmap = {}
        for e in ENGS:
            aw = sorted(self.awaited[e])
            vmap[e] = {idx: n + 1 for n, idx in enumerate(aw)}

        def val(k, i):
            return vmap[k][i] if k in vmap else i

        def run(e, engobj):
            for waits, fn, kind, extra in self.ops[e]:
                for k, i in waits:
                    engobj.wait_ge(sems[k], val(k, i))
                if kind == "w":
                    continue
                ins = fn(engobj)
                if kind == "c":
                    if extra in vmap[e]:
                        ins.then_inc(sems[e], 1)
                elif kind == "d":
                    ins.then_inc(sems[extra], 16)
                else:
                    ins.then_inc(sems[extra], 1)

        with nc.Block() as block:
            @block.tensor
            def _(t):
                run("pe", t)

            @block.scalar
            def _(t):
                run("act", t)

            @block.vector
            def _(t):
                run("dve", t)

            @block.gpsimd
            def _(t):
                run("pool", t)

            @block.sync
            def _(t):
                run("sp", t)


class Rot:
    def __init__(self, items):
        self.items = items
        self.i = 0

    def next(self):
        it = self.items[self.i % len(self.items)]
        self.i += 1
        return it


def const_tables():
    c = {}
    p = np.arange(128)
    q = np.arange(512)
    win = np.zeros((128, 8, 512), np.float32)
    for j in range(8):
        dist = q[None, :] - (128 * (j - 4) + p[:, None])
        win[:, j, :] = np.where((dist >= 0) & (dist < 512), 0.0, NEG)
    c["winT"] = win.astype(ml_dtypes.bfloat16)
    n = (np.arange(2)[None, :, None] * 128 + p[:, None, None])
    t = np.arange(T)[None, None, :]
    c["cmT"] = np.where((t >= 16 * n + 31) & (n < 255), 0.0, NEG).astype(ml_dtypes.bfloat16)
    s = np.arange(64)[:, None, None]
    kc = np.arange(32)[None, :, None]
    m = np.arange(128)[None, None, :]
    c["selE"] = (s == 2 * kc + m // 64).astype(np.float32).astype(ml_dtypes.bfloat16)
    n = (np.arange(2)[None, :, None] * 128 + p[:, None, None])
    sb = np.arange(64)[None, None, :]
    ov = ((16 * n <= 64 * sb + 63) & (16 * n + 31 >= 64 * sb) & (n < 255)).astype(np.float32)
    c["ovl"] = ov.astype(ml_dtypes.bfloat16)
    tt = (np.arange(32)[None, :, None] * 128 + p[:, None, None])
    cur = tt // 64
    blk = np.arange(64)[None, None, :]
    frc = np.zeros((128, 32, 64), np.float32)
    forced = (blk == 0) | ((cur - blk) < 2)
    frc = np.where(forced, 1e30 * (1.0 + 0.25 * (blk % 4)), frc)
    frc = np.where(blk <= cur, frc, -1e30)
    c["frc"] = frc.astype(np.float32)
    c["triu"] = (p[:, None] <= p[None, :]).astype(np.float32)
    c["dmask"] = np.where(p[None, :] >= p[:, None], 0.0, NEG).astype(np.float32)
    oh = (np.arange(32)[:, None, None] == np.arange(32)[None, :, None]).astype(np.float32)
    c["oneh"] = np.broadcast_to(oh, (32, 32, 128)).copy().astype(np.float32)
    c["trow"] = np.arange(512, dtype=np.float32)[None, :]
    a_ = p[:, None]; b_ = p[None, :]
    mu = np.zeros((128, 7, 128), np.float32)
    for l in range(7):
        sz_ = 2 ** l
        mu[:, l, :] = ((a_ // (2 * sz_) == b_ // (2 * sz_)) & ((a_ % (2 * sz_)) < sz_) & ((b_ % (2 * sz_)) >= sz_)).astype(np.float32)
    c["mskU"] = mu
    c["mskL"] = np.ascontiguousarray(mu.transpose(2, 1, 0))
    return c


def alibi_tables(hh):
    slopes = 2.0 ** (-8.0 * np.arange(1, 17, dtype=np.float64) / 16.0)
    p = np.arange(128, dtype=np.float64)
    hs = slopes[8 * hh:8 * hh + 8]
    rel = np.arange(32, dtype=np.float64)
    kb = hs[None, :, None] * (128.0 * (rel[None, None, :] - 28.0) + p[:, None, None])
    ncx = np.arange(2, dtype=np.float64)
    G = np.arange(8, dtype=np.float64)
    cb = hs[None, :, None, None] * (16.0 * (ncx[None, None, :, None] * 128 + p[:, None, None, None]) + 31.0
                                    - 512.0 * G[None, None, None, :])
    nsl = -hs[None, :]
    return kb.astype(np.float32), cb.astype(np.float32), nsl.astype(np.float32)


def w_in_cols(hh):
    GDN_DK, NSA_DQ, DKV = 2048, 2048, 512
    sizes = (GDN_DK, GDN_DK, GDN_DK, GDN_DK, 16, 16, NSA_DQ, DKV, DKV, DKV, DKV, DKV, DKV, 48)
    off = np.concatenate([[0], np.cumsum(sizes)])
    gq, gk, gv, gz, gb, ga, nq, nkc, nvc, nks, nvs, nkw, nvw, ngate = [int(o) for o in off[:-1]]
    h8 = np.arange(1024) + 1024 * hh
    g2 = np.arange(256) + 256 * hh
    cols = []
    for base in (gq, gk, gv):
        cols.append(base + h8)
    cols.append(nq + h8)
    for base in (nkc, nvc, nks, nkw):
        cols.append(base + g2)
    cols.append(gz + h8)
    cols.append(nvs + g2)
    cols.append(nvw + g2)
    cols.append(gb + 8 * hh + np.arange(8))
    cols.append(ga + 8 * hh + np.arange(8))
    cols.append(ngate + 24 * hh + np.arange(24))
    return np.concatenate(cols)


N_FM = 40
W_IN_COLS = 40 * 128 + 3 * 512 + 64


def build_program(dbg=None, nphase=99, start=0, only_heads=8, gdn_chunks=32, gdbg=None, lite=False):
    nc = bass.Bass("TRN2", target_bir_lowering=False)
    P = Prog(nc)
    ck = "ExternalOutput" if dbg else "Internal"

    need = {"x_b": (1,), "ada_w": (0,), "w_in_c": (0, 1), "w_out": (4,), "w_up": (4,), "w_down": (4,), "x_h": (4,),
            "w1_k": (3,), "w1_v": (3,)}

    def din(name, shape, dt=F32):
        if lite and name in need and not any(start <= ph_ <= nphase - 0 for ph_ in need[name]):
            shape = [1, 1]
        return nc.dram_tensor(name, list(shape), dt, kind="ExternalInput").ap()

    def dscr(name, shape, dt, dbgout=False, ph=99, last=99):
        if ph < start <= last:
            return nc.dram_tensor(name, list(shape), dt, kind="ExternalInput").ap()
        return nc.dram_tensor(name, list(shape), dt, kind=("ExternalOutput" if (dbg and dbgout) else "Internal")).ap()

    x_b = din("x_b", [T, D]); x_h = din("x_h", [2048, D]); cT = din("cT", [128, 32])
    ada_w = din("ada_w", [D, 6 * D]); ada_b = din("ada_b", [1, 6 * D])
    n1w = din("n1w", [1, D]); n2w = din("n2w", [1, D]); fnw = din("fnw", [1, D])
    w_in_c = din("w_in_c", [D, W_IN_COLS])
    convw = din("convw", [128, 24 * 4]); alog = din("alog", [1, 8]); dtb = din("dtb", [1, 8]); gnw = din("gnw", [1, 128])
    pos_k = din("pos_k", [32, 128]); w1_k = din("w1_k", [4096, 128]); w2_k = din("w2_k", [128, 128])
    pos_v = din("pos_v", [32, 128]); w1_v = din("w1_v", [4096, 128]); w2_v = din("w2_v", [128, 128])
    w_out = din("w_out", [D, D]); w_up = din("w_up", [D, DFF]); w_down = din("w_down", [DFF, D])
    selv = din("selv", [128, 2])
    t_winT = din("t_winT", [128, 8, 512], BF16); t_cmT = din("t_cmT", [128, 2, T], BF16)
    t_selE = din("t_selE", [64, 32, 128], BF16); t_ovl = din("t_ovl", [128, 2, 64], BF16)
    t_frc = din("t_frc", [128, 32, 64]); t_triu = din("t_triu", [128, 128]); t_dmask = din("t_dmask", [128, 128])
    t_oneh = din("t_oneh", [32, 32, 128]); t_trow = din("t_trow", [1, 512])
    t_mskU = din("t_mskU", [128, 7, 128]); t_mskL = din("t_mskL", [128, 7, 128])
    t_kb = din("t_kb", [128, 8, 32]); t_cb = din("t_cb", [128, 8, 2, 8]); t_nsl = din("t_nsl", [1, 8])

    out_h = nc.dram_tensor("out_h", [2048, D], F32, kind="ExternalOutput").ap()

    modv = dscr("modv", [8, D], F32, dbgout=True, ph=0)
    w_fm = dscr("w_fm", [N_FM, 128, 32, 128], BF16)
    w_tm = dscr("w_tm", [4, 128, 32, 512], BF16)
    gqT = dscr("gqT", [8, 128, T], BF16, True, ph=1, last=2); gkT = dscr("gkT", [8, 128, T], BF16, True, ph=1, last=2)
    gk_tm = dscr("gk_tm", [8, T, 128], BF16, True, ph=1, last=2); gv_tm = dscr("gv_tm", [8, T, 128], BF16, True, ph=1, last=2)
    sz_tm = dscr("sz_tm", [T, 1024], BF16, True, ph=1, last=2)
    nqT = dscr("nqT", [8, 128, T], BF16, True, ph=1, last=3)
    kvT = dscr("kvT", [8, 128, T], BF16, True, ph=1, last=3)
    vsw_tm = dscr("vsw_tm", [T, 512], BF16, True, ph=1, last=3)
    sm_tm = dscr("sm_tm", [T, 64], F32, True, ph=1, last=3)
    mixT = dscr("mixT", [16, 128, T], BF16, True)
    mixG = dscr("mixG", [16, 2 * 128, T], BF16, ph=3)
    w_out_b = dscr("w_out_b", [8, 128, 32, 512], BF16)
    w_up_b = dscr("w_up_b", [128, 128, 32, 128], BF16)
    w_dn_b = dscr("w_dn_b", [8, 16, 128, 8, 512], BF16)
    x1s = dscr("x1s", [2048, D], F32, True)

    ident = P.sb("ident", [128, 128], BF16)
    identf = P.sb("identf", [128, 128], F32)
    onesb = P.sb("onesb", [128, 128], BF16)
    onesf = P.sb("onesf", [128, 128], F32)
    P.op("pool", lambda e: e.memset(ident[:], 1.0), writes=["ident"])
    P.op("pool", lambda e: e.affine_select(ident[:], ident[:], pattern=[[-1, 128]], compare_op=ALU.is_equal,
                                           fill=0.0, base=0, channel_multiplier=1), reads=["ident"], writes=["ident"])
    P.op("pool", lambda e: e.memset(identf[:], 1.0), writes=["identf"])
    P.op("pool", lambda e: e.affine_select(identf[:], identf[:], pattern=[[-1, 128]], compare_op=ALU.is_equal,
                                           fill=0.0, base=0, channel_multiplier=1), reads=["identf"], writes=["identf"])
    P.op("pool", lambda e: e.memset(onesb[:], 1.0), writes=["onesb"])
    P.op("pool", lambda e: e.memset(onesf[:], 1.0), writes=["onesf"])

    psb = [P.ps("psb%d" % i, [128, 512], F32) for i in range(8)]

    for f in range(N_FM if start <= 1 else 0):
        P.dma("pool", w_fm[f], w_in_c[:, f * 128:(f + 1) * 128].rearrange("(kc p) f -> p kc f", p=128), "cvt",
              writes=["w_fm%d" % f])
    for g in range(3 if start <= 1 else 0):
        o = N_FM * 128 + g * 512
        P.dma("pool", w_tm[g], w_in_c[:, o:o + 512].rearrange("(kc p) f -> p kc f", p=128), "cvt", writes=["w_tm%d" % g])
    o = N_FM * 128 + 3 * 512
    if start <= 1:
      P.dma("pool", w_tm[3][:, :, 0:64], w_in_c[:, o:o + 64].rearrange("(kc p) f -> p kc f", p=128), "cvt", writes=["w_tm3"])
    cast_q = []
    if nphase >= 4:
        for n in range(8):
            cast_q.append((w_out_b[n], w_out[:, n * 512:(n + 1) * 512].rearrange("(kc p) f -> p kc f", p=128), "w_out_b%d" % n))
        for fg in range(16):
            src = w_up[:, fg * 1024:(fg + 1) * 1024].rearrange("(kc p) (c f) -> p c kc f", p=128, f=128)
            for cc_ in range(8):
                cast_q.append((w_up_b[fg * 8 + cc_], src[:, cc_], "w_up_b%d" % (fg * 8 + cc_)))
        for n in range(8):
            for fg in range(16):
                src = w_down[fg * 1024:(fg + 1) * 1024, n * 512:(n + 1) * 512].rearrange("(j p) f -> p j f", p=128)
                cast_q.append((w_dn_b[n, fg], src, "w_dn_b%d_%d" % (n, fg)))

    def cast_step(nmax):
        for _ in range(nmax):
            if not cast_q:
                return
            o_, i_, k_ = cast_q.pop(0)
            P.dma("pool", o_, i_, "cvt", writes=[k_])

    if start > 2:
        cast_step(10 ** 6)

    with ExitStack() as st:
      def _ph0():
        sT = P.sb("p0_sT", [128, 32], F32, st)
        awt = [P.sb("p0_aw%d" % i, [128, 32, 512], F32, st) for i in range(2)]
        row = [P.sb("p0_row%d" % i, [1, 512], F32, st) for i in range(2)]
        bro = [P.sb("p0_bro%d" % i, [1, 512], F32, st) for i in range(2)]
        nro = [P.sb("p0_nro%d" % i, [1, 512], F32, st) for i in range(2)]
        P.dma("sp", sT[:], cT[:, :], "p0c", writes=["sT"])
        P.op("act", lambda e: e.activation(sT[:], sT[:], AF.Silu), reads=["sT"], writes=["sT"])
        P.dma("sp", modv[6:7, :], fnw[:, :], "p0s", writes=["modv"])
        NB = 48
        for nb in range(NB):
            a = awt[nb % 2]
            kind = nb // 8
            cs = (nb % 8) * 512
            P.dma("sp", a[:], ada_w[:, nb * 512:(nb + 1) * 512].rearrange("(kc p) f -> p kc f", p=128), "p0w%d" % (nb % 2),
                  writes=["aw%d" % (nb % 2)])
            P.dma("sp", bro[nb % 2][:], ada_b[:, nb * 512:(nb + 1) * 512], "p0b%d" % (nb % 2), writes=["bro%d" % (nb % 2)])
            if kind in (1, 4):
                P.dma("sp", nro[nb % 2][:], (n1w if kind == 1 else n2w)[:, cs:cs + 512], "p0n%d" % (nb % 2), writes=["nro%d" % (nb % 2)])
            pb = psb[nb % 2]
            for kc in range(32):
                P.op("pe", lambda e, a=a, pb=pb, kc=kc: e.matmul(pb[0:1, :], sT[:, kc:kc + 1], a[:, kc, :], start=(kc == 0), stop=(kc == 31)),
                     reads=["sT", "aw%d" % (nb % 2)], writes=["ps%d" % (nb % 2)])
            r = row[nb % 2]
            br_ = bro[nb % 2]; nr_ = nro[nb % 2]
            P.op("dve", lambda e, r=r, pb=pb, br_=br_: e.tensor_tensor(r[:], pb[0:1, :], br_[:], op=ALU.add),
                 reads=["ps%d" % (nb % 2), "bro%d" % (nb % 2)], writes=["row%d" % (nb % 2)])
            if kind in (1, 4):
                P.op("dve", lambda e, r=r, nr_=nr_: e.scalar_tensor_tensor(out=r[:], in0=r[:], scalar=1.0, in1=nr_[:], op0=ALU.add, op1=ALU.mult),
                     reads=["row%d" % (nb % 2), "nro%d" % (nb % 2)], writes=["row%d" % (nb % 2)])
            dst = {0: 1, 1: 0, 2: 2, 3: 4, 4: 3, 5: 5}[kind]
            P.dma("sp", modv[dst:dst + 1, cs:cs + 512], r[:], "p0s", reads=["row%d" % (nb % 2)], writes=["modv"])
      if start <= 0:
        _ph0()
    P.barrier()
    if nphase < 1:
        P.finish(); P.emit(); return nc

    with ExitStack() as st:
      def _ph1():
        w1b = P.sb("p1_w1b", [128, D], F32, st)
        sh1b = P.sb("p1_sh1b", [128, D], F32, st)
        hT = P.sb("p1_hT", [128, 32, 1024], BF16, st)
        xt = [P.sb("p1_x%d" % i, [128, D], F32, st) for i in range(2)]
        hb = P.sb("p1_hb", [128, D], BF16, st)
        wfb = [P.sb("p1_wf%d" % i, [128, 32, 128], BF16, st) for i in range(3)]
        wtb = [P.sb("p1_wt%d" % i, [128, 32, 256], BF16, st) for i in range(1)]
        cw = P.sb("p1_cw", [128, 96], F32, st)
        halo = P.sb("p1_halo", [128, 24, 3], F32, st)
        raw = [P.sb("p1_raw%d" % i, [128, 515], F32, st) for i in range(2)]
        acc = [P.sb("p1_acc%d" % i, [128, 512], F32, st) for i in range(2)]
        sq = [P.sb("p1_sq%d" % i, [128, 512], BF16, st) for i in range(2)]
        rin = [P.sb("p1_rin%d" % i, [128, 512], F32, st) for i in range(2)]
        ob = [P.sb("p1_ob%d" % i, [128, 512], BF16, st) for i in range(3)]
        tmb = [P.sb("p1_tm%d" % i, [128, 512], BF16, st) for i in range(2)]
        smf = [P.sb("p1_smf%d" % i, [128, 64], F32, st) for i in range(2)]
        stat = P.sb("p1_stat", [128, 8], F32, st)
        dtbb = P.sb("p1_dtbb", [128, 8], F32, st)
        nab = P.sb("p1_nab", [128, 8], F32, st)
        P.dma("sp", w1b[:], modv[0:1, :].partition_broadcast(128), "p1c", reads=["modv"], writes=["w1b"])
        P.dma("sp", sh1b[:], modv[1:2, :].partition_broadcast(128), "p1c", reads=["modv"], writes=["sh1b"])
        P.dma("sp", cw[:], convw[:, :], "p1c", writes=["cw"])
        P.dma("sp", dtbb[:], dtb[0:1, :].partition_broadcast(128), "p1c", writes=["dtbb"])
        P.dma("sp", nab[:], alog[0:1, :].partition_broadcast(128), "p1c", writes=["nab"])
        P.op("act", lambda e: e.activation(nab[:], nab[:], AF.Exp), reads=["nab"], writes=["nab"])
        P.op("dve", lambda e: e.tensor_scalar(nab[:], nab[:], -1.0, None, op0=ALU.mult), reads=["nab"], writes=["nab"])
        P.op("pool", lambda e: e.memset(halo[:], 0.0), writes=["halo"])
        psT = [psb[0], psb[1]]
        psM = Rot([(psb[2], "psb2"), (psb[3], "psb3"), (psb[4], "psb4"), (psb[5], "psb5")])
        psN = Rot([(psb[6], "psb6"), (psb[7], "psb7")])
        rawR = Rot(list(zip(raw, ["raw0", "raw1"]))); accR = Rot(list(zip(acc, ["acc0", "acc1"])))
        sqR = Rot(list(zip(sq, ["sq0", "sq1"]))); rinR = Rot(list(zip(rin, ["rin0", "rin1"])))
        obR = Rot(list(zip(ob, ["ob0", "ob1", "ob2"]))); tmR = Rot(list(zip(tmb, ["tm0", "tm1"])))
        smR = Rot(list(zip(smf, ["smf0", "smf1"])))
        evq = Rot(["act", "dve"])

        def load_x(sbk, i):
            j = (sbk * 8 + i)
            P.dma("sp", xt[j % 2][:], x_b[j * 128:(j + 1) * 128, :], "p1x%d" % (j % 2), writes=["xt%d" % (j % 2)])

        load_x(0, 0)
        for sbk in range(4):
            t0s = sbk * 1024
            for i in range(8):
                j = sbk * 8 + i
                if j + 1 < 32:
                    load_x((j + 1) // 8, (j + 1) % 8)
                x = xt[j % 2]; xk = "xt%d" % (j % 2)
                P.op("act", lambda e, x=x: e.activation(hb[:], x[:], AF.Square, accum_out=stat[:, 0:1]), reads=[xk], writes=["hb", "st0"])
                P.op("act", lambda e: e.activation(stat[:, 1:2], stat[:, 0:1], AF.Sqrt, bias=1e-6, scale=1.0 / D), reads=["st0"], writes=["st1"])
                P.op("dve", lambda e: e.reciprocal(stat[:, 2:3], stat[:, 1:2]), reads=["st1"], writes=["st2"])
                P.op("dve", lambda e, x=x: e.scalar_tensor_tensor(out=x[:], in0=x[:], scalar=stat[:, 2:3], in1=w1b[:], op0=ALU.mult, op1=ALU.mult),
                     reads=[xk, "st2", "w1b"], writes=[xk])
                P.op("pool", lambda e, x=x: e.tensor_tensor(hb[:], x[:], sh1b[:], op=ALU.add), reads=[xk, "sh1b"], writes=["hb"])
                for g in range(4):
                    pt = psT[g % 2]; pk = "psb%d" % (g % 2)
                    ptb = pt[:].bitcast(BF16)
                    for u in range(8):
                        kc = g * 8 + u
                        P.op("pe", lambda e, ptb=ptb, u=u, kc=kc: e.transpose(ptb[:, u * 128:(u + 1) * 128], hb[:, kc * 128:(kc + 1) * 128], ident[:]),
                             reads=["hb", "ident"], writes=[pk])
                    q_ = evq.next()
                    dstap = hT[:, g * 8:(g + 1) * 8, i * 128:(i + 1) * 128]
                    srcap = ptb[:, 0:1024].rearrange("p (u t) -> p u t", u=8)
                    if q_ == "act":
                        P.op("act", lambda e, d=dstap, s=srcap: e.copy(d, s), reads=[pk], writes=["hT%d_%d" % (i, g)])
                    else:
                        P.op("dve", lambda e, d=dstap, s=srcap: e.tensor_copy(d, s), reads=[pk], writes=["hT%d_%d" % (i, g)])
            hT_keys = ["hT%d_%d" % (i, g) for i in range(8) for g in range(4)]
            hTh = [[("hT%d_%d" % (i, g)) for i in range(th * 4, th * 4 + 4) for g in range(4)] for th in range(2)]

            def load_wf(f):
                P.dma("sp", wfb[f % 3][:], w_fm[f], "p1wf%d" % (f % 3), reads=["w_fm%d" % f], writes=["wf%d" % (f % 3)])

            post_q = []

            def post_tile(f, th, pm, pmk, tok0):
                if f < 24:
                    kind = f // 8
                    hd = f % 8
                    rw, rk = rawR.next(); ac, ak = accR.next()
                    P.op("pool", lambda e, rw=rw, f=f: e.tensor_copy(rw[:, 0:3], halo[:, f, :]), reads=["halo%d" % f], writes=[rk + "h"])
                    P.op("act", lambda e, rw=rw, pm=pm: e.copy(rw[:, 3:515], pm[:, :]), reads=[pmk], writes=[rk])
                    P.op("pool", lambda e, rw=rw, f=f: e.tensor_copy(halo[:, f, :], rw[:, 512:515]), reads=[rk], writes=["halo%d" % f])
                    P.op("dve", lambda e, rw=rw, ac=ac, f=f: e.tensor_scalar(ac[:], rw[:, 3:515], cw[:, f * 4 + 3:f * 4 + 4], None, op0=ALU.mult),
                         reads=[rk, "cw"], writes=[ak])
                    for jj in (2, 1, 0):
                        P.op("dve", lambda e, rw=rw, ac=ac, f=f, jj=jj: e.scalar_tensor_tensor(out=ac[:], in0=rw[:, jj:jj + 512], scalar=cw[:, f * 4 + jj:f * 4 + jj + 1],
                                                                                            in1=ac[:], op0=ALU.mult, op1=ALU.add),
                             reads=[rk, rk + "h", ak, "cw"], writes=[ak])
                    o_, ok = obR.next()
                    if kind == 2:
                        P.op("act", lambda e, ac=ac, o_=o_: e.activation(o_[:], ac[:], AF.Silu), reads=[ak], writes=[ok])
                    else:
                        P.op("act", lambda e, ac=ac: e.activation(ac[:], ac[:], AF.Silu), reads=[ak], writes=[ak])
                        s_, sk = sqR.next(); ri, rik = rinR.next()
                        P.op("pool", lambda e, ac=ac, s_=s_: e.tensor_tensor(s_[:], ac[:], ac[:], op=ALU.mult), reads=[ak], writes=[sk])
                        pn, pnk = psN.next()
                        P.op("pe", lambda e, pn=pn, s_=s_: e.matmul(pn[:, :], onesb[:], s_[:], start=True, stop=True), reads=[sk, "onesb"], writes=[pnk])
                        P.op("act", lambda e, pn=pn, ri=ri: e.activation(ri[:], pn[:, :], AF.Sqrt, bias=1e-6, scale=1.0), reads=[pnk], writes=[rik])
                        P.op("dve", lambda e, ri=ri: e.reciprocal(ri[:], ri[:]), reads=[rik], writes=[rik])
                        scl = (128.0 ** -0.5) if kind == 0 else 1.0
                        P.op("dve", lambda e, ac=ac, ri=ri, o_=o_, scl=scl: e.scalar_tensor_tensor(out=o_[:], in0=ac[:], scalar=scl, in1=ri[:], op0=ALU.mult, op1=ALU.mult),
                             reads=[ak, rik], writes=[ok])
                    if kind == 0:
                        P.dma("sp", gqT[hd, :, tok0:tok0 + 512], o_[:], "p1o", reads=[ok], writes=["gqT"])
                    elif kind == 1:
                        P.dma("sp", gkT[hd, :, tok0:tok0 + 512], o_[:], "p1o", reads=[ok], writes=["gkT"])
                    if kind >= 1:
                        pt = psT[(f + th) % 2]; pk = "psb%d" % ((f + th) % 2)
                        ptb = pt[:].bitcast(BF16)
                        for u in range(4):
                            P.op("pe", lambda e, ptb=ptb, u=u, o_=o_: e.transpose(ptb[:, u * 128:(u + 1) * 128], o_[:, u * 128:(u + 1) * 128], ident[:]),
                                 reads=[ok, "ident"], writes=[pk])
                        tm_, tk = tmR.next()
                        P.op("act", lambda e, tm_=tm_, ptb=ptb: e.copy(tm_[:], ptb[:, 0:512]), reads=[pk], writes=[tk])
                        dstt = (gk_tm if kind == 1 else gv_tm)[hd, tok0:tok0 + 512, :].rearrange("(u p) d -> p u d", p=128)
                        P.dma("sp", dstt, tm_[:].rearrange("p (u d) -> p u d", u=4), "p1o", reads=[tk], writes=["gtm"])
                else:
                    o_, ok = obR.next()
                    if f < 32:
                        P.op("act", lambda e, o_=o_, pm=pm: e.activation(o_[:], pm[:, :], AF.Copy, scale=128.0 ** -0.5), reads=[pmk], writes=[ok])
                        P.dma("sp", nqT[f - 24, :, tok0:tok0 + 512], o_[:], "p1o", reads=[ok], writes=["nqT"])
                    else:
                        P.op("act", lambda e, o_=o_, pm=pm: e.copy(o_[:], pm[:, :]), reads=[pmk], writes=[ok])
                        P.dma("sp", kvT[f - 32, :, tok0:tok0 + 512], o_[:], "p1o", reads=[ok], writes=["kvT"])

            load_wf(0); load_wf(1)
            for f in range(N_FM):
                if f + 2 < N_FM:
                    load_wf(f + 2)
                w = wfb[f % 3]; wk = "wf%d" % (f % 3)
                for th in range(2):
                    pm, pmk = psM.next()
                    for kc in range(32):
                        P.op("pe", lambda e, pm=pm, w=w, kc=kc, th=th: e.matmul(pm[:, :], w[:, kc, :], hT[:, kc, th * 512:(th + 1) * 512],
                                                                              start=(kc == 0), stop=(kc == 31)),
                             reads=[wk] + (hTh[th] if kc in (0, 31) else []), writes=[pmk])
                    tok0 = t0s + th * 512
                    post_q.append((f, th, pm, pmk, tok0))
                    if len(post_q) > 1:
                        post_tile(*post_q.pop(0))
            while post_q:
                post_tile(*post_q.pop(0))
            for g2 in range(7):
                wt = wtb[0]
                g = g2 // 2 if g2 < 6 else 3
                hf = g2 % 2
                ncol = 256 if g < 3 else 64
                c0 = hf * 256 if g < 3 else 0
                P.dma("sp", wt[:, :, 0:ncol], w_tm[g][:, :, c0:c0 + ncol], "p1wt", reads=["w_tm%d" % g], writes=["wt"])
                for i in range(8):
                    pm, pmk = psM.next()
                    for kc in range(32):
                        P.op("pe", lambda e, pm=pm, wt=wt, kc=kc, i=i, ncol=ncol: e.matmul(pm[:, 0:ncol], hT[:, kc, i * 128:(i + 1) * 128], wt[:, kc, 0:ncol],
                                                                                        start=(kc == 0), stop=(kc == 31)),
                             reads=["wt"] + (["hT%d_%d" % (i, gg) for gg in range(4)] if kc in (0, 31) else []), writes=[pmk])
                    tok0 = t0s + i * 128
                    if g < 2:
                        o_, ok = obR.next()
                        P.op("act", lambda e, o_=o_, pm=pm: e.activation(o_[:, 0:256], pm[:, 0:256], AF.Silu), reads=[pmk], writes=[ok])
                        P.dma("sp", sz_tm[tok0:tok0 + 128, g * 512 + c0:g * 512 + c0 + 256], o_[:, 0:256], "p1o", reads=[ok], writes=["sz_tm"])
                    elif g == 2:
                        o_, ok = obR.next()
                        P.op("act", lambda e, o_=o_, pm=pm: e.copy(o_[:, 0:256], pm[:, 0:256]), reads=[pmk], writes=[ok])
                        P.dma("sp", vsw_tm[tok0:tok0 + 128, c0:c0 + 256], o_[:, 0:256], "p1o", reads=[ok], writes=["vsw_tm"])
                    else:
                        s_, sk = smR.next()
                        P.op("act", lambda e, s_=s_, pm=pm: e.activation(s_[:, 0:8], pm[:, 0:8], AF.Sigmoid), reads=[pmk], writes=[sk + "a"])
                        P.op("act", lambda e, s_=s_, pm=pm: e.activation(s_[:, 16:40], pm[:, 16:40], AF.Sigmoid), reads=[pmk], writes=[sk + "c"])
                        P.op("dve", lambda e, s_=s_, pm=pm: e.tensor_tensor(s_[:, 8:16], pm[:, 8:16], dtbb[:], op=ALU.add), reads=[pmk, "dtbb"], writes=[sk + "b"])
                        P.op("act", lambda e, s_=s_: e.activation(s_[:, 8:16], s_[:, 8:16], AF.Exp), reads=[sk + "b"], writes=[sk + "b"])
                        P.op("act", lambda e, s_=s_: e.activation(s_[:, 8:16], s_[:, 8:16], AF.Ln, bias=1.0, scale=1.0), reads=[sk + "b"], writes=[sk + "b"])
                        P.op("dve", lambda e, s_=s_: e.tensor_tensor(s_[:, 8:16], s_[:, 8:16], nab[:], op=ALU.mult), reads=[sk + "b", "nab"], writes=[sk + "b"])
                        P.op("pool", lambda e, s_=s_: e.memset(s_[:, 40:64], 0.0), writes=[sk + "d"])
                        P.dma("sp", sm_tm[tok0:tok0 + 128, :], s_[:], "p1o", reads=[sk + "a", sk + "b", sk + "c", sk + "d"], writes=["sm_tm"])
      if start <= 1:
        _ph1()
    P.barrier()
    if nphase < 2:
        P.finish(); P.emit(); return nc

    with ExitStack() as st:
      def _ph2():
        sm_all = P.sb("p2_sm", [128, 32, 64], F32, st)
        triu = P.sb("p2_triu", [128, 128], F32, st)
        dmask = P.sb("p2_dmask", [128, 128], F32, st)
        gnwb = P.sb("p2_gnwb", [128, 128], F32, st)
        mskU = P.sb("p2_mskU", [128, 7, 128], F32, st)
        mskL = P.sb("p2_mskL", [128, 7, 128], F32, st)
        P.dma("sp", mskU[:], t_mskU[:, :, :], "p2c", writes=["mskU"])
        P.dma("sp", mskL[:], t_mskL[:, :, :], "p2c", writes=["mskL"])
        P.dma("sp", sm_all[:], sm_tm.rearrange("(c p) f -> p c f", p=128), "p2c", reads=["sm_tm"], writes=["sm_all"])
        P.dma("sp", triu[:], t_triu[:, :], "p2c", writes=["triu"])
        P.dma("sp", dmask[:], t_dmask[:, :], "p2c", writes=["dmask"])
        P.dma("sp", gnwb[:], gnw[0:1, :].partition_broadcast(128), "p2c", writes=["gnwb"])
        HB = []
        for par in range(2):
            hbuf = {}
            for nm in ("qT", "kT"):
                hbuf[nm] = P.sb("p2_%s%d" % (nm, par), [128, T], BF16, st)
            for nm in ("ktm", "vtm", "sz", "XT", "QKD", "QdT", "Kd"):
                hbuf[nm] = P.sb("p2_%s%d" % (nm, par), [128, 32, 128], BF16, st)
            hbuf["mix"] = P.sb("p2_mix%d" % par, [128, T], BF16, st) if par == 0 else HB[0]["mix"]
            hbuf["sc"] = P.sb("p2_sc%d" % par, [128, 6, 32], F32, st)
            HB.append(hbuf)
        S = P.sb("p2_S", [128, 128], F32, st)
        Sb = P.sb("p2_Sb", [128, 128], BF16, st)
        tA = [dict(dg=P.sb("p2_dg%d" % i, [128, 128], F32, st), dd=P.sb("p2_dd%d" % i, [128, 128], F32, st), DT=P.sb("p2_DT%d" % i, [128, 128], F32, st),
                   Eg=P.sb("p2_Eg%d" % i, [128, 128], F32, st), Mf=P.sb("p2_Mf%d" % i, [128, 128], F32, st),
                   Lf=P.sb("p2_Lf%d" % i, [128, 128], F32, st), Cu=P.sb("p2_Cu%d" % i, [128, 128], F32, st), Cl=P.sb("p2_Cl%d" % i, [128, 128], F32, st),
                   T1=P.sb("p2_T1%d" % i, [128, 128], F32, st), T1p=P.sb("p2_T1p%d" % i, [128, 128], F32, st),
                   U=[P.sb("p2_U%d_%d" % (i, j), [128, 128], F32, st) for j in range(2)],
                   L=[P.sb("p2_L%d_%d" % (i, j), [128, 128], F32, st) for j in range(2)]) for i in range(4)]
        tB = [dict(R=P.sb("p2_R%d" % i, [128, 128], BF16, st), vn=P.sb("p2_vn%d" % i, [128, 128], BF16, st),
                   j1=P.sb("p2_j1%d" % i, [128, 128], F32, st), j2=P.sb("p2_j2%d" % i, [128, 128], F32, st),
                   om=P.sb("p2_om%d" % i, [128, 128], BF16, st), st=P.sb("p2_st%d" % i, [128, 4], F32, st)) for i in range(2)]

        def head_load(hd):
            par = hd % 2; hb_ = HB[par]; k = "h%d" % par
            P.dma("sp", hb_["qT"][:], gqT[hd], "p2l%d" % par, reads=["gqT"], writes=[k + "qT"])
            P.dma("sp", hb_["kT"][:], gkT[hd], "p2l%d" % par, reads=["gkT"], writes=[k + "kT"])
            P.dma("sp", hb_["ktm"][:], gk_tm[hd].rearrange("(c p) d -> p c d", p=128), "p2l%d" % par, reads=["gtm"], writes=[k + "ktm"])
            P.dma("sp", hb_["vtm"][:], gv_tm[hd].rearrange("(c p) d -> p c d", p=128), "p2l%d" % par, reads=["gtm"], writes=[k + "vtm"])
            P.dma("sp", hb_["sz"][:], sz_tm[:, hd * 128:(hd + 1) * 128].rearrange("(c p) d -> p c d", p=128), "p2l%d" % par,
                  reads=["sz_tm"], writes=[k + "sz"])

        def head_pre(hd):
            par = hd % 2; hb_ = HB[par]; k = "h%d" % par
            sc = hb_["sc"]
            g_h = sc[:, 5, :]
            P.op("dve", lambda e: e.tensor_copy(sc[:, 5, :], sm_all[:, :, 8 + hd]), reads=["sm_all"], writes=[k + "gh"])
            P.op("pe", lambda e: e.matmul(psb[0][:, 0:32], triu[:], g_h, start=True, stop=True), reads=[k + "gh", "triu"], writes=["ps0a"])
            P.op("pe", lambda e: e.matmul(psb[0][:, 32:64], onesf[:], g_h, start=True, stop=True), reads=[k + "gh", "onesf"], writes=["ps0b"])
            if gdbg == "pre0":
                return
            P.op("act", lambda e: e.copy(sc[:, 0, :], psb[0][:, 0:32]), reads=["ps0a"], writes=[k + "gam"])
            P.op("dve", lambda e: e.tensor_scalar(sc[:, 1, :], psb[0][:, 0:32], -1.0, None, op0=ALU.mult), reads=["ps0a"], writes=[k + "ngam"])
            P.op("act", lambda e: e.activation(sc[:, 2, :], psb[0][:, 0:32], AF.Exp), reads=["ps0a"], writes=[k + "negeg"])
            P.op("dve", lambda e: e.tensor_scalar(sc[:, 2, :], sc[:, 2, :], -1.0, None, op0=ALU.mult), reads=[k + "negeg"], writes=[k + "negeg"])
            P.op("dve", lambda e: e.tensor_tensor(sc[:, 3, :], psb[0][:, 32:64], sc[:, 0, :], op=ALU.subtract), reads=["ps0b", k + "gam"], writes=[k + "kdec"])
            P.op("act", lambda e: e.activation(sc[:, 3, :], sc[:, 3, :], AF.Exp), reads=[k + "kdec"], writes=[k + "kdec"])
            P.op("act", lambda e: e.activation(sc[:, 4, :], psb[0][:, 32:64], AF.Exp), reads=["ps0b"], writes=[k + "egl"])

        def GA(hd, c):
            par = hd % 2; hb_ = HB[par]; k = "h%d" % par
            pc = c % 4; ta = tA[pc]; a_ = "A%d" % pc
            sc = hb_["sc"]
            cs = slice(c * 128, (c + 1) * 128)
            qTc = hb_["qT"][:, cs]; kTc = hb_["kT"][:, cs]
            bA = 1 + pc; bB = bA
            rgn = [psb[bA][:, i_ * 128:(i_ + 1) * 128] for i_ in range(4)]
            rk_ = ["ps%dr%d" % (bA, i_) for i_ in range(4)]
            pG, pKK, pQK, pLt = rgn
            pGk = rk_[0]; pLk = rk_[3]
            P.op("dve", lambda e: e.tensor_scalar(ta["dg"][:], identf[:], sc[:, 0, c:c + 1], None, op0=ALU.mult), reads=["identf", k + "gam"], writes=[a_ + "dg"])
            yield
            P.op("pe", lambda e: e.matmul(pG, onesf[:], ta["dg"][:], start=True, stop=True), reads=["onesf", a_ + "dg"], writes=[pGk])
            yield
            P.op("dve", lambda e: e.scalar_tensor_tensor(out=ta["dd"][:], in0=pG, scalar=sc[:, 1, c:c + 1], in1=dmask[:], op0=ALU.add, op1=ALU.add),
                 reads=[pGk, k + "ngam", "dmask"], writes=[a_ + "dd"])
            yield
            P.op("act", lambda e: e.activation(ta["DT"][:], ta["dd"][:], AF.Exp), reads=[a_ + "dd"], writes=[a_ + "DT"])
            yield
            P.op("act", lambda e: e.activation(ta["Eg"][:], pG, AF.Exp), reads=[pGk], writes=[a_ + "Eg"])
            yield
            P.op("pool", lambda e: e.tensor_tensor(hb_["QdT"][:, c, :], qTc, ta["Eg"][:], op=ALU.mult), reads=[k + "qT", a_ + "Eg"], writes=[k + "QdT%d" % c])
            yield
            P.op("pool", lambda e: e.tensor_scalar(hb_["Kd"][:, c, :], hb_["ktm"][:, c, :], sc[:, 3, c:c + 1], None, op0=ALU.mult),
                 reads=[k + "ktm", k + "kdec"], writes=[k + "Kd%d" % c])
            yield
            P.op("pe", lambda e: e.matmul(pKK, kTc, kTc, start=True, stop=True), reads=[k + "kT"], writes=[rk_[1]])
            yield
            P.op("pe", lambda e: e.matmul(pQK, kTc, qTc, start=True, stop=True), reads=[k + "kT", k + "qT"], writes=[rk_[2]])
            yield
            P.op("dve", lambda e: e.scalar_tensor_tensor(out=ta["Mf"][:], in0=pKK, scalar=sm_all[:, c, hd:hd + 1], in1=ta["DT"][:], op0=ALU.mult, op1=ALU.mult),
                 reads=[rk_[1], "sm_all", a_ + "DT"], writes=[a_ + "Mf"])
            yield
            P.op("dve", lambda e: e.tensor_tensor(hb_["QKD"][:, c, :], pQK, ta["DT"][:], op=ALU.mult), reads=[rk_[2], a_ + "DT"], writes=[k + "QKD%d" % c])
            yield
            Mf, Lf, Cu, Cl, T1, T1p, U, L = ta["Mf"], ta["Lf"], ta["Cu"], ta["Cl"], ta["T1"], ta["T1p"], ta["U"], ta["L"]
            pT1, pT1p, pT2, pT2p = rgn
            P.op("pe", lambda e: e.matmul(pLt, Mf[:], identf[:], start=True, stop=True), reads=[a_ + "Mf", "identf"], writes=[pLk])
            yield
            P.op("act", lambda e: e.copy(Lf[:], pLt), reads=[pLk], writes=[a_ + "Lf"])
            yield
            P.op("pool", lambda e: e.tensor_tensor(Cu[:], Mf[:], mskU[:, 0, :], op=ALU.mult), reads=[a_ + "Mf", "mskU"], writes=[a_ + "Cu"])
            yield
            P.op("pool", lambda e: e.tensor_tensor(Cl[:], Lf[:], mskL[:, 0, :], op=ALU.mult), reads=[a_ + "Lf", "mskL"], writes=[a_ + "Cl"])
            yield
            P.op("dve", lambda e: e.tensor_tensor(U[0][:], identf[:], Cu[:], op=ALU.subtract), reads=["identf", a_ + "Cu"], writes=[a_ + "U0"])
            yield
            P.op("dve", lambda e: e.tensor_tensor(L[0][:], identf[:], Cl[:], op=ALU.subtract), reads=["identf", a_ + "Cl"], writes=[a_ + "L0"])
            yield
            cur = 0
            for lvl in range(1, 7):
                nx = 1 - cur
                lastl = (lvl == 6)
                P.op("pool", lambda e, lvl=lvl: e.tensor_tensor(Cl[:], Lf[:], mskL[:, lvl, :], op=ALU.mult), reads=[a_ + "Lf", "mskL"], writes=[a_ + "Cl"])
                yield
                P.op("pe", lambda e, cur=cur: e.matmul(pT1, Cl[:], U[cur][:], start=True, stop=True), reads=[a_ + "Cl", a_ + "U%d" % cur], writes=[rk_[0]])
                yield
                P.op("act", lambda e: e.copy(T1[:], pT1), reads=[rk_[0]], writes=[a_ + "T1"])
                yield
                if not lastl:
                    P.op("pool", lambda e, lvl=lvl: e.tensor_tensor(Cu[:], Mf[:], mskU[:, lvl, :], op=ALU.mult), reads=[a_ + "Mf", "mskU"], writes=[a_ + "Cu"])
                    yield
                    P.op("pe", lambda e, cur=cur: e.matmul(pT1p, Cu[:], L[cur][:], start=True, stop=True), reads=[a_ + "Cu", a_ + "L%d" % cur], writes=[rk_[1]])
                    yield
                    P.op("dve", lambda e: e.tensor_copy(T1p[:], pT1p), reads=[rk_[1]], writes=[a_ + "T1p"])
                    yield
                P.op("pe", lambda e, cur=cur: e.matmul(pT2, L[cur][:], T1[:], start=True, stop=True), reads=[a_ + "L%d" % cur, a_ + "T1"], writes=[rk_[2]])
                yield
                if not lastl:
                    P.op("pe", lambda e, cur=cur: e.matmul(pT2p, U[cur][:], T1p[:], start=True, stop=True), reads=[a_ + "U%d" % cur, a_ + "T1p"], writes=[rk_[3]])
                    yield
                    P.op("dve", lambda e, cur=cur, nx=nx: e.tensor_tensor(U[nx][:], U[cur][:], pT2, op=ALU.subtract), reads=[rk_[2], a_ + "U%d" % cur], writes=[a_ + "U%d" % nx])
                    yield
                    P.op("dve", lambda e, cur=cur, nx=nx: e.tensor_tensor(L[nx][:], L[cur][:], pT2p, op=ALU.subtract), reads=[rk_[3], a_ + "L%d" % cur], writes=[a_ + "L%d" % nx])
                    yield
                else:
                    P.op("dve", lambda e, cur=cur: e.tensor_tensor(hb_["XT"][:, c, :], U[cur][:], pT2, op=ALU.subtract), reads=[rk_[2], a_ + "U%d" % cur], writes=[k + "XT%d" % c])
                    yield
                cur = nx

        def GB(hd, c):
            par = hd % 2; hb_ = HB[par]; k = "h%d" % par
            pc = c % 2; tb = tB[pc]; b_ = "B%d" % pc
            sc = hb_["sc"]
            cs = slice(c * 128, (c + 1) * 128)
            kTc = hb_["kT"][:, cs]
            pKS = psb[5][:, 0:128]; pVN = psb[5][:, 128:256]
            pO = psb[6][:, 0:128] if pc == 0 else psb[0][:, 256:384]
            pOk = "ps6_O" if pc == 0 else "ps0_O"
            pD = psb[7][:, 0:128]
            pOt = psb[7][:].bitcast(BF16)[:, 512:640]
            if c == 0:
                P.op("pool", lambda e: e.memset(S[:], 0.0), writes=["S"])
                yield
                P.op("pool", lambda e: e.memset(Sb[:], 0.0), writes=["Sb"])
                yield
            P.op("pe", lambda e: e.matmul(pKS, kTc, Sb[:], start=True, stop=True), reads=[k + "kT", "Sb"], writes=["ps5a"])
            yield
            P.op("dve", lambda e: e.scalar_tensor_tensor(out=tb["R"][:], in0=pKS, scalar=sc[:, 2, c:c + 1], in1=hb_["vtm"][:, c, :], op0=ALU.mult, op1=ALU.add),
                 reads=["ps5a", k + "negeg", k + "vtm"], writes=[b_ + "R"])
            yield
            P.op("pe", lambda e: e.matmul(pVN, hb_["XT"][:, c, :], tb["R"][:], start=True, stop=True), reads=[k + "XT%d" % c, b_ + "R"], writes=["ps5b"])
            yield
            P.op("act", lambda e: e.activation(tb["vn"][:], pVN, AF.Copy, scale=sm_all[:, c, hd:hd + 1]), reads=["ps5b", "sm_all"], writes=[b_ + "vn"])
            yield
            P.op("pe", lambda e: e.matmul(pO, hb_["QdT"][:, c, :], Sb[:], start=True, stop=False), reads=[k + "QdT%d" % c, "Sb"], writes=[pOk])
            yield
            P.op("pe", lambda e: e.matmul(pO, hb_["QKD"][:, c, :], tb["vn"][:], start=False, stop=True), reads=[k + "QKD%d" % c, b_ + "vn"], writes=[pOk])
            yield
            P.op("pe", lambda e: e.matmul(pD, hb_["Kd"][:, c, :], tb["vn"][:], start=True, stop=True), reads=[k + "Kd%d" % c, b_ + "vn"], writes=["ps7d"])
            yield
            P.op("dve", lambda e: e.scalar_tensor_tensor(out=S[:], in0=S[:], scalar=sc[:, 4, c:c + 1], in1=pD, op0=ALU.mult, op1=ALU.add),
                 reads=["S", k + "egl", "ps7d"], writes=["S"])
            yield
            P.op("act", lambda e: e.copy(Sb[:], S[:]), reads=["S"], writes=["Sb"])
            yield
            P.op("act", lambda e: e.activation(tb["j1"][:], pO, AF.Square, accum_out=tb["st"][:, 0:1]), reads=[pOk], writes=[b_ + "j1", b_ + "s0"])
            yield
            P.op("act", lambda e: e.activation(tb["st"][:, 1:2], tb["st"][:, 0:1], AF.Sqrt, bias=1e-6, scale=1.0 / 128.0), reads=[b_ + "s0"], writes=[b_ + "s1"])
            yield
            P.op("dve", lambda e: e.reciprocal(tb["st"][:, 2:3], tb["st"][:, 1:2]), reads=[b_ + "s1"], writes=[b_ + "s2"])
            yield
            P.op("dve", lambda e: e.scalar_tensor_tensor(out=tb["j2"][:], in0=pO, scalar=tb["st"][:, 2:3], in1=gnwb[:], op0=ALU.mult, op1=ALU.mult),
                 reads=[pOk, b_ + "s2", "gnwb"], writes=[b_ + "j2"])
            yield
            P.op("pool", lambda e: e.tensor_tensor(tb["om"][:], tb["j2"][:], hb_["sz"][:, c, :], op=ALU.mult), reads=[b_ + "j2", k + "sz"], writes=[b_ + "om"])
            yield
            P.op("pe", lambda e: e.transpose(pOt, tb["om"][:], ident[:]), reads=[b_ + "om", "ident"], writes=["ps7t"])
            yield
            P.op("act", lambda e: e.copy(hb_["mix"][:, cs], pOt), reads=["ps7t"], writes=["mix%d" % c])
            yield
            if c == 31:
                P.dma("sp", mixT[hd], hb_["mix"][:], "p2o", reads=["mix%d" % cc_ for cc_ in range(32)], writes=["mixT"])
                yield

        head_load(0)
        if gdbg == "load":
            pass
        elif gdbg in ("pre", "pre0", "pre1"):
            head_pre(0)
        elif gdbg == "ga1":
            head_pre(0); list(GA(0, 0))
        elif gdbg == "ga":
            head_pre(0)
            for c in range(32):
                list(GA(0, c))
        elif gdbg == "gb1":
            head_pre(0)
            for c in range(32):
                list(GA(0, c))
            list(GB(0, 0))
        else:
          NH = only_heads
          def dump(name, ap_, shape, dt, reads):
              t_ = nc.dram_tensor(name, list(shape), dt, kind="ExternalOutput").ap()
              P.dma("sp", t_, ap_, "dbgd", reads=reads)
          for hd in range(NH + 1):
            if dbg and gdbg == "dump" and hd == NH:
                hb_ = HB[(NH - 1) % 2]; k = "h%d" % ((NH - 1) % 2)
                allk = [k + "%s%d" % (nm, c_) for nm in ("XT", "QKD", "QdT", "Kd") for c_ in range(32)]
                for nm in ("XT", "QKD", "QdT", "Kd", "ktm", "vtm", "sz"):
                    dump("d_" + nm, hb_[nm][:], [128, 32, 128], BF16, allk + [k + "ktm", k + "vtm", k + "sz"])
                dump("d_sc", hb_["sc"][:], [128, 6, 32], F32, [k + x for x in ("gam", "ngam", "negeg", "kdec", "egl", "gh")])
                dump("d_gamT", hb_["gamT"][:], [32, 128], F32, [k + "gamT"])
                dump("d_kT", hb_["kT"][:], [128, T], BF16, [k + "kT"])
            if hd < NH:
                head_pre(hd)
            for c0 in range(0, 32, 4):
                gens = []
                if hd < NH:
                    cast_step(6 if hd == 0 else 4)
                    gens += [GA(hd, c0 + i_) for i_ in range(4)]
                if hd > 0:
                    def gbchain(h_=hd - 1, c_=c0):
                        for i_ in range(4):
                            yield from GB(h_, c_ + i_)
                    gens.append(gbchain())
                while gens:
                    for g_ in list(gens):
                        try:
                            next(g_)
                        except StopIteration:
                            gens.remove(g_)
            if hd + 1 < NH:
                head_load(hd + 1)
      if start <= 2:
        _ph2()
        cast_step(10 ** 6)
    P.barrier()
    if nphase < 3:
        P.finish(); P.emit(); return nc

    rg = [[0, 1], [2, 3], [4, 5], [6, 7]]

    def exch(js):
        if start > 3 or nphase < 4:
            return
        for j in js:
            P.cc(lambda e, j=j: e.collective_compute("AllGather", ALU.bypass, replica_groups=rg, ins=[mixT[j]], outs=[mixG[j]]),
                 "ccx", reads=["mixT"], writes=["mixG%d" % j])

    if start <= 3:
        exch(range(0, 8))

    with ExitStack() as st:
      def _ph3():
        sm_all = P.sb("p3_sm", [128, 32, 64], F32, st)
        winT = P.sb("p3_winT", [128, 8, 512], BF16, st)
        cmT = P.sb("p3_cmT", [128, 2, T], BF16, st)
        selE = P.sb("p3_selE", [64, 32, 128], BF16, st)
        frc = P.sb("p3_frc", [128, 32, 64], F32, st)
        kbt = P.sb("p3_kb", [128, 8, 32], F32, st)
        cbt = P.sb("p3_cb", [128, 8, 2, 8], F32, st)
        nslt = P.sb("p3_nsl", [1, 8], F32, st)
        trow = P.sb("p3_trow", [1, 512], F32, st)
        zer = P.sb("p3_zer", [128, 512], BF16, st)
        w1b = [P.sb("p3_w1%d" % i, [128, 32, 128], BF16, st) for i in range(2)]
        w2b = [P.sb("p3_w2%d" % i, [128, 128], BF16, st) for i in range(2)]
        posf = [P.sb("p3_pos%d" % i, [32, 128], F32, st) for i in range(2)]
        posT = [P.sb("p3_posT%d" % i, [128, 32], BF16, st) for i in range(2)]
        cvec = [P.sb("p3_cvec%d" % i, [128, 1], F32, st) for i in range(2)]
        kT4 = [P.sb("p3_kT%d" % i, [128, T], BF16, st) for i in range(4)]
        kcD = P.sb("p3_kcD", [128, 16, 256], BF16, st)
        hcm = P.sb("p3_hcm", [128, 256], BF16, st)
        kcmpT = P.sb("p3_kcmpT", [128, 256], BF16, st)
        VC = P.sb("p3_VC", [128, 2, 193], BF16, st)
        VS = P.sb("p3_VS", [128, 32, 129], BF16, st)
        VW = P.sb("p3_VW", [128, 32, 129], BF16, st)
        negK = P.sb("p3_negK", [1, 4], F32, st)
        kmx = P.sb("p3_kmx", [1, 32], F32, st)
        sqt = [P.sb("p3_sq%d" % i, [128, 512], BF16, st) for i in range(2)]
        qsb = [P.sb("p3_q%d" % i, [128, 4, 512], BF16, st) for i in range(2)]
        qn = P.sb("p3_qn", [1, 512], F32, st)
        srow = P.sb("p3_srow", [1, 512], F32, st)
        rrow = P.sb("p3_rrow", [1, 3, 512], BF16, st)
        PT = [P.sb("p3_PT%d" % i, [128, 512], BF16, st) for i in range(3)]
        oacc = P.sb("p3_oacc", [128, 4, 4, 128], F32, st)
        imp = P.sb("p3_imp", [128, 4, 64], F32, st)
        impp = P.sb("p3_impp", [128, 64], F32, st)
        impq = P.sb("p3_impq", [128, 64], F32, st)
        mx8 = P.sb("p3_mx8", [128, 16], F32, st)
        nsel = P.sb("p3_nsel", [128, 64], BF16, st)
        negselT = P.sb("p3_nselT", [64, 512], BF16, st)
        stt = P.sb("p3_stt", [128, 16], F32, st)
        ofin = P.sb("p3_ofin", [128, 128], BF16, st)
        mstage = [P.sb("p3_ms%d" % i, [128, 512], BF16, st) for i in range(2)]
        pcp = P.sb("p3_pcp", [128, 4, 386], F32, st)
        P.dma("sp", sm_all[:], sm_tm.rearrange("(c p) f -> p c f", p=128), "x", reads=["sm_tm"], writes=["sm_all"])
        for nm_, t_, src_ in (("winT", winT, t_winT), ("cmT", cmT, t_cmT), ("selE", selE, t_selE), ("frc", frc, t_frc),
                              ("kbt", kbt, t_kb), ("cbt", cbt, t_cb), ("nslt", nslt, t_nsl), ("trow", trow, t_trow)):
            P.dma("sp", t_[:], src_, "x", writes=[nm_])
        P.op("pool", lambda e: e.memset(zer[:], 0.0), writes=["zer"])
        for i, (w1_, w2_, pos_) in enumerate(((w1_k, w2_k, pos_k), (w1_v, w2_v, pos_v))):
            P.dma("pool", w1b[i][:], w1_.rearrange("(j d) o -> d j o", d=128), "x", writes=["w1b%d" % i])
            P.dma("pool", w2b[i][:], w2_[:, :], "x", writes=["w2b%d" % i])
            P.dma("sp", posf[i][:], pos_[:, :], "x", writes=["posf%d" % i])
            P.op("pe", lambda e, i=i: e.matmul(psb[6][:, 0:32], posf[i][:], identf[0:32, 0:32], start=True, stop=True), reads=["posf%d" % i, "identf"], writes=["ps6"])
            P.op("act", lambda e, i=i: e.copy(posT[i][:], psb[6][:, 0:32]), reads=["ps6"], writes=["posT%d" % i])
            for j in range(32):
                P.op("pe", lambda e, i=i, j=j: e.matmul(psb[6][:, 64:65], w1b[i][:, j, :], posT[i][:, j:j + 1], start=(j == 0), stop=(j == 31)),
                     reads=["w1b%d" % i, "posT%d" % i], writes=["ps6"])
            P.op("act", lambda e, i=i: e.copy(cvec[i][:], psb[6][:, 64:65]), reads=["ps6"], writes=["cvec%d" % i])

        psS = Rot([(psb[0], "ps0"), (psb[1], "ps1")])
        PTR = Rot([(PT[0], "PT0"), (PT[1], "PT1"), (PT[2], "PT2")])
        sqR = Rot([(sqt[0], "sq0"), (sqt[1], "sq1")])

        def zero_bank(b):
            P.op("pe", lambda e, b=b: e.matmul(psb[b][:, :], zer[:, 0:128], zer[:, :], start=True, stop=True, skip_group_check=True),
                 reads=["zer"], writes=["ps%d" % b])

        def do_group(gl):
            gk = "g"
            for i in range(4):
                P.dma("sp", kT4[i][:], kvT[2 * i + gl], "x", reads=["kvT"], writes=["kT4_%d" % i])
            P.dma("sp", VS[:, :, 0:128], vsw_tm[:, gl * 128:(gl + 1) * 128].rearrange("(c p) d -> p c d", p=128), "x", reads=["vsw_tm"], writes=["VSd"])
            P.dma("sp", VW[:, :, 0:128], vsw_tm[:, 256 + gl * 128:256 + (gl + 1) * 128].rearrange("(c p) d -> p c d", p=128), "x", reads=["vsw_tm"], writes=["VWd"])
            P.op("pool", lambda e: e.memset(VS[:, :, 128:129], 1.0), writes=["VS1"])
            P.op("pool", lambda e: e.memset(VW[:, :, 128:129], 1.0), writes=["VW1"])
            for i in range(2):
                src = kT4[i]
                P.op("dve", lambda e, src=src: e.tensor_copy(kcD[:], src[:].rearrange("p (n r) -> p r n", r=16)), reads=["kT4_%d" % i], writes=["kcD"])
                for j in range(32):
                    P.op("pe", lambda e, i=i, j=j: e.matmul(psb[6][:, 0:255], w1b[i][:, j, :], kcD[:, j % 16, j // 16:j // 16 + 255], start=(j == 0), stop=(j == 31)),
                         reads=["w1b%d" % i, "kcD"], writes=["ps6"])
                P.op("pool", lambda e: e.memset(hcm[:, 255:256], 0.0), writes=["hcm1"])
                P.op("act", lambda e, i=i: e.activation(hcm[:, 0:255], psb[6][:, 0:255], AF.Silu, bias=cvec[i][:, 0:1], scale=1.0), reads=["ps6", "cvec%d" % i], writes=["hcm"])
                if i == 0:
                    P.op("pe", lambda e: e.matmul(psb[7][:, 0:256], w2b[0][:], hcm[:], start=True, stop=True), reads=["w2b0", "hcm", "hcm1"], writes=["ps7"])
                    P.op("act", lambda e: e.copy(kcmpT[:], psb[7][:, 0:256]), reads=["ps7"], writes=["kcmpT"])
                else:
                    for nc_ in range(2):
                        P.op("pe", lambda e, nc_=nc_: e.matmul(psb[7][:, nc_ * 128:(nc_ + 1) * 128], hcm[:, nc_ * 128:(nc_ + 1) * 128], w2b[1][:], start=True, stop=True),
                             reads=["w2b1", "hcm", "hcm1"], writes=["ps7"])
                    P.op("act", lambda e: e.copy(VC[:, :, 0:128], psb[7][:, 0:256].rearrange("p (c d) -> p c d", c=2)), reads=["ps7"], writes=["VCd"])
                    P.op("pool", lambda e: e.memset(VC[:, :, 128:129], 1.0), writes=["VC1"])
                    P.dma("sp", VC[:, :, 129:193], t_ovl[:, :, :], "x", writes=["VCo"])
            for br, (src, ncols) in enumerate(((kcmpT, 256), (kT4[2], T), (kT4[3], T))):
                nch = max(1, ncols // 512)
                for cc_ in range(nch):
                    w_ = min(512, ncols)
                    s_, sk = sqR.next()
                    P.op("pool", lambda e, s_=s_, src=src, cc_=cc_, w_=w_: e.tensor_tensor(s_[:, 0:w_], src[:, cc_ * 512:cc_ * 512 + w_], src[:, cc_ * 512:cc_ * 512 + w_], op=ALU.mult),
                         reads=["kcmpT", "kT4_2", "kT4_3"], writes=[sk])
                    P.op("pe", lambda e, s_=s_, w_=w_: e.matmul(psb[6][0:1, 0:w_], onesb[:, 0:1], s_[:, 0:w_], start=True, stop=True), reads=[sk, "onesb"], writes=["ps6"])
                    P.op("dve", lambda e, br=br, cc_=cc_, w_=w_: e.reduce_max(kmx[:, br * 8 + cc_:br * 8 + cc_ + 1], psb[6][0:1, 0:w_], axis=AX.X), reads=["ps6"], writes=["kmx"])
                P.op("dve", lambda e, br=br, nch=nch: e.reduce_max(negK[:, br:br + 1], kmx[:, br * 8:br * 8 + nch], axis=AX.X), reads=["kmx"], writes=["negK%d" % br])
                P.op("act", lambda e, br=br: e.activation(negK[:, br:br + 1], negK[:, br:br + 1], AF.Sqrt), reads=["negK%d" % br], writes=["negK%d" % br])
                P.op("dve", lambda e, br=br: e.tensor_scalar(negK[:, br:br + 1], negK[:, br:br + 1], -1.0, None, op0=ALU.mult), reads=["negK%d" % br], writes=["negK%d" % br])

            def do_G(G):
                q_ = qsb[G % 2]; qk_ = "q%d" % (G % 2)
                for r in range(4):
                    P.dma("sp", q_[:, r, :], nqT[gl * 4 + r][:, G * 512:(G + 1) * 512], "x", reads=["nqT"], writes=[qk_ + "_%d" % r])
                P.op("pool", lambda e: e.memset(imp[:], 0.0), writes=["imp"])

                def make_rrow(r):
                    hr = gl * 4 + r
                    s_, sk = sqR.next()
                    P.op("pool", lambda e, s_=s_: e.tensor_tensor(s_[:], q_[:, r, :], q_[:, r, :], op=ALU.mult), reads=[qk_ + "_%d" % r], writes=[sk])
                    P.op("pe", lambda e, s_=s_: e.matmul(psb[6][0:1, :], onesb[:, 0:1], s_[:], start=True, stop=True), reads=[sk, "onesb"], writes=["ps6"])
                    P.op("act", lambda e: e.activation(qn[:], psb[6][0:1, :], AF.Sqrt), reads=["ps6"], writes=["qn"])
                    P.op("dve", lambda e: e.tensor_scalar(srow[:], trow[:], nslt[0:1, hr:hr + 1], None, op0=ALU.mult), reads=["trow", "nslt"], writes=["srow"])
                    for br in range(3):
                        P.op("dve", lambda e, br=br: e.scalar_tensor_tensor(out=rrow[:, br, :], in0=qn[:], scalar=negK[0:1, br:br + 1], in1=srow[:], op0=ALU.mult, op1=ALU.add),
                             reads=["qn", "negK%d" % br, "srow"], writes=["rrow%d" % br])

                def scores(kTsrc, kkey, kc, r, br, extra):
                    ps_, psk = psS.next()
                    P.op("pe", lambda e: e.matmul(ps_[:, :], kTsrc[:, kc * 128:(kc + 1) * 128], q_[:, r, :], start=True, stop=False),
                         reads=[kkey, qk_ + "_%d" % r], writes=[psk])
                    P.op("pe", lambda e: e.matmul(ps_[:, :], onesb[0:1, :], rrow[0:1, br, :], start=False, stop=(len(extra) == 0)),
                         reads=["onesb", "rrow%d" % br], writes=[psk])
                    for ei, (l_, r_, rk_) in enumerate(extra):
                        P.op("pe", lambda e, l_=l_, r_=r_, ei=ei: e.matmul(ps_[:, :], l_, r_, start=False, stop=(ei == len(extra) - 1)), reads=rk_, writes=[psk])
                    return ps_, psk

                def pass1(r):
                    hr = gl * 4 + r
                    make_rrow(r)
                    zero_bank(2); zero_bank(3)
                    ncs = (0, 1) if G >= 4 else (0,)
                    pend = [scores(kcmpT, "kcmpT", ncs[0], r, 0, [(ident[:], cmT[:, ncs[0], G * 512:(G + 1) * 512], ["ident", "cmT"])])]
                    for ni, nc_ in enumerate(ncs):
                        if ni + 1 < len(ncs):
                            n2 = ncs[ni + 1]
                            pend.append(scores(kcmpT, "kcmpT", n2, r, 0, [(ident[:], cmT[:, n2, G * 512:(G + 1) * 512], ["ident", "cmT"])]))
                        ps_, psk = pend.pop(0)
                        pt_, ptk = PTR.next()
                        P.op("act", lambda e, ps_=ps_, pt_=pt_, nc_=nc_: e.activation(pt_[:], ps_[:, :], AF.Exp, bias=cbt[:, hr, nc_, G:G + 1], scale=1.0), reads=[psk, "cbt"], writes=[ptk])
                        for m in range(4):
                            b_ = 2 + m // 2; o_ = (m % 2) * 193
                            P.op("pe", lambda e, pt_=pt_, m=m, b_=b_, o_=o_, nc_=nc_: e.matmul(psb[b_][:, o_:o_ + 193], pt_[:, m * 128:(m + 1) * 128], VC[:, nc_, :], start=False, stop=(nc_ == ncs[-1]), skip_group_check=True),
                                 reads=[ptk, "VCd", "VC1", "VCo"], writes=["ps%d" % b_])
                    P.op("act", lambda e: e.copy(pcp[:, 0, 0:386], psb[2][:, 0:386]), reads=["ps2"], writes=["pcp0"])
                    P.op("dve", lambda e: e.tensor_copy(pcp[:, 1, 0:386], psb[3][:, 0:386]), reads=["ps3"], writes=["pcp1"])
                    for m in range(4):
                        b_ = 2 + m // 2; o_ = (m % 2) * 193; qt = G * 4 + m
                        po = pcp[:, b_ - 2, :]
                        P.op("dve", lambda e, po=po, o_=o_: e.tensor_scalar(stt[:, 0:1], po[:, o_ + 128:o_ + 129], 1e-30, None, op0=ALU.max), reads=["pcp%d" % (b_ - 2)], writes=["stt0"])
                        P.op("dve", lambda e: e.reciprocal(stt[:, 1:2], stt[:, 0:1]), reads=["stt0"], writes=["stt1"])
                        P.op("dve", lambda e, po=po, o_=o_, m=m: e.scalar_tensor_tensor(out=imp[:, m, :], in0=po[:, o_ + 129:o_ + 193], scalar=stt[:, 1:2], in1=imp[:, m, :], op0=ALU.mult, op1=ALU.add),
                             reads=["pcp%d" % (b_ - 2), "stt1", "imp"], writes=["imp"])
                        P.op("dve", lambda e, qt=qt, hr=hr: e.tensor_tensor(stt[:, 2:3], stt[:, 1:2], sm_all[:, qt, 16 + hr * 3:16 + hr * 3 + 1], op=ALU.mult), reads=["stt1", "sm_all"], writes=["stt2"])
                        P.op("act", lambda e, po=po, o_=o_, r=r, m=m: e.activation(oacc[:, r, m, :], po[:, o_:o_ + 128], AF.Copy, scale=stt[:, 2:3]), reads=["pcp%d" % (b_ - 2), "stt2"], writes=["oacc%d_%d" % (r, m)])
                def select(m):
                    qt = G * 4 + m
                    P.op("dve", lambda e, m=m, qt=qt: e.tensor_tensor(impp[:], imp[:, m, :], frc[:, qt, :], op=ALU.add), reads=["imp", "frc"], writes=["impp"])
                    P.op("dve", lambda e: e.max(out=mx8[:, 0:8], in_=impp[:]), reads=["impp"], writes=["mx8a"])
                    P.op("dve", lambda e: e.match_replace(out=impq[:], in_to_replace=mx8[:, 0:8], in_values=impp[:], imm_value=-3e38), reads=["impp", "mx8a"], writes=["impq"])
                    P.op("dve", lambda e: e.max(out=mx8[:, 8:16], in_=impq[:]), reads=["impq"], writes=["mx8b"])
                    P.op("dve", lambda e: e.tensor_scalar(impq[:], impp[:], mx8[:, 15:16], None, op0=ALU.is_ge), reads=["impp", "mx8b"], writes=["impq"])
                    P.op("dve", lambda e: e.tensor_scalar(nsel[:], impq[:], 1.0, -NEG, op0=ALU.subtract, op1=ALU.mult), reads=["impq"], writes=["nsel"])
                    P.op("pe", lambda e: e.transpose(psb[7][:].bitcast(BF16)[0:64, 0:128], nsel[:], ident[:]), reads=["nsel", "ident"], writes=["ps7"])
                    P.op("act", lambda e, m=m: e.copy(negselT[:, m * 128:(m + 1) * 128], psb[7][:].bitcast(BF16)[0:64, 0:128]), reads=["ps7"], writes=["nselT%d" % m])
                def pass2(r):
                    hr = gl * 4 + r
                    make_rrow(r)
                    for b_ in (2, 3, 4, 5):
                        zero_bank(b_)
                    last = 4 * G + 3
                    def sel_scores(kc):
                        extra = [(selE[:, kc, :], negselT[:, :], ["selE"] + ["nselT%d" % m for m in range(4)])]
                        if kc >= 4 * G:
                            extra.append((ident[:], winT[:, 4 + kc - 4 * G, :], ["ident", "winT"]))
                        return scores(kT4[2], "kT4_2", kc, r, 1, extra)

                    def win_scores(kc):
                        return scores(kT4[3], "kT4_3", kc, r, 2, [(ident[:], winT[:, 4 + kc - 4 * G, :], ["ident", "winT"])])

                    wlist = list(range(max(0, 4 * G - 4), last + 1))
                    pend = [sel_scores(0)]
                    for kc in range(0, last + 1):
                        if kc + 1 <= last:
                            pend.append(sel_scores(kc + 1))
                        else:
                            pend.append(win_scores(wlist[0]))
                        ps_, psk = pend.pop(0)
                        pt_, ptk = PTR.next()
                        rel_ = kc - 4 * G + 28
                        P.op("act", lambda e, ps_=ps_, pt_=pt_, rel_=rel_: e.activation(pt_[:], ps_[:, :], AF.Exp, bias=kbt[:, hr, rel_:rel_ + 1], scale=1.0), reads=[psk, "kbt"], writes=[ptk])
                        for m in range(4):
                            b_ = 2 + m // 2; o_ = (m % 2) * 129
                            P.op("pe", lambda e, pt_=pt_, m=m, b_=b_, o_=o_, kc=kc: e.matmul(psb[b_][:, o_:o_ + 129], pt_[:, m * 128:(m + 1) * 128], VS[:, kc, :], start=False, stop=(kc == last), skip_group_check=True),
                                 reads=[ptk, "VSd", "VS1"], writes=["ps%d" % b_])
                    for wi, kc in enumerate(wlist):
                        if wi + 1 < len(wlist):
                            pend.append(win_scores(wlist[wi + 1]))
                        ps_, psk = pend.pop(0)
                        pt_, ptk = PTR.next()
                        rel_ = kc - 4 * G + 28
                        P.op("act", lambda e, ps_=ps_, pt_=pt_, rel_=rel_: e.activation(pt_[:], ps_[:, :], AF.Exp, bias=kbt[:, hr, rel_:rel_ + 1], scale=1.0), reads=[psk, "kbt"], writes=[ptk])
                        for m in range(4):
                            b_ = 4 + m // 2; o_ = (m % 2) * 129
                            P.op("pe", lambda e, pt_=pt_, m=m, b_=b_, o_=o_, kc=kc: e.matmul(psb[b_][:, o_:o_ + 129], pt_[:, m * 128:(m + 1) * 128], VW[:, kc, :], start=False, stop=(kc == last), skip_group_check=True),
                                 reads=[ptk, "VWd", "VW1"], writes=["ps%d" % b_])
                    ms_ = mstage[(G * 4 + r) % 2]; msk = "ms%d" % ((G * 4 + r) % 2)
                    for bb in (2, 3, 4, 5):
                        if bb % 2 == 0:
                            P.op("act", lambda e, bb=bb: e.copy(pcp[:, bb - 2, 0:258], psb[bb][:, 0:258]), reads=["ps%d" % bb], writes=["pcp%d" % (bb - 2)])
                        else:
                            P.op("dve", lambda e, bb=bb: e.tensor_copy(pcp[:, bb - 2, 0:258], psb[bb][:, 0:258]), reads=["ps%d" % bb], writes=["pcp%d" % (bb - 2)])
                    for m in range(4):
                        qt = G * 4 + m
                        oa = oacc[:, r, m, :]; oak = "oacc%d_%d" % (r, m)
                        for bi, (bb, gcol) in enumerate(((2 + m // 2, 1), (4 + m // 2, 2))):
                            po = pcp[:, bb - 2, :]; o_ = (m % 2) * 129
                            P.op("dve", lambda e, po=po, o_=o_, bi=bi: e.reciprocal(stt[:, 4 + bi:5 + bi], po[:, o_ + 128:o_ + 129]), reads=["pcp%d" % (bb - 2)], writes=["stt%d" % (4 + bi)])
                            P.op("dve", lambda e, bi=bi, qt=qt, gcol=gcol: e.tensor_tensor(stt[:, 6 + bi:7 + bi], stt[:, 4 + bi:5 + bi], sm_all[:, qt, 16 + hr * 3 + gcol:16 + hr * 3 + gcol + 1], op=ALU.mult),
                                 reads=["stt%d" % (4 + bi), "sm_all"], writes=["stt%d" % (6 + bi)])
                            P.op("dve", lambda e, po=po, o_=o_, bi=bi, oa=oa: e.scalar_tensor_tensor(out=oa, in0=po[:, o_:o_ + 128], scalar=stt[:, 6 + bi:7 + bi], in1=oa, op0=ALU.mult, op1=ALU.add),
                                 reads=["pcp%d" % (bb - 2), "stt%d" % (6 + bi), oak], writes=[oak])
                        P.op("act", lambda e, oa=oa: e.copy(ofin[:], oa), reads=[oak], writes=["ofin"])
                        P.op("pe", lambda e: e.transpose(psb[7][:].bitcast(BF16)[:, 256:384], ofin[:], ident[:]), reads=["ofin", "ident"], writes=["ps7"])
                        P.op("act", lambda e, ms_=ms_, m=m: e.copy(ms_[:, m * 128:(m + 1) * 128], psb[7][:].bitcast(BF16)[:, 256:384]), reads=["ps7"], writes=[msk + "_%d" % m])
                    P.dma("sp", mixT[8 + hr][:, G * 512:(G + 1) * 512], ms_[:], "x", reads=[msk + "_%d" % m for m in range(4)], writes=["mixT"])
                for r in range(4):
                    pass1(r)
                for m in range(4):
                    select(m)
                for r in range(4):
                    pass2(r)

            for G in range(8):
                do_G(G)

        for gl in range(2):
            do_group(gl)
            if gl == 0:
                exch(range(8, 12))
      if start <= 3:
        _ph3()
    P.barrier()
    if nphase < 4:
        P.finish(); P.emit(); return nc

    if start <= 3:
        exch(range(12, 16))
        P.barrier()

    def _ph4():
        def gsrc(kc):
            if kc < 16:
                r_, j_ = kc // 8, kc % 8
            else:
                r_, j_ = (kc - 16) // 8, 8 + (kc - 16) % 8
            return mixG[j_][r_ * 128:(r_ + 1) * 128, :]

        selt = P.sb("p4_sel", [128, 2], F32)
        rst = P.sb("p4_rst", [128, 4, 4], F32)
        P.dma("sp", selt[:], selv[:, :], "x", writes=["selt"])
        for tb in range(4):
            tok0 = tb * 512
            with ExitStack() as st:
                mixsel = P.sb("p4_mixsel", [128, 32, 512], BF16, st)
                mA = [P.sb("p4_mA%d" % i, [128, 4, 512], BF16, st) for i in range(2)]
                mB = [P.sb("p4_mB%d" % i, [128, 4, 512], BF16, st) for i in range(2)]
                wo = [P.sb("p4_wo%d" % i, [128, 32, 512], BF16, st) for i in range(2)]
                g1b = P.sb("p4_g1b", [128, D], F32, st)
                xp = [P.sb("p4_xp%d" % i, [128, 512], F32, st) for i in range(3)]
                yp = [P.sb("p4_yp%d" % i, [128, 512], F32, st) for i in range(3)]
                jk = P.sb("p4_jk", [128, 512], BF16, st)
                ss1 = P.sb("p4_ss1", [128, 4, 8], F32, st)
                P.dma("sp", g1b[:], modv[2:3, :].partition_broadcast(128), "x", reads=["modv"], writes=["g1b"])
                for q4 in range(8):
                    a_ = mA[q4 % 2]; b_ = mB[q4 % 2]
                    for u in range(4):
                        kc = q4 * 4 + u
                        P.dma("sp", a_[:, u, :], gsrc(kc)[:, tok0:tok0 + 512], "x", reads=["mixG"], writes=["mA%d" % (q4 % 2)])
                        P.dma("sp", b_[:, u, :], gsrc(kc)[:, 2048 + tok0:2048 + tok0 + 512], "x", reads=["mixG"], writes=["mB%d" % (q4 % 2)])
                    dst = mixsel[:, q4 * 4:(q4 + 1) * 4, :]
                    P.op("dve", lambda e, a_=a_, dst=dst: e.tensor_scalar(dst, a_[:], selt[:, 0:1], None, op0=ALU.mult), reads=["mA%d" % (q4 % 2), "selt"], writes=["mixsel%d" % q4])
                    P.op("dve", lambda e, b_=b_, dst=dst: e.scalar_tensor_tensor(out=dst, in0=b_[:], scalar=selt[:, 1:2], in1=dst, op0=ALU.mult, op1=ALU.add),
                         reads=["mB%d" % (q4 % 2), "selt", "mixsel%d" % q4], writes=["mixsel%d" % q4])
                msk_all = ["mixsel%d" % q4 for q4 in range(8)]
                P.dma("sp", wo[0][:], w_out_b[0], "x", reads=["w_out_b0"], writes=["wo0"])
                it = 0
                for n in range(8):
                    if n + 1 < 8:
                        P.dma("sp", wo[(n + 1) % 2][:], w_out_b[n + 1], "x", reads=["w_out_b%d" % (n + 1)], writes=["wo%d" % ((n + 1) % 2)])
                    w_ = wo[n % 2]; wk = "wo%d" % (n % 2)
                    for m in range(4):
                        b = 2 + (it % 4); it += 1
                        x_ = xp[it % 3]; xk = "xp%d" % (it % 3); y_ = yp[it % 3]; yk = "yp%d" % (it % 3)
                        P.dma("sp", x_[:], x_h[tok0 + m * 128:tok0 + (m + 1) * 128, n * 512:(n + 1) * 512], "x", writes=[xk])
                        for kc in range(32):
                            P.op("pe", lambda e, b=b, w_=w_, kc=kc, m=m: e.matmul(psb[b][:, :], mixsel[:, kc, m * 128:(m + 1) * 128], w_[:, kc, :], start=(kc == 0), stop=(kc == 31)),
                                 reads=[wk] + (msk_all if kc in (0, 31) else []), writes=["ps%d" % b])
                        P.op("dve", lambda e, b=b, y_=y_, n=n: e.tensor_tensor(y_[:], psb[b][:, :], g1b[:, n * 512:(n + 1) * 512], op=ALU.mult), reads=["ps%d" % b, "g1b"], writes=[yk])
                        P.op("pool", lambda e, y_=y_, x_=x_: e.tensor_tensor(y_[:], y_[:], x_[:], op=ALU.add), reads=[yk, xk], writes=[yk])
                        P.op("act", lambda e, y_=y_, m=m, n=n: e.activation(jk[:], y_[:], AF.Square, accum_out=ss1[:, m, n:n + 1]), reads=[yk], writes=["jk", "ss1_%d_%d" % (m, n)])
                        P.dma("sp", x1s[tok0 + m * 128:tok0 + (m + 1) * 128, n * 512:(n + 1) * 512], y_[:], "x", reads=[yk], writes=["x1s"])
                for m in range(4):
                    P.op("dve", lambda e, m=m: e.reduce_sum(rst[:, m, 0:1], ss1[:, m, :], axis=AX.X), reads=["ss1_%d_%d" % (m, n) for n in range(8)], writes=["rst%d" % m])
                    P.op("act", lambda e, m=m: e.activation(rst[:, m, 1:2], rst[:, m, 0:1], AF.Sqrt, bias=1e-6, scale=1.0 / D), reads=["rst%d" % m], writes=["rst%d" % m])
                    P.op("dve", lambda e, m=m: e.reciprocal(rst[:, m, 2:3], rst[:, m, 1:2]), reads=["rst%d" % m], writes=["rst%d" % m])
            P.barrier()
            with ExitStack() as st, ExitStack() as sth:
                hidT = P.sb("p4_hidT", [128, 128, 512], BF16, st)
                h2T = P.sb("p4_h2T", [128, 32, 512], BF16, sth)
                with ExitStack() as st2:
                    w2b = P.sb("p4_w2b", [128, 2048], F32, st2)
                    sh2b = P.sb("p4_sh2b", [128, 2048], F32, st2)
                    xr = P.sb("p4_xr", [128, D], F32, st2)
                    hb2 = P.sb("p4_hb2", [128, D], BF16, st2)
                    for m in range(4):
                        P.dma("sp", xr[:], x1s[tok0 + m * 128:tok0 + (m + 1) * 128, :], "x", reads=["x1s"], writes=["xr"])
                        for hf in range(2):
                            cs_ = slice(hf * 2048, (hf + 1) * 2048)
                            P.dma("sp", w2b[:], modv[3:4, cs_].partition_broadcast(128), "x", reads=["modv"], writes=["w2b"])
                            P.dma("sp", sh2b[:], modv[4:5, cs_].partition_broadcast(128), "x", reads=["modv"], writes=["sh2b"])
                            P.op("dve", lambda e, m=m, cs_=cs_: e.scalar_tensor_tensor(out=xr[:, cs_], in0=xr[:, cs_], scalar=rst[:, m, 2:3], in1=w2b[:], op0=ALU.mult, op1=ALU.mult),
                                 reads=["xr", "rst%d" % m, "w2b"], writes=["xr"])
                            P.op("pool", lambda e, cs_=cs_: e.tensor_tensor(hb2[:, cs_], xr[:, cs_], sh2b[:], op=ALU.add), reads=["xr", "sh2b"], writes=["hb2"])
                        for g in range(4):
                            pk = "ps%d" % (g % 2)
                            ptb = psb[g % 2][:].bitcast(BF16)
                            for u in range(8):
                                kc = g * 8 + u
                                P.op("pe", lambda e, ptb=ptb, u=u, kc=kc: e.transpose(ptb[:, u * 128:(u + 1) * 128], hb2[:, kc * 128:(kc + 1) * 128], ident[:]),
                                     reads=["hb2", "ident"], writes=[pk])
                            dstap = h2T[:, g * 8:(g + 1) * 8, m * 128:(m + 1) * 128]
                            srcap = ptb[:, 0:1024].rearrange("p (u t) -> p u t", u=8)
                            P.op("act", lambda e, d=dstap, s_=srcap: e.copy(d, s_), reads=[pk], writes=["h2T%d_%d" % (m, g)])
                P.barrier()
                st3 = ExitStack()
                wu = [P.sb("p4_wu%d" % i, [128, 32, 128], BF16, st3) for i in range(3)]
                rl = [P.sb("p4_rl%d" % i, [128, 512], F32, st3) for i in range(2)]
                P.dma("sp", wu[0][:], w_up_b[0], "x", reads=["w_up_b0"], writes=["wu0"])
                P.dma("sp", wu[1][:], w_up_b[1], "x", reads=["w_up_b1"], writes=["wu1"])
                for fc in range(128):
                    if fc + 2 < 128:
                        P.dma("sp", wu[(fc + 2) % 3][:], w_up_b[fc + 2], "x", reads=["w_up_b%d" % (fc + 2)], writes=["wu%d" % ((fc + 2) % 3)])
                    w_ = wu[fc % 3]; wk = "wu%d" % (fc % 3)
                    b = 2 + fc % 4
                    for kc in range(32):
                        P.op("pe", lambda e, b=b, w_=w_, kc=kc: e.matmul(psb[b][:, :], w_[:, kc, :], h2T[:, kc, :], start=(kc == 0), stop=(kc == 31)),
                             reads=[wk, "h2T"], writes=["ps%d" % b])
                    r_ = rl[fc % 2]; rk = "rl%d" % (fc % 2)
                    P.op("act", lambda e, b=b, r_=r_: e.activation(r_[:], psb[b][:, :], AF.Relu), reads=["ps%d" % b], writes=[rk])
                    P.op("dve", lambda e, r_=r_, fc=fc: e.tensor_tensor(hidT[:, fc, :], r_[:], r_[:], op=ALU.mult), reads=[rk], writes=["hidT"])
                P.barrier()
                st3.close()
                sth.close()
                wd = [P.sb("p4_wd%d" % i, [128, 8, 512], BF16, st) for i in range(3)]
                g2b = P.sb("p4_g2b", [128, D], F32, st)
                xp2 = [P.sb("p4_xq%d" % i, [128, 512], F32, st) for i in range(3)]
                yp2 = [P.sb("p4_yq%d" % i, [128, 512], F32, st) for i in range(3)]
                jk2 = P.sb("p4_jk2", [128, 512], BF16, st)
                ss2 = P.sb("p4_ss2", [128, 4, 8], F32, st)
                P.dma("sp", g2b[:], modv[5:6, :].partition_broadcast(128), "x", reads=["modv"], writes=["g2b"])
                seq = [(n, fg) for n in range(8) for fg in range(16)]
                for i_ in range(2):
                    n_, fg_ = seq[i_]
                    P.dma("sp", wd[i_ % 3][:], w_dn_b[n_, fg_], "x", reads=["w_dn_b%d_%d" % (n_, fg_)], writes=["wd%d" % (i_ % 3)])
                it = 0
                for si, (n, fg) in enumerate(seq):
                    if si + 2 < len(seq):
                        n_, fg_ = seq[si + 2]
                        P.dma("sp", wd[(si + 2) % 3][:], w_dn_b[n_, fg_], "x", reads=["w_dn_b%d_%d" % (n_, fg_)], writes=["wd%d" % ((si + 2) % 3)])
                    w_ = wd[si % 3]; wk = "wd%d" % (si % 3)
                    for m in range(4):
                        b = 2 + m
                        for j in range(8):
                            P.op("pe", lambda e, b=b, w_=w_, j=j, m=m, fg=fg: e.matmul(psb[b][:, :], hidT[:, fg * 8 + j, m * 128:(m + 1) * 128], w_[:, j, :],
                                                                                  start=(fg == 0 and j == 0), stop=(fg == 15 and j == 7)),
                                 reads=[wk, "hidT"], writes=["ps%d" % b])
                    if fg == 15:
                        for m in range(4):
                            b = 2 + m; it += 1
                            x_ = xp2[it % 3]; xk = "xq%d" % (it % 3); y_ = yp2[it % 3]; yk = "yq%d" % (it % 3)
                            rows = slice(tok0 + m * 128, tok0 + (m + 1) * 128); cols = slice(n * 512, (n + 1) * 512)
                            P.dma("sp", x_[:], x1s[rows, cols], "x", reads=["x1s"], writes=[xk])
                            P.op("dve", lambda e, b=b, y_=y_, n=n: e.tensor_tensor(y_[:], psb[b][:, :], g2b[:, n * 512:(n + 1) * 512], op=ALU.mult), reads=["ps%d" % b, "g2b"], writes=[yk])
                            P.op("pool", lambda e, y_=y_, x_=x_: e.tensor_tensor(y_[:], y_[:], x_[:], op=ALU.add), reads=[yk, xk], writes=[yk])
                            P.op("act", lambda e, y_=y_, m=m, n=n: e.activation(jk2[:], y_[:], AF.Square, accum_out=ss2[:, m, n:n + 1]), reads=[yk], writes=["jk2", "ss2_%d_%d" % (m, n)])
                            P.dma("sp", x1s[rows, cols], y_[:], "x", reads=[yk, "x1s"], writes=["x1s"])
                for m in range(4):
                    P.op("dve", lambda e, m=m: e.reduce_sum(rst[:, m, 0:1], ss2[:, m, :], axis=AX.X), reads=["ss2_%d_%d" % (m, n) for n in range(8)], writes=["rst%d" % m])
                    P.op("act", lambda e, m=m: e.activation(rst[:, m, 1:2], rst[:, m, 0:1], AF.Sqrt, bias=1e-6, scale=1.0 / D), reads=["rst%d" % m], writes=["rst%d" % m])
                    P.op("dve", lambda e, m=m: e.reciprocal(rst[:, m, 2:3], rst[:, m, 1:2]), reads=["rst%d" % m], writes=["rst%d" % m])
            P.barrier()
            with ExitStack() as st:
                fnb = P.sb("p4_fnb", [128, D], F32, st)
                xo = [P.sb("p4_xo%d" % i, [128, D], F32, st) for i in range(2)]
                P.dma("sp", fnb[:], modv[6:7, :].partition_broadcast(128), "x", reads=["modv"], writes=["fnb"])
                for m in range(4):
                    x_ = xo[m % 2]; xk = "xo%d" % (m % 2)
                    rows = slice(tok0 + m * 128, tok0 + (m + 1) * 128)
                    P.dma("sp", x_[:], x1s[rows, :], "x", reads=["x1s"], writes=[xk])
                    P.op("dve", lambda e, x_=x_, m=m: e.scalar_tensor_tensor(out=x_[:], in0=x_[:], scalar=rst[:, m, 2:3], in1=fnb[:], op0=ALU.mult, op1=ALU.mult),
                         reads=[xk, "rst%d" % m, "fnb"], writes=[xk])
                    P.dma("sp", out_h[rows, :], x_[:], "x", reads=[xk], writes=["out_h"])
            P.barrier()

    _ph4()
    P.finish()
    P.emit()
    return nc


def core_inputs(inp, b, hh, consts):
    f32 = np.float32
    d = {}
    d["x_b"] = np.ascontiguousarray(inp["x"][b])
    d["x_h"] = np.ascontiguousarray(inp["x"][b, hh * 2048:(hh + 1) * 2048])
    d["cT"] = np.ascontiguousarray(inp["c"][b].reshape(32, 128).T)
    d["ada_w"] = inp["ada_w"][0]
    d["ada_b"] = inp["ada_b"][0][None, :]
    d["n1w"] = inp["norm1_w"][0][None, :]
    d["n2w"] = inp["norm2_w"][0][None, :]
    d["fnw"] = inp["final_norm_w"][None, :]
    cols = w_in_cols(hh)
    wc = np.zeros((D, W_IN_COLS), f32)
    wc[:, :cols.size] = inp["w_in"][0][:, cols]
    d["w_in_c"] = wc
    cwv = inp["gdn_conv_w"][0]
    cw = np.zeros((128, 24, 4), f32)
    for f in range(24):
        kind, hd = f // 8, f % 8
        ch = kind * 2048 + (8 * hh + hd) * 128 + np.arange(128)
        cw[:, f, :] = cwv[:, ch].T
    d["convw"] = cw.reshape(128, 96)
    d["alog"] = inp["gdn_a_log"][0][None, 8 * hh:8 * hh + 8].astype(f32)
    d["dtb"] = inp["gdn_dt_bias"][0][None, 8 * hh:8 * hh + 8].astype(f32)
    d["gnw"] = inp["gdn_norm_w"][0][None, :]
    for s in ("k", "v"):
        d["pos_" + s] = inp["cmp_pos_" + s][0]
        d["w1_" + s] = inp["cmp_w1_" + s][0]
        d["w2_" + s] = inp["cmp_w2_" + s][0]
    d["w_out"] = inp["w_out"][0]
    d["w_up"] = inp["w_up"][0]
    d["w_down"] = inp["w_down"][0]
    sv = np.zeros((128, 2), f32)
    sv[:, hh] = 1.0
    d["selv"] = sv
    for k, v in consts.items():
        d["t_" + k] = v
    kb, cb, nsl = alibi_tables(hh)
    d["t_kb"] = kb
    d["t_cb"] = cb
    d["t_nsl"] = nsl
    return {k: np.ascontiguousarray(v) for k, v in d.items()}


def kernel(**inputs):
    inp = {k: np.asarray(v) for k, v in inputs.items()}
    consts = const_tables()
    nc = build_program()
    in_maps = [core_inputs(inp, cid // 2, cid % 2, consts) for cid in range(8)]
    res = run_bass_kernel_spmd(nc, in_maps, core_ids=list(range(8)))
    out = np.zeros((4, T, D), np.float32)
    for cid in range(8):
        b, hh = cid // 2, cid % 2
        out[b, hh * 2048:(hh + 1) * 2048] = res.results[cid]["out_h"]
    return out
```

```python
import numpy as np
import ml_dtypes
from contextlib import ExitStack
import concourse.bass as bass
import concourse.mybir as mybir
from concourse.bass_utils import run_bass_kernel_spmd

F32 = mybir.dt.float32
BF16 = mybir.dt.bfloat16
I32 = mybir.dt.int32
ALU = mybir.AluOpType
AF = mybir.ActivationFunctionType
AX = mybir.AxisListType

ENGS = ("pe", "act", "dve", "pool", "sp")

T = 4096
D = 4096
DFF = 16384
NEG = -30000.0


class Prog:
    def __init__(self, nc):
        self.nc = nc
        self.ops = {e: [] for e in ENGS}
        self.nops = {e: 0 for e in ENGS}
        self.dcount = {}
        self.res = {}
        self.waited = {}
        self.awaited = {e: set() for e in ENGS}
        self.bankacc = {}
        self.dring = {}
        self.bgsem = {}
        self.keep = set()
        self.stack = ExitStack()

    def sb(self, name, shape, dt, stack=None):
        self.uid = getattr(self, "uid", 0) + 1
        name = "%s_u%d" % (name, self.uid)
        return (stack or self.stack).enter_context(self.nc.sbuf_tensor(name, list(shape), dt))

    def ps(self, name, shape, dt=F32, stack=None):
        return (stack or self.stack).enter_context(self.nc.psum_tensor(name, list(shape), dt))

    def _deps(self, eng, reads, writes):
        deps = {}
        for r in reads:
            ent = self.res.get(r)
            if ent is not None and ent[0] is not None:
                k, i = ent[0]
                if deps.get(k, -1) < i:
                    deps[k] = i
        for w in writes:
            ent = self.res.get(w)
            if ent is not None:
                if ent[0] is not None:
                    k, i = ent[0]
                    if deps.get(k, -1) < i:
                        deps[k] = i
                for k, i in ent[1].items():
                    if deps.get(k, -1) < i:
                        deps[k] = i
        waits = []
        for k, i in deps.items():
            if k == eng and eng == "pe":
                continue
            if self.waited.get((eng, k), -1) >= i:
                continue
            self.waited[(eng, k)] = i
            waits.append((k, i))
            if k in self.awaited:
                self.awaited[k].add(i)
        return waits

    def _update(self, tok, reads, writes):
        k, i = tok
        for w in writes:
            self.res[w] = [tok, {}]
        for r in reads:
            ent = self.res.get(r)
            if ent is None:
                ent = self.res[r] = [None, {}]
            if ent[1].get(k, -1) < i:
                ent[1][k] = i

    @staticmethod
    def _bank(key):
        if key.startswith("psb"):
            return int(key[3])
        if key.startswith("ps"):
            return int(key[2])
        return None

    def _bank_deps(self, eng, reads, writes, waits):
        banks = set()
        for k_ in list(reads) + list(writes):
            b = self._bank(k_)
            if b is not None:
                banks.add(b)
        for b in banks:
            acc = self.bankacc.setdefault(b, {})
            for k, i in acc.items():
                if k == eng:
                    continue
                if self.waited.get((eng, k), -1) >= i:
                    continue
                self.waited[(eng, k)] = i
                waits.append((k, i))
                self.awaited[k].add(i)
        return banks

    def op(self, eng, fn, reads=(), writes=()):
        waits = self._deps(eng, reads, writes)
        banks = self._bank_deps(eng, reads, writes, waits)
        for b in banks:
            self.bankacc[b][eng] = self.nops[eng] + 1
        self.nops[eng] += 1
        tok = (eng, self.nops[eng])
        self.ops[eng].append([waits, fn, "c", self.nops[eng]])
        self._update(tok, reads, writes)
        return tok

    NRING = {"sp": 40, "pool": 16, "act": 8}

    def dma(self, q, out, in_, sem, reads=(), writes=(), bg=False, **kw):
        waits = self._deps(q, reads, writes)
        n = self.dring.get(q, 0)
        self.dring[q] = n + 1
        sem = "%s_r%d" % (q, n % self.NRING[q])
        prev = self.dcount.get(sem, 0)
        if prev and self.waited.get((q, sem), -1) < prev:
            self.waited[(q, sem)] = prev
            waits.append((sem, prev))
        self.dcount[sem] = prev + 16
        self.bgsem[sem] = bg
        if bg:
            self.keep.update(writes)
        tok = (sem, self.dcount[sem])
        self.ops[q].append([waits, lambda e: e.dma_start(out=out, in_=in_, **kw), "d", sem])
        self._update(tok, reads, writes)
        return tok

    def cc(self, fn, sem, reads=(), writes=()):
        waits = self._deps("pool", reads, writes)
        self.dcount[sem] = self.dcount.get(sem, 0) + 1
        tok = (sem, self.dcount[sem])
        self.ops["pool"].append([waits, fn, "k", sem])
        self._update(tok, reads, writes)
        return tok

    def barrier(self):
        for e in ENGS:
            waits = []
            for k in ENGS:
                if k == e or self.nops[k] == 0:
                    continue
                i = self.nops[k]
                if self.waited.get((e, k), -1) >= i:
                    continue
                self.waited[(e, k)] = i
                self.awaited[k].add(i)
                waits.append((k, i))
            for k, c in self.dcount.items():
                if self.bgsem.get(k):
                    continue
                if self.waited.get((e, k), -1) >= c:
                    continue
                self.waited[(e, k)] = c
                waits.append((k, c))
            if waits:
                self.ops[e].append([waits, None, "w", None])
        self.res = {k: v for k, v in self.res.items() if k in self.keep}

    def finish(self):
        waits = [(k, c) for k, c in self.dcount.items()]
        self.ops["sp"].append([waits, None, "w", None])

    def emit(self):
        nc = self.nc
        sems = {}
        for e in ENGS:
            if self.nops[e]:
                sems[e] = self.stack.enter_context(nc.semaphore("s_" + e))
        for k in self.dcount:
            sems[k] = self.stack.enter_context(nc.semaphore("d_" + k))
        vmap = {}
        for e in ENGS:
            aw = sorted(self.awaited[e])
            vmap[e] = {idx: n + 1 for n, idx in enumerate(aw)}

        def val(k, i):
            return vmap[k][i] if k in vmap else i

        def run(e, engobj):
            for waits, fn, kind, extra in self.ops[e]:
                for k, i in waits:
                    engobj.wait_ge(sems[k], val(k, i))
                if kind == "w":
                    continue
                ins = fn(engobj)
                if kind == "c":
                    if extra in vmap[e]:
                        ins.then_inc(sems[e], 1)
                elif kind == "d":
                    ins.then_inc(sems[extra], 16)
                else:
                    ins.then_inc(sems[extra], 1)

        with nc.Block() as block:
            @block.tensor
            def _(t):
                run("pe", t)

            @block.scalar
            def _(t):
                run("act", t)

            @block.vector
            def _(t):
                run("dve", t)

            @block.gpsimd
            def _(t):
                run("pool", t)

            @block.sync
            def _(t):
                run("sp", t)


class Rot:
    def __init__(self, items):
        self.items = items
        self.i = 0

    def next(self):
        it = self.items[self.i % len(self.items)]
        self.i += 1
        return it


def const_tables():
    c = {}
    p = np.arange(128)
    q = np.arange(512)
    win = np.zeros((128, 8, 512), np.float32)
    for j in range(8):
        dist = q[None, :] - (128 * (j - 4) + p[:, None])
        win[:, j, :] = np.where((dist >= 0) & (dist < 512), 0.0, NEG)
    c["winT"] = win.astype(ml_dtypes.bfloat16)
    n = (np.arange(2)[None, :, None] * 128 + p[:, None, None])
    t = np.arange(T)[None, None, :]
    c["cmT"] = np.where((t >= 16 * n + 31) & (n < 255), 0.0, NEG).astype(ml_dtypes.bfloat16)
    s = np.arange(64)[:, None, None]
    kc = np.arange(32)[None, :, None]
    m = np.arange(128)[None, None, :]
    c["selE"] = (s == 2 * kc + m // 64).astype(np.float32).astype(ml_dtypes.bfloat16)
    n = (np.arange(2)[None, :, None] * 128 + p[:, None, None])
    sb = np.arange(64)[None, None, :]
    ov = ((16 * n <= 64 * sb + 63) & (16 * n + 31 >= 64 * sb) & (n < 255)).astype(np.float32)
    c["ovl"] = ov.astype(ml_dtypes.bfloat16)
    tt = (np.arange(32)[None, :, None] * 128 + p[:, None, None])
    cur = tt // 64
    blk = np.arange(64)[None, None, :]
    frc = np.zeros((128, 32, 64), np.float32)
    forced = (blk == 0) | ((cur - blk) < 2)
    frc = np.where(forced, 1e30 * (1.0 + 0.25 * (blk % 4)), frc)
    frc = np.where(blk <= cur, frc, -1e30)
    c["frc"] = frc.astype(np.float32)
    c["triu"] = (p[:, None] <= p[None, :]).astype(np.float32)
    c["dmask"] = np.where(p[None, :] >= p[:, None], 0.0, NEG).astype(np.float32)
    oh = (np.arange(32)[:, None, None] == np.arange(32)[None, :, None]).astype(np.float32)
    c["oneh"] = np.broadcast_to(oh, (32, 32, 128)).copy().astype(np.float32)
    c["trow"] = np.arange(512, dtype=np.float32)[None, :]
    a_ = p[:, None]; b_ = p[None, :]
    mu = np.zeros((128, 7, 128), np.float32)
    for l in range(7):
        sz_ = 2 ** l
        mu[:, l, :] = ((a_ // (2 * sz_) == b_ // (2 * sz_)) & ((a_ % (2 * sz_)) < sz_) & ((b_ % (2 * sz_)) >= sz_)).astype(np.float32)
    c["mskU"] = mu
    c["mskL"] = np.ascontiguousarray(mu.transpose(2, 1, 0))
    return c


def alibi_tables(hh):
    slopes = 2.0 ** (-8.0 * np.arange(1, 17, dtype=np.float64) / 16.0)
    p = np.arange(128, dtype=np.float64)
    hs = slopes[8 * hh:8 * hh + 8]
    rel = np.arange(32, dtype=np.float64)
    kb = hs[None, :, None] * (128.0 * (rel[None, None, :] - 28.0) + p[:, None, None])
    ncx = np.arange(2, dtype=np.float64)
    G = np.arange(8, dtype=np.float64)
    cb = hs[None, :, None, None] * (16.0 * (ncx[None, None, :, None] * 128 + p[:, None, None, None]) + 31.0
                                    - 512.0 * G[None, None, None, :])
    nsl = -hs[None, :]
    return kb.astype(np.float32), cb.astype(np.float32), nsl.astype(np.float32)


def w_in_cols(hh):
    GDN_DK, NSA_DQ, DKV = 2048, 2048, 512
    sizes = (GDN_DK, GDN_DK, GDN_DK, GDN_DK, 16, 16, NSA_DQ, DKV, DKV, DKV, DKV, DKV, DKV, 48)
    off = np.concatenate([[0], np.cumsum(sizes)])
    gq, gk, gv, gz, gb, ga, nq, nkc, nvc, nks, nvs, nkw, nvw, ngate = [int(o) for o in off[:-1]]
    h8 = np.arange(1024) + 1024 * hh
    g2 = np.arange(256) + 256 * hh
    cols = []
    for base in (gq, gk, gv):
        cols.append(base + h8)
    cols.append(nq + h8)
    for base in (nkc, nvc, nks, nkw):
        cols.append(base + g2)
    cols.append(gz + h8)
    cols.append(nvs + g2)
    cols.append(nvw + g2)
    cols.append(gb + 8 * hh + np.arange(8))
    cols.append(ga + 8 * hh + np.arange(8))
    cols.append(ngate + 24 * hh + np.arange(24))
    return np.concatenate(cols)


N_FM = 40
W_IN_COLS = 40 * 128 + 3 * 512 + 64


def build_program(dbg=None, nphase=99, start=0, only_heads=8, gdn_chunks=32, gdbg=None, lite=False):
    nc = bass.Bass("TRN2", target_bir_lowering=False)
    P = Prog(nc)
    ck = "ExternalOutput" if dbg else "Internal"

    need = {"x_b": (1,), "ada_w": (0,), "w_in_c": (0, 1), "w_out": (4,), "w_up": (4,), "w_down": (4,), "x_h": (4,),
            "w1_k": (3,), "w1_v": (3,)}

    def din(name, shape, dt=F32):
        if lite and name in need and not any(start <= ph_ <= nphase - 0 for ph_ in need[name]):
            shape = [1, 1]
        return nc.dram_tensor(name, list(shape), dt, kind="ExternalInput").ap()

    def dscr(name, shape, dt, dbgout=False, ph=99, last=99):
        if ph < start <= last:
            return nc.dram_tensor(name, list(shape), dt, kind="ExternalInput").ap()
        return nc.dram_tensor(name, list(shape), dt, kind=("ExternalOutput" if (dbg and dbgout) else "Internal")).ap()

    x_b = din("x_b", [T, D]); x_h = din("x_h", [2048, D]); cT = din("cT", [128, 32])
    ada_w = din("ada_w", [D, 6 * D]); ada_b = din("ada_b", [1, 6 * D])
    n1w = din("n1w", [1, D]); n2w = din("n2w", [1, D]); fnw = din("fnw", [1, D])
    w_in_c = din("w_in_c", [D, W_IN_COLS])
    convw = din("convw", [128, 24 * 4]); alog = din("alog", [1, 8]); dtb = din("dtb", [1, 8]); gnw = din("gnw", [1, 128])
    pos_k = din("pos_k", [32, 128]); w1_k = din("w1_k", [4096, 128]); w2_k = din("w2_k", [128, 128])
    pos_v = din("pos_v", [32, 128]); w1_v = din("w1_v", [4096, 128]); w2_v = din("w2_v", [128, 128])
    w_out = din("w_out", [D, D]); w_up = din("w_up", [D, DFF]); w_down = din("w_down", [DFF, D])
    selv = din("selv", [128, 2])
    t_winT = din("t_winT", [128, 8, 512], BF16); t_cmT = din("t_cmT", [128, 2, T], BF16)
    t_selE = din("t_selE", [64, 32, 128], BF16); t_ovl = din("t_ovl", [128, 2, 64], BF16)
    t_frc = din("t_frc", [128, 32, 64]); t_triu = din("t_triu", [128, 128]); t_dmask = din("t_dmask", [128, 128])
    t_oneh = din("t_oneh", [32, 32, 128]); t_trow = din("t_trow", [1, 512])
    t_mskU = din("t_mskU", [128, 7, 128]); t_mskL = din("t_mskL", [128, 7, 128])
    t_kb = din("t_kb", [128, 8, 32]); t_cb = din("t_cb", [128, 8, 2, 8]); t_nsl = din("t_nsl", [1, 8])

    out_h = nc.dram_tensor("out_h", [2048, D], F32, kind="ExternalOutput").ap()

    modv = dscr("modv", [8, D], F32, dbgout=True, ph=0)
    w_fm = dscr("w_fm", [N_FM, 128, 32, 128], BF16)
    w_tm = dscr("w_tm", [4, 128, 32, 512], BF16)
    gqT = dscr("gqT", [8, 128, T], BF16, True, ph=1, last=2); gkT = dscr("gkT", [8, 128, T], BF16, True, ph=1, last=2)
    gk_tm = dscr("gk_tm", [8, T, 128], BF16, True, ph=1, last=2); gv_tm = dscr("gv_tm", [8, T, 128], BF16, True, ph=1, last=2)
    sz_tm = dscr("sz_tm", [T, 1024], BF16, True, ph=1, last=2)
    nqT = dscr("nqT", [8, 128, T], BF16, True, ph=1, last=3)
    kvT = dscr("kvT", [8, 128, T], BF16, True, ph=1, last=3)
    vsw_tm = dscr("vsw_tm", [T, 512], BF16, True, ph=1, last=3)
    sm_tm = dscr("sm_tm", [T, 64], F32, True, ph=1, last=3)
    mixT = dscr("mixT", [16, 128, T], BF16, True)
    mixG = dscr("mixG", [16, 2 * 128, T], BF16, ph=3)
    w_out_b = dscr("w_out_b", [8, 128, 32, 512], BF16)
    w_up_b = dscr("w_up_b", [128, 128, 32, 128], BF16)
    w_dn_b = dscr("w_dn_b", [8, 16, 128, 8, 512], BF16)
    x1s = dscr("x1s", [2048, D], F32, True)

    ident = P.sb("ident", [128, 128], BF16)
    identf = P.sb("identf", [128, 128], F32)
    onesb = P.sb("onesb", [128, 128], BF16)
    onesf = P.sb("onesf", [128, 128], F32)
    P.op("pool", lambda e: e.memset(ident[:], 1.0), writes=["ident"])
    P.op("pool", lambda e: e.affine_select(ident[:], ident[:], pattern=[[-1, 128]], compare_op=ALU.is_equal,
                                           fill=0.0, base=0, channel_multiplier=1), reads=["ident"], writes=["ident"])
    P.op("pool", lambda e: e.memset(identf[:], 1.0), writes=["identf"])
    P.op("pool", lambda e: e.affine_select(identf[:], identf[:], pattern=[[-1, 128]], compare_op=ALU.is_equal,
                                           fill=0.0, base=0, channel_multiplier=1), reads=["identf"], writes=["identf"])
    P.op("pool", lambda e: e.memset(onesb[:], 1.0), writes=["onesb"])
    P.op("pool", lambda e: e.memset(onesf[:], 1.0), writes=["onesf"])

    psb = [P.ps("psb%d" % i, [128, 512], F32) for i in range(8)]

    for f in range(N_FM if start <= 1 else 0):
        P.dma("pool", w_fm[f], w_in_c[:, f * 128:(f + 1) * 128].rearrange("(kc p) f -> p kc f", p=128), "cvt",
              writes=["w_fm%d" % f])
    for g in range(3 if start <= 1 else 0):
        o = N_FM * 128 + g * 512
        P.dma("pool", w_tm[g], w_in_c[:, o:o + 512].rearrange("(kc p) f -> p kc f", p=128), "cvt", writes=["w_tm%d" % g])
    o = N_FM * 128 + 3 * 512
    if start <= 1:
      P.dma("pool", w_tm[3][:, :, 0:64], w_in_c[:, o:o + 64].rearrange("(kc p) f -> p kc f", p=128), "cvt", writes=["w_tm3"])
    cast_q = []
    if nphase >= 4:
        for n in range(8):
            cast_q.append((w_out_b[n], w_out[:, n * 512:(n + 1) * 512].rearrange("(kc p) f -> p kc f", p=128), "w_out_b%d" % n))
        for fg in range(16):
            src = w_up[:, fg * 1024:(fg + 1) * 1024].rearrange("(kc p) (c f) -> p c kc f", p=128, f=128)
            for cc_ in range(8):
                cast_q.append((w_up_b[fg * 8 + cc_], src[:, cc_], "w_up_b%d" % (fg * 8 + cc_)))
        for n in range(8):
            for fg in range(16):
                src = w_down[fg * 1024:(fg + 1) * 1024, n * 512:(n + 1) * 512].rearrange("(j p) f -> p j f", p=128)
                cast_q.append((w_dn_b[n, fg], src, "w_dn_b%d_%d" % (n, fg)))

    def cast_step(nmax):
        for _ in range(nmax):
            if not cast_q:
                return
            o_, i_, k_ = cast_q.pop(0)
            P.dma("pool", o_, i_, "cvt", writes=[k_], bg=True)

    if start > 2:
        cast_step(10 ** 6)

    with ExitStack() as st:
      def _ph0():
        sT = P.sb("p0_sT", [128, 32], F32, st)
        awt = [P.sb("p0_aw%d" % i, [128, 32, 512], F32, st) for i in range(2)]
        row = [P.sb("p0_row%d" % i, [1, 512], F32, st) for i in range(2)]
        bro = [P.sb("p0_bro%d" % i, [1, 512], F32, st) for i in range(2)]
        nro = [P.sb("p0_nro%d" % i, [1, 512], F32, st) for i in range(2)]
        P.dma("sp", sT[:], cT[:, :], "p0c", writes=["sT"])
        P.op("act", lambda e: e.activation(sT[:], sT[:], AF.Silu), reads=["sT"], writes=["sT"])
        P.dma("sp", modv[6:7, :], fnw[:, :], "p0s", writes=["modv"])
        NB = 48
        for nb in range(NB):
            a = awt[nb % 2]
            kind = nb // 8
            cs = (nb % 8) * 512
            P.dma("sp", a[:], ada_w[:, nb * 512:(nb + 1) * 512].rearrange("(kc p) f -> p kc f", p=128), "p0w%d" % (nb % 2),
                  writes=["aw%d" % (nb % 2)])
            P.dma("sp", bro[nb % 2][:], ada_b[:, nb * 512:(nb + 1) * 512], "p0b%d" % (nb % 2), writes=["bro%d" % (nb % 2)])
            if kind in (1, 4):
                P.dma("sp", nro[nb % 2][:], (n1w if kind == 1 else n2w)[:, cs:cs + 512], "p0n%d" % (nb % 2), writes=["nro%d" % (nb % 2)])
            pb = psb[nb % 2]
            for kc in range(32):
                P.op("pe", lambda e, a=a, pb=pb, kc=kc: e.matmul(pb[0:1, :], sT[:, kc:kc + 1], a[:, kc, :], start=(kc == 0), stop=(kc == 31)),
                     reads=["sT", "aw%d" % (nb % 2)], writes=["ps%d" % (nb % 2)])
            r = row[nb % 2]
            br_ = bro[nb % 2]; nr_ = nro[nb % 2]
            P.op("dve", lambda e, r=r, pb=pb, br_=br_: e.tensor_tensor(r[:], pb[0:1, :], br_[:], op=ALU.add),
                 reads=["ps%d" % (nb % 2), "bro%d" % (nb % 2)], writes=["row%d" % (nb % 2)])
            if kind in (1, 4):
                P.op("dve", lambda e, r=r, nr_=nr_: e.scalar_tensor_tensor(out=r[:], in0=r[:], scalar=1.0, in1=nr_[:], op0=ALU.add, op1=ALU.mult),
                     reads=["row%d" % (nb % 2), "nro%d" % (nb % 2)], writes=["row%d" % (nb % 2)])
            dst = {0: 1, 1: 0, 2: 2, 3: 4, 4: 3, 5: 5}[kind]
            P.dma("sp", modv[dst:dst + 1, cs:cs + 512], r[:], "p0s", reads=["row%d" % (nb % 2)], writes=["modv"])
      if start <= 0:
        _ph0()
    P.barrier()
    if nphase < 1:
        P.finish(); P.emit(); return nc

    with ExitStack() as st:
      def _ph1():
        w1b = P.sb("p1_w1b", [128, D], F32, st)
        sh1b = P.sb("p1_sh1b", [128, D], F32, st)
        hT = P.sb("p1_hT", [128, 32, 1024], BF16, st)
        xt = [P.sb("p1_x%d" % i, [128, D], F32, st) for i in range(2)]
        hb = P.sb("p1_hb", [128, D], BF16, st)
        wfb = [P.sb("p1_wf%d" % i, [128, 32, 128], BF16, st) for i in range(3)]
        wtb = [P.sb("p1_wt%d" % i, [128, 32, 256], BF16, st) for i in range(1)]
        cw = P.sb("p1_cw", [128, 96], F32, st)
        halo = P.sb("p1_halo", [128, 24, 3], F32, st)
        raw = [P.sb("p1_raw%d" % i, [128, 515], F32, st) for i in range(2)]
        acc = [P.sb("p1_acc%d" % i, [128, 512], F32, st) for i in range(2)]
        sq = [P.sb("p1_sq%d" % i, [128, 512], BF16, st) for i in range(2)]
        rin = [P.sb("p1_rin%d" % i, [128, 512], F32, st) for i in range(2)]
        ob = [P.sb("p1_ob%d" % i, [128, 512], BF16, st) for i in range(3)]
        tmb = [P.sb("p1_tm%d" % i, [128, 512], BF16, st) for i in range(2)]
        smf = [P.sb("p1_smf%d" % i, [128, 64], F32, st) for i in range(2)]
        stat = P.sb("p1_stat", [128, 8], F32, st)
        dtbb = P.sb("p1_dtbb", [128, 8], F32, st)
        nab = P.sb("p1_nab", [128, 8], F32, st)
        P.dma("sp", w1b[:], modv[0:1, :].partition_broadcast(128), "p1c", reads=["modv"], writes=["w1b"])
        P.dma("sp", sh1b[:], modv[1:2, :].partition_broadcast(128), "p1c", reads=["modv"], writes=["sh1b"])
        P.dma("sp", cw[:], convw[:, :], "p1c", writes=["cw"])
        P.dma("sp", dtbb[:], dtb[0:1, :].partition_broadcast(128), "p1c", writes=["dtbb"])
        P.dma("sp", nab[:], alog[0:1, :].partition_broadcast(128), "p1c", writes=["nab"])
        P.op("act", lambda e: e.activation(nab[:], nab[:], AF.Exp), reads=["nab"], writes=["nab"])
        P.op("dve", lambda e: e.tensor_scalar(nab[:], nab[:], -1.0, None, op0=ALU.mult), reads=["nab"], writes=["nab"])
        P.op("pool", lambda e: e.memset(halo[:], 0.0), writes=["halo"])
        psT = [psb[0], psb[1]]
        psM = Rot([(psb[2], "psb2"), (psb[3], "psb3"), (psb[4], "psb4"), (psb[5], "psb5")])
        psN = Rot([(psb[6], "psb6"), (psb[7], "psb7")])
        rawR = Rot(list(zip(raw, ["raw0", "raw1"]))); accR = Rot(list(zip(acc, ["acc0", "acc1"])))
        sqR = Rot(list(zip(sq, ["sq0", "sq1"]))); rinR = Rot(list(zip(rin, ["rin0", "rin1"])))
        obR = Rot(list(zip(ob, ["ob0", "ob1", "ob2"]))); tmR = Rot(list(zip(tmb, ["tm0", "tm1"])))
        smR = Rot(list(zip(smf, ["smf0", "smf1"])))
        evq = Rot(["act", "dve"])

        def load_x(sbk, i):
            j = (sbk * 8 + i)
            P.dma("sp", xt[j % 2][:], x_b[j * 128:(j + 1) * 128, :], "p1x%d" % (j % 2), writes=["xt%d" % (j % 2)])

        load_x(0, 0)
        for sbk in range(4):
            t0s = sbk * 1024
            for i in range(8):
                j = sbk * 8 + i
                if j + 1 < 32:
                    load_x((j + 1) // 8, (j + 1) % 8)
                x = xt[j % 2]; xk = "xt%d" % (j % 2)
                P.op("act", lambda e, x=x: e.activation(hb[:], x[:], AF.Square, accum_out=stat[:, 0:1]), reads=[xk], writes=["hb", "st0"])
                P.op("act", lambda e: e.activation(stat[:, 1:2], stat[:, 0:1], AF.Sqrt, bias=1e-6, scale=1.0 / D), reads=["st0"], writes=["st1"])
                P.op("dve", lambda e: e.reciprocal(stat[:, 2:3], stat[:, 1:2]), reads=["st1"], writes=["st2"])
                P.op("dve", lambda e, x=x: e.scalar_tensor_tensor(out=x[:], in0=x[:], scalar=stat[:, 2:3], in1=w1b[:], op0=ALU.mult, op1=ALU.mult),
                     reads=[xk, "st2", "w1b"], writes=[xk])
                P.op("pool", lambda e, x=x: e.tensor_tensor(hb[:], x[:], sh1b[:], op=ALU.add), reads=[xk, "sh1b"], writes=["hb"])
                for g in range(4):
                    pt = psT[g % 2]; pk = "psb%d" % (g % 2)
                    ptb = pt[:].bitcast(BF16)
                    for u in range(8):
                        kc = g * 8 + u
                        P.op("pe", lambda e, ptb=ptb, u=u, kc=kc: e.transpose(ptb[:, u * 128:(u + 1) * 128], hb[:, kc * 128:(kc + 1) * 128], ident[:]),
                             reads=["hb", "ident"], writes=[pk])
                    q_ = evq.next()
                    dstap = hT[:, g * 8:(g + 1) * 8, i * 128:(i + 1) * 128]
                    srcap = ptb[:, 0:1024].rearrange("p (u t) -> p u t", u=8)
                    if q_ == "act":
                        P.op("act", lambda e, d=dstap, s=srcap: e.copy(d, s), reads=[pk], writes=["hT%d_%d" % (i, g)])
                    else:
                        P.op("dve", lambda e, d=dstap, s=srcap: e.tensor_copy(d, s), reads=[pk], writes=["hT%d_%d" % (i, g)])
            hT_keys = ["hT%d_%d" % (i, g) for i in range(8) for g in range(4)]
            hTh = [[("hT%d_%d" % (i, g)) for i in range(th * 4, th * 4 + 4) for g in range(4)] for th in range(2)]

            def load_wf(f):
                P.dma("sp", wfb[f % 3][:], w_fm[f], "p1wf%d" % (f % 3), reads=["w_fm%d" % f], writes=["wf%d" % (f % 3)])

            post_q = []

            def post_tile(f, th, pm, pmk, tok0):
                if f < 24:
                    kind = f // 8
                    hd = f % 8
                    rw, rk = rawR.next(); ac, ak = accR.next()
                    P.op("pool", lambda e, rw=rw, f=f: e.tensor_copy(rw[:, 0:3], halo[:, f, :]), reads=["halo%d" % f], writes=[rk + "h"])
                    P.op("act", lambda e, rw=rw, pm=pm: e.copy(rw[:, 3:515], pm[:, :]), reads=[pmk], writes=[rk])
                    P.op("pool", lambda e, rw=rw, f=f: e.tensor_copy(halo[:, f, :], rw[:, 512:515]), reads=[rk], writes=["halo%d" % f])
                    P.op("dve", lambda e, rw=rw, ac=ac, f=f: e.tensor_scalar(ac[:], rw[:, 3:515], cw[:, f * 4 + 3:f * 4 + 4], None, op0=ALU.mult),
                         reads=[rk, "cw"], writes=[ak])
                    for jj in (2, 1, 0):
                        P.op("dve", lambda e, rw=rw, ac=ac, f=f, jj=jj: e.scalar_tensor_tensor(out=ac[:], in0=rw[:, jj:jj + 512], scalar=cw[:, f * 4 + jj:f * 4 + jj + 1],
                                                                                            in1=ac[:], op0=ALU.mult, op1=ALU.add),
                             reads=[rk, rk + "h", ak, "cw"], writes=[ak])
                    o_, ok = obR.next()
                    if kind == 2:
                        P.op("act", lambda e, ac=ac, o_=o_: e.activation(o_[:], ac[:], AF.Silu), reads=[ak], writes=[ok])
                    else:
                        P.op("act", lambda e, ac=ac: e.activation(ac[:], ac[:], AF.Silu), reads=[ak], writes=[ak])
                        s_, sk = sqR.next(); ri, rik = rinR.next()
                        P.op("pool", lambda e, ac=ac, s_=s_: e.tensor_tensor(s_[:], ac[:], ac[:], op=ALU.mult), reads=[ak], writes=[sk])
                        pn, pnk = psN.next()
                        P.op("pe", lambda e, pn=pn, s_=s_: e.matmul(pn[:, :], onesb[:], s_[:], start=True, stop=True), reads=[sk, "onesb"], writes=[pnk])
                        P.op("act", lambda e, pn=pn, ri=ri: e.activation(ri[:], pn[:, :], AF.Sqrt, bias=1e-6, scale=1.0), reads=[pnk], writes=[rik])
                        P.op("dve", lambda e, ri=ri: e.reciprocal(ri[:], ri[:]), reads=[rik], writes=[rik])
                        scl = (128.0 ** -0.5) if kind == 0 else 1.0
                        P.op("dve", lambda e, ac=ac, ri=ri, o_=o_, scl=scl: e.scalar_tensor_tensor(out=o_[:], in0=ac[:], scalar=scl, in1=ri[:], op0=ALU.mult, op1=ALU.mult),
                             reads=[ak, rik], writes=[ok])
                    if kind == 0:
                        P.dma("sp", gqT[hd, :, tok0:tok0 + 512], o_[:], "p1o", reads=[ok], writes=["gqT"])
                    elif kind == 1:
                        P.dma("sp", gkT[hd, :, tok0:tok0 + 512], o_[:], "p1o", reads=[ok], writes=["gkT"])
                    if kind >= 1:
                        pt = psT[(f + th) % 2]; pk = "psb%d" % ((f + th) % 2)
                        ptb = pt[:].bitcast(BF16)
                        for u in range(4):
                            P.op("pe", lambda e, ptb=ptb, u=u, o_=o_: e.transpose(ptb[:, u * 128:(u + 1) * 128], o_[:, u * 128:(u + 1) * 128], ident[:]),
                                 reads=[ok, "ident"], writes=[pk])
                        tm_, tk = tmR.next()
                        P.op("act", lambda e, tm_=tm_, ptb=ptb: e.copy(tm_[:], ptb[:, 0:512]), reads=[pk], writes=[tk])
                        dstt = (gk_tm if kind == 1 else gv_tm)[hd, tok0:tok0 + 512, :].rearrange("(u p) d -> p u d", p=128)
                        P.dma("sp", dstt, tm_[:].rearrange("p (u d) -> p u d", u=4), "p1o", reads=[tk], writes=["gtm"])
                else:
                    o_, ok = obR.next()
                    if f < 32:
                        P.op("act", lambda e, o_=o_, pm=pm: e.activation(o_[:], pm[:, :], AF.Copy, scale=128.0 ** -0.5), reads=[pmk], writes=[ok])
                        P.dma("sp", nqT[f - 24, :, tok0:tok0 + 512], o_[:], "p1o", reads=[ok], writes=["nqT"])
                    else:
                        P.op("act", lambda e, o_=o_, pm=pm: e.copy(o_[:], pm[:, :]), reads=[pmk], writes=[ok])
                        P.dma("sp", kvT[f - 32, :, tok0:tok0 + 512], o_[:], "p1o", reads=[ok], writes=["kvT"])

            load_wf(0); load_wf(1)
            for f in range(N_FM):
                if f + 2 < N_FM:
                    load_wf(f + 2)
                w = wfb[f % 3]; wk = "wf%d" % (f % 3)
                for th in range(2):
                    pm, pmk = psM.next()
                    for kc in range(32):
                        P.op("pe", lambda e, pm=pm, w=w, kc=kc, th=th: e.matmul(pm[:, :], w[:, kc, :], hT[:, kc, th * 512:(th + 1) * 512],
                                                                              start=(kc == 0), stop=(kc == 31)),
                             reads=[wk] + (hTh[th] if kc in (0, 31) else []), writes=[pmk])
                    tok0 = t0s + th * 512
                    post_q.append((f, th, pm, pmk, tok0))
                    if len(post_q) > 1:
                        post_tile(*post_q.pop(0))
            while post_q:
                post_tile(*post_q.pop(0))
            for g2 in range(7):
                wt = wtb[0]
                g = g2 // 2 if g2 < 6 else 3
                hf = g2 % 2
                ncol = 256 if g < 3 else 64
                c0 = hf * 256 if g < 3 else 0
                P.dma("sp", wt[:, :, 0:ncol], w_tm[g][:, :, c0:c0 + ncol], "p1wt", reads=["w_tm%d" % g], writes=["wt"])
                for i in range(8):
                    pm, pmk = psM.next()
                    for kc in range(32):
                        P.op("pe", lambda e, pm=pm, wt=wt, kc=kc, i=i, ncol=ncol: e.matmul(pm[:, 0:ncol], hT[:, kc, i * 128:(i + 1) * 128], wt[:, kc, 0:ncol],
                                                                                        start=(kc == 0), stop=(kc == 31)),
                             reads=["wt"] + (["hT%d_%d" % (i, gg) for gg in range(4)] if kc in (0, 31) else []), writes=[pmk])
                    tok0 = t0s + i * 128
                    if g < 2:
                        o_, ok = obR.next()
                        P.op("act", lambda e, o_=o_, pm=pm: e.activation(o_[:, 0:256], pm[:, 0:256], AF.Silu), reads=[pmk], writes=[ok])
                        P.dma("sp", sz_tm[tok0:tok0 + 128, g * 512 + c0:g * 512 + c0 + 256], o_[:, 0:256], "p1o", reads=[ok], writes=["sz_tm"])
                    elif g == 2:
                        o_, ok = obR.next()
                        P.op("act", lambda e, o_=o_, pm=pm: e.copy(o_[:, 0:256], pm[:, 0:256]), reads=[pmk], writes=[ok])
                        P.dma("sp", vsw_tm[tok0:tok0 + 128, c0:c0 + 256], o_[:, 0:256], "p1o", reads=[ok], writes=["vsw_tm"])
                    else:
                        s_, sk = smR.next()
                        P.op("act", lambda e, s_=s_, pm=pm: e.activation(s_[:, 0:8], pm[:, 0:8], AF.Sigmoid), reads=[pmk], writes=[sk + "a"])
                        P.op("act", lambda e, s_=s_, pm=pm: e.activation(s_[:, 16:40], pm[:, 16:40], AF.Sigmoid), reads=[pmk], writes=[sk + "c"])
                        P.op("dve", lambda e, s_=s_, pm=pm: e.tensor_tensor(s_[:, 8:16], pm[:, 8:16], dtbb[:], op=ALU.add), reads=[pmk, "dtbb"], writes=[sk + "b"])
                        P.op("act", lambda e, s_=s_: e.activation(s_[:, 8:16], s_[:, 8:16], AF.Exp), reads=[sk + "b"], writes=[sk + "b"])
                        P.op("act", lambda e, s_=s_: e.activation(s_[:, 8:16], s_[:, 8:16], AF.Ln, bias=1.0, scale=1.0), reads=[sk + "b"], writes=[sk + "b"])
                        P.op("dve", lambda e, s_=s_: e.tensor_tensor(s_[:, 8:16], s_[:, 8:16], nab[:], op=ALU.mult), reads=[sk + "b", "nab"], writes=[sk + "b"])
                        P.op("pool", lambda e, s_=s_: e.memset(s_[:, 40:64], 0.0), writes=[sk + "d"])
                        P.dma("sp", sm_tm[tok0:tok0 + 128, :], s_[:], "p1o", reads=[sk + "a", sk + "b", sk + "c", sk + "d"], writes=["sm_tm"])
      if start <= 1:
        _ph1()
    P.barrier()
    if nphase < 2:
        P.finish(); P.emit(); return nc

    with ExitStack() as st:
      def _ph2():
        sm_all = P.sb("p2_sm", [128, 32, 64], F32, st)
        triu = P.sb("p2_triu", [128, 128], F32, st)
        dmask = P.sb("p2_dmask", [128, 128], F32, st)
        gnwb = P.sb("p2_gnwb", [128, 128], F32, st)
        mskU = P.sb("p2_mskU", [128, 7, 128], F32, st)
        mskL = P.sb("p2_mskL", [128, 7, 128], F32, st)
        P.dma("sp", mskU[:], t_mskU[:, :, :], "p2c", writes=["mskU"])
        P.dma("sp", mskL[:], t_mskL[:, :, :], "p2c", writes=["mskL"])
        P.dma("sp", sm_all[:], sm_tm.rearrange("(c p) f -> p c f", p=128), "p2c", reads=["sm_tm"], writes=["sm_all"])
        P.dma("sp", triu[:], t_triu[:, :], "p2c", writes=["triu"])
        P.dma("sp", dmask[:], t_dmask[:, :], "p2c", writes=["dmask"])
        P.dma("sp", gnwb[:], gnw[0:1, :].partition_broadcast(128), "p2c", writes=["gnwb"])
        HB = []
        for par in range(2):
            hbuf = {}
            for nm in ("qT", "kT"):
                hbuf[nm] = P.sb("p2_%s%d" % (nm, par), [128, T], BF16, st)
            for nm in ("ktm", "vtm", "sz", "XT", "QKD", "QdT", "Kd"):
                hbuf[nm] = P.sb("p2_%s%d" % (nm, par), [128, 32, 128], BF16, st)
            hbuf["mix"] = P.sb("p2_mix%d" % par, [128, T], BF16, st) if par == 0 else HB[0]["mix"]
            hbuf["sc"] = P.sb("p2_sc%d" % par, [128, 6, 32], F32, st)
            HB.append(hbuf)
        S = P.sb("p2_S", [128, 128], F32, st)
        Sb = P.sb("p2_Sb", [128, 128], BF16, st)
        tA = [dict(dg=P.sb("p2_dg%d" % i, [128, 128], F32, st), dd=P.sb("p2_dd%d" % i, [128, 128], F32, st), DT=P.sb("p2_DT%d" % i, [128, 128], F32, st),
                   Eg=P.sb("p2_Eg%d" % i, [128, 128], F32, st), Mf=P.sb("p2_Mf%d" % i, [128, 128], F32, st),
                   Lf=P.sb("p2_Lf%d" % i, [128, 128], F32, st), Cu=P.sb("p2_Cu%d" % i, [128, 128], F32, st), Cl=P.sb("p2_Cl%d" % i, [128, 128], F32, st),
                   T1=P.sb("p2_T1%d" % i, [128, 128], F32, st), T1p=P.sb("p2_T1p%d" % i, [128, 128], F32, st),
                   U=[P.sb("p2_U%d_%d" % (i, j), [128, 128], F32, st) for j in range(2)],
                   L=[P.sb("p2_L%d_%d" % (i, j), [128, 128], F32, st) for j in range(2)]) for i in range(4)]
        tB = [dict(R=P.sb("p2_R%d" % i, [128, 128], BF16, st), vn=P.sb("p2_vn%d" % i, [128, 128], BF16, st),
                   j1=P.sb("p2_j1%d" % i, [128, 128], F32, st), j2=P.sb("p2_j2%d" % i, [128, 128], F32, st),
                   om=P.sb("p2_om%d" % i, [128, 128], BF16, st), st=P.sb("p2_st%d" % i, [128, 4], F32, st)) for i in range(2)]

        def head_load(hd):
            par = hd % 2; hb_ = HB[par]; k = "h%d" % par
            P.dma("sp", hb_["qT"][:], gqT[hd], "p2l%d" % par, reads=["gqT"], writes=[k + "qT"])
            P.dma("sp", hb_["kT"][:], gkT[hd], "p2l%d" % par, reads=["gkT"], writes=[k + "kT"])
            P.dma("sp", hb_["ktm"][:], gk_tm[hd].rearrange("(c p) d -> p c d", p=128), "p2l%d" % par, reads=["gtm"], writes=[k + "ktm"])
            P.dma("sp", hb_["vtm"][:], gv_tm[hd].rearrange("(c p) d -> p c d", p=128), "p2l%d" % par, reads=["gtm"], writes=[k + "vtm"])
            P.dma("sp", hb_["sz"][:], sz_tm[:, hd * 128:(hd + 1) * 128].rearrange("(c p) d -> p c d", p=128), "p2l%d" % par,
                  reads=["sz_tm"], writes=[k + "sz"])

        def head_pre(hd):
            par = hd % 2; hb_ = HB[par]; k = "h%d" % par
            sc = hb_["sc"]
            g_h = sc[:, 5, :]
            P.op("dve", lambda e: e.tensor_copy(sc[:, 5, :], sm_all[:, :, 8 + hd]), reads=["sm_all"], writes=[k + "gh"])
            P.op("pe", lambda e: e.matmul(psb[0][:, 0:32], triu[:], g_h, start=True, stop=True), reads=[k + "gh", "triu"], writes=["ps0a"])
            P.op("pe", lambda e: e.matmul(psb[0][:, 32:64], onesf[:], g_h, start=True, stop=True), reads=[k + "gh", "onesf"], writes=["ps0b"])
            if gdbg == "pre0":
                return
            P.op("act", lambda e: e.copy(sc[:, 0, :], psb[0][:, 0:32]), reads=["ps0a"], writes=[k + "gam"])
            P.op("dve", lambda e: e.tensor_scalar(sc[:, 1, :], psb[0][:, 0:32], -1.0, None, op0=ALU.mult), reads=["ps0a"], writes=[k + "ngam"])
            P.op("act", lambda e: e.activation(sc[:, 2, :], psb[0][:, 0:32], AF.Exp), reads=["ps0a"], writes=[k + "negeg"])
            P.op("dve", lambda e: e.tensor_scalar(sc[:, 2, :], sc[:, 2, :], -1.0, None, op0=ALU.mult), reads=[k + "negeg"], writes=[k + "negeg"])
            P.op("dve", lambda e: e.tensor_tensor(sc[:, 3, :], psb[0][:, 32:64], sc[:, 0, :], op=ALU.subtract), reads=["ps0b", k + "gam"], writes=[k + "kdec"])
            P.op("act", lambda e: e.activation(sc[:, 3, :], sc[:, 3, :], AF.Exp), reads=[k + "kdec"], writes=[k + "kdec"])
            P.op("act", lambda e: e.activation(sc[:, 4, :], psb[0][:, 32:64], AF.Exp), reads=["ps0b"], writes=[k + "egl"])

        def GA(hd, c):
            par = hd % 2; hb_ = HB[par]; k = "h%d" % par
            pc = c % 4; ta = tA[pc]; a_ = "A%d" % pc
            sc = hb_["sc"]
            cs = slice(c * 128, (c + 1) * 128)
            qTc = hb_["qT"][:, cs]; kTc = hb_["kT"][:, cs]
            bA = 1 + pc; bB = bA
            rgn = [psb[bA][:, i_ * 128:(i_ + 1) * 128] for i_ in range(4)]
            rk_ = ["ps%dr%d" % (bA, i_) for i_ in range(4)]
            pG, pKK, pQK, pLt = rgn
            pGk = rk_[0]; pLk = rk_[3]
            P.op("dve", lambda e: e.tensor_scalar(ta["dg"][:], identf[:], sc[:, 0, c:c + 1], None, op0=ALU.mult), reads=["identf", k + "gam"], writes=[a_ + "dg"])
            yield
            P.op("pe", lambda e: e.matmul(pG, onesf[:], ta["dg"][:], start=True, stop=True), reads=["onesf", a_ + "dg"], writes=[pGk])
            yield
            P.op("dve", lambda e: e.scalar_tensor_tensor(out=ta["dd"][:], in0=pG, scalar=sc[:, 1, c:c + 1], in1=dmask[:], op0=ALU.add, op1=ALU.add),
                 reads=[pGk, k + "ngam", "dmask"], writes=[a_ + "dd"])
            yield
            P.op("act", lambda e: e.activation(ta["DT"][:], ta["dd"][:], AF.Exp), reads=[a_ + "dd"], writes=[a_ + "DT"])
            yield
            P.op("act", lambda e: e.activation(ta["Eg"][:], pG, AF.Exp), reads=[pGk], writes=[a_ + "Eg"])
            yield
            P.op("pool", lambda e: e.tensor_tensor(hb_["QdT"][:, c, :], qTc, ta["Eg"][:], op=ALU.mult), reads=[k + "qT", a_ + "Eg"], writes=[k + "QdT%d" % c])
            yield
            P.op("pool", lambda e: e.tensor_scalar(hb_["Kd"][:, c, :], hb_["ktm"][:, c, :], sc[:, 3, c:c + 1], None, op0=ALU.mult),
                 reads=[k + "ktm", k + "kdec"], writes=[k + "Kd%d" % c])
            yield
            P.op("pe", lambda e: e.matmul(pKK, kTc, kTc, start=True, stop=True), reads=[k + "kT"], writes=[rk_[1]])
            yield
            P.op("pe", lambda e: e.matmul(pQK, kTc, qTc, start=True, stop=True), reads=[k + "kT", k + "qT"], writes=[rk_[2]])
            yield
            P.op("dve", lambda e: e.scalar_tensor_tensor(out=ta["Mf"][:], in0=pKK, scalar=sm_all[:, c, hd:hd + 1], in1=ta["DT"][:], op0=ALU.mult, op1=ALU.mult),
                 reads=[rk_[1], "sm_all", a_ + "DT"], writes=[a_ + "Mf"])
            yield
            P.op("dve", lambda e: e.tensor_tensor(hb_["QKD"][:, c, :], pQK, ta["DT"][:], op=ALU.mult), reads=[rk_[2], a_ + "DT"], writes=[k + "QKD%d" % c])
            yield
            Mf, Lf, Cu, Cl, T1, T1p, U, L = ta["Mf"], ta["Lf"], ta["Cu"], ta["Cl"], ta["T1"], ta["T1p"], ta["U"], ta["L"]
            pT1, pT1p, pT2, pT2p = rgn
            P.op("pe", lambda e: e.matmul(pLt, Mf[:], identf[:], start=True, stop=True), reads=[a_ + "Mf", "identf"], writes=[pLk])
            yield
            P.op("act", lambda e: e.copy(Lf[:], pLt), reads=[pLk], writes=[a_ + "Lf"])
            yield
            P.op("pool", lambda e: e.tensor_tensor(Cu[:], Mf[:], mskU[:, 0, :], op=ALU.mult), reads=[a_ + "Mf", "mskU"], writes=[a_ + "Cu"])
            yield
            P.op("pool", lambda e: e.tensor_tensor(Cl[:], Lf[:], mskL[:, 0, :], op=ALU.mult), reads=[a_ + "Lf", "mskL"], writes=[a_ + "Cl"])
            yield
            P.op("dve", lambda e: e.tensor_tensor(U[0][:], identf[:], Cu[:], op=ALU.subtract), reads=["identf", a_ + "Cu"], writes=[a_ + "U0"])
            yield
            P.op("dve", lambda e: e.tensor_tensor(L[0][:], identf[:], Cl[:], op=ALU.subtract), reads=["identf", a_ + "Cl"], writes=[a_ + "L0"])
            yield
            cur = 0
            for lvl in range(1, 7):
                nx = 1 - cur
                lastl = (lvl == 6)
                P.op("pool", lambda e, lvl=lvl: e.tensor_tensor(Cl[:], Lf[:], mskL[:, lvl, :], op=ALU.mult), reads=[a_ + "Lf", "mskL"], writes=[a_ + "Cl"])
                yield
                P.op("pe", lambda e, cur=cur: e.matmul(pT1, Cl[:], U[cur][:], start=True, stop=True), reads=[a_ + "Cl", a_ + "U%d" % cur], writes=[rk_[0]])
                yield
                P.op("act", lambda e: e.copy(T1[:], pT1), reads=[rk_[0]], writes=[a_ + "T1"])
                yield
                if not lastl:
                    P.op("pool", lambda e, lvl=lvl: e.tensor_tensor(Cu[:], Mf[:], mskU[:, lvl, :], op=ALU.mult), reads=[a_ + "Mf", "mskU"], writes=[a_ + "Cu"])
                    yield
                    P.op("pe", lambda e, cur=cur: e.matmul(pT1p, Cu[:], L[cur][:], start=True, stop=True), reads=[a_ + "Cu", a_ + "L%d" % cur], writes=[rk_[1]])
                    yield
                    P.op("dve", lambda e: e.tensor_copy(T1p[:], pT1p), reads=[rk_[1]], writes=[a_ + "T1p"])
                    yield
                P.op("pe", lambda e, cur=cur: e.matmul(pT2, L[cur][:], T1[:], start=True, stop=True), reads=[a_ + "L%d" % cur, a_ + "T1"], writes=[rk_[2]])
                yield
                if not lastl:
                    P.op("pe", lambda e, cur=cur: e.matmul(pT2p, U[cur][:], T1p[:], start=True, stop=True), reads=[a_ + "U%d" % cur, a_ + "T1p"], writes=[rk_[3]])
                    yield
                    P.op("dve", lambda e, cur=cur, nx=nx: e.tensor_tensor(U[nx][:], U[cur][:], pT2, op=ALU.subtract), reads=[rk_[2], a_ + "U%d" % cur], writes=[a_ + "U%d" % nx])
                    yield
                    P.op("dve", lambda e, cur=cur, nx=nx: e.tensor_tensor(L[nx][:], L[cur][:], pT2p, op=ALU.subtract), reads=[rk_[3], a_ + "L%d" % cur], writes=[a_ + "L%d" % nx])
                    yield
                else:
                    P.op("dve", lambda e, cur=cur: e.tensor_tensor(hb_["XT"][:, c, :], U[cur][:], pT2, op=ALU.subtract), reads=[rk_[2], a_ + "U%d" % cur], writes=[k + "XT%d" % c])
                    yield
                cur = nx

        def GB(hd, c):
            par = hd % 2; hb_ = HB[par]; k = "h%d" % par
            pc = c % 2; tb = tB[pc]; b_ = "B%d" % pc
            sc = hb_["sc"]
            cs = slice(c * 128, (c + 1) * 128)
            kTc = hb_["kT"][:, cs]
            pKS = psb[5][:, 0:128]; pVN = psb[5][:, 128:256]
            pO = psb[6][:, 0:128] if pc == 0 else psb[0][:, 256:384]
            pOk = "ps6_O" if pc == 0 else "ps0_O"
            pD = psb[7][:, 0:128]
            pOt = psb[7][:].bitcast(BF16)[:, 512:640]
            if c == 0:
                P.op("pool", lambda e: e.memset(S[:], 0.0), writes=["S"])
                yield
                P.op("pool", lambda e: e.memset(Sb[:], 0.0), writes=["Sb"])
                yield
            P.op("pe", lambda e: e.matmul(pKS, kTc, Sb[:], start=True, stop=True), reads=[k + "kT", "Sb"], writes=["ps5a"])
            yield
            P.op("dve", lambda e: e.scalar_tensor_tensor(out=tb["R"][:], in0=pKS, scalar=sc[:, 2, c:c + 1], in1=hb_["vtm"][:, c, :], op0=ALU.mult, op1=ALU.add),
                 reads=["ps5a", k + "negeg", k + "vtm"], writes=[b_ + "R"])
            yield
            P.op("pe", lambda e: e.matmul(pVN, hb_["XT"][:, c, :], tb["R"][:], start=True, stop=True), reads=[k + "XT%d" % c, b_ + "R"], writes=["ps5b"])
            yield
            P.op("act", lambda e: e.activation(tb["vn"][:], pVN, AF.Copy, scale=sm_all[:, c, hd:hd + 1]), reads=["ps5b", "sm_all"], writes=[b_ + "vn"])
            yield
            P.op("pe", lambda e: e.matmul(pO, hb_["QdT"][:, c, :], Sb[:], start=True, stop=False), reads=[k + "QdT%d" % c, "Sb"], writes=[pOk])
            yield
            P.op("pe", lambda e: e.matmul(pO, hb_["QKD"][:, c, :], tb["vn"][:], start=False, stop=True), reads=[k + "QKD%d" % c, b_ + "vn"], writes=[pOk])
            yield
            P.op("pe", lambda e: e.matmul(pD, hb_["Kd"][:, c, :], tb["vn"][:], start=True, stop=True), reads=[k + "Kd%d" % c, b_ + "vn"], writes=["ps7d"])
            yield
            P.op("dve", lambda e: e.scalar_tensor_tensor(out=S[:], in0=S[:], scalar=sc[:, 4, c:c + 1], in1=pD, op0=ALU.mult, op1=ALU.add),
                 reads=["S", k + "egl", "ps7d"], writes=["S"])
            yield
            P.op("act", lambda e: e.copy(Sb[:], S[:]), reads=["S"], writes=["Sb"])
            yield
            P.op("act", lambda e: e.activation(tb["j1"][:], pO, AF.Square, accum_out=tb["st"][:, 0:1]), reads=[pOk], writes=[b_ + "j1", b_ + "s0"])
            yield
            P.op("act", lambda e: e.activation(tb["st"][:, 1:2], tb["st"][:, 0:1], AF.Sqrt, bias=1e-6, scale=1.0 / 128.0), reads=[b_ + "s0"], writes=[b_ + "s1"])
            yield
            P.op("dve", lambda e: e.reciprocal(tb["st"][:, 2:3], tb["st"][:, 1:2]), reads=[b_ + "s1"], writes=[b_ + "s2"])
            yield
            P.op("dve", lambda e: e.scalar_tensor_tensor(out=tb["j2"][:], in0=pO, scalar=tb["st"][:, 2:3], in1=gnwb[:], op0=ALU.mult, op1=ALU.mult),
                 reads=[pOk, b_ + "s2", "gnwb"], writes=[b_ + "j2"])
            yield
            P.op("pool", lambda e: e.tensor_tensor(tb["om"][:], tb["j2"][:], hb_["sz"][:, c, :], op=ALU.mult), reads=[b_ + "j2", k + "sz"], writes=[b_ + "om"])
            yield
            P.op("pe", lambda e: e.transpose(pOt, tb["om"][:], ident[:]), reads=[b_ + "om", "ident"], writes=["ps7t"])
            yield
            P.op("act", lambda e: e.copy(hb_["mix"][:, cs], pOt), reads=["ps7t"], writes=["mix%d" % c])
            yield
            if c == 31:
                P.dma("sp", mixT[hd], hb_["mix"][:], "p2o", reads=["mix%d" % cc_ for cc_ in range(32)], writes=["mixT"])
                yield

        head_load(0)
        if gdbg == "load":
            pass
        elif gdbg in ("pre", "pre0", "pre1"):
            head_pre(0)
        elif gdbg == "ga1":
            head_pre(0); list(GA(0, 0))
        elif gdbg == "ga":
            head_pre(0)
            for c in range(32):
                list(GA(0, c))
        elif gdbg == "gb1":
            head_pre(0)
            for c in range(32):
                list(GA(0, c))
            list(GB(0, 0))
        else:
          NH = only_heads
          def dump(name, ap_, shape, dt, reads):
              t_ = nc.dram_tensor(name, list(shape), dt, kind="ExternalOutput").ap()
              P.dma("sp", t_, ap_, "dbgd", reads=reads)
          for hd in range(NH + 1):
            if dbg and gdbg == "dump" and hd == NH:
                hb_ = HB[(NH - 1) % 2]; k = "h%d" % ((NH - 1) % 2)
                allk = [k + "%s%d" % (nm, c_) for nm in ("XT", "QKD", "QdT", "Kd") for c_ in range(32)]
                for nm in ("XT", "QKD", "QdT", "Kd", "ktm", "vtm", "sz"):
                    dump("d_" + nm, hb_[nm][:], [128, 32, 128], BF16, allk + [k + "ktm", k + "vtm", k + "sz"])
                dump("d_sc", hb_["sc"][:], [128, 6, 32], F32, [k + x for x in ("gam", "ngam", "negeg", "kdec", "egl", "gh")])
                dump("d_gamT", hb_["gamT"][:], [32, 128], F32, [k + "gamT"])
                dump("d_kT", hb_["kT"][:], [128, T], BF16, [k + "kT"])
            if hd < NH:
                head_pre(hd)
            for c0 in range(0, 32, 4):
                gens = []
                if hd < NH:
                    cast_step(2)
                    gens += [GA(hd, c0 + i_) for i_ in range(4)]
                if hd > 0:
                    def gbchain(h_=hd - 1, c_=c0):
                        for i_ in range(4):
                            yield from GB(h_, c_ + i_)
                    gens.append(gbchain())
                while gens:
                    for g_ in list(gens):
                        try:
                            next(g_)
                        except StopIteration:
                            gens.remove(g_)
            if hd + 1 < NH:
                head_load(hd + 1)
      if start <= 2:
        _ph2()
    P.barrier()
    if nphase < 3:
        P.finish(); P.emit(); return nc

    rg = [[0, 1], [2, 3], [4, 5], [6, 7]]

    def exch(js):
        if start > 3 or nphase < 4:
            return
        for j in js:
            P.cc(lambda e, j=j: e.collective_compute("AllGather", ALU.bypass, replica_groups=rg, ins=[mixT[j]], outs=[mixG[j]]),
                 "ccx", reads=["mixT"], writes=["mixG%d" % j])

    if start <= 3:
        exch(range(0, 8))

    with ExitStack() as st:
      def _ph3():
        sm_all = P.sb("p3_sm", [128, 32, 64], F32, st)
        winT = P.sb("p3_winT", [128, 8, 512], BF16, st)
        cmT = P.sb("p3_cmT", [128, 2, T], BF16, st)
        selE = P.sb("p3_selE", [64, 32, 128], BF16, st)
        frc = P.sb("p3_frc", [128, 32, 64], F32, st)
        kbt = P.sb("p3_kb", [128, 8, 32], F32, st)
        cbt = P.sb("p3_cb", [128, 8, 2, 8], F32, st)
        nslt = P.sb("p3_nsl", [1, 8], F32, st)
        trow = P.sb("p3_trow", [1, 512], F32, st)
        zer = P.sb("p3_zer", [128, 512], BF16, st)
        w1b = [P.sb("p3_w1%d" % i, [128, 32, 128], BF16, st) for i in range(2)]
        w2b = [P.sb("p3_w2%d" % i, [128, 128], BF16, st) for i in range(2)]
        posf = [P.sb("p3_pos%d" % i, [32, 128], F32, st) for i in range(2)]
        posT = [P.sb("p3_posT%d" % i, [128, 32], BF16, st) for i in range(2)]
        cvec = [P.sb("p3_cvec%d" % i, [128, 1], F32, st) for i in range(2)]
        kT4 = [P.sb("p3_kT%d" % i, [128, T], BF16, st) for i in range(4)]
        kcD = P.sb("p3_kcD", [128, 16, 256], BF16, st)
        hcm = P.sb("p3_hcm", [128, 256], BF16, st)
        kcmpT = P.sb("p3_kcmpT", [128, 256], BF16, st)
        VC = P.sb("p3_VC", [128, 2, 193], BF16, st)
        VS = P.sb("p3_VS", [128, 32, 129], BF16, st)
        VW = P.sb("p3_VW", [128, 32, 129], BF16, st)
        negK = P.sb("p3_negK", [1, 4], F32, st)
        kmx = P.sb("p3_kmx", [1, 32], F32, st)
        sqt = [P.sb("p3_sq%d" % i, [128, 512], BF16, st) for i in range(2)]
        qsb = [P.sb("p3_q%d" % i, [128, 4, 512], BF16, st) for i in range(2)]
        qn = P.sb("p3_qn", [1, 512], F32, st)
        srow = P.sb("p3_srow", [1, 512], F32, st)
        rrow = P.sb("p3_rrow", [1, 3, 512], BF16, st)
        PT = [P.sb("p3_PT%d" % i, [128, 512], BF16, st) for i in range(3)]
        oacc = P.sb("p3_oacc", [128, 4, 4, 128], F32, st)
        imp = P.sb("p3_imp", [128, 4, 64], F32, st)
        impp = P.sb("p3_impp", [128, 64], F32, st)
        impq = P.sb("p3_impq", [128, 64], F32, st)
        mx8 = P.sb("p3_mx8", [128, 16], F32, st)
        nsel = P.sb("p3_nsel", [128, 64], BF16, st)
        negselT = P.sb("p3_nselT", [64, 512], BF16, st)
        stt = P.sb("p3_stt", [128, 16], F32, st)
        ofin = P.sb("p3_ofin", [128, 128], BF16, st)
        mstage = [P.sb("p3_ms%d" % i, [128, 512], BF16, st) for i in range(2)]
        pcp = P.sb("p3_pcp", [128, 4, 386], F32, st)
        P.dma("sp", sm_all[:], sm_tm.rearrange("(c p) f -> p c f", p=128), "x", reads=["sm_tm"], writes=["sm_all"])
        for nm_, t_, src_ in (("winT", winT, t_winT), ("cmT", cmT, t_cmT), ("selE", selE, t_selE), ("frc", frc, t_frc),
                              ("kbt", kbt, t_kb), ("cbt", cbt, t_cb), ("nslt", nslt, t_nsl), ("trow", trow, t_trow)):
            P.dma("sp", t_[:], src_, "x", writes=[nm_])
        P.op("pool", lambda e: e.memset(zer[:], 0.0), writes=["zer"])
        for i, (w1_, w2_, pos_) in enumerate(((w1_k, w2_k, pos_k), (w1_v, w2_v, pos_v))):
            P.dma("pool", w1b[i][:], w1_.rearrange("(j d) o -> d j o", d=128), "x", writes=["w1b%d" % i])
            P.dma("pool", w2b[i][:], w2_[:, :], "x", writes=["w2b%d" % i])
            P.dma("sp", posf[i][:], pos_[:, :], "x", writes=["posf%d" % i])
            P.op("pe", lambda e, i=i: e.matmul(psb[6][:, 0:32], posf[i][:], identf[0:32, 0:32], start=True, stop=True), reads=["posf%d" % i, "identf"], writes=["ps6"])
            P.op("act", lambda e, i=i: e.copy(posT[i][:], psb[6][:, 0:32]), reads=["ps6"], writes=["posT%d" % i])
            for j in range(32):
                P.op("pe", lambda e, i=i, j=j: e.matmul(psb[6][:, 64:65], w1b[i][:, j, :], posT[i][:, j:j + 1], start=(j == 0), stop=(j == 31)),
                     reads=["w1b%d" % i, "posT%d" % i], writes=["ps6"])
            P.op("act", lambda e, i=i: e.copy(cvec[i][:], psb[6][:, 64:65]), reads=["ps6"], writes=["cvec%d" % i])

        psS = Rot([(psb[0], "ps0"), (psb[1], "ps1")])
        PTR = Rot([(PT[0], "PT0"), (PT[1], "PT1"), (PT[2], "PT2")])
        sqR = Rot([(sqt[0], "sq0"), (sqt[1], "sq1")])

        def zero_bank(b):
            P.op("pe", lambda e, b=b: e.matmul(psb[b][:, :], zer[:, 0:128], zer[:, :], start=True, stop=True, skip_group_check=True),
                 reads=["zer"], writes=["ps%d" % b])

        def do_group(gl):
            gk = "g"
            for i in range(4):
                P.dma("sp", kT4[i][:], kvT[2 * i + gl], "x", reads=["kvT"], writes=["kT4_%d" % i])
            P.dma("sp", VS[:, :, 0:128], vsw_tm[:, gl * 128:(gl + 1) * 128].rearrange("(c p) d -> p c d", p=128), "x", reads=["vsw_tm"], writes=["VSd"])
            P.dma("sp", VW[:, :, 0:128], vsw_tm[:, 256 + gl * 128:256 + (gl + 1) * 128].rearrange("(c p) d -> p c d", p=128), "x", reads=["vsw_tm"], writes=["VWd"])
            P.op("pool", lambda e: e.memset(VS[:, :, 128:129], 1.0), writes=["VS1"])
            P.op("pool", lambda e: e.memset(VW[:, :, 128:129], 1.0), writes=["VW1"])
            for i in range(2):
                src = kT4[i]
                P.op("dve", lambda e, src=src: e.tensor_copy(kcD[:], src[:].rearrange("p (n r) -> p r n", r=16)), reads=["kT4_%d" % i], writes=["kcD"])
                for j in range(32):
                    P.op("pe", lambda e, i=i, j=j: e.matmul(psb[6][:, 0:255], w1b[i][:, j, :], kcD[:, j % 16, j // 16:j // 16 + 255], start=(j == 0), stop=(j == 31)),
                         reads=["w1b%d" % i, "kcD"], writes=["ps6"])
                P.op("pool", lambda e: e.memset(hcm[:, 255:256], 0.0), writes=["hcm1"])
                P.op("act", lambda e, i=i: e.activation(hcm[:, 0:255], psb[6][:, 0:255], AF.Silu, bias=cvec[i][:, 0:1], scale=1.0), reads=["ps6", "cvec%d" % i], writes=["hcm"])
                if i == 0:
                    P.op("pe", lambda e: e.matmul(psb[7][:, 0:256], w2b[0][:], hcm[:], start=True, stop=True), reads=["w2b0", "hcm", "hcm1"], writes=["ps7"])
                    P.op("act", lambda e: e.copy(kcmpT[:], psb[7][:, 0:256]), reads=["ps7"], writes=["kcmpT"])
                else:
                    for nc_ in range(2):
                        P.op("pe", lambda e, nc_=nc_: e.matmul(psb[7][:, nc_ * 128:(nc_ + 1) * 128], hcm[:, nc_ * 128:(nc_ + 1) * 128], w2b[1][:], start=True, stop=True),
                             reads=["w2b1", "hcm", "hcm1"], writes=["ps7"])
                    P.op("act", lambda e: e.copy(VC[:, :, 0:128], psb[7][:, 0:256].rearrange("p (c d) -> p c d", c=2)), reads=["ps7"], writes=["VCd"])
                    P.op("pool", lambda e: e.memset(VC[:, :, 128:129], 1.0), writes=["VC1"])
                    P.dma("sp", VC[:, :, 129:193], t_ovl[:, :, :], "x", writes=["VCo"])
            for br, (src, ncols) in enumerate(((kcmpT, 256), (kT4[2], T), (kT4[3], T))):
                nch = max(1, ncols // 512)
                for cc_ in range(nch):
                    w_ = min(512, ncols)
                    s_, sk = sqR.next()
                    P.op("pool", lambda e, s_=s_, src=src, cc_=cc_, w_=w_: e.tensor_tensor(s_[:, 0:w_], src[:, cc_ * 512:cc_ * 512 + w_], src[:, cc_ * 512:cc_ * 512 + w_], op=ALU.mult),
                         reads=["kcmpT", "kT4_2", "kT4_3"], writes=[sk])
                    P.op("pe", lambda e, s_=s_, w_=w_: e.matmul(psb[6][0:1, 0:w_], onesb[:, 0:1], s_[:, 0:w_], start=True, stop=True), reads=[sk, "onesb"], writes=["ps6"])
                    P.op("dve", lambda e, br=br, cc_=cc_, w_=w_: e.reduce_max(kmx[:, br * 8 + cc_:br * 8 + cc_ + 1], psb[6][0:1, 0:w_], axis=AX.X), reads=["ps6"], writes=["kmx"])
                P.op("dve", lambda e, br=br, nch=nch: e.reduce_max(negK[:, br:br + 1], kmx[:, br * 8:br * 8 + nch], axis=AX.X), reads=["kmx"], writes=["negK%d" % br])
                P.op("act", lambda e, br=br: e.activation(negK[:, br:br + 1], negK[:, br:br + 1], AF.Sqrt), reads=["negK%d" % br], writes=["negK%d" % br])
                P.op("dve", lambda e, br=br: e.tensor_scalar(negK[:, br:br + 1], negK[:, br:br + 1], -1.0, None, op0=ALU.mult), reads=["negK%d" % br], writes=["negK%d" % br])

            def do_G(G):
                q_ = qsb[G % 2]; qk_ = "q%d" % (G % 2)
                for r in range(4):
                    P.dma("sp", q_[:, r, :], nqT[gl * 4 + r][:, G * 512:(G + 1) * 512], "x", reads=["nqT"], writes=[qk_ + "_%d" % r])
                P.op("pool", lambda e: e.memset(imp[:], 0.0), writes=["imp"])

                def make_rrow(r):
                    hr = gl * 4 + r
                    s_, sk = sqR.next()
                    P.op("pool", lambda e, s_=s_: e.tensor_tensor(s_[:], q_[:, r, :], q_[:, r, :], op=ALU.mult), reads=[qk_ + "_%d" % r], writes=[sk])
                    P.op("pe", lambda e, s_=s_: e.matmul(psb[6][0:1, :], onesb[:, 0:1], s_[:], start=True, stop=True), reads=[sk, "onesb"], writes=["ps6"])
                    P.op("act", lambda e: e.activation(qn[:], psb[6][0:1, :], AF.Sqrt), reads=["ps6"], writes=["qn"])
                    P.op("dve", lambda e: e.tensor_scalar(srow[:], trow[:], nslt[0:1, hr:hr + 1], None, op0=ALU.mult), reads=["trow", "nslt"], writes=["srow"])
                    for br in range(3):
                        P.op("dve", lambda e, br=br: e.scalar_tensor_tensor(out=rrow[:, br, :], in0=qn[:], scalar=negK[0:1, br:br + 1], in1=srow[:], op0=ALU.mult, op1=ALU.add),
                             reads=["qn", "negK%d" % br, "srow"], writes=["rrow%d" % br])

                def scores(kTsrc, kkey, kc, r, br, extra, m0=0, m1=4):
                    ps_, psk = psS.next()
                    c0, c1 = m0 * 128, m1 * 128
                    P.op("pe", lambda e: e.matmul(ps_[:, c0:c1], kTsrc[:, kc * 128:(kc + 1) * 128], q_[:, r, c0:c1], start=True, stop=False),
                         reads=[kkey, qk_ + "_%d" % r], writes=[psk])
                    P.op("pe", lambda e: e.matmul(ps_[:, c0:c1], onesb[0:1, :], rrow[0:1, br, c0:c1], start=False, stop=(len(extra) == 0)),
                         reads=["onesb", "rrow%d" % br], writes=[psk])
                    for ei, (l_, r_, rk_) in enumerate(extra):
                        P.op("pe", lambda e, l_=l_, r_=r_, ei=ei: e.matmul(ps_[:, c0:c1], l_, r_[:, c0:c1], start=False, stop=(ei == len(extra) - 1)), reads=rk_, writes=[psk])
                    return ps_, psk, m0, m1

                def pass1(r):
                    hr = gl * 4 + r
                    cast_step(1)
                    make_rrow(r)
                    zero_bank(2); zero_bank(3)
                    ncs = (0, 1) if G >= 4 else (0,)
                    pend = [scores(kcmpT, "kcmpT", ncs[0], r, 0, [(ident[:], cmT[:, ncs[0], G * 512:(G + 1) * 512], ["ident", "cmT"])])]
                    for ni, nc_ in enumerate(ncs):
                        if ni + 1 < len(ncs):
                            n2 = ncs[ni + 1]
                            pend.append(scores(kcmpT, "kcmpT", n2, r, 0, [(ident[:], cmT[:, n2, G * 512:(G + 1) * 512], ["ident", "cmT"])]))
                        ps_, psk, _m0, _m1 = pend.pop(0)
                        pt_, ptk = PTR.next()
                        P.op("act", lambda e, ps_=ps_, pt_=pt_, nc_=nc_: e.activation(pt_[:], ps_[:, :], AF.Exp, bias=cbt[:, hr, nc_, G:G + 1], scale=1.0), reads=[psk, "cbt"], writes=[ptk])
                        for m in range(4):
                            b_ = 2 + m // 2; o_ = (m % 2) * 193
                            P.op("pe", lambda e, pt_=pt_, m=m, b_=b_, o_=o_, nc_=nc_: e.matmul(psb[b_][:, o_:o_ + 193], pt_[:, m * 128:(m + 1) * 128], VC[:, nc_, :], start=False, stop=(nc_ == ncs[-1]), skip_group_check=True),
                                 reads=[ptk, "VCd", "VC1", "VCo"], writes=["ps%d" % b_])
                    P.op("act", lambda e: e.copy(pcp[:, 0, 0:386], psb[2][:, 0:386]), reads=["ps2"], writes=["pcp0"])
                    P.op("dve", lambda e: e.tensor_copy(pcp[:, 1, 0:386], psb[3][:, 0:386]), reads=["ps3"], writes=["pcp1"])
                    for m in range(4):
                        b_ = 2 + m // 2; o_ = (m % 2) * 193; qt = G * 4 + m
                        po = pcp[:, b_ - 2, :]
                        P.op("dve", lambda e, po=po, o_=o_: e.tensor_scalar(stt[:, 0:1], po[:, o_ + 128:o_ + 129], 1e-30, None, op0=ALU.max), reads=["pcp%d" % (b_ - 2)], writes=["stt0"])
                        P.op("dve", lambda e: e.reciprocal(stt[:, 1:2], stt[:, 0:1]), reads=["stt0"], writes=["stt1"])
                        P.op("dve", lambda e, po=po, o_=o_, m=m: e.scalar_tensor_tensor(out=imp[:, m, :], in0=po[:, o_ + 129:o_ + 193], scalar=stt[:, 1:2], in1=imp[:, m, :], op0=ALU.mult, op1=ALU.add),
                             reads=["pcp%d" % (b_ - 2), "stt1", "imp"], writes=["imp"])
                        P.op("dve", lambda e, qt=qt, hr=hr: e.tensor_tensor(stt[:, 2:3], stt[:, 1:2], sm_all[:, qt, 16 + hr * 3:16 + hr * 3 + 1], op=ALU.mult), reads=["stt1", "sm_all"], writes=["stt2"])
                        P.op("act", lambda e, po=po, o_=o_, r=r, m=m: e.activation(oacc[:, r, m, :], po[:, o_:o_ + 128], AF.Copy, scale=stt[:, 2:3]), reads=["pcp%d" % (b_ - 2), "stt2"], writes=["oacc%d_%d" % (r, m)])
                def select(m):
                    qt = G * 4 + m
                    P.op("dve", lambda e, m=m, qt=qt: e.tensor_tensor(impp[:], imp[:, m, :], frc[:, qt, :], op=ALU.add), reads=["imp", "frc"], writes=["impp"])
                    P.op("dve", lambda e: e.max(out=mx8[:, 0:8], in_=impp[:]), reads=["impp"], writes=["mx8a"])
                    P.op("dve", lambda e: e.match_replace(out=impq[:], in_to_replace=mx8[:, 0:8], in_values=impp[:], imm_value=-3e38), reads=["impp", "mx8a"], writes=["impq"])
                    P.op("dve", lambda e: e.max(out=mx8[:, 8:16], in_=impq[:]), reads=["impq"], writes=["mx8b"])
                    P.op("dve", lambda e: e.tensor_scalar(impq[:], impp[:], mx8[:, 15:16], None, op0=ALU.is_ge), reads=["impp", "mx8b"], writes=["impq"])
                    P.op("dve", lambda e: e.tensor_scalar(nsel[:], impq[:], 1.0, -NEG, op0=ALU.subtract, op1=ALU.mult), reads=["impq"], writes=["nsel"])
                    P.op("pe", lambda e: e.transpose(psb[7][:].bitcast(BF16)[0:64, 0:128], nsel[:], ident[:]), reads=["nsel", "ident"], writes=["ps7"])
                    P.op("act", lambda e, m=m: e.copy(negselT[:, m * 128:(m + 1) * 128], psb[7][:].bitcast(BF16)[0:64, 0:128]), reads=["ps7"], writes=["nselT%d" % m])
                def pass2(r):
                    hr = gl * 4 + r
                    cast_step(1)
                    make_rrow(r)
                    for b_ in (2, 3, 4, 5):
                        zero_bank(b_)
                    last = 4 * G + 3
                    def sel_scores(kc):
                        extra = [(selE[:, kc, :], negselT[:, :], ["selE"] + ["nselT%d" % m for m in range(4)])]
                        if kc >= 4 * G:
                            extra.append((ident[:], winT[:, 4 + kc - 4 * G, :], ["ident", "winT"]))
                        return scores(kT4[2], "kT4_2", kc, r, 1, extra, max(0, kc - 4 * G), 4)

                    def win_scores(kc):
                        j_ = kc - 4 * G + 4
                        m0_, m1_ = (0, j_ + 1) if j_ <= 3 else (j_ - 4, 4)
                        return scores(kT4[3], "kT4_3", kc, r, 2, [(ident[:], winT[:, 4 + kc - 4 * G, :], ["ident", "winT"])], m0_, m1_)

                    wlist = list(range(max(0, 4 * G - 4), last + 1))
                    pend = [sel_scores(0)]
                    for kc in range(0, last + 1):
                        if kc + 1 <= last:
                            pend.append(sel_scores(kc + 1))
                        else:
                            pend.append(win_scores(wlist[0]))
                        ps_, psk, m0_, m1_ = pend.pop(0)
                        pt_, ptk = PTR.next()
                        rel_ = kc - 4 * G + 28
                        P.op("act", lambda e, ps_=ps_, pt_=pt_, rel_=rel_, m0_=m0_, m1_=m1_: e.activation(pt_[:, m0_ * 128:m1_ * 128], ps_[:, m0_ * 128:m1_ * 128], AF.Exp, bias=kbt[:, hr, rel_:rel_ + 1], scale=1.0), reads=[psk, "kbt"], writes=[ptk])
                        for m in range(m0_, m1_):
                            b_ = 2 + m // 2; o_ = (m % 2) * 129
                            P.op("pe", lambda e, pt_=pt_, m=m, b_=b_, o_=o_, kc=kc: e.matmul(psb[b_][:, o_:o_ + 129], pt_[:, m * 128:(m + 1) * 128], VS[:, kc, :], start=False, stop=(kc == 4 * G + m), skip_group_check=True),
                                 reads=[ptk, "VSd", "VS1"], writes=["ps%d" % b_])
                    for wi, kc in enumerate(wlist):
                        if wi + 1 < len(wlist):
                            pend.append(win_scores(wlist[wi + 1]))
                        ps_, psk, m0_, m1_ = pend.pop(0)
                        pt_, ptk = PTR.next()
                        rel_ = kc - 4 * G + 28
                        P.op("act", lambda e, ps_=ps_, pt_=pt_, rel_=rel_, m0_=m0_, m1_=m1_: e.activation(pt_[:, m0_ * 128:m1_ * 128], ps_[:, m0_ * 128:m1_ * 128], AF.Exp, bias=kbt[:, hr, rel_:rel_ + 1], scale=1.0), reads=[psk, "kbt"], writes=[ptk])
                        for m in range(m0_, m1_):
                            b_ = 4 + m // 2; o_ = (m % 2) * 129
                            P.op("pe", lambda e, pt_=pt_, m=m, b_=b_, o_=o_, kc=kc: e.matmul(psb[b_][:, o_:o_ + 129], pt_[:, m * 128:(m + 1) * 128], VW[:, kc, :], start=False, stop=(kc == 4 * G + m), skip_group_check=True),
                                 reads=[ptk, "VWd", "VW1"], writes=["ps%d" % b_])
                    ms_ = mstage[(G * 4 + r) % 2]; msk = "ms%d" % ((G * 4 + r) % 2)
                    for bb in (2, 3, 4, 5):
                        if bb % 2 == 0:
                            P.op("act", lambda e, bb=bb: e.copy(pcp[:, bb - 2, 0:258], psb[bb][:, 0:258]), reads=["ps%d" % bb], writes=["pcp%d" % (bb - 2)])
                        else:
                            P.op("dve", lambda e, bb=bb: e.tensor_copy(pcp[:, bb - 2, 0:258], psb[bb][:, 0:258]), reads=["ps%d" % bb], writes=["pcp%d" % (bb - 2)])
                    for m in range(4):
                        qt = G * 4 + m
                        oa = oacc[:, r, m, :]; oak = "oacc%d_%d" % (r, m)
                        for bi, (bb, gcol) in enumerate(((2 + m // 2, 1), (4 + m // 2, 2))):
                            po = pcp[:, bb - 2, :]; o_ = (m % 2) * 129
                            P.op("dve", lambda e, po=po, o_=o_, bi=bi: e.reciprocal(stt[:, 4 + bi:5 + bi], po[:, o_ + 128:o_ + 129]), reads=["pcp%d" % (bb - 2)], writes=["stt%d" % (4 + bi)])
                            P.op("dve", lambda e, bi=bi, qt=qt, gcol=gcol: e.tensor_tensor(stt[:, 6 + bi:7 + bi], stt[:, 4 + bi:5 + bi], sm_all[:, qt, 16 + hr * 3 + gcol:16 + hr * 3 + gcol + 1], op=ALU.mult),
                                 reads=["stt%d" % (4 + bi), "sm_all"], writes=["stt%d" % (6 + bi)])
                            P.op("dve", lambda e, po=po, o_=o_, bi=bi, oa=oa: e.scalar_tensor_tensor(out=oa, in0=po[:, o_:o_ + 128], scalar=stt[:, 6 + bi:7 + bi], in1=oa, op0=ALU.mult, op1=ALU.add),
                                 reads=["pcp%d" % (bb - 2), "stt%d" % (6 + bi), oak], writes=[oak])
                        P.op("act", lambda e, oa=oa: e.copy(ofin[:], oa), reads=[oak], writes=["ofin"])
                        P.op("pe", lambda e: e.transpose(psb[7][:].bitcast(BF16)[:, 256:384], ofin[:], ident[:]), reads=["ofin", "ident"], writes=["ps7"])
                        P.op("act", lambda e, ms_=ms_, m=m: e.copy(ms_[:, m * 128:(m + 1) * 128], psb[7][:].bitcast(BF16)[:, 256:384]), reads=["ps7"], writes=[msk + "_%d" % m])
                    P.dma("sp", mixT[8 + hr][:, G * 512:(G + 1) * 512], ms_[:], "x", reads=[msk + "_%d" % m for m in range(4)], writes=["mixT"])
                for r in range(4):
                    pass1(r)
                for m in range(4):
                    select(m)
                for r in range(4):
                    pass2(r)

            for G in range(8):
                do_G(G)

        for gl in range(2):
            do_group(gl)
            if gl == 0:
                exch(range(8, 12))
      if start <= 3:
        _ph3()
        cast_step(10 ** 6)
    P.barrier()
    if nphase < 4:
        P.finish(); P.emit(); return nc

    if start <= 3:
        exch(range(12, 16))
        P.barrier()

    def _ph4():
        def gsrc(kc):
            if kc < 16:
                r_, j_ = kc // 8, kc % 8
            else:
                r_, j_ = (kc - 16) // 8, 8 + (kc - 16) % 8
            return mixG[j_][r_ * 128:(r_ + 1) * 128, :]

        selt = P.sb("p4_sel", [128, 2], F32)
        rst = P.sb("p4_rst", [128, 4, 4], F32)
        P.dma("sp", selt[:], selv[:, :], "x", writes=["selt"])
        for tb in range(4):
            tok0 = tb * 512
            with ExitStack() as st:
                mixsel = P.sb("p4_mixsel", [128, 32, 512], BF16, st)
                mA = [P.sb("p4_mA%d" % i, [128, 4, 512], BF16, st) for i in range(2)]
                mB = [P.sb("p4_mB%d" % i, [128, 4, 512], BF16, st) for i in range(2)]
                wo = [P.sb("p4_wo%d" % i, [128, 32, 512], BF16, st) for i in range(2)]
                g1b = P.sb("p4_g1b", [128, D], F32, st)
                xp = [P.sb("p4_xp%d" % i, [128, 512], F32, st) for i in range(3)]
                yp = [P.sb("p4_yp%d" % i, [128, 512], F32, st) for i in range(3)]
                jk = P.sb("p4_jk", [128, 512], BF16, st)
                ss1 = P.sb("p4_ss1", [128, 4, 8], F32, st)
                P.dma("sp", g1b[:], modv[2:3, :].partition_broadcast(128), "x", reads=["modv"], writes=["g1b"])
                for q4 in range(8):
                    a_ = mA[q4 % 2]; b_ = mB[q4 % 2]
                    for u in range(4):
                        kc = q4 * 4 + u
                        P.dma("sp", a_[:, u, :], gsrc(kc)[:, tok0:tok0 + 512], "x", reads=["mixG"], writes=["mA%d" % (q4 % 2)])
                        P.dma("sp", b_[:, u, :], gsrc(kc)[:, 2048 + tok0:2048 + tok0 + 512], "x", reads=["mixG"], writes=["mB%d" % (q4 % 2)])
                    dst = mixsel[:, q4 * 4:(q4 + 1) * 4, :]
                    P.op("dve", lambda e, a_=a_, dst=dst: e.tensor_scalar(dst, a_[:], selt[:, 0:1], None, op0=ALU.mult), reads=["mA%d" % (q4 % 2), "selt"], writes=["mixsel%d" % q4])
                    P.op("dve", lambda e, b_=b_, dst=dst: e.scalar_tensor_tensor(out=dst, in0=b_[:], scalar=selt[:, 1:2], in1=dst, op0=ALU.mult, op1=ALU.add),
                         reads=["mB%d" % (q4 % 2), "selt", "mixsel%d" % q4], writes=["mixsel%d" % q4])
                msk_all = ["mixsel%d" % q4 for q4 in range(8)]
                P.dma("sp", wo[0][:], w_out_b[0], "x", reads=["w_out_b0"], writes=["wo0"])
                it = 0
                for n in range(8):
                    if n + 1 < 8:
                        P.dma("sp", wo[(n + 1) % 2][:], w_out_b[n + 1], "x", reads=["w_out_b%d" % (n + 1)], writes=["wo%d" % ((n + 1) % 2)])
                    w_ = wo[n % 2]; wk = "wo%d" % (n % 2)
                    for m in range(4):
                        b = 2 + (it % 4); it += 1
                        x_ = xp[it % 3]; xk = "xp%d" % (it % 3); y_ = yp[it % 3]; yk = "yp%d" % (it % 3)
                        P.dma("sp", x_[:], x_h[tok0 + m * 128:tok0 + (m + 1) * 128, n * 512:(n + 1) * 512], "x", writes=[xk])
                        for kc in range(32):
                            P.op("pe", lambda e, b=b, w_=w_, kc=kc, m=m: e.matmul(psb[b][:, :], mixsel[:, kc, m * 128:(m + 1) * 128], w_[:, kc, :], start=(kc == 0), stop=(kc == 31)),
                                 reads=[wk] + (msk_all if kc in (0, 31) else []), writes=["ps%d" % b])
                        P.op("dve", lambda e, b=b, y_=y_, n=n: e.tensor_tensor(y_[:], psb[b][:, :], g1b[:, n * 512:(n + 1) * 512], op=ALU.mult), reads=["ps%d" % b, "g1b"], writes=[yk])
                        P.op("pool", lambda e, y_=y_, x_=x_: e.tensor_tensor(y_[:], y_[:], x_[:], op=ALU.add), reads=[yk, xk], writes=[yk])
                        P.op("act", lambda e, y_=y_, m=m, n=n: e.activation(jk[:], y_[:], AF.Square, accum_out=ss1[:, m, n:n + 1]), reads=[yk], writes=["jk", "ss1_%d_%d" % (m, n)])
                        P.dma("sp", x1s[tok0 + m * 128:tok0 + (m + 1) * 128, n * 512:(n + 1) * 512], y_[:], "x", reads=[yk], writes=["x1s"])
                for m in range(4):
                    P.op("dve", lambda e, m=m: e.reduce_sum(rst[:, m, 0:1], ss1[:, m, :], axis=AX.X), reads=["ss1_%d_%d" % (m, n) for n in range(8)], writes=["rst%d" % m])
                    P.op("act", lambda e, m=m: e.activation(rst[:, m, 1:2], rst[:, m, 0:1], AF.Sqrt, bias=1e-6, scale=1.0 / D), reads=["rst%d" % m], writes=["rst%d" % m])
                    P.op("dve", lambda e, m=m: e.reciprocal(rst[:, m, 2:3], rst[:, m, 1:2]), reads=["rst%d" % m], writes=["rst%d" % m])
            P.barrier()
            with ExitStack() as st, ExitStack() as sth:
                hidT = P.sb("p4_hidT", [128, 128, 512], BF16, st)
                h2T = P.sb("p4_h2T", [128, 32, 512], BF16, sth)
                with ExitStack() as st2:
                    w2b = P.sb("p4_w2b", [128, 2048], F32, st2)
                    sh2b = P.sb("p4_sh2b", [128, 2048], F32, st2)
                    xr = P.sb("p4_xr", [128, D], F32, st2)
                    hb2 = P.sb("p4_hb2", [128, D], BF16, st2)
                    for m in range(4):
                        P.dma("sp", xr[:], x1s[tok0 + m * 128:tok0 + (m + 1) * 128, :], "x", reads=["x1s"], writes=["xr"])
                        for hf in range(2):
                            cs_ = slice(hf * 2048, (hf + 1) * 2048)
                            P.dma("sp", w2b[:], modv[3:4, cs_].partition_broadcast(128), "x", reads=["modv"], writes=["w2b"])
                            P.dma("sp", sh2b[:], modv[4:5, cs_].partition_broadcast(128), "x", reads=["modv"], writes=["sh2b"])
                            P.op("dve", lambda e, m=m, cs_=cs_: e.scalar_tensor_tensor(out=xr[:, cs_], in0=xr[:, cs_], scalar=rst[:, m, 2:3], in1=w2b[:], op0=ALU.mult, op1=ALU.mult),
                                 reads=["xr", "rst%d" % m, "w2b"], writes=["xr"])
                            P.op("pool", lambda e, cs_=cs_: e.tensor_tensor(hb2[:, cs_], xr[:, cs_], sh2b[:], op=ALU.add), reads=["xr", "sh2b"], writes=["hb2"])
                        for g in range(4):
                            pk = "ps%d" % (g % 2)
                            ptb = psb[g % 2][:].bitcast(BF16)
                            for u in range(8):
                                kc = g * 8 + u
                                P.op("pe", lambda e, ptb=ptb, u=u, kc=kc: e.transpose(ptb[:, u * 128:(u + 1) * 128], hb2[:, kc * 128:(kc + 1) * 128], ident[:]),
                                     reads=["hb2", "ident"], writes=[pk])
                            dstap = h2T[:, g * 8:(g + 1) * 8, m * 128:(m + 1) * 128]
                            srcap = ptb[:, 0:1024].rearrange("p (u t) -> p u t", u=8)
                            P.op("act", lambda e, d=dstap, s_=srcap: e.copy(d, s_), reads=[pk], writes=["h2T%d_%d" % (m, g)])
                P.barrier()
                st3 = ExitStack()
                wu = [P.sb("p4_wu%d" % i, [128, 32, 128], BF16, st3) for i in range(3)]
                rl = [P.sb("p4_rl%d" % i, [128, 512], F32, st3) for i in range(2)]
                P.dma("sp", wu[0][:], w_up_b[0], "x", reads=["w_up_b0"], writes=["wu0"])
                P.dma("sp", wu[1][:], w_up_b[1], "x", reads=["w_up_b1"], writes=["wu1"])
                for fc in range(128):
                    if fc + 2 < 128:
                        P.dma("sp", wu[(fc + 2) % 3][:], w_up_b[fc + 2], "x", reads=["w_up_b%d" % (fc + 2)], writes=["wu%d" % ((fc + 2) % 3)])
                    w_ = wu[fc % 3]; wk = "wu%d" % (fc % 3)
                    b = 2 + fc % 4
                    for kc in range(32):
                        P.op("pe", lambda e, b=b, w_=w_, kc=kc: e.matmul(psb[b][:, :], w_[:, kc, :], h2T[:, kc, :], start=(kc == 0), stop=(kc == 31)),
                             reads=[wk, "h2T"], writes=["ps%d" % b])
                    r_ = rl[fc % 2]; rk = "rl%d" % (fc % 2)
                    P.op("act", lambda e, b=b, r_=r_: e.activation(r_[:], psb[b][:, :], AF.Relu), reads=["ps%d" % b], writes=[rk])
                    P.op("dve", lambda e, r_=r_, fc=fc: e.tensor_tensor(hidT[:, fc, :], r_[:], r_[:], op=ALU.mult), reads=[rk], writes=["hidT"])
                P.barrier()
                st3.close()
                sth.close()
                wd = [P.sb("p4_wd%d" % i, [128, 8, 512], BF16, st) for i in range(3)]
                g2b = P.sb("p4_g2b", [128, D], F32, st)
                xp2 = [P.sb("p4_xq%d" % i, [128, 512], F32, st) for i in range(3)]
                yp2 = [P.sb("p4_yq%d" % i, [128, 512], F32, st) for i in range(3)]
                jk2 = P.sb("p4_jk2", [128, 512], BF16, st)
                ss2 = P.sb("p4_ss2", [128, 4, 8], F32, st)
                P.dma("sp", g2b[:], modv[5:6, :].partition_broadcast(128), "x", reads=["modv"], writes=["g2b"])
                seq = [(n, fg) for n in range(8) for fg in range(16)]
                for i_ in range(2):
                    n_, fg_ = seq[i_]
                    P.dma("sp", wd[i_ % 3][:], w_dn_b[n_, fg_], "x", reads=["w_dn_b%d_%d" % (n_, fg_)], writes=["wd%d" % (i_ % 3)])
                it = 0
                for si, (n, fg) in enumerate(seq):
                    if si + 2 < len(seq):
                        n_, fg_ = seq[si + 2]
                        P.dma("sp", wd[(si + 2) % 3][:], w_dn_b[n_, fg_], "x", reads=["w_dn_b%d_%d" % (n_, fg_)], writes=["wd%d" % ((si + 2) % 3)])
                    w_ = wd[si % 3]; wk = "wd%d" % (si % 3)
                    for m in range(4):
                        b = 2 + m
                        for j in range(8):
                            P.op("pe", lambda e, b=b, w_=w_, j=j, m=m, fg=fg: e.matmul(psb[b][:, :], hidT[:, fg * 8 + j, m * 128:(m + 1) * 128], w_[:, j, :],
                                                                                  start=(fg == 0 and j == 0), stop=(fg == 15 and j == 7)),
                                 reads=[wk, "hidT"], writes=["ps%d" % b])
                    if fg == 15:
                        for m in range(4):
                            b = 2 + m; it += 1
                            x_ = xp2[it % 3]; xk = "xq%d" % (it % 3); y_ = yp2[it % 3]; yk = "yq%d" % (it % 3)
                            rows = slice(tok0 + m * 128, tok0 + (m + 1) * 128); cols = slice(n * 512, (n + 1) * 512)
                            P.dma("sp", x_[:], x1s[rows, cols], "x", reads=["x1s"], writes=[xk])
                            P.op("dve", lambda e, b=b, y_=y_, n=n: e.tensor_tensor(y_[:], psb[b][:, :], g2b[:, n * 512:(n + 1) * 512], op=ALU.mult), reads=["ps%d" % b, "g2b"], writes=[yk])
                            P.op("pool", lambda e, y_=y_, x_=x_: e.tensor_tensor(y_[:], y_[:], x_[:], op=ALU.add), reads=[yk, xk], writes=[yk])
                            P.op("act", lambda e, y_=y_, m=m, n=n: e.activation(jk2[:], y_[:], AF.Square, accum_out=ss2[:, m, n:n + 1]), reads=[yk], writes=["jk2", "ss2_%d_%d" % (m, n)])
                            P.dma("sp", x1s[rows, cols], y_[:], "x", reads=[yk, "x1s"], writes=["x1s"])
                for m in range(4):
                    P.op("dve", lambda e, m=m: e.reduce_sum(rst[:, m, 0:1], ss2[:, m, :], axis=AX.X), reads=["ss2_%d_%d" % (m, n) for n in range(8)], writes=["rst%d" % m])
                    P.op("act", lambda e, m=m: e.activation(rst[:, m, 1:2], rst[:, m, 0:1], AF.Sqrt, bias=1e-6, scale=1.0 / D), reads=["rst%d" % m], writes=["rst%d" % m])
                    P.op("dve", lambda e, m=m: e.reciprocal(rst[:, m, 2:3], rst[:, m, 1:2]), reads=["rst%d" % m], writes=["rst%d" % m])
            P.barrier()
            with ExitStack() as st:
                fnb = P.sb("p4_fnb", [128, D], F32, st)
                xo = [P.sb("p4_xo%d" % i, [128, D], F32, st) for i in range(2)]
                P.dma("sp", fnb[:], modv[6:7, :].partition_broadcast(128), "x", reads=["modv"], writes=["fnb"])
                for m in range(4):
                    x_ = xo[m % 2]; xk = "xo%d" % (m % 2)
                    rows = slice(tok0 + m * 128, tok0 + (m + 1) * 128)
                    P.dma("sp", x_[:], x1s[rows, :], "x", reads=["x1s"], writes=[xk])
                    P.op("dve", lambda e, x_=x_, m=m: e.scalar_tensor_tensor(out=x_[:], in0=x_[:], scalar=rst[:, m, 2:3], in1=fnb[:], op0=ALU.mult, op1=ALU.mult),
                         reads=[xk, "rst%d" % m, "fnb"], writes=[xk])
                    P.dma("sp", out_h[rows, :], x_[:], "x", reads=[xk], writes=["out_h"])
            P.barrier()

    _ph4()
    P.finish()
    P.emit()
    return nc


def core_inputs(inp, b, hh, consts):
    f32 = np.float32
    d = {}
    d["x_b"] = np.ascontiguousarray(inp["x"][b])
    d["x_h"] = np.ascontiguousarray(inp["x"][b, hh * 2048:(hh + 1) * 2048])
    d["cT"] = np.ascontiguousarray(inp["c"][b].reshape(32, 128).T)
    d["ada_w"] = inp["ada_w"][0]
    d["ada_b"] = inp["ada_b"][0][None, :]
    d["n1w"] = inp["norm1_w"][0][None, :]
    d["n2w"] = inp["norm2_w"][0][None, :]
    d["fnw"] = inp["final_norm_w"][None, :]
    cols = w_in_cols(hh)
    wc = np.zeros((D, W_IN_COLS), f32)
    wc[:, :cols.size] = inp["w_in"][0][:, cols]
    d["w_in_c"] = wc
    cwv = inp["gdn_conv_w"][0]
    cw = np.zeros((128, 24, 4), f32)
    for f in range(24):
        kind, hd = f // 8, f % 8
        ch = kind * 2048 + (8 * hh + hd) * 128 + np.arange(128)
        cw[:, f, :] = cwv[:, ch].T
    d["convw"] = cw.reshape(128, 96)
    d["alog"] = inp["gdn_a_log"][0][None, 8 * hh:8 * hh + 8].astype(f32)
    d["dtb"] = inp["gdn_dt_bias"][0][None, 8 * hh:8 * hh + 8].astype(f32)
    d["gnw"] = inp["gdn_norm_w"][0][None, :]
    for s in ("k", "v"):
        d["pos_" + s] = inp["cmp_pos_" + s][0]
        d["w1_" + s] = inp["cmp_w1_" + s][0]
        d["w2_" + s] = inp["cmp_w2_" + s][0]
    d["w_out"] = inp["w_out"][0]
    d["w_up"] = inp["w_up"][0]
    d["w_down"] = inp["w_down"][0]
    sv = np.zeros((128, 2), f32)
    sv[:, hh] = 1.0
    d["selv"] = sv
    for k, v in consts.items():
        d["t_" + k] = v
    kb, cb, nsl = alibi_tables(hh)
    d["t_kb"] = kb
    d["t_cb"] = cb
    d["t_nsl"] = nsl
    return {k: np.ascontiguousarray(v) for k, v in d.items()}


def kernel(**inputs):
    inp = {k: np.asarray(v) for k, v in inputs.items()}
    consts = const_tables()
    nc = build_program()
    in_maps = [core_inputs(inp, cid // 2, cid % 2, consts) for cid in range(8)]
    res = run_bass_kernel_spmd(nc, in_maps, core_ids=list(range(8)))
    out = np.zeros((4, T, D), np.float32)
    for cid in range(8):
        b, hh = cid // 2, cid % 2
        out[b, hh * 2048:(hh + 1) * 2048] = res.results[cid]["out_h"]
    return out
```
